# Optimizing a Trainium2 kernel written in Bass

```python
import math
import jax
import jax.numpy as jnp
from jax import lax
import numpy as np

D_MODEL = 1024
BATCH = 4
SEQ = 8192
DEPTH = 4

GRID_W = 64
CTX_LEN = 256
N_BRANCH = 3
BRANCH_WIDTH = 1024
EPS = 1e-6
CONV_K = 3
SSD_HEADS = 16
SSD_HEAD_DIM = 64
SSD_GROUPS = 2
SSD_STATE = 128
SSD_CHUNK = 128
DIFF_HEADS = 8
DIFF_HEAD_DIM = 64
Q_BLOCK = 128
ROPE_BASE = 10000.0
ML_HEADS = 4
ML_QK_DIM = 128
ML_V_DIM = 256
ML_CHUNK = 128

SSD_BC = 2 * SSD_GROUPS * SSD_STATE
SSD_XBC = BRANCH_WIDTH + SSD_BC
ML_QK = 2 * ML_HEADS * ML_QK_DIM
IN_SIZES = (SSD_XBC, BRANCH_WIDTH, 2 * SSD_HEADS,
            BRANCH_WIDTH, BRANCH_WIDTH, BRANCH_WIDTH, BRANCH_WIDTH,
            ML_QK, BRANCH_WIDTH, BRANCH_WIDTH, BRANCH_WIDTH, 2 * ML_HEADS, 2 * ML_HEADS,
            N_BRANCH * D_MODEL)
D_IN = SSD_XBC + 2 * SSD_HEADS + 8 * BRANCH_WIDTH + ML_QK + 4 * ML_HEADS + N_BRANCH * D_MODEL

kernel_name = 'hybrid_ssd_diffattn_mlstm_prefix_dit'


def rms_norm(x, g):
    xf = x.astype(jnp.float32)
    y = xf * lax.rsqrt(jnp.mean(xf * xf, axis=-1, keepdims=True) + EPS)
    return (y * g.astype(jnp.float32)).astype(x.dtype)


def split_cols(t, sizes):
    idx = [int(i) for i in np.cumsum(sizes)[:-1]]
    return jnp.split(t, idx, axis=-1)


def flip(t):
    return jnp.flip(t, axis=1)


def dwconv(u, w, b):
    ch = u.shape[-1]
    y = lax.conv_general_dilated(u, w.astype(u.dtype)[:, None, :], (1,),
                                 ((CONV_K // 2, CONV_K // 2),),
                                 dimension_numbers=('NWC', 'WIO', 'NWC'),
                                 feature_group_count=ch)
    return y + b.astype(u.dtype)


def axial_rope_tables(n_tokens):
    rows = n_tokens // GRID_W
    row = jnp.repeat(jnp.arange(rows, dtype=jnp.float32), GRID_W)
    col = jnp.tile(jnp.arange(GRID_W, dtype=jnp.float32), rows)
    half = DIFF_HEAD_DIM // 2
    inv_freq = ROPE_BASE ** (-jnp.arange(0, half, 2, dtype=jnp.float32) / half)
    ang_r = row[:, None] * inv_freq
    ang_c = col[:, None] * inv_freq
    cos = jnp.concatenate([jnp.cos(ang_r), jnp.cos(ang_r), jnp.cos(ang_c), jnp.cos(ang_c)], axis=-1)
    sin = jnp.concatenate([jnp.sin(ang_r), jnp.sin(ang_r), jnp.sin(ang_c), jnp.sin(ang_c)], axis=-1)
    return cos, sin


def apply_rope(t, cos, sin):
    tr = t.reshape(t.shape[:-1] + (2, 2, DIFF_HEAD_DIM // 4))
    rot = jnp.stack([-tr[..., 1, :], tr[..., 0, :]], axis=-2).reshape(t.shape)
    return (t * cos[:, None, None, :] + rot * sin[:, None, None, :]).astype(t.dtype)


def ssd_scan(x, dt, A, Bm, Cm, s0):
    b, T, H, P = x.shape
    G, N = Bm.shape[-2], Bm.shape[-1]
    R = H // G
    L = SSD_CHUNK
    nc = T // L
    xd = (x * dt[..., None]).reshape(b, nc, L, G, R, P)
    a = (dt * A).reshape(b, nc, L, G, R)
    acs = jnp.cumsum(jnp.moveaxis(a, 2, -1), axis=-1)
    Bc = Bm.reshape(b, nc, L, G, N)
    Cc = Cm.reshape(b, nc, L, G, N)
    tri = jnp.tril(jnp.ones((L, L), dtype=bool))
    seg = acs[..., :, None] - acs[..., None, :]
    decay = jnp.exp(jnp.where(tri, seg, -jnp.inf))
    scores = jnp.einsum('bclgn,bcsgn->bcgls', Cc, Bc)[:, :, :, None] * decay
    y_diag = jnp.einsum('bcgrls,bcsgrp->bclgrp', scores, xd)
    decay_end = jnp.exp(acs[..., -1:] - acs)
    st = jnp.einsum('bclgn,bcgrl,bclgrp->bcgrpn', Bc, decay_end, xd)
    chunk_decay = jnp.exp(acs[..., -1])

    def step(s, inp):
        st_c, cd_c = inp
        return s * cd_c[..., None, None] + st_c, s

    final, prev = lax.scan(step, s0.reshape(b, G, R, P, N),
                           (jnp.moveaxis(st, 1, 0), jnp.moveaxis(chunk_decay, 1, 0)))
    prev = jnp.moveaxis(prev, 0, 1)
    y_off = jnp.einsum('bclgn,bcgrpn,bcgrl->bclgrp', Cc, prev, jnp.exp(acs))
    y = (y_diag + y_off).reshape(b, T, H, P)
    return y.astype(x.dtype), final.reshape(b, H, P, N)


def ssd_branch(xbc, z, dt_raw, xbc_c, z_c, dt_raw_c, conv_w, conv_b, a_log, dt_bias, d_skip, norm_g):
    A = -jnp.exp(a_log.astype(jnp.float32))

    def prep(xbc, dt_raw):
        b, T, _ = xbc.shape
        u = jax.nn.silu(dwconv(xbc, conv_w, conv_b))
        xs, Bm, Cm = split_cols(u, (BRANCH_WIDTH, SSD_BC // 2, SSD_BC // 2))
        dt = jax.nn.softplus(dt_raw.astype(jnp.float32).reshape(b, T, 2, SSD_HEADS)
                             + dt_bias.astype(jnp.float32))
        return (xs.reshape(b, T, SSD_HEADS, SSD_HEAD_DIM), Bm.reshape(b, T, SSD_GROUPS, SSD_STATE),
                Cm.reshape(b, T, SSD_GROUPS, SSD_STATE), dt)

    xs, Bm, Cm, dt = prep(xbc, dt_raw)
    xsc, Bc, Cc, dtc = prep(xbc_c, dt_raw_c)
    b = xs.shape[0]
    s0 = jnp.zeros((b, SSD_HEADS, SSD_HEAD_DIM, SSD_STATE), jnp.float32)
    yc_f, s_f = ssd_scan(xsc, dtc[:, :, 0], A[0], Bc, Cc, s0)
    yc_b, s_b = ssd_scan(flip(xsc), flip(dtc[:, :, 1]), A[1], flip(Bc), flip(Cc), s0)
    y_f, _ = ssd_scan(xs, dt[:, :, 0], A[0], Bm, Cm, s_f)
    y_b, _ = ssd_scan(flip(xs), flip(dt[:, :, 1]), A[1], flip(Bm), flip(Cm), s_b)

    def finish(xs, y_f, y_b_rev, z):
        bb, T = xs.shape[:2]
        y = (y_f + flip(y_b_rev) + d_skip[:, None].astype(xs.dtype) * xs).reshape(bb, T, BRANCH_WIDTH)
        return rms_norm(y * jax.nn.silu(z), norm_g)

    return finish(xs, y_f, y_b, z), finish(xsc, yc_f, yc_b, z_c)


def diff_attend(qb, k_all, v_all, lam):
    s = jnp.einsum('bqhcd,bkhcd->bhcqk', qb, k_all).astype(jnp.float32)
    p = jax.nn.softmax(s, axis=-1)
    a = p[:, :, 0] - lam * p[:, :, 1]
    return jnp.einsum('bhqk,bkhe->bqhe', a.astype(v_all.dtype), v_all)


def diff_branch(q, k, v, z, q_c, k_c, v_c, z_c, qn_g, kn_g, lam_p, subln_g, lam_init, cos, sin):
    scale = DIFF_HEAD_DIM ** -0.5

    def heads(t, g):
        bb, T, _ = t.shape
        return rms_norm(t.reshape(bb, T, DIFF_HEADS, 2, DIFF_HEAD_DIM), g)

    qn = apply_rope(heads(q, qn_g), cos, sin) * scale
    kn = apply_rope(heads(k, kn_g), cos, sin)
    qcn = heads(q_c, qn_g) * scale
    kcn = heads(k_c, kn_g)
    b, T = q.shape[:2]
    vh = v.reshape(b, T, DIFF_HEADS, 2 * DIFF_HEAD_DIM)
    vch = v_c.reshape(b, v_c.shape[1], DIFF_HEADS, 2 * DIFF_HEAD_DIM)
    lp = lam_p.astype(jnp.float32)
    lam = jnp.exp(jnp.sum(lp[0] * lp[1])) - jnp.exp(jnp.sum(lp[2] * lp[3])) + lam_init
    k_all = jnp.concatenate([kcn, kn], axis=1)
    v_all = jnp.concatenate([vch, vh], axis=1)
    nb = T // Q_BLOCK
    qb = jnp.swapaxes(qn.reshape(b, nb, Q_BLOCK, DIFF_HEADS, 2, DIFF_HEAD_DIM), 0, 1)
    o = lax.map(lambda blk: diff_attend(blk, k_all, v_all, lam), qb)
    o = jnp.swapaxes(o, 0, 1).reshape(b, T, DIFF_HEADS, 2 * DIFF_HEAD_DIM)
    oc = diff_attend(qcn, kcn, vch, lam)

    def finish(o, z):
        o = rms_norm(o, subln_g) * (1.0 - lam_init)
        return o.reshape(o.shape[0], o.shape[1], BRANCH_WIDTH) * jax.nn.silu(z)

    return finish(o, z), finish(oc, z_c)


def mlstm_scan(q, k, v, log_i, log_f, state):
    b, T, H, dk = q.shape
    dv = v.shape[-1]
    L = ML_CHUNK
    nc = T // L
    q = q.reshape(b, nc, L, H, dk)
    k = (k * dk ** -0.5).reshape(b, nc, L, H, dk)
    v = v.reshape(b, nc, L, H, dv)
    li = jnp.moveaxis(log_i.reshape(b, nc, L, H), 2, -1)
    lf = jnp.moveaxis(log_f.reshape(b, nc, L, H), 2, -1)
    bcum = jnp.cumsum(lf, axis=-1)
    b_last = bcum[..., -1]
    g = b_last[..., None] - bcum + li
    m_loc = jnp.max(g, axis=-1)
    w = jnp.exp(g - m_loc[..., None])
    C_loc = jnp.einsum('bchl,bclhk,bclhv->bchkv', w, k, v)
    n_loc = jnp.einsum('bchl,bclhk->bchk', w, k)

    def step(carry, inp):
        C, n, m = carry
        Cl, nl, ml, bl = inp
        m_new = jnp.maximum(bl + m, ml)
        a = jnp.exp(bl + m - m_new)
        s = jnp.exp(ml - m_new)
        return (a[..., None, None] * C + s[..., None, None] * Cl,
                a[..., None] * n + s[..., None] * nl, m_new), (C, n, m)

    final, (Cp, npv, mp) = lax.scan(step, state, (jnp.moveaxis(C_loc, 1, 0), jnp.moveaxis(n_loc, 1, 0),
                                                  jnp.moveaxis(m_loc, 1, 0), jnp.moveaxis(b_last, 1, 0)))
    Cp = jnp.moveaxis(Cp, 0, 1)
    npv = jnp.moveaxis(npv, 0, 1)
    mp = jnp.moveaxis(mp, 0, 1)
    tri = jnp.tril(jnp.ones((L, L), dtype=bool))
    Dm = jnp.where(tri, bcum[..., :, None] - bcum[..., None, :] + li[..., None, :], -jnp.inf)
    e = bcum + mp[..., None]
    m_t = jnp.maximum(e, jnp.max(Dm, axis=-1))
    Wts = jnp.exp(Dm - m_t[..., None]) * jnp.einsum('bclhk,bcshk->bchls', q, k)
    sc = jnp.exp(e - m_t)
    num = (jnp.einsum('bchls,bcshv->bclhv', Wts, v)
           + jnp.einsum('bclhk,bchkv->bclhv', q, Cp) * jnp.swapaxes(sc, -1, -2)[..., None])
    den = jnp.sum(Wts, axis=-1) + jnp.einsum('bclhk,bchk->bchl', q, npv) * sc
    den = jnp.maximum(jnp.abs(den), jnp.exp(-m_t))
    h = num / jnp.swapaxes(den, -1, -2)[..., None]
    return h.reshape(b, T, H, dv).astype(v.dtype), final


def mlstm_branch(qk, v, o, z, ig, fg, qk_c, v_c, o_c, z_c, ig_c, fg_c,
                 conv_w, conv_b, i_bias, f_bias, norm_g):
    def prep(qk, v, ig, fg):
        b, T, _ = qk.shape
        u = jax.nn.silu(dwconv(qk, conv_w, conv_b))
        q, k = jnp.split(u, 2, axis=-1)
        log_i = ig.astype(jnp.float32).reshape(b, T, 2, ML_HEADS) + i_bias.astype(jnp.float32)
        log_f = jax.nn.log_sigmoid(fg.astype(jnp.float32).reshape(b, T, 2, ML_HEADS)
                                   + f_bias.astype(jnp.float32))
        return (q.reshape(b, T, ML_HEADS, ML_QK_DIM), k.reshape(b, T, ML_HEADS, ML_QK_DIM),
                v.reshape(b, T, ML_HEADS, ML_V_DIM), log_i, log_f)

    q, k, vh, li, lf = prep(qk, v, ig, fg)
    qc, kc, vc, lic, lfc = prep(qk_c, v_c, ig_c, fg_c)
    b = q.shape[0]
    s0 = (jnp.zeros((b, ML_HEADS, ML_QK_DIM, ML_V_DIM), jnp.float32),
          jnp.zeros((b, ML_HEADS, ML_QK_DIM), jnp.float32),
          jnp.zeros((b, ML_HEADS), jnp.float32))
    hc_f, st_f = mlstm_scan(qc, kc, vc, lic[:, :, 0], lfc[:, :, 0], s0)
    hc_b, st_b = mlstm_scan(flip(qc), flip(kc), flip(vc), flip(lic[:, :, 1]), flip(lfc[:, :, 1]), s0)
    h_f, _ = mlstm_scan(q, k, vh, li[:, :, 0], lf[:, :, 0], st_f)
    h_b, _ = mlstm_scan(flip(q), flip(k), flip(vh), flip(li[:, :, 1]), flip(lf[:, :, 1]), st_b)

    def finish(h_f, h_b_rev, o, z):
        bb, T = o.shape[:2]
        h = (h_f + flip(h_b_rev)) * jax.nn.sigmoid(o).reshape(bb, T, ML_HEADS, ML_V_DIM)
        h = rms_norm(h, norm_g.reshape(ML_HEADS, ML_V_DIM)).reshape(bb, T, BRANCH_WIDTH)
        return h * jax.nn.silu(z)

    return finish(h_f, h_b, o, z), finish(hc_f, hc_b, o_c, z_c)


def layer_fwd(x, xc, c, c_ctx, cos, sin, lam_init, w_mod, b_mod, norm_g, w_in,
              ssd_conv_w, ssd_conv_b, ssd_a_log, ssd_dt_bias, ssd_d, ssd_norm_g,
              diff_qn_g, diff_kn_g, diff_lambda, diff_subln_g,
              ml_conv_w, ml_conv_b, ml_i_bias, ml_f_bias, ml_norm_g, w_branch, w_out):
    shift, scale, gate = jnp.split(jax.nn.silu(c) @ w_mod + b_mod, 3, axis=-1)
    shift_c, scale_c, gate_c = jnp.split(jax.nn.silu(c_ctx) @ w_mod + b_mod, 3, axis=-1)
    h = rms_norm(x, norm_g) * (1.0 + scale[:, None]) + shift[:, None]
    hc = rms_norm(xc, norm_g) * (1.0 + scale_c) + shift_c
    w_parts = split_cols(w_in, IN_SIZES)
    p = [h @ w for w in w_parts]
    pc = [hc @ w for w in w_parts]
    y_a, yc_a = ssd_branch(p[0], p[1], p[2], pc[0], pc[1], pc[2], ssd_conv_w, ssd_conv_b,
                           ssd_a_log, ssd_dt_bias, ssd_d, ssd_norm_g)
    y_b, yc_b = diff_branch(p[3], p[4], p[5], p[6], pc[3], pc[4], pc[5], pc[6],
                            diff_qn_g, diff_kn_g, diff_lambda, diff_subln_g, lam_init, cos, sin)
    y_c, yc_c = mlstm_branch(p[7], p[8], p[9], p[10], p[11], p[12],
                             pc[7], pc[8], pc[9], pc[10], pc[11], pc[12],
                             ml_conv_w, ml_conv_b, ml_i_bias, ml_f_bias, ml_norm_g)

    def merge(ya, yb, yc, gates):
        g = jax.nn.sigmoid(gates).reshape(gates.shape[:-1] + (N_BRANCH, D_MODEL))
        mixed = (g[..., 0, :] * (ya @ w_branch[0]) + g[..., 1, :] * (yb @ w_branch[1])
                 + g[..., 2, :] * (yc @ w_branch[2]))
        return mixed @ w_out

    x_new = x + gate[:, None] * merge(y_a, y_b, y_c, p[13])
    xc_new = xc + gate_c * merge(yc_a, yc_b, yc_c, pc[13])
    return xc_new, x_new


def setup_inputs(seed: int = 0) -> dict:
    key = jax.random.key(seed)
    ks = jax.random.split(key, 28)
    f32 = jnp.float32

    def nrm(k, shape, s):
        return jax.random.normal(k, shape, f32) * s

    dt0 = jnp.exp(jax.random.uniform(ks[11], (DEPTH, 2, SSD_HEADS), f32,
                                     minval=math.log(1e-3), maxval=math.log(1e-1)))
    return {
        'x': nrm(ks[0], (BATCH, SEQ, D_MODEL), 1.0),
        'c': nrm(ks[1], (BATCH, D_MODEL), 1.0),
        'ctx': nrm(ks[2], (BATCH, CTX_LEN, D_MODEL), 1.0),
        'c_ctx': nrm(ks[3], (D_MODEL,), 1.0),
        'w_mod': nrm(ks[4], (DEPTH, D_MODEL, 3 * D_MODEL), 0.3 * D_MODEL ** -0.5),
        'b_mod': nrm(ks[5], (DEPTH, 3 * D_MODEL), 0.02),
        'norm_g': 1.0 + nrm(ks[6], (DEPTH, D_MODEL), 0.02),
        'w_in': nrm(ks[7], (DEPTH, D_MODEL, D_IN), D_MODEL ** -0.5),
        'ssd_conv_w': nrm(ks[8], (DEPTH, CONV_K, SSD_XBC), CONV_K ** -0.5),
        'ssd_conv_b': nrm(ks[9], (DEPTH, SSD_XBC), 0.02),
        'ssd_a_log': jnp.log(jax.random.uniform(ks[10], (DEPTH, 2, SSD_HEADS), f32, minval=1.0, maxval=16.0)),
        'ssd_dt_bias': dt0 + jnp.log(-jnp.expm1(-dt0)),
        'ssd_d': 1.0 + nrm(ks[12], (DEPTH, SSD_HEADS), 0.02),
        'ssd_norm_g': 1.0 + nrm(ks[13], (DEPTH, BRANCH_WIDTH), 0.02),
        'diff_qn_g': 1.0 + nrm(ks[14], (DEPTH, DIFF_HEAD_DIM), 0.02),
        'diff_kn_g': 1.0 + nrm(ks[15], (DEPTH, DIFF_HEAD_DIM), 0.02),
        'diff_lambda': nrm(ks[16], (DEPTH, 4, DIFF_HEAD_DIM), 0.1),
        'diff_subln_g': 1.0 + nrm(ks[17], (DEPTH, 2 * DIFF_HEAD_DIM), 0.02),
        'ml_conv_w': nrm(ks[18], (DEPTH, CONV_K, ML_QK), CONV_K ** -0.5),
        'ml_conv_b': nrm(ks[19], (DEPTH, ML_QK), 0.02),
        'ml_i_bias': nrm(ks[20], (DEPTH, 2, ML_HEADS), 0.1),
        'ml_f_bias': jnp.linspace(3.0, 6.0, ML_HEADS, dtype=f32) + nrm(ks[21], (DEPTH, 2, ML_HEADS), 0.1),
        'ml_norm_g': 1.0 + nrm(ks[22], (DEPTH, BRANCH_WIDTH), 0.02),
        'w_branch': nrm(ks[23], (DEPTH, N_BRANCH, BRANCH_WIDTH, D_MODEL), BRANCH_WIDTH ** -0.5),
        'w_out': nrm(ks[24], (DEPTH, D_MODEL, D_MODEL), D_MODEL ** -0.5),
    }


def reference(x, c, ctx, c_ctx, w_mod, b_mod, norm_g, w_in, ssd_conv_w, ssd_conv_b, ssd_a_log,
              ssd_dt_bias, ssd_d, ssd_norm_g, diff_qn_g, diff_kn_g, diff_lambda, diff_subln_g,
              ml_conv_w, ml_conv_b, ml_i_bias, ml_f_bias, ml_norm_g, w_branch, w_out):
    cos, sin = axial_rope_tables(x.shape[1])
    xc = ctx
    for l in range(DEPTH):
        lam_init = 0.8 - 0.6 * math.exp(-0.3 * l)
        xc, x = layer_fwd(x, xc, c, c_ctx, cos, sin, lam_init, w_mod[l], b_mod[l], norm_g[l], w_in[l],
                          ssd_conv_w[l], ssd_conv_b[l], ssd_a_log[l], ssd_dt_bias[l], ssd_d[l], ssd_norm_g[l],
                          diff_qn_g[l], diff_kn_g[l], diff_lambda[l], diff_subln_g[l],
                          ml_conv_w[l], ml_conv_b[l], ml_i_bias[l], ml_f_bias[l], ml_norm_g[l],
                          w_branch[l], w_out[l])
    return x
```

```python
import math
import numpy as np
import ml_dtypes
from contextlib import ExitStack
import concourse.bass as bass
import concourse.mybir as mybir
from concourse.bass_utils import run_bass_kernel_spmd

F32 = mybir.dt.float32
BF16 = mybir.dt.bfloat16
AF = mybir.ActivationFunctionType
ALU = mybir.AluOpType
AX = mybir.AxisListType

ENGS = ['pe', 'act', 'dve', 'pool', 'sp']
NDMA = {'sp': 12, 'act': 6, 'pool': 6}
SAME_ENG_SYNC = True
INTERNAL_OK = 'ALL'

DEPTH = 4
D = 1024
TC = 256
TL = 8192
NT = TC + TL
NCH = NT // 128
D_IN = 13872
EPS = 1e-6
FWD_ORDER = list(range(NCH))
BWD_ORDER = [1, 0] + list(range(NCH - 1, 1, -1))


class Res:
    __slots__ = ('name', 'w', 'r', 'pr')

    def __init__(self, name=''):
        self.name = name
        self.w = {}
        self.r = {}
        self.pr = {}


class Prog:
    def __init__(self, nc):
        self.nc = nc
        self.ops = {e: [] for e in ENGS}
        self.cnt = {e: 0 for e in ENGS}
        self.known = {e: {} for e in ENGS}
        self.dcnt = {}
        self.dnext = {q: 0 for q in NDMA}
        self.rr = 0
        self.floor = {}

    def barrier(self):
        for e in ENGS:
            if self.cnt[e] > 0:
                self.floor[e] = self.cnt[e]
        for sk, v in self.dcnt.items():
            self.floor[sk] = v

    def _deps(self, eng, reads, writes, extra=(), part=True):
        toks = {}

        def add(sk, v):
            if toks.get(sk, 0) < v:
                toks[sk] = v
        for r in reads:
            for sk, v in r.w.items():
                add(sk, v)
        for w in writes:
            if w.r or not part:
                for sk, v in w.w.items():
                    add(sk, v)
            for sk, v in w.r.items():
                add(sk, v)
            for sk, v in w.pr.items():
                add(sk, v)
        for sk, v in extra:
            add(sk, v)
        for sk, v in self.floor.items():
            add(sk, v)
        waits = []
        kn = self.known[eng]
        for sk, v in toks.items():
            if sk == eng and (eng == 'pe' or not SAME_ENG_SYNC):
                continue
            if kn.get(sk, 0) < v:
                kn[sk] = v
                waits.append((sk, v))
        return waits

    def _mark(self, tok, reads, writes, part=True):
        sk, v = tok
        for w in writes:
            if w.r or not part:
                pr = dict(w.r)
                for k2, v2 in w.w.items():
                    if pr.get(k2, 0) < v2:
                        pr[k2] = v2
                w.pr = pr
                w.w = {sk: v}
                w.r = {}
            elif w.w.get(sk, 0) < v:
                w.w[sk] = v
        for r in reads:
            if r in writes:
                continue
            if r.r.get(sk, 0) < v:
                r.r[sk] = v

    def op(self, eng, fn, reads=(), writes=(), inc=True):
        waits = self._deps(eng, reads, writes)
        if inc:
            self.cnt[eng] += 1
            tok = (eng, self.cnt[eng])
        else:
            tok = (eng, self.cnt[eng] + 1)
        self.ops[eng].append((waits, fn, eng if inc else None, 1))
        self._mark(tok, reads, writes)

    def dma(self, q, out, in_, reads=(), writes=(), **kw):
        if q is None:
            q = ('sp', 'sp', 'act')[self.rr % 3]
            self.rr += 1
        j = self.dnext[q]
        self.dnext[q] = (j + 1) % NDMA[q]
        sk = '%s_d%d' % (q, j)
        prev = self.dcnt.get(sk, 0)
        waits = self._deps(q, reads, writes, extra=[(sk, prev)] if prev else [])
        self.dcnt[sk] = prev + 16
        tok = (sk, prev + 16)
        self.ops[q].append((waits, lambda e: e.dma_start(out=out, in_=in_, **kw), sk, 16))
        self._mark(tok, reads, writes)

    def finish(self):
        nc = self.nc
        sems = {}
        for e in ENGS:
            sems[e] = nc.alloc_semaphore(name='s_' + e)
        for sk in self.dcnt:
            sems[sk] = nc.alloc_semaphore(name='s_' + sk)
        fin = [(e, self.cnt[e]) for e in ENGS if e != 'sp' and self.cnt[e] > 0]
        fin += [(sk, v) for sk, v in self.dcnt.items()]
        handles = {'pe': 'tensor', 'act': 'scalar', 'dve': 'vector', 'pool': 'gpsimd', 'sp': 'sync'}
        ops = self.ops

        def replay(eng):
            def body(e):
                for waits, fn, sk, n in ops[eng]:
                    for wk, wv in waits:
                        e.wait_ge(sems[wk], wv)
                    ins = fn(e)
                    if sk is not None:
                        ins.then_inc(sems[sk], n)
                if eng == 'sp':
                    for wk, wv in fin:
                        e.wait_ge(sems[wk], wv)
            return body

        with nc.Block() as block:
            for eng in ENGS:
                getattr(block, handles[eng])(replay(eng))


class Ops:
    def __init__(self, P):
        self.P = P

    def tt(self, eng, out, in0, in1, op, R, W):
        self.P.op(eng, lambda e: e.tensor_tensor(out=out, in0=in0, in1=in1, op=op), R, W)

    def ts(self, eng, out, in0, s1, s2, op0, op1, R, W):
        if op1 is None:
            self.P.op(eng, lambda e: e.tensor_scalar(out=out, in0=in0, scalar1=s1, scalar2=None, op0=op0), R, W)
        else:
            self.P.op(eng, lambda e: e.tensor_scalar(out=out, in0=in0, scalar1=s1, scalar2=s2, op0=op0, op1=op1), R, W)

    def stt(self, out, in0, scalar, in1, op0, op1, R, W):
        self.P.op('dve', lambda e: e.scalar_tensor_tensor(out=out, in0=in0, scalar=scalar, in1=in1, op0=op0, op1=op1), R, W)

    def act(self, out, in_, func, R, W, bias=None, scale=None, accum=None):
        kw = {}
        if bias is not None:
            kw['bias'] = bias
        if scale is not None:
            kw['scale'] = scale
        if accum is not None:
            kw['accum_out'] = accum
        self.P.op('act', lambda e: e.activation(out=out, in_=in_, func=func, **kw), R, W)

    def copy(self, eng, out, in_, R, W):
        if eng == 'act':
            self.P.op('act', lambda e: e.copy(out=out, in_=in_), R, W)
        else:
            self.P.op(eng, lambda e: e.tensor_copy(out=out, in_=in_), R, W)

    def memset(self, eng, ap, val, W):
        self.P.op(eng, lambda e: e.memset(ap, val), (), W)

    def recip(self, out, in_, R, W):
        self.P.op('dve', lambda e: e.reciprocal(out=out, in_=in_), R, W)

    def reduce(self, out, in_, op, R, W, absval=None):
        self.P.op('dve', lambda e: e.tensor_reduce(out=out, in_=in_, axis=AX.X, op=op, apply_absolute_value=absval), R, W)

    def mm(self, out, lhsT, rhs, start, stop, R, W, inc=True):
        self.P.op('pe', lambda e: e.matmul(out, lhsT, rhs, start=start, stop=stop, skip_group_check=True), R, W, inc=inc)

    def tr(self, out, in_, ident, R, W, inc=True):
        self.P.op('pe', lambda e: e.transpose(out=out, in_=in_, identity=ident), R, W, inc=inc)

    def dma(self, out, in_, R, W, q=None):
        self.P.dma(q, out, in_, R, W)


def bc_mid(ap, shape):
    return ap.unsqueeze(2).to_broadcast(shape)


def bc_lead(ap, shape):
    return ap.unsqueeze(1).to_broadcast(shape)


class B:
    def __init__(self, nc):
        self.nc = nc
        self.P = Prog(nc)
        self.uid = 0

    def name(self, n):
        self.uid += 1
        return '%s_%d' % (n, self.uid)

    def sb(self, es, n, shape, dt):
        t = es.enter_context(self.nc.sbuf_tensor(self.name(n), shape, dt))
        return t, Res(n)

    def ps(self, es, n, shape, dt=F32):
        t = es.enter_context(self.nc.psum_tensor(self.name(n), shape, dt))
        return t, Res(n)

    def dram(self, n, shape, dt, nres=1):
        t = self.nc.dram_tensor(self.name(n), shape, dt, kind="Internal").ap()
        return t, [Res(n) for _ in range(nres)]


def blk_of(t0, n):
    return list(range(t0 // 512, (t0 + n - 1) // 512 + 1))


NBLK = (NT + 511) // 512
TBLKS = [(0, 256)] + [(256 + 512 * i, 512) for i in range(16)]


def rs_of(rl, t0, n):
    return [rl[i] for i in blk_of(t0, n)]


C_XBC = 0
C_ZS = 1536
C_DT = 2560
C_Q = 2592
C_K = 3616
C_V = 4640
C_ZA = 5664
C_MQK = 6688
C_MV = 7712
C_MO = 8736
C_MZ = 9760
C_IG = 10784
C_FG = 10792
C_GT = 10800


def build_program(n_layers, dbg=None, only=None):
    dbg = dbg or ()
    scr_cache = {}
    nc = bass.Bass("TRN2", target_bir_lowering=False)
    b = B(nc)
    P = b.P
    L = n_layers

    def inp(n, shape, dt=F32):
        return nc.dram_tensor(n, shape, dt, kind="ExternalInput").ap()

    xin = inp("xin", [NT, D])
    cc = inp("cc", [128, 16])
    ident_in = inp("ident", [128, 128])
    masks_in = inp("masks", [128, 4, 128])
    cos_in = inp("cosb", [TL, 64])
    sin_in = inp("sinb", [TL, 64])
    lami_in = inp("lami", [128, L])
    w_mod = inp("w_mod", [L, D, 3 * D])
    bmod = inp("bmod", [128, L, 24])
    ng = inp("ng", [128, L, 8])
    w_in = inp("w_in", [L, D, D_IN])
    cw = inp("cw", [128, L, 12, 3])
    cb = inp("cb", [128, L, 12])
    alog = inp("alog", [128, L, 32])
    dtb = inp("dtb", [128, L, 32])
    dsk = inp("dsk", [128, L, 16])
    sng = inp("sng", [128, L, D])
    gq = inp("gq", [128, L, 64])
    gk = inp("gk", [128, L, 64])
    lamp = inp("lamp", [128, L, 4, 64])
    subg = inp("subg", [128, L, 8])
    mcw = inp("mcw", [128, L, 8, 3])
    mcb = inp("mcb", [128, L, 8])
    ib = inp("ib", [128, L, 8])
    fb = inp("fb", [128, L, 8])
    mng = inp("mng", [128, L, D])
    sel_in = inp("sel", [2, 2, 128])
    bgate = inp("bgate", [2, L, D])
    w_br = inp("w_br", [L, 3, D, D])
    w_out = inp("w_out", [L, D, D])
    xout = nc.dram_tensor("xout", [NT, D], F32, kind="ExternalOutput").ap()

    def scratch(n, shape, dt, nres=NBLK):
        if only is not None and n in only.get('inputs', ()):
            t = nc.dram_tensor("dbg_" + n, shape, dt, kind="ExternalInput").ap()
            return t, [Res(n) for _ in range(nres)]
        if n in dbg or INTERNAL_OK is None or (INTERNAL_OK != 'ALL' and n not in INTERNAL_OK):
            t = nc.dram_tensor("dbg_" + n, shape, dt, kind="ExternalOutput").ap()
            return t, [Res(n) for _ in range(nres)]
        return b.dram(n, shape, dt, nres)

    xbufs = [(xin, [Res('xin') for _ in range(NBLK)])]
    for l in range(1, L):
        xbufs.append(scratch("xres%d" % l, [NT, D], F32))
    xout_res = [Res('xout') for _ in range(NBLK)]

    hT, hT_r = scratch("hT", [8, 128, NT], BF16)
    xbcT, xbcT_r = scratch("xbcT", [12, 128, NT], BF16)
    mqkT, mqkT_r = scratch("mqkT", [8, 128, NT], BF16)
    zs, zs_r = scratch("zs", [NT, 1024], BF16)
    small, small_r = scratch("small", [NT, 48], F32)
    aqk, aqk_r = scratch("aqk", [NT, 2048], BF16)
    avz, avz_r = scratch("avz", [NT, 1024], BF16)
    azT, azT_r = scratch("azT", [8, 128, NT], BF16)
    mvo, mvo_r = scratch("mvo", [NT, 2048], BF16)
    mz, mz_r = scratch("mz", [NT, 1024], BF16)
    gtT, gtT_r = scratch("gtT", [24, 128, NT], BF16)
    yT, yT_r = scratch("yT", [3, 8, 128, NT], BF16)

    with ExitStack() as g_es:
        identf, identf_r = b.sb(g_es, "identf", [128, 128], F32)
        identb, identb_r = b.sb(g_es, "identb", [128, 128], BF16)
        masks, masks_r = b.sb(g_es, "masks", [128, 4, 128], F32)
        onesb, onesb_r = b.sb(g_es, "onesb", [128, 128], BF16)
        onesf, onesf_r = b.sb(g_es, "onesf", [128, 128], F32)
        gbc, gbc_r = b.sb(g_es, "gbc", [128, L, 2, D], F32)
        modt, modt_r = b.sb(g_es, "modt", [128, L, 24, 2], F32)
        a1t, a1t_r = b.sb(g_es, "a1t", [128, L, 8, 2], F32)
        P.dma('sp', identf[:], ident_in[:, :], writes=[identf_r])
        P.dma('sp', masks[:], masks_in[:, :, :], writes=[masks_r])
        P.op('dve', lambda e: e.tensor_copy(out=identb[:], in_=identf[:]), reads=[identf_r], writes=[identb_r])
        P.op('dve', lambda e: e.memset(onesb[:], 1.0), writes=[onesb_r])
        P.op('dve', lambda e: e.memset(onesf[:], 1.0), writes=[onesf_r])

        with ExitStack() as es:
            cct, cct_r = b.sb(es, "cct", [128, 16], F32)
            sct, sct_r = b.sb(es, "sct", [128, 16], F32)
            bmt, bmt_r = b.sb(es, "bmt", [128, L, 24], F32)
            ngt, ngt_r = b.sb(es, "ngt", [128, L, 8], F32)
            wm, wm_r = b.sb(es, "wm", [128, 8, 3 * D], F32)
            pm, pm_r = b.ps(es, "pm", [128, 512], F32)
            pgr, pgr_r = b.ps(es, "pgr", [128, 2, 512], F32)
            pgb, pgb_r = b.ps(es, "pgb", [128, 2, 512], F32)
            selt, selt_r = b.sb(es, "selt", [2, 2, 128], F32)
            bgt, bgt_r = b.sb(es, "bgt", [2, L, D], F32)
            grow, grow_r = b.sb(es, "grow", [2, D], F32)
            P.dma('sp', selt[:], sel_in[:, :, :], writes=[selt_r])
            P.dma('sp', bgt[:], bgate[:, :, :], writes=[bgt_r])
            P.dma('sp', cct[:], cc[:, :], writes=[cct_r])
            P.dma('sp', bmt[:], bmod[:, :, :], writes=[bmt_r])
            P.dma('sp', ngt[:], ng[:, :, :], writes=[ngt_r])
            P.op('act', lambda e: e.activation(out=sct[:], in_=cct[:], func=AF.Silu), reads=[cct_r], writes=[sct_r])
            if 'mod' in dbg:
                dsc = nc.dram_tensor("dbg_sct", [128, 16], F32, kind="ExternalOutput").ap()
                P.dma('sp', dsc[:, :], sct[:], reads=[sct_r], writes=[Res('x')])
            for l in range(L):
                for kc in range(8):
                    P.dma(None, wm[:, kc, :], w_mod[l, kc * 128:(kc + 1) * 128, :], writes=[wm_r])
                for fc in range(24):
                    for kc in range(8):
                        P.op('pe', lambda e, fc=fc, kc=kc: e.matmul(
                            pm[:, fc * 2:fc * 2 + 2], wm[:, kc, fc * 128:(fc + 1) * 128], sct[:, kc * 2:kc * 2 + 2],
                            start=(kc == 0), stop=(kc == 7)),
                            reads=[wm_r, sct_r], writes=[pm_r], inc=(fc == 23 and kc == 7))
                for hf in range(2):
                    for kc in range(8):
                        P.op('pe', lambda e, hf=hf, kc=kc: e.matmul(
                            pgr[0:2, hf, :], sct[:, kc * 2:kc * 2 + 2], wm[:, kc, 2 * D + hf * 512:2 * D + (hf + 1) * 512],
                            start=(kc == 0), stop=(kc == 7)), reads=[wm_r, sct_r], writes=[pgr_r], inc=(kc == 7))
                P.op('dve', lambda e, l=l: e.tensor_tensor(out=grow[:, :].rearrange("p (h f) -> p h f", h=2), in0=pgr[0:2, :, :],
                                                       in1=bgt[:, l, :].rearrange("p (h f) -> p h f", h=2), op=ALU.add),
                     reads=[pgr_r, bgt_r], writes=[grow_r])
                for j in range(2):
                    for hf in range(2):
                        P.op('pe', lambda e, j=j, hf=hf: e.matmul(pgb[:, hf, :], selt[:, j, :], grow[:, hf * 512:(hf + 1) * 512],
                                                                start=True, stop=True), reads=[selt_r, grow_r], writes=[pgb_r])
                    P.op('dve', lambda e, l=l, j=j: e.tensor_copy(out=gbc[:, l, j, :].rearrange("p (h f) -> p h f", h=2), in_=pgb[:, :, :]),
                         reads=[pgb_r], writes=[gbc_r])
                for j in range(2):
                    P.op('dve', lambda e, l=l, j=j: e.tensor_tensor(
                        out=modt[:, l, :, j], in0=pm[:, 0:48].rearrange("p (f j) -> p f j", j=2)[:, :, j],
                        in1=bmt[:, l, :], op=ALU.add), reads=[pm_r, bmt_r], writes=[modt_r])
                for j in range(2):
                    P.op('dve', lambda e, l=l, j=j: e.scalar_tensor_tensor(
                        out=a1t[:, l, :, j], in0=modt[:, l, 8:16, j], scalar=1.0, in1=ngt[:, l, :],
                        op0=ALU.add, op1=ALU.mult), reads=[modt_r, ngt_r], writes=[a1t_r])
            P.barrier()

        if 'mod' in dbg:
            dmod = nc.dram_tensor("dbg_mod", [128, L * 48], F32, kind="ExternalOutput").ap()
            da1 = nc.dram_tensor("dbg_a1", [128, L * 16], F32, kind="ExternalOutput").ap()
            P.dma('sp', dmod[:, :], modt[:].rearrange("p l f j -> p (l f j)"), reads=[modt_r], writes=[Res('x')])
            P.dma('sp', da1[:, :], a1t[:].rearrange("p l f j -> p (l f j)"), reads=[a1t_r], writes=[Res('x')])
        for l in range(L):
            xsrc, xsrc_r = xbufs[l]
            if l + 1 < L:
                xdst, xdst_r = xbufs[l + 1]
                dst_off = 0
            else:
                xdst, xdst_r = xout, xout_res
                dst_off = 0
            layer(b, l, L, locals())

        P.finish()
    return nc


def layer(b, l, L, G):
    cache = G['scr_cache']

    def S(n, shape, dt, nres=NBLK):
        if n not in cache:
            cache[n] = G['scratch'](n, shape, dt, nres)
        return cache[n]
    G['scr'] = S
    only = G.get('only')
    with ExitStack() as es:
        C = layer_consts(b, es, l, G)
        if only is None or 'a' in only:
            phase_a1(b, l, G)
            phase_a2(b, l, G)
        if only is None or 'ssd' in only:
            phase_ssd(b, l, G, C)
        if only is None or 'ml' in only:
            phase_mlstm(b, l, G, C)
        if only is None or 'attn' in only:
            phase_attn(b, l, G, C)
        if only is None or 'merge' in only:
            phase_merge(b, l, G, C)
        b.P.barrier()


def phase_a1(b, l, G):
    P = b.P
    xsrc, xsrc_r = G['xsrc'], G['xsrc_r']
    hT, hT_r = G['hT'], G['hT_r']
    identb, identb_r = G['identb'], G['identb_r']
    modt, modt_r, a1t, a1t_r = G['modt'], G['modt_r'], G['a1t'], G['a1t_r']
    with ExitStack() as es:
        xt = [b.sb(es, "xt", [128, D], F32) for _ in range(2)]
        junk, junk_r = b.sb(es, "junk", [128, D], BF16)
        xnb = [b.sb(es, "xnb", [128, D], BF16) for _ in range(2)]
        ss = [b.sb(es, "ss", [128, 1], F32) for _ in range(2)]
        pT = [b.ps(es, "pT", [128, D], BF16) for _ in range(2)]
        hb = [b.sb(es, "hb", [128, 8, 512], BF16) for _ in range(2)]
        for bi, (t0, bw) in enumerate(TBLKS):
            hbt, hb_r = hb[bi % 2]
            j = 1 if t0 < TC else 0
            for ti in range(bw // 128):
                i = (t0 // 128 + ti)
                xtt, xt_r = xt[i % 2]
                xn, xn_r = xnb[i % 2]
                st, st_r = ss[i % 2]
                pt, pt_r = pT[i % 2]
                P.dma(None, xtt[:], xsrc[i * 128:(i + 1) * 128, :], reads=rs_of(xsrc_r, i * 128, 128), writes=[xt_r])
                P.op('act', lambda e, xtt=xtt, st=st: e.activation(out=junk[:], in_=xtt[:], func=AF.Square, accum_out=st[:]),
                     reads=[xt_r], writes=[junk_r, st_r])
                P.op('act', lambda e, st=st: e.activation(out=st[:], in_=st[:], func=AF.Sqrt, scale=1.0 / D, bias=EPS),
                     reads=[st_r], writes=[st_r])
                P.op('dve', lambda e, st=st: e.reciprocal(out=st[:], in_=st[:]), reads=[st_r], writes=[st_r])
                P.op('dve', lambda e, xn=xn, xtt=xtt, st=st: e.tensor_scalar(out=xn[:], in0=xtt[:], scalar1=st[:, 0:1], scalar2=None, op0=ALU.mult),
                     reads=[xt_r, st_r], writes=[xn_r])
                for kc in range(8):
                    P.op('pe', lambda e, kc=kc, pt=pt, xn=xn: e.transpose(out=pt[:, kc * 128:(kc + 1) * 128], in_=xn[:, kc * 128:(kc + 1) * 128], identity=identb[:]),
                         reads=[xn_r, identb_r], writes=[pt_r], inc=(kc == 7))
                for kc in range(8):
                    P.op('dve', lambda e, kc=kc, pt=pt, hbt=hbt, ti=ti, j=j: e.tensor_scalar(
                        out=hbt[:, kc, ti * 128:(ti + 1) * 128], in0=pt[:, kc * 128:(kc + 1) * 128],
                        scalar1=a1t[:, l, kc, j:j + 1], scalar2=modt[:, l, kc, j:j + 1], op0=ALU.mult, op1=ALU.add),
                        reads=[pt_r, a1t_r, modt_r], writes=[hb_r])
            P.dma(None, hT[:, :, t0:t0 + bw].rearrange("k p t -> p k t"), hbt[:, :, 0:bw],
                  reads=[hb_r], writes=rs_of(hT_r, t0, bw))
        P.barrier()


def gemm_jobs():
    jobs = []
    jobs.append(('F', C_XBC, 1536, 'xbcT'))
    jobs.append(('F', C_MQK, 1024, 'mqkT'))
    jobs.append(('T', C_ZS, 1024, 'zs', 0))
    jobs.append(('T', C_Q, 2048, 'aqk', 0))
    jobs.append(('T', C_V, 1024, 'avz', 0))
    jobs.append(('F', C_ZA, 1024, 'azT'))
    jobs.append(('T', C_MV, 2048, 'mvo', 0))
    jobs.append(('T', C_MZ, 1024, 'mz', 0))
    jobs.append(('F', C_GT, 2048, 'gtT', 0))
    jobs.append(('F', C_GT + 2048, 1024, 'gtT', 16))
    return jobs


def phase_a2(b, l, G):
    P = b.P
    w_in = G['w_in']
    hT, hT_r = G['hT'], G['hT_r']
    with ExitStack() as es:
        stg = [b.sb(es, "stg", [128, 2048], F32) for _ in range(2)]
        Wg = [b.sb(es, "Wg", [128, 8, 2048], BF16) for _ in range(2)]
        hb = [b.sb(es, "hb2", [128, 8, 512], BF16) for _ in range(2)]
        ob = [b.sb(es, "ob", [128, 2048], BF16) for _ in range(2)]
        obs = [b.sb(es, "obs", [128, 48], F32) for _ in range(2)]
        pg = [b.ps(es, "pg", [128, 512], F32) for _ in range(4)]
        ws, ws_r = b.sb(es, "ws", [128, 8, 48], BF16)
        cnt = {'stg': 0, 'w': 0, 'hb': 0, 'ob': 0, 'pg': 0, 'ev': 0}

        def load_w(c0, n):
            wt, wt_r = Wg[cnt['w'] % 2]
            cnt['w'] += 1
            for kc in range(8):
                st, st_r = stg[cnt['stg'] % 2]
                cnt['stg'] += 1
                P.dma(None, st[:, 0:n], w_in[l, kc * 128:(kc + 1) * 128, c0:c0 + n], writes=[st_r])
                P.op('pool', lambda e, wt=wt, st=st, kc=kc, n=n: e.tensor_copy(out=wt[:, kc, 0:n], in_=st[:, 0:n]),
                     reads=[st_r], writes=[wt_r])
            return wt, wt_r

        def load_h(t0, bw):
            ht, ht_r = hb[cnt['hb'] % 2]
            cnt['hb'] += 1
            P.dma(None, ht[:, :, 0:bw], hT[:, :, t0:t0 + bw].rearrange("k p t -> p k t"),
                  reads=rs_of(hT_r, t0, bw), writes=[ht_r])
            return ht, ht_r

        def evac(out_ap, in_ap, reads, writes):
            eng = 'act' if cnt['ev'] % 2 == 0 else 'dve'
            cnt['ev'] += 1
            if eng == 'act':
                P.op('act', lambda e: e.copy(out=out_ap, in_=in_ap), reads=reads, writes=writes)
            else:
                P.op('dve', lambda e: e.tensor_copy(out=out_ap, in_=in_ap), reads=reads, writes=writes)

        for job in gemm_jobs():
            mode, c0, n = job[0], job[1], job[2]
            dst, dst_r = G[job[3]], G[job[3] + '_r']
            wt, wt_r = load_w(c0, n)
            for (t0, bw) in TBLKS:
                ht, ht_r = load_h(t0, bw)
                if mode == 'F':
                    for ch in range(n // 128):
                        pt, pt_r = pg[cnt['pg'] % 4]
                        cnt['pg'] += 1
                        for kc in range(8):
                            P.op('pe', lambda e, pt=pt, wt=wt, ht=ht, kc=kc, ch=ch, bw=bw: e.matmul(
                                pt[:, 0:bw], wt[:, kc, ch * 128:(ch + 1) * 128], ht[:, kc, 0:bw],
                                start=(kc == 0), stop=(kc == 7)), reads=[wt_r, ht_r], writes=[pt_r], inc=(kc == 7))
                        if ch % 4 == 0:
                            ot, ot_r = ob[cnt['ob'] % 2]
                            cnt['ob'] += 1
                        evac(ot[:, (ch % 4) * 512:(ch % 4) * 512 + bw], pt[:, 0:bw], [pt_r], [ot_r])
                        if ch % 4 == 3 or ch == n // 128 - 1:
                            cb0 = ch - (ch % 4) + (job[4] if len(job) > 4 else 0)
                            nchk = ch % 4 + 1
                            P.dma(None, dst[cb0:cb0 + nchk, :, t0:t0 + bw].rearrange("c p t -> p c t"),
                                  ot[:, 0:nchk * 512].rearrange("p (c t) -> p c t", c=nchk)[:, :, 0:bw],
                                  reads=[ot_r], writes=rs_of(dst_r, t0, bw))
                else:
                    dc0 = job[4]
                    for ti in range(bw // 128):
                        ot, ot_r = ob[cnt['ob'] % 2]
                        cnt['ob'] += 1
                        for sg in range(n // 512):
                            pt, pt_r = pg[cnt['pg'] % 4]
                            cnt['pg'] += 1
                            for kc in range(8):
                                P.op('pe', lambda e, pt=pt, wt=wt, ht=ht, kc=kc, sg=sg, ti=ti: e.matmul(
                                    pt[:, :], ht[:, kc, ti * 128:(ti + 1) * 128], wt[:, kc, sg * 512:(sg + 1) * 512],
                                    start=(kc == 0), stop=(kc == 7)), reads=[wt_r, ht_r], writes=[pt_r], inc=(kc == 7))
                            evac(ot[:, sg * 512:(sg + 1) * 512], pt[:, :], [pt_r], [ot_r])
                        r0 = t0 + ti * 128
                        P.dma(None, dst[r0:r0 + 128, dc0:dc0 + n], ot[:, 0:n], reads=[ot_r], writes=rs_of(dst_r, r0, 128))

        small, small_r = G['small'], G['small_r']
        for kc in range(8):
            st, st_r = stg[cnt['stg'] % 2]
            cnt['stg'] += 1
            P.dma(None, st[:, 0:32], w_in[l, kc * 128:(kc + 1) * 128, C_DT:C_DT + 32], writes=[st_r])
            P.dma(None, st[:, 32:48], w_in[l, kc * 128:(kc + 1) * 128, C_IG:C_IG + 16], writes=[st_r])
            P.op('pool', lambda e, st=st, kc=kc: e.tensor_copy(out=ws[:, kc, :], in_=st[:, 0:48]), reads=[st_r], writes=[ws_r])
        for (t0, bw) in TBLKS:
            ht, ht_r = load_h(t0, bw)
            for ti in range(bw // 128):
                pt, pt_r = pg[cnt['pg'] % 4]
                cnt['pg'] += 1
                ot, ot_r = obs[cnt['ob'] % 2]
                cnt['ob'] += 1
                for kc in range(8):
                    P.op('pe', lambda e, pt=pt, ht=ht, kc=kc, ti=ti: e.matmul(
                        pt[:, 0:48], ht[:, kc, ti * 128:(ti + 1) * 128], ws[:, kc, :],
                        start=(kc == 0), stop=(kc == 7)), reads=[ws_r, ht_r], writes=[pt_r], inc=(kc == 7))
                evac(ot[:, :], pt[:, 0:48], [pt_r], [ot_r])
                r0 = t0 + ti * 128
                P.dma(None, small[r0:r0 + 128, :], ot[:, :], reads=[ot_r], writes=rs_of(small_r, r0, 128))
        P.barrier()


def rope_tables():
    t = np.arange(TL)
    row = (t // 64).astype(np.float32)
    col = (t % 64).astype(np.float32)
    half = 32
    inv_freq = (10000.0 ** (-np.arange(0, half, 2, dtype=np.float32) / half)).astype(np.float32)
    ang_r = row[:, None] * inv_freq
    ang_c = col[:, None] * inv_freq
    cos = np.concatenate([np.cos(ang_r), np.cos(ang_r), np.cos(ang_c), np.cos(ang_c)], -1).astype(np.float32)
    sin = np.concatenate([np.sin(ang_r), np.sin(ang_r), np.sin(ang_c), np.sin(ang_c)], -1).astype(np.float32)
    sgn = np.tile(np.concatenate([-np.ones(16), np.ones(16)]), 2).astype(np.float32)
    return cos, sin * sgn


def bc(a):
    a = np.asarray(a, np.float32)
    return np.ascontiguousarray(np.broadcast_to(a[None], (128,) + a.shape))


def pmaj(v, nchunk):
    v = np.asarray(v, np.float32)
    lead = v.shape[:-1]
    r = v.reshape(lead + (nchunk, 128))
    r = np.moveaxis(r, -1, 0)
    return np.ascontiguousarray(r)


def make_inputs(inputs, bidx, layers, xin=None):
    L = len(layers)
    f = lambda k: np.asarray(inputs[k], np.float32)
    d = {}
    d['xin'] = np.ascontiguousarray(np.concatenate([f('ctx')[bidx], f('x')[bidx]], 0)) if xin is None else np.ascontiguousarray(xin, np.float32)
    cc = np.stack([f('c')[bidx], f('c_ctx')], -1)
    d['cc'] = np.ascontiguousarray(cc.reshape(8, 128, 2).transpose(1, 0, 2).reshape(128, 16))
    d['ident'] = np.eye(128, dtype=np.float32)
    t = np.arange(128)
    U = (t[:, None] <= t[None, :]).astype(np.float32)
    Lw = (t[:, None] >= t[None, :]).astype(np.float32)
    Sg = (t[:, None] > t[None, :]).astype(np.float32)
    Sl = (t[:, None] < t[None, :]).astype(np.float32)
    d['masks'] = np.ascontiguousarray(np.stack([U, Lw, Sg, Sl], 1))
    cos, sin = rope_tables()
    d['cosb'] = cos
    d['sinb'] = sin
    d['lami'] = bc(np.array([0.8 - 0.6 * math.exp(-0.3 * l) for l in layers], np.float32))
    d['w_mod'] = np.ascontiguousarray(f('w_mod')[layers])
    d['bmod'] = pmaj(f('b_mod')[layers], 24)
    d['ng'] = pmaj(f('norm_g')[layers], 8)
    d['w_in'] = np.ascontiguousarray(f('w_in')[layers])
    d['cw'] = np.ascontiguousarray(pmaj(f('ssd_conv_w')[layers], 12).transpose(0, 1, 3, 2))
    d['cb'] = pmaj(f('ssd_conv_b')[layers], 12)
    d['alog'] = bc(f('ssd_a_log')[layers].reshape(L, 32))
    d['dtb'] = bc(f('ssd_dt_bias')[layers].reshape(L, 32))
    d['dsk'] = bc(f('ssd_d')[layers])
    d['sng'] = bc(f('ssd_norm_g')[layers])
    d['gq'] = bc(f('diff_qn_g')[layers])
    d['gk'] = bc(f('diff_kn_g')[layers])
    d['lamp'] = bc(f('diff_lambda')[layers])
    d['subg'] = np.ascontiguousarray(np.broadcast_to(pmaj(f('diff_subln_g')[layers], 1), (128, L, 8)))
    d['mcw'] = np.ascontiguousarray(pmaj(f('ml_conv_w')[layers], 8).transpose(0, 1, 3, 2))
    d['mcb'] = pmaj(f('ml_conv_b')[layers], 8)
    d['ib'] = bc(f('ml_i_bias')[layers].reshape(L, 8))
    d['fb'] = bc(f('ml_f_bias')[layers].reshape(L, 8))
    d['mng'] = bc(f('ml_norm_g')[layers])
    sel = np.zeros((2, 2, 128), np.float32)
    sel[0, 0] = 1.0
    sel[1, 1] = 1.0
    d['sel'] = sel
    d['bgate'] = np.ascontiguousarray(np.broadcast_to(f('b_mod')[layers][None, :, 2 * D:3 * D], (2, L, D)))
    d['w_br'] = np.ascontiguousarray(f('w_branch')[layers])
    d['w_out'] = np.ascontiguousarray(f('w_out')[layers])
    return d


_NC_CACHE = {}


FUSED = True


def kernel(**inputs):
    nb = 4
    if FUSED:
        if DEPTH not in _NC_CACHE:
            _NC_CACHE[DEPTH] = build_program(DEPTH)
        nc = _NC_CACHE[DEPTH]
        in_maps = [make_inputs(inputs, bidx, list(range(DEPTH))) for bidx in range(nb)]
        res = run_bass_kernel_spmd(nc, in_maps, core_ids=list(range(nb)))
        return np.stack([np.asarray(r["xout"], np.float32)[TC:] for r in res.results], 0)
    if 1 not in _NC_CACHE:
        _NC_CACHE[1] = build_program(1)
    nc = _NC_CACHE[1]
    xcur = [None] * nb
    for l in range(DEPTH):
        in_maps = [make_inputs(inputs, bidx, [l], xin=xcur[bidx]) for bidx in range(nb)]
        res = run_bass_kernel_spmd(nc, in_maps, core_ids=list(range(nb)))
        xcur = [np.asarray(r["xout"], np.float32) for r in res.results]
    return np.stack([x[TC:] for x in xcur], 0)


def layer_consts(b, es, l, G):
    P = b.P
    O = Ops(P)
    C = {}

    def ld(name, src, shape, **kw):
        t, r = b.sb(es, "c_" + name, shape, F32)
        P.dma(None, t[:], src, writes=[r], **kw)
        C[name] = (t, r)
    ld('cw', G['cw'][:, l, :, :], [128, 12, 3])
    ld('cb', G['cb'][:, l, :], [128, 12])
    ld('alog', G['alog'][:, l, :], [128, 32])
    ld('dtb', G['dtb'][:, l, :], [128, 32])
    ld('dsk', G['dsk'][:, l, :], [128, 16])
    ld('sng', G['sng'][:, l, :], [128, D])
    ld('gq', G['gq'][:, l, :], [128, 64])
    ld('gk', G['gk'][:, l, :], [128, 64])
    ld('lamp', G['lamp'][:, l, :, :], [128, 4, 64])
    ld('subg', G['subg'][:, l, :], [128, 8])
    ld('mcw', G['mcw'][:, l, :, :], [128, 8, 3])
    ld('mcb', G['mcb'][:, l, :], [128, 8])
    ld('ib', G['ib'][:, l, :], [128, 8])
    ld('fb', G['fb'][:, l, :], [128, 8])
    ld('mng', G['mng'][:, l, :], [128, D])
    ld('lami', G['lami_in'][:, l:l + 1], [128, 1], allow_slow_non_contiguous=True)
    at, ar = C['alog']
    O.act(at[:], at[:], AF.Exp, [ar], [ar])
    O.ts('dve', at[:], at[:], -1.0, None, ALU.mult, None, [ar], [ar])
    return C


def softplus(b, O, out, x, tmp1, tmp2, R, W):
    O.act(tmp1, x, AF.Abs, R + W, W)
    O.act(tmp1, tmp1, AF.Exp, R + W, W, scale=-1.0)
    O.act(tmp2, tmp1, AF.Ln, R + W, W, bias=1.0)
    O.stt(out, x, 0.0, tmp2, ALU.max, ALU.add, R + W, W)


def seq_bounds(t0, bw):
    return (t0 == 0 or t0 == TC), (t0 + bw == TC or t0 + bw == NT)


def conv_silu(b, O, l, xT, xT_r, t0, bw, nchunk, cwt, cw_r, cbt, cb_r, xh, xh_r, tmps, uT, uT_r):
    P = O.P
    at_start, at_end = seq_bounds(t0, bw)
    lo = t0 if at_start else t0 - 1
    hi = t0 + bw if at_end else t0 + bw + 1
    if at_start:
        O.memset('pool', xh[:, :, 0:1], 0.0, [xh_r])
    if at_end:
        O.memset('pool', xh[:, :, bw + 1:bw + 2], 0.0, [xh_r])
    O.dma(xh[:, :, lo - (t0 - 1):hi - (t0 - 1)], xT[:, :, lo:hi].rearrange("c p t -> p c t"),
          rs_of(xT_r, lo, hi - lo), [xh_r])
    for ch in range(nchunk):
        tm, tm_r = tmps[ch % 2]
        O.ts('dve', tm[:, 0:bw], xh[:, ch, 1:bw + 1], cwt[:, ch, 1:2], None, ALU.mult, None, [xh_r, cw_r], [tm_r])
        O.stt(tm[:, 0:bw], xh[:, ch, 0:bw], cwt[:, ch, 0:1], tm[:, 0:bw], ALU.mult, ALU.add, [xh_r, cw_r, tm_r], [tm_r])
        O.stt(tm[:, 0:bw], xh[:, ch, 2:bw + 2], cwt[:, ch, 2:3], tm[:, 0:bw], ALU.mult, ALU.add, [xh_r, cw_r, tm_r], [tm_r])
        O.act(uT[:, ch, 0:bw], tm[:, 0:bw], AF.Silu, [tm_r, cb_r], [uT_r], bias=cbt[:, ch:ch + 1])


def decay_scalars(b, O, c, a_t, a_r, NA, masks, masks_r, onesf, onesf_r, pc, pc_r, PT, PT_r, E1, E2, E_r,
                  rs, de, cdall, cd_r):
    n2 = 2 * NA
    O.mm(pc[:, 0:n2], masks[:, 0, :], a_t[:, 0:n2], True, True, [masks_r, a_r], [pc_r], inc=False)
    O.mm(pc[:, n2:2 * n2], onesf[:, :], a_t[:, 0:n2], True, True, [onesf_r, a_r], [pc_r])
    O.copy('dve', PT[:, 0:2 * n2], pc[:, 0:2 * n2], [pc_r], [PT_r])
    Pf, Pb = PT[:, 0:NA], PT[:, NA:n2]
    Tf, Tb = PT[:, n2:n2 + NA], PT[:, n2 + NA:2 * n2]
    O.tt('dve', E2[:, NA:n2], Pb, a_t[:, NA:n2], ALU.subtract, [PT_r, a_r], [E_r])
    O.tt('dve', E1[:, NA:n2], Tb, E2[:, NA:n2], ALU.subtract, [PT_r, E_r], [E_r])
    O.tt('dve', E2[:, 0:NA], Tf, Pf, ALU.subtract, [PT_r], [E_r])
    O.copy('dve', E1[:, 0:NA], Pf, [PT_r], [E_r])
    O.act(rs, E1[:, 0:n2], AF.Exp, [E_r], [E_r])
    O.act(de, E2[:, 0:n2], AF.Exp, [E_r], [E_r])
    O.act(cdall[:, c, 0:n2], PT[:, n2:2 * n2], AF.Exp, [PT_r], [cd_r])


def phase_ssd(b, l, G, C):
    P = b.P
    O = Ops(P)
    nc = b.nc
    S = G['scr']
    masks, masks_r, onesf, onesf_r = G['masks'], G['masks_r'], G['onesf'], G['onesf_r']
    identb, identb_r = G['identb'], G['identb_r']
    xbcT, xbcT_r = G['xbcT'], G['xbcT_r']
    small, small_r = G['small'], G['small_r']
    sBT, sBT_r = S('sBT', [2, 128, NT], BF16)
    sCT, sCT_r = S('sCT', [2, 128, NT], BF16)
    sVt, sVt_r = S('sVt', [2, NT, D], BF16)
    sxs, sxs_r = S('sxs', [NT, D], BF16)
    ssm, ssm_r = S('ssm', [NT, 64], F32)
    sprev, sprev_r = S('sprev', [2, NCH, 128, D], BF16, 2 * NCH)
    sLb, sLb_r = S('sLb', [NCH, 128, D], F32, NCH)
    cwt, cw_r = C['cw']
    cbt, cb_r = C['cb']
    At, A_r = C['alog']
    dtbt, dtb_r = C['dtb']

    with ExitStack() as es0:
        cdall, cd_r = b.sb(es0, "cdall", [128, NCH, 32], F32)
        with ExitStack() as es:
            xh, xh_r = b.sb(es, "xh", [128, 12, 514], BF16)
            tmps = [b.sb(es, "ctmp", [128, 512], F32) for _ in range(2)]
            uT, uT_r = b.sb(es, "uT", [128, 12, 512], BF16)
            pX, pX_r = b.ps(es, "pX", [128, D], BF16)
            pB, pB_r = b.ps(es, "pB", [128, 1024], BF16)
            pc, pc_r = b.ps(es, "pc", [128, 512], F32)
            pL, pL_r = b.ps(es, "pL", [128, 4, 512], F32)
            xsb = [b.sb(es, "xsb", [128, D], BF16) for _ in range(2)]
            Btm, Btm_r = b.sb(es, "Btm", [128, 256], BF16)
            sml, sml_r = b.sb(es, "sml", [128, 256], F32)
            smo = [b.sb(es, "smo", [128, 64], F32) for _ in range(2)]
            PT, PT_r = b.sb(es, "PT", [128, 128], F32)
            Et, E_r = b.sb(es, "Et", [128, 128], F32)
            Vt = [b.sb(es, "Vt", [128, 2, D], BF16) for _ in range(2)]
            Vw, Vw_r = b.sb(es, "Vw", [128, 2, D], BF16)
            Sf, Sf_r = b.sb(es, "Sf", [128, D], F32)
            Stmp, Stmp_r = b.sb(es, "Stmp", [128, D], F32)
            Sfb = [b.sb(es, "Sfb", [128, D], BF16) for _ in range(2)]
            Lb = [b.sb(es, "Lb", [128, D], F32) for _ in range(2)]
            O.memset('dve', Sf[:], 0.0, [Sf_r])
            for (t0, bw) in TBLKS:
                conv_silu(b, O, l, xbcT, xbcT_r, t0, bw, 12, cwt, cw_r, cbt, cb_r, xh, xh_r, tmps, uT, uT_r)
                O.dma(sBT[:, :, t0:t0 + bw].rearrange("g p t -> p g t"), uT[:, 8:10, 0:bw], [uT_r], rs_of(sBT_r, t0, bw))
                O.dma(sCT[:, :, t0:t0 + bw].rearrange("g p t -> p g t"), uT[:, 10:12, 0:bw], [uT_r], rs_of(sCT_r, t0, bw))
                for ti in range(bw // 128):
                    c = t0 // 128 + ti
                    r0 = c * 128
                    tsl = slice(ti * 128, (ti + 1) * 128)
                    for ch in range(8):
                        O.tr(pX[:, ch * 128:(ch + 1) * 128], uT[:, ch, tsl], identb[:], [uT_r, identb_r], [pX_r], inc=(ch == 7))
                    for g in range(2):
                        O.tr(pB[:, g * 128:(g + 1) * 128], uT[:, 8 + g, tsl], identb[:], [uT_r, identb_r], [pB_r], inc=(g == 1))
                    xs_t, xs_r = xsb[c % 2]
                    O.copy('act', xs_t[:], pX[:], [pX_r], [xs_r])
                    O.dma(sxs[r0:r0 + 128, :], xs_t[:], [xs_r], rs_of(sxs_r, r0, 128))
                    O.copy('act', Btm[:], pB[:, 0:256], [pB_r], [Btm_r])
                    so, so_r = smo[c % 2]
                    O.dma(sml[:, 0:32], small[r0:r0 + 128, 0:32], rs_of(small_r, r0, 128), [sml_r])
                    O.tt('dve', sml[:, 32:64], sml[:, 0:32], dtbt[:], ALU.add, [sml_r, dtb_r], [sml_r])
                    softplus(b, O, sml[:, 128:160], sml[:, 32:64], sml[:, 64:96], sml[:, 96:128], [], [sml_r])
                    dt = sml[:, 128:160]
                    O.tt('dve', so[:, 0:32], dt, At[:], ALU.mult, [sml_r, A_r], [so_r])
                    decay_scalars(b, O, c, so, so_r, 16, masks, masks_r, onesf, onesf_r, pc, pc_r, PT, PT_r,
                                  Et[:, 0:32], Et[:, 32:64], E_r, so[:, 32:64], Et[:, 64:96], cdall, cd_r)
                    O.dma(ssm[r0:r0 + 128, :], so[:, :], [so_r, E_r], rs_of(ssm_r, r0, 128))
                    O.tt('dve', Et[:, 96:128], dt, Et[:, 64:96], ALU.mult, [sml_r, E_r], [E_r])
                    vt, vt_r = Vt[c % 2]
                    pXv = pX[:].rearrange("p (h e) -> p h e", h=16)
                    for d in range(2):
                        O.tt('dve', vt[:, d, :].rearrange("p (h e) -> p h e", h=16), pXv,
                             bc_mid(sml[:, 128 + d * 16:128 + (d + 1) * 16], [128, 16, 64]), ALU.mult, [pX_r, sml_r], [vt_r])
                        O.tt('dve', Vw[:, d, :].rearrange("p (h e) -> p h e", h=16), pXv,
                             bc_mid(Et[:, 96 + d * 16:96 + (d + 1) * 16], [128, 16, 64]), ALU.mult, [pX_r, E_r], [Vw_r])
                        O.dma(sVt[d, r0:r0 + 128, :], vt[:, d, :], [vt_r], rs_of(sVt_r, r0, 128))
                    for d in range(2):
                        for g in range(2):
                            O.mm(pL[:, d * 2 + g, :], Btm[:, g * 128:(g + 1) * 128], Vw[:, d, g * 512:(g + 1) * 512], True, True,
                                 [Btm_r, Vw_r], [pL_r], inc=(d == 1 and g == 1))
                    sb_t, sb_r = Sfb[c % 2]
                    O.copy('pool', sb_t[:], Sf[:], [Sf_r], [sb_r])
                    O.dma(sprev[0, c, :, :], sb_t[:], [sb_r], [sprev_r[c]])
                    O.tt('dve', Stmp[:].rearrange("p (h e) -> p h e", h=16), Sf[:].rearrange("p (h e) -> p h e", h=16),
                         bc_mid(cdall[:, c, 0:16], [128, 16, 64]), ALU.mult, [Sf_r, cd_r], [Stmp_r])
                    O.tt('dve', Sf[:].rearrange("p (g n) -> p g n", g=2), Stmp[:].rearrange("p (g n) -> p g n", g=2),
                         pL[:, 0:2, :], ALU.add, [Stmp_r, pL_r], [Sf_r])
                    lb_t, lb_r = Lb[c % 2]
                    O.copy('act', lb_t[:].rearrange("p (g n) -> p g n", g=2), pL[:, 2:4, :], [pL_r], [lb_r])
                    O.dma(sLb[c, :, :], lb_t[:], [lb_r], [sLb_r[c]])
            O.memset('dve', Sf[:], 0.0, [Sf_r])
            for c in BWD_ORDER:
                lb_t, lb_r = Lb[c % 2]
                sb_t, sb_r = Sfb[c % 2]
                O.dma(lb_t[:], sLb[c, :, :], [sLb_r[c]], [lb_r])
                O.copy('pool', sb_t[:], Sf[:], [Sf_r], [sb_r])
                O.dma(sprev[1, c, :, :], sb_t[:], [sb_r], [sprev_r[NCH + c]])
                O.tt('dve', Stmp[:].rearrange("p (h e) -> p h e", h=16), Sf[:].rearrange("p (h e) -> p h e", h=16),
                     bc_mid(cdall[:, c, 16:32], [128, 16, 64]), ALU.mult, [Sf_r, cd_r], [Stmp_r])
                O.tt('dve', Sf[:], Stmp[:], lb_t[:], ALU.add, [Stmp_r, lb_r], [Sf_r])
            P.barrier()

        zs, zs_r = G['zs'], G['zs_r']
        dskt, dsk_r = C['dsk']
        sngt, sng_r = C['sng']
        yT, yT_r = G['yT'], G['yT_r']
        with ExitStack() as es:
            xs2 = [b.sb(es, "xs2", [128, D], BF16) for _ in range(2)]
            z2 = [b.sb(es, "z2", [128, D], BF16) for _ in range(2)]
            yacc, yacc_r = b.sb(es, "yacc", [128, D], F32)
            szt, sz_r = b.sb(es, "szt", [128, D], F32)
            ssq, ssq_r = b.sb(es, "ssq", [128, 2], F32)
            ybf, ybf_r = b.sb(es, "ybf", [128, D], BF16)
            yTs = [b.sb(es, "yTs", [128, 8, 128], BF16) for _ in range(2)]
            pYT, pYT_r = b.ps(es, "pYT", [128, D], BF16)

            def load_extra(c):
                r0 = c * 128
                x_t, x_r = xs2[c % 2]
                z_t, z_r = z2[c % 2]
                O.dma(x_t[:], sxs[r0:r0 + 128, :], rs_of(sxs_r, r0, 128), [x_r])
                O.dma(z_t[:], zs[r0:r0 + 128, :], rs_of(zs_r, r0, 128), [z_r])

            def finish(c, Yf, Yf_r, Yb, Yb_r):
                r0 = c * 128
                x_t, x_r = xs2[c % 2]
                z_t, z_r = z2[c % 2]
                O.tt('pool', yacc[:], Yf[:, 0:D], Yb[:, 0:D], ALU.add, [Yf_r, Yb_r], [yacc_r])
                O.tt('dve', szt[:].rearrange("p (h e) -> p h e", h=16), x_t[:].rearrange("p (h e) -> p h e", h=16),
                     bc_mid(dskt[:, :], [128, 16, 64]), ALU.mult, [x_r, dsk_r], [sz_r])
                O.tt('dve', yacc[:], yacc[:], szt[:], ALU.add, [yacc_r, sz_r], [yacc_r])
                O.act(szt[:], z_t[:], AF.Silu, [z_r, sz_r], [sz_r])
                O.tt('dve', yacc[:], yacc[:], szt[:], ALU.mult, [yacc_r, sz_r], [yacc_r])
                O.act(szt[:], yacc[:], AF.Square, [yacc_r, sz_r], [sz_r, ssq_r], accum=ssq[:, 0:1])
                O.act(ssq[:, 1:2], ssq[:, 0:1], AF.Sqrt, [ssq_r], [ssq_r], scale=1.0 / D, bias=EPS)
                O.recip(ssq[:, 1:2], ssq[:, 1:2], [ssq_r], [ssq_r])
                O.stt(ybf[:], yacc[:], ssq[:, 1:2], sngt[:], ALU.mult, ALU.mult, [yacc_r, ssq_r, sng_r], [ybf_r])
                yt, yt_r = yTs[c % 2]
                for k in range(8):
                    O.tr(pYT[:, k * 128:(k + 1) * 128], ybf[:, k * 128:(k + 1) * 128], identb[:], [ybf_r, identb_r], [pYT_r], inc=(k == 7))
                O.copy('act', yt[:].rearrange("p k t -> p (k t)"), pYT[:], [pYT_r], [yt_r])
                O.dma(yT[0, :, :, r0:r0 + 128].rearrange("k p t -> p k t"), yt[:], [yt_r], rs_of(yT_r, r0, 128))

            la_pass2(b, O, G, es, NG=2, NSUB=8, PV=64, QT=sCT, QT_r=sCT_r, KT=sBT, KT_r=sBT_r, g0=0,
                     Vt=sVt, Vt_r=sVt_r, vcol0=0, sm=ssm, sm_r=ssm_r, NA=16, a0=0, sprev=sprev, sprev_r=sprev_r,
                     load_extra=load_extra, finish=finish)
            P.barrier()


def la_pass2(b, O, G, es, NG, NSUB, PV, QT, QT_r, KT, KT_r, g0, Vt, Vt_r, vcol0, sm, sm_r, NA, a0, sprev, sprev_r,
             load_extra, finish):
    P = O.P
    masks, masks_r = G['masks'], G['masks_r']
    GW = NSUB * PV
    NH = NG * NSUB
    W = NG * GW
    qt = [b.sb(es, "qt", [128, NG, 128], BF16) for _ in range(2)]
    kt = [b.sb(es, "kt", [128, NG, 128], BF16) for _ in range(2)]
    vt = [b.sb(es, "vt2", [128, 2, W], BF16) for _ in range(2)]
    smt = [b.sb(es, "smt", [128, 4 * NA], F32) for _ in range(2)]
    spt = [b.sb(es, "spt", [128, 2, W], BF16) for _ in range(2)]
    Gm, Gm_r = b.sb(es, "Gm", [128, 2, NG, 128], F32)
    segL = [b.sb(es, "segL", [128, 4, 128], F32) for _ in range(2)]
    Ee = [b.sb(es, "Ee", [128, 4, 128], F32) for _ in range(2)]
    MT = [b.sb(es, "MT", [128, 4, 128], BF16) for _ in range(2)]
    ytmp, ytmp_r = b.sb(es, "ytmp", [128, NG, GW], F32)
    Yd = [b.sb(es, "Yd", [128, NG, GW], F32) for _ in range(2)]
    pG, pG_r = b.ps(es, "pG", [128, 512], F32)
    pS = [b.ps(es, "pS", [128, 4, 128], F32) for _ in range(2)]
    ydg, ydg_r = b.ps(es, "ydg", [128, NG, 512], F32)
    yof, yof_r = b.ps(es, "yof", [128, NG, 512], F32)
    nseg = 0
    for c in range(NCH):
        r0 = c * 128
        q_t, q_r = qt[c % 2]
        k_t, k_r = kt[c % 2]
        v_t, v_r = vt[c % 2]
        s_t, s_r = smt[c % 2]
        p_t, p_r = spt[c % 2]
        O.dma(q_t[:], QT[g0:g0 + NG, :, r0:r0 + 128].rearrange("g p t -> p g t"), rs_of(QT_r, r0, 128), [q_r])
        O.dma(k_t[:], KT[g0:g0 + NG, :, r0:r0 + 128].rearrange("g p t -> p g t"), rs_of(KT_r, r0, 128), [k_r])
        for d in range(2):
            O.dma(v_t[:, d, :], Vt[d, r0:r0 + 128, vcol0:vcol0 + W], rs_of(Vt_r, r0, 128), [v_r])
            O.dma(p_t[:, d, :], sprev[d, c, :, vcol0:vcol0 + W], [sprev_r[d * NCH + c]], [p_r])
        O.dma(s_t[:], sm[r0:r0 + 128, :], rs_of(sm_r, r0, 128), [s_r])
        load_extra(c)
        for g in range(NG):
            O.mm(pG[:, g * 128:(g + 1) * 128], k_t[:, g, :], q_t[:, g, :], True, True, [k_r, q_r], [pG_r], inc=(g == NG - 1))
        for d in range(2):
            O.tt('dve', Gm[:, d, :, :], pG[:, 0:NG * 128].rearrange("p (g l) -> p g l", g=NG),
                 bc_lead(masks[:, d, :], [128, NG, 128]), ALU.mult, [pG_r, masks_r], [Gm_r])
        for d in range(2):
            y_t, y_r = Yd[d]
            first_in_bank = [True] * NG
            for h0 in range(0, NH, 4):
                nh = min(4, NH - h0)
                sl_t, sl_r = segL[nseg % 2]
                e_t, e_r = Ee[nseg % 2]
                m_t, m_r = MT[nseg % 2]
                ps_t, ps_r = pS[nseg % 2]
                nseg += 1
                for i in range(nh):
                    h = h0 + i
                    col = d * NA + a0 + h
                    O.ts('pool', sl_t[:, i, :], masks[:, 2 + d, :], s_t[:, col:col + 1], None, ALU.mult, None, [masks_r, s_r], [sl_r])
                for i in range(nh):
                    O.mm(ps_t[:, i, :], sl_t[:, i, :], masks[:, d, :], True, True, [sl_r, masks_r], [ps_r], inc=(i == nh - 1))
                O.act(e_t[:, 0:nh, :], ps_t[:, 0:nh, :], AF.Exp, [ps_r], [e_r])
                g = h0 // NSUB
                if NSUB >= 4:
                    O.tt('dve', m_t[:, 0:nh, :], e_t[:, 0:nh, :], bc_lead(Gm[:, d, g, :], [128, nh, 128]), ALU.mult, [e_r, Gm_r], [m_r])
                else:
                    for i in range(nh):
                        gi = (h0 + i) // NSUB
                        O.tt('dve', m_t[:, i, :], e_t[:, i, :], Gm[:, d, gi, :], ALU.mult, [e_r, Gm_r], [m_r])
                for i in range(nh):
                    h = h0 + i
                    gi, sub = h // NSUB, h % NSUB
                    O.mm(ydg[:, gi, sub * PV:(sub + 1) * PV], m_t[:, i, :], v_t[:, d, h * PV:(h + 1) * PV],
                         first_in_bank[gi], True, [m_r, v_r], [ydg_r], inc=(i == nh - 1))
                    first_in_bank[gi] = False
            for g in range(NG):
                O.mm(yof[:, g, 0:GW], q_t[:, g, :], p_t[:, d, g * GW:(g + 1) * GW], True, True, [q_r, p_r], [yof_r], inc=(g == NG - 1))
            rsc = s_t[:, 2 * NA + d * NA + a0:2 * NA + d * NA + a0 + NH]
            O.tt('dve', ytmp[:].rearrange("p g (s e) -> p (g s) e", s=NSUB), yof_view(yof, NG, NSUB, PV),
                 bc_mid(rsc, [128, NH, PV]), ALU.mult, [yof_r, s_r], [ytmp_r])
            O.tt('dve', y_t[:], ytmp[:], ydg[:, :, 0:GW], ALU.add, [ytmp_r, ydg_r], [y_r])
        finish(c, Yd[0][0][:].rearrange("p g w -> p (g w)"), Yd[0][1], Yd[1][0][:].rearrange("p g w -> p (g w)"), Yd[1][1])


def yof_view(yof, NG, NSUB, PV):
    if NSUB * PV == 512:
        return yof[:].rearrange("p g (s e) -> p (g s) e", s=NSUB)
    assert NSUB == 1
    return yof[:, :, 0:PV]


MLW = 4 * 257


def phase_mlstm(b, l, G, C):
    P = b.P
    O = Ops(P)
    S = G['scr']
    masks, masks_r, onesf, onesf_r = G['masks'], G['masks_r'], G['onesf'], G['onesf_r']
    identb, identb_r = G['identb'], G['identb_r']
    mqkT, mqkT_r = G['mqkT'], G['mqkT_r']
    small, small_r = G['small'], G['small_r']
    mvo, mvo_r = G['mvo'], G['mvo_r']
    mz, mz_r = G['mz'], G['mz_r']
    mQT, mQT_r = S('mQT', [4, 128, NT], BF16)
    mKT, mKT_r = S('mKT', [4, 128, NT], BF16)
    mVt, mVt_r = S('mVt', [2, NT, MLW], BF16)
    msm, msm_r = S('msm', [NT, 16], F32)
    mprev, mprev_r = S('mprev', [2, NCH, 128, MLW], BF16, 2 * NCH)
    mLb, mLb_r = S('mLb', [NCH, 128, MLW], F32, NCH)
    mh, mh_r = S('mh', [NT, 512], F32)
    cwt, cw_r = C['mcw']
    cbt, cb_r = C['mcb']
    ibt, ib_r = C['ib']
    fbt, fb_r = C['fb']
    LNS = math.log(128.0 ** -0.5)

    with ExitStack() as es0:
        cdall, cd_r = b.sb(es0, "mcdall", [128, NCH, 8], F32)
        with ExitStack() as es:
            xh, xh_r = b.sb(es, "mxh", [128, 8, 514], BF16)
            tmps = [b.sb(es, "mctmp", [128, 512], F32) for _ in range(2)]
            uT, uT_r = b.sb(es, "muT", [128, 8, 512], BF16)
            pK, pK_r = b.ps(es, "pK", [128, 1024], BF16)
            pc, pc_r = b.ps(es, "mpc", [128, 512], F32)
            pL, pL_r = b.ps(es, "mpL", [128, 4, 512], F32)
            Ktm, Ktm_r = b.sb(es, "Ktm", [128, 512], BF16)
            vin = [b.sb(es, "vin", [128, D], BF16) for _ in range(2)]
            sml, sml_r = b.sb(es, "msml", [128, 128], F32)
            smo = [b.sb(es, "msmo", [128, 16], F32) for _ in range(2)]
            PT, PT_r = b.sb(es, "mPT", [128, 32], F32)
            Et, E_r = b.sb(es, "mEt", [128, 32], F32)
            Vt = [b.sb(es, "mVt", [128, 2, MLW], BF16) for _ in range(2)]
            Vw, Vw_r = b.sb(es, "mVw", [128, 2, MLW], BF16)
            Sf, Sf_r = b.sb(es, "mSf", [128, MLW], F32)
            Stmp, Stmp_r = b.sb(es, "mStmp", [128, MLW], F32)
            Sfb = [b.sb(es, "mSfb", [128, MLW], BF16) for _ in range(2)]
            Lb = [b.sb(es, "mLb", [128, MLW], F32) for _ in range(2)]
            O.memset('dve', Sf[:], 0.0, [Sf_r])
            for (t0, bw) in TBLKS:
                conv_silu(b, O, l, mqkT, mqkT_r, t0, bw, 8, cwt, cw_r, cbt, cb_r, xh, xh_r, tmps, uT, uT_r)
                O.dma(mQT[:, :, t0:t0 + bw].rearrange("g p t -> p g t"), uT[:, 0:4, 0:bw], [uT_r], rs_of(mQT_r, t0, bw))
                O.dma(mKT[:, :, t0:t0 + bw].rearrange("g p t -> p g t"), uT[:, 4:8, 0:bw], [uT_r], rs_of(mKT_r, t0, bw))
                for ti in range(bw // 128):
                    c = t0 // 128 + ti
                    r0 = c * 128
                    tsl = slice(ti * 128, (ti + 1) * 128)
                    for h in range(4):
                        O.tr(pK[:, h * 128:(h + 1) * 128], uT[:, 4 + h, tsl], identb[:], [uT_r, identb_r], [pK_r], inc=(h == 3))
                    O.copy('act', Ktm[:], pK[:, 0:512], [pK_r], [Ktm_r])
                    v_t, v_r = vin[c % 2]
                    O.dma(v_t[:], mvo[r0:r0 + 128, 0:D], rs_of(mvo_r, r0, 128), [v_r])
                    so, so_r = smo[c % 2]
                    O.dma(sml[:, 0:16], small[r0:r0 + 128, 32:48], rs_of(small_r, r0, 128), [sml_r])
                    O.tt('dve', sml[:, 16:24], sml[:, 0:8], ibt[:], ALU.add, [sml_r, ib_r], [sml_r])
                    O.act(sml[:, 16:24], sml[:, 16:24], AF.Exp, [sml_r], [sml_r])
                    O.ts('dve', sml[:, 16:24], sml[:, 16:24], 128.0 ** -0.5, None, ALU.mult, None, [sml_r], [sml_r])
                    O.tt('dve', sml[:, 24:32], sml[:, 8:16], fbt[:], ALU.add, [sml_r, fb_r], [sml_r])
                    O.ts('dve', sml[:, 24:32], sml[:, 24:32], -1.0, None, ALU.mult, None, [sml_r], [sml_r])
                    softplus(b, O, sml[:, 32:40], sml[:, 24:32], sml[:, 40:48], sml[:, 48:56], [], [sml_r])
                    O.ts('dve', so[:, 0:8], sml[:, 32:40], -1.0, None, ALU.mult, None, [sml_r], [so_r])
                    decay_scalars(b, O, c, so, so_r, 4, masks, masks_r, onesf, onesf_r, pc, pc_r, PT, PT_r,
                                  Et[:, 0:8], Et[:, 8:16], E_r, so[:, 8:16], Et[:, 16:24], cdall, cd_r)
                    O.dma(msm[r0:r0 + 128, :], so[:, :], [so_r, E_r], rs_of(msm_r, r0, 128))
                    O.tt('dve', Et[:, 24:32], sml[:, 16:24], Et[:, 16:24], ALU.mult, [sml_r, E_r], [E_r])
                    vt, vt_r = Vt[c % 2]
                    vv = v_t[:].rearrange("p (h e) -> p h e", h=4)
                    for d in range(2):
                        sw = sml[:, 16 + d * 4:16 + (d + 1) * 4]
                        ww = Et[:, 24 + d * 4:24 + (d + 1) * 4]
                        vo = vt[:, d, :].rearrange("p (h e) -> p h e", h=4)
                        wo = Vw[:, d, :].rearrange("p (h e) -> p h e", h=4)
                        O.tt('dve', vo[:, :, 0:256], vv, bc_mid(sw, [128, 4, 256]), ALU.mult, [v_r, sml_r], [vt_r])
                        O.copy('dve', vo[:, :, 256], sw, [sml_r], [vt_r])
                        O.tt('dve', wo[:, :, 0:256], vv, bc_mid(ww, [128, 4, 256]), ALU.mult, [v_r, E_r], [Vw_r])
                        O.copy('dve', wo[:, :, 256], ww, [E_r], [Vw_r])
                        O.dma(mVt[d, r0:r0 + 128, :], vt[:, d, :], [vt_r], rs_of(mVt_r, r0, 128))
                    sb_t, sb_r = Sfb[c % 2]
                    lb_t, lb_r = Lb[c % 2]
                    for d in range(2):
                        for h in range(4):
                            O.mm(pL[:, h, 0:257], Ktm[:, h * 128:(h + 1) * 128], Vw[:, d, h * 257:(h + 1) * 257], True, True,
                                 [Ktm_r, Vw_r], [pL_r], inc=(h == 3))
                        if d == 0:
                            O.copy('pool', sb_t[:], Sf[:], [Sf_r], [sb_r])
                            O.dma(mprev[0, c, :, :], sb_t[:], [sb_r], [mprev_r[c]])
                            O.tt('dve', Stmp[:].rearrange("p (h e) -> p h e", h=4), Sf[:].rearrange("p (h e) -> p h e", h=4),
                                 bc_mid(cdall[:, c, 0:4], [128, 4, 257]), ALU.mult, [Sf_r, cd_r], [Stmp_r])
                            O.tt('dve', Sf[:].rearrange("p (h e) -> p h e", h=4), Stmp[:].rearrange("p (h e) -> p h e", h=4),
                                 pL[:, :, 0:257], ALU.add, [Stmp_r, pL_r], [Sf_r])
                        else:
                            O.copy('act', lb_t[:].rearrange("p (h e) -> p h e", h=4), pL[:, :, 0:257], [pL_r], [lb_r])
                            O.dma(mLb[c, :, :], lb_t[:], [lb_r], [mLb_r[c]])
            O.memset('dve', Sf[:], 0.0, [Sf_r])
            for c in BWD_ORDER:
                lb_t, lb_r = Lb[c % 2]
                sb_t, sb_r = Sfb[c % 2]
                O.dma(lb_t[:], mLb[c, :, :], [mLb_r[c]], [lb_r])
                O.copy('pool', sb_t[:], Sf[:], [Sf_r], [sb_r])
                O.dma(mprev[1, c, :, :], sb_t[:], [sb_r], [mprev_r[NCH + c]])
                O.tt('dve', Stmp[:].rearrange("p (h e) -> p h e", h=4), Sf[:].rearrange("p (h e) -> p h e", h=4),
                     bc_mid(cdall[:, c, 4:8], [128, 4, 257]), ALU.mult, [Sf_r, cd_r], [Stmp_r])
                O.tt('dve', Sf[:], Stmp[:], lb_t[:], ALU.add, [Stmp_r, lb_r], [Sf_r])
            P.barrier()

        mngt, mng_r = C['mng']
        yT, yT_r = G['yT'], G['yT_r']
        for pair in range(2):
            with ExitStack() as es:
                dn, dn_r = b.sb(es, "dn", [128, 8], F32)
                hp, hp_r = b.sb(es, "hp", [128, 2, 256], F32)
                hq, hq_r = b.sb(es, "hq", [128, 2, 256], F32)
                hall = [b.sb(es, "hall", [128, D], F32) for _ in range(2)]
                o2 = [b.sb(es, "o2", [128, D], BF16) for _ in range(2)]
                z2 = [b.sb(es, "mz2", [128, D], BF16) for _ in range(2)]
                sg, sg_r = b.sb(es, "sg", [128, D], F32)
                st4, st4_r = b.sb(es, "st4", [128, 8], F32)
                ybf, ybf_r = b.sb(es, "mybf", [128, D], BF16)
                yTs = [b.sb(es, "myTs", [128, 8, 128], BF16) for _ in range(2)]
                pYT, pYT_r = b.ps(es, "mpYT", [128, D], BF16)

                def load_extra(c, pair=pair):
                    if pair == 0:
                        return
                    r0 = c * 128
                    h_t, h_r = hall[c % 2]
                    o_t, o_r = o2[c % 2]
                    z_t, z_r = z2[c % 2]
                    O.dma(h_t[:, 0:512], mh[r0:r0 + 128, :], rs_of(mh_r, r0, 128), [h_r])
                    O.dma(o_t[:], mvo[r0:r0 + 128, D:2 * D], rs_of(mvo_r, r0, 128), [o_r])
                    O.dma(z_t[:], mz[r0:r0 + 128, :], rs_of(mz_r, r0, 128), [z_r])

                def finish(c, Yf, Yf_r, Yb, Yb_r, pair=pair):
                    r0 = c * 128
                    h_t, h_r = hall[c % 2]
                    Yfv = Yf.rearrange("p (g w) -> p g w", g=2)
                    Ybv = Yb.rearrange("p (g w) -> p g w", g=2)
                    O.act(dn[:, 0:2], Yfv[:, :, 256], AF.Abs, [Yf_r], [dn_r])
                    O.act(dn[:, 2:4], Ybv[:, :, 256], AF.Abs, [Yb_r], [dn_r])
                    O.ts('dve', dn[:, 0:4], dn[:, 0:4], 1.0, None, ALU.max, None, [dn_r], [dn_r])
                    O.recip(dn[:, 4:8], dn[:, 0:4], [dn_r], [dn_r])
                    O.tt('dve', hp[:], Yfv[:, :, 0:256], bc_mid(dn[:, 4:6], [128, 2, 256]), ALU.mult, [Yf_r, dn_r], [hp_r])
                    O.tt('dve', hq[:], Ybv[:, :, 0:256], bc_mid(dn[:, 6:8], [128, 2, 256]), ALU.mult, [Yb_r, dn_r], [hq_r])
                    if pair == 0:
                        O.tt('pool', hp[:], hp[:], hq[:], ALU.add, [hp_r, hq_r], [hp_r])
                        O.dma(mh[r0:r0 + 128, :], hp[:].rearrange("p g e -> p (g e)"), [hp_r], rs_of(mh_r, r0, 128))
                        return
                    o_t, o_r = o2[c % 2]
                    z_t, z_r = z2[c % 2]
                    O.tt('pool', h_t[:, 512:1024], hp[:].rearrange("p g e -> p (g e)"), hq[:].rearrange("p g e -> p (g e)"),
                         ALU.add, [hp_r, hq_r], [h_r])
                    O.act(sg[:], o_t[:], AF.Sigmoid, [o_r], [sg_r])
                    O.tt('dve', h_t[:], h_t[:], sg[:], ALU.mult, [h_r, sg_r], [h_r])
                    O.tt('dve', sg[:], h_t[:], h_t[:], ALU.mult, [h_r, sg_r], [sg_r])
                    O.reduce(st4[:, 0:4], sg[:].rearrange("p (h e) -> p h e", h=4), ALU.add, [sg_r], [st4_r])
                    O.act(st4[:, 4:8], st4[:, 0:4], AF.Sqrt, [st4_r], [st4_r], scale=1.0 / 256, bias=EPS)
                    O.recip(st4[:, 4:8], st4[:, 4:8], [st4_r], [st4_r])
                    O.tt('dve', h_t[:].rearrange("p (h e) -> p h e", h=4), h_t[:].rearrange("p (h e) -> p h e", h=4),
                         bc_mid(st4[:, 4:8], [128, 4, 256]), ALU.mult, [h_r, st4_r], [h_r])
                    O.tt('dve', h_t[:], h_t[:], mngt[:], ALU.mult, [h_r, mng_r], [h_r])
                    O.act(sg[:], z_t[:], AF.Silu, [z_r, sg_r], [sg_r])
                    O.tt('dve', ybf[:], h_t[:], sg[:], ALU.mult, [h_r, sg_r], [ybf_r])
                    yt, yt_r = yTs[c % 2]
                    for k in range(8):
                        O.tr(pYT[:, k * 128:(k + 1) * 128], ybf[:, k * 128:(k + 1) * 128], identb[:], [ybf_r, identb_r], [pYT_r], inc=(k == 7))
                    O.copy('act', yt[:].rearrange("p k t -> p (k t)"), pYT[:], [pYT_r], [yt_r])
                    O.dma(yT[2, :, :, r0:r0 + 128].rearrange("k p t -> p k t"), yt[:], [yt_r], rs_of(yT_r, r0, 128))

                la_pass2(b, O, G, es, NG=2, NSUB=1, PV=257, QT=mQT, QT_r=mQT_r, KT=mKT, KT_r=mKT_r, g0=2 * pair,
                         Vt=mVt, Vt_r=mVt_r, vcol0=2 * pair * 257, sm=msm, sm_r=msm_r, NA=4, a0=2 * pair,
                         sprev=mprev, sprev_r=mprev_r, load_extra=load_extra, finish=finish)
                P.barrier()


def phase_attn(b, l, G, C):
    P = b.P
    O = Ops(P)
    S = G['scr']
    identb, identb_r = G['identb'], G['identb_r']
    onesb, onesb_r = G['onesb'], G['onesb_r']
    aqk, aqk_r = G['aqk'], G['aqk_r']
    avz, avz_r = G['avz'], G['avz_r']
    azT, azT_r = G['azT'], G['azT_r']
    yT, yT_r = G['yT'], G['yT_r']
    cos_in, sin_in = G['cos_in'], G['sin_in']
    aQT, aQT_r = S('aQT', [8, 128, NT], BF16)
    aKT, aKT_r = S('aKT', [8, 128, NT], BF16)
    gqt, gq_r = C['gq']
    gkt, gk_r = C['gk']
    lampt, lamp_r = C['lamp']
    subgt, subg_r = C['subg']
    lamit, lami_r = C['lami']

    with ExitStack() as es0:
        sc, sc_r = b.sb(es0, "asc", [128, 16], F32)
        ggt, gg_r = b.sb(es0, "ggt", [128, 2, 64], F32)
        tmp64, tmp64_r = b.sb(es0, "tmp64", [128, 64], F32)
        O.ts('dve', ggt[:, 0, :], gqt[:], 0.125, None, ALU.mult, None, [gq_r], [gg_r])
        O.copy('dve', ggt[:, 1, :], gkt[:], [gk_r], [gg_r])
        O.reduce(sc[:, 4:6], ggt[:], ALU.max, [gg_r], [sc_r], absval=True)
        O.tt('dve', sc[:, 6:7], sc[:, 4:5], sc[:, 5:6], ALU.mult, [sc_r], [sc_r])
        O.ts('dve', sc[:, 0:1], sc[:, 6:7], -64.0, None, ALU.mult, None, [sc_r], [sc_r])
        for i in range(2):
            O.tt('dve', tmp64[:], lampt[:, 2 * i, :], lampt[:, 2 * i + 1, :], ALU.mult, [lamp_r], [tmp64_r])
            O.reduce(sc[:, 7 + i:8 + i], tmp64[:], ALU.add, [tmp64_r], [sc_r])
        O.act(sc[:, 7:9], sc[:, 7:9], AF.Exp, [sc_r], [sc_r])
        O.tt('dve', sc[:, 9:10], sc[:, 7:8], sc[:, 8:9], ALU.subtract, [sc_r], [sc_r])
        O.tt('dve', sc[:, 1:2], sc[:, 9:10], lamit[:, 0:1], ALU.add, [sc_r, lami_r], [sc_r])
        O.ts('dve', sc[:, 2:3], sc[:, 1:2], -1.0, None, ALU.mult, None, [sc_r], [sc_r])
        O.ts('dve', sc[:, 10:11], lamit[:, 0:1], -1.0, 1.0, ALU.mult, ALU.add, [lami_r], [sc_r])
        O.tt('dve', sc[:, 3:4], sc[:, 10:11], subgt[:, 0:1], ALU.mult, [sc_r, subg_r], [sc_r])

        with ExitStack() as es:
            qk = [b.sb(es, "qkraw", [128, 2048], BF16) for _ in range(2)]
            cs = [b.sb(es, "cs", [128, 2, 64], F32) for _ in range(2)]
            sq, sq_r = b.sb(es, "sq", [128, 2048], F32)
            xn, xn_r = b.sb(es, "xn", [128, 2048], F32)
            t2, t2_r = b.sb(es, "t2", [128, 2048], F32)
            st, st_r = b.sb(es, "ast", [128, 64], F32)
            xr, xr_r = b.sb(es, "xr", [128, 2048], BF16)
            pQ = [b.ps(es, "pQ", [128, 1024], BF16) for _ in range(2)]
            qTt = [b.sb(es, "qTt", [128, 2, 8, 128], BF16) for _ in range(2)]
            for i in range(NCH):
                r0 = i * 128
                q_t, q_r = qk[i % 2]
                O.dma(q_t[:], aqk[r0:r0 + 128, :], rs_of(aqk_r, r0, 128), [q_r])
                O.tt('dve', sq[:], q_t[:], q_t[:], ALU.mult, [q_r], [sq_r])
                O.reduce(st[:, 0:32], sq[:].rearrange("p (g e) -> p g e", e=64), ALU.add, [sq_r], [st_r])
                O.act(st[:, 32:64], st[:, 0:32], AF.Sqrt, [st_r], [st_r], scale=1.0 / 64, bias=EPS)
                O.recip(st[:, 32:64], st[:, 32:64], [st_r], [st_r])
                O.tt('dve', xn[:].rearrange("p (g e) -> p g e", e=64), q_t[:].rearrange("p (g e) -> p g e", e=64),
                     bc_mid(st[:, 32:64], [128, 32, 64]), ALU.mult, [q_r, st_r], [xn_r])
                for hf in range(2):
                    xv = xn[:, hf * 1024:(hf + 1) * 1024].rearrange("p (g e) -> p g e", e=64)
                    O.tt('pool', xv, xv, bc_lead(ggt[:, hf, :], [128, 16, 64]), ALU.mult, [xn_r, gg_r], [xn_r])
                if i >= 2:
                    c_t, c_r = cs[i % 2]
                    lt = r0 - TC
                    O.dma(c_t[:, 0, :], cos_in[lt:lt + 128, :], [], [c_r])
                    O.dma(c_t[:, 1, :], sin_in[lt:lt + 128, :], [], [c_r])
                    xg = xn[:].rearrange("p (g r q e) -> p g r q e", g=32, r=2, q=2, e=16)
                    tg = t2[:].rearrange("p (g r q e) -> p g r q e", g=32, r=2, q=2, e=16)
                    sv = c_t[:, 1, :].rearrange("p (r q e) -> p r q e", r=2, q=2, e=16)
                    for qq in range(2):
                        O.tt('dve', tg[:, :, :, qq, :], xg[:, :, :, 1 - qq, :],
                             sv[:, :, qq, :].unsqueeze(1).to_broadcast([128, 32, 2, 16]), ALU.mult, [xn_r, c_r], [t2_r])
                    O.tt('pool', xn[:].rearrange("p (g e) -> p g e", e=64), xn[:].rearrange("p (g e) -> p g e", e=64),
                         bc_lead(c_t[:, 0, :], [128, 32, 64]), ALU.mult, [xn_r, c_r], [xn_r])
                    O.tt('dve', xr[:], xn[:], t2[:], ALU.add, [xn_r, t2_r], [xr_r])
                else:
                    O.copy('dve', xr[:], xn[:], [xn_r], [xr_r])
                qt_t, qt_r = qTt[i % 2]
                for w in range(2):
                    p_t, p_r = pQ[w]
                    for h in range(8):
                        O.tr(p_t[:, h * 128:(h + 1) * 128], xr[:, w * 1024 + h * 128:w * 1024 + (h + 1) * 128], identb[:],
                             [xr_r, identb_r], [p_r], inc=(h == 7))
                    O.copy('act', qt_t[:, w, :, :].rearrange("p h t -> p (h t)"), p_t[:], [p_r], [qt_r])
                O.dma(aQT[:, :, r0:r0 + 128].rearrange("h p t -> p h t"), qt_t[:, 0, :, :], [qt_r], rs_of(aQT_r, r0, 128))
                O.dma(aKT[:, :, r0:r0 + 128].rearrange("h p t -> p h t"), qt_t[:, 1, :, :], [qt_r], rs_of(aKT_r, r0, 128))
            P.barrier()

        with ExitStack() as es:
            KTs = [b.sb(es, "KTs", [128, NT], BF16) for _ in range(2)]
            Vh = [b.sb(es, "Vh", [128, NCH, 128], BF16) for _ in range(2)]
            qb = [b.sb(es, "qb", [128, 512], BF16) for _ in range(2)]
            zb = [b.sb(es, "zb", [128, 512], BF16) for _ in range(2)]
            Pt = [b.sb(es, "Pt", [128, 2, 512], BF16) for _ in range(3)]
            rc, rc_r = b.sb(es, "rc", [128, 2, 512], F32)
            o0, o0_r = b.sb(es, "o0", [128, 512], F32)
            o1, o1_r = b.sb(es, "o1", [128, 512], F32)
            osq, osq_r = b.sb(es, "osq", [128, 512], BF16)
            rst, rst_r = b.sb(es, "rst", [128, 512], F32)
            szt, sz_r = b.sb(es, "aszt", [128, 512], F32)
            yo = [b.sb(es, "yo", [128, 512], BF16) for _ in range(2)]
            pSc = [b.ps(es, "pSc", [128, 2, 512], F32) for _ in range(2)]
            pO, pO_r = b.ps(es, "pO", [128, 2, 512], F32)
            pSm, pSm_r = b.ps(es, "pSm", [128, 2, 512], F32)
            it = 0
            nb = 0
            for h in range(8):
                k_t, k_r = KTs[h % 2]
                v_t, v_r = Vh[h % 2]
                O.dma(k_t[:], aKT[h, :, :], aKT_r, [k_r])
                for q4 in range(0, NCH, 11):
                    O.dma(v_t[:, q4:q4 + 11, :], avz[q4 * 128:(q4 + 11) * 128, h * 128:(h + 1) * 128].rearrange("(c p) e -> p c e", p=128),
                          rs_of(avz_r, q4 * 128, 11 * 128), [v_r])
                for (t0, bw) in TBLKS:
                    nk = 2 if t0 < TC else NCH
                    q_t, q_r = qb[nb % 2]
                    z_t, z_r = zb[nb % 2]
                    y_t, y_r = yo[nb % 2]
                    nb += 1
                    O.dma(q_t[:, 0:bw], aQT[h, :, t0:t0 + bw], rs_of(aQT_r, t0, bw), [q_r])
                    O.dma(z_t[:, 0:bw], azT[h, :, t0:t0 + bw], rs_of(azT_r, t0, bw), [z_r])
                    for kc in range(nk):
                        ps_t, ps_r = pSc[it % 2]
                        p_t, p_r = Pt[it % 3]
                        it += 1
                        for cm in range(2):
                            O.mm(ps_t[:, cm, 0:bw], k_t[cm * 64:(cm + 1) * 64, kc * 128:(kc + 1) * 128],
                                 q_t[cm * 64:(cm + 1) * 64, 0:bw], True, True, [k_r, q_r], [ps_r], inc=(cm == 1))
                        O.act(p_t[:, :, 0:bw], ps_t[:, :, 0:bw], AF.Exp, [ps_r, sc_r], [p_r], bias=sc[:, 0:1])
                        for cm in range(2):
                            O.mm(pO[:, cm, 0:bw], v_t[:, kc, :], p_t[:, cm, 0:bw], kc == 0, kc == nk - 1, [v_r, p_r], [pO_r], inc=False)
                            O.mm(pSm[:, cm, 0:bw], onesb[:, :], p_t[:, cm, 0:bw], kc == 0, kc == nk - 1, [onesb_r, p_r], [pSm_r], inc=(cm == 1))
                    O.P.op('dve', lambda e, bw=bw: e.reciprocal(out=rc[:, :, 0:bw], in_=pSm[:, :, 0:bw]), [pSm_r], [rc_r])
                    O.tt('dve', o0[:, 0:bw], pO[:, 0, 0:bw], rc[:, 0, 0:bw], ALU.mult, [pO_r, rc_r], [o0_r])
                    O.tt('dve', o1[:, 0:bw], pO[:, 1, 0:bw], rc[:, 1, 0:bw], ALU.mult, [pO_r, rc_r], [o1_r])
                    O.stt(o0[:, 0:bw], o1[:, 0:bw], sc[:, 2:3], o0[:, 0:bw], ALU.mult, ALU.add, [o1_r, o0_r, sc_r], [o0_r])
                    O.act(osq[:, 0:bw], o0[:, 0:bw], AF.Square, [o0_r], [osq_r])
                    ps_t, ps_r = pSc[it % 2]
                    it += 1
                    O.mm(ps_t[:, 0, 0:bw], onesb[:, :], osq[:, 0:bw], True, True, [onesb_r, osq_r], [ps_r])
                    O.act(rst[:, 0:bw], ps_t[:, 0, 0:bw], AF.Sqrt, [ps_r], [rst_r], scale=1.0 / 128, bias=EPS)
                    O.recip(rst[:, 0:bw], rst[:, 0:bw], [rst_r], [rst_r])
                    O.act(szt[:, 0:bw], z_t[:, 0:bw], AF.Silu, [z_r], [sz_r])
                    O.stt(o0[:, 0:bw], o0[:, 0:bw], sc[:, 3:4], rst[:, 0:bw], ALU.mult, ALU.mult, [o0_r, sc_r, rst_r], [o0_r])
                    O.tt('dve', y_t[:, 0:bw], o0[:, 0:bw], szt[:, 0:bw], ALU.mult, [o0_r, sz_r], [y_r])
                    O.dma(yT[1, h, :, t0:t0 + bw], y_t[:, 0:bw], [y_r], rs_of(yT_r, t0, bw))
            P.barrier()


def phase_merge(b, l, G, C):
    P = b.P
    O = Ops(P)
    yT, yT_r = G['yT'], G['yT_r']
    gtT, gtT_r = G['gtT'], G['gtT_r']
    w_br, w_out = G['w_br'], G['w_out']
    xsrc, xsrc_r, xdst, xdst_r, dst_off = G['xsrc'], G['xsrc_r'], G['xdst'], G['xdst_r'], G['dst_off']
    gbc, gbc_r = G['gbc'], G['gbc_r']
    with ExitStack() as es:
        stg = [b.sb(es, "wstg", [128, D], F32) for _ in range(2)]
        Wb, Wb_r = b.sb(es, "Wb", [128, 3, 8, D], BF16)
        Wo, Wo_r = b.sb(es, "Wo", [128, 8, D], BF16)
        yb = [b.sb(es, "yb", [128, 3, 8, 256], BF16) for _ in range(2)]
        gb = [b.sb(es, "gb", [128, 8, 256], BF16) for _ in range(2)]
        sg, sg_r = b.sb(es, "msg", [128, 8, 256], F32)
        macc, macc_r = b.sb(es, "macc", [128, 8, 256], F32)
        mt = [b.sb(es, "mtmp", [128, 512], F32) for _ in range(2)]
        mxT, mxT_r = b.sb(es, "mxT", [128, 8, 256], BF16)
        xin_t = [b.sb(es, "xin_t", [128, D], F32) for _ in range(2)]
        xo_t = [b.sb(es, "xo_t", [128, D], F32) for _ in range(2)]
        pm_ = [b.ps(es, "pmg", [128, 512], F32) for _ in range(4)]
        n = 0
        for br in range(4):
            for kc in range(8):
                st, st_r = stg[n % 2]
                n += 1
                src = w_br[l, br, kc * 128:(kc + 1) * 128, :] if br < 3 else w_out[l, kc * 128:(kc + 1) * 128, :]
                O.dma(st[:], src, [], [st_r])
                dst = Wb[:, br, kc, :] if br < 3 else Wo[:, kc, :]
                O.copy('pool', dst, st[:], [st_r], [Wb_r if br < 3 else Wo_r])
        npm = 0
        nm = 0
        for bi, (t0, bw) in enumerate([(256 * i, 256) for i in range(NT // 256)]):
            y_t, y_r = yb[bi % 2]
            for br in range(3):
                O.dma(y_t[:, br, :, 0:bw], yT[br, :, :, t0:t0 + bw].rearrange("k p t -> p k t"), rs_of(yT_r, t0, bw), [y_r])
            for br in range(3):
                g_t, g_r = gb[(bi * 3 + br) % 2]
                O.dma(g_t[:, :, 0:bw], gtT[br * 8:(br + 1) * 8, :, t0:t0 + bw].rearrange("k p t -> p k t"), rs_of(gtT_r, t0, bw), [g_r])
                O.act(sg[:, :, 0:bw], g_t[:, :, 0:bw], AF.Sigmoid, [g_r], [sg_r])
                for fc in range(8):
                    p_t, p_r = pm_[npm % 4]
                    npm += 1
                    for kc in range(8):
                        O.mm(p_t[:, 0:bw], Wb[:, br, kc, fc * 128:(fc + 1) * 128], y_t[:, br, kc, 0:bw], kc == 0, kc == 7,
                             [Wb_r, y_r], [p_r], inc=(kc == 7))
                    if br == 0:
                        O.tt('dve', macc[:, fc, 0:bw], p_t[:, 0:bw], sg[:, fc, 0:bw], ALU.mult, [p_r, sg_r], [macc_r])
                    else:
                        m_t, m_r = mt[nm % 2]
                        nm += 1
                        O.tt('dve', m_t[:, 0:bw], p_t[:, 0:bw], sg[:, fc, 0:bw], ALU.mult, [p_r, sg_r], [m_r])
                        if br == 1:
                            O.tt('pool', macc[:, fc, 0:bw], macc[:, fc, 0:bw], m_t[:, 0:bw], ALU.add, [macc_r, m_r], [macc_r])
                        else:
                            O.tt('pool', mxT[:, fc, 0:bw], macc[:, fc, 0:bw], m_t[:, 0:bw], ALU.add, [macc_r, m_r], [mxT_r])
            j = 1 if t0 < TC else 0
            for ti in range(bw // 128):
                r0 = t0 + ti * 128
                if r0 < dst_off:
                    continue
                xi, xi_r = xin_t[ti % 2]
                xo, xo_r = xo_t[ti % 2]
                O.dma(xi[:], xsrc[r0:r0 + 128, :], rs_of(xsrc_r, r0, 128), [xi_r])
                for hf in range(2):
                    p_t, p_r = pm_[npm % 4]
                    npm += 1
                    for kc in range(8):
                        O.mm(p_t[:, :], mxT[:, kc, ti * 128:(ti + 1) * 128], Wo[:, kc, hf * 512:(hf + 1) * 512], kc == 0, kc == 7,
                             [mxT_r, Wo_r], [p_r], inc=(kc == 7))
                    O.tt('dve', xo[:, hf * 512:(hf + 1) * 512], p_t[:, :], gbc[:, l, j, hf * 512:(hf + 1) * 512], ALU.mult,
                         [p_r, gbc_r], [xo_r])
                O.tt('pool', xo[:], xo[:], xi[:], ALU.add, [xo_r, xi_r], [xo_r])
                O.dma(xdst[r0 - dst_off:r0 - dst_off + 128, :], xo[:], [xo_r], rs_of(xdst_r, r0, 128))
        P.barrier()
```

```python
import math
import numpy as np
import ml_dtypes
from contextlib import ExitStack
import concourse.bass as bass
import concourse.mybir as mybir
from concourse.bass_utils import run_bass_kernel_spmd

F32 = mybir.dt.float32
BF16 = mybir.dt.bfloat16
AF = mybir.ActivationFunctionType
ALU = mybir.AluOpType
AX = mybir.AxisListType

ENGS = ['pe', 'act', 'dve', 'pool', 'sp']
NDMA = {'sp': 12, 'act': 6, 'pool': 6}
SAME_ENG_SYNC = True
INTERNAL_OK = 'ALL'

DEPTH = 4
D = 1024
TC = 256
TL = 8192
NT = TC + TL
NCH = NT // 128
D_IN = 13872
EPS = 1e-6
FWD_ORDER = list(range(NCH))
BWD_ORDER = [1, 0] + list(range(NCH - 1, 1, -1))


class Res:
    __slots__ = ('name', 'w', 'r', 'pr')

    def __init__(self, name=''):
        self.name = name
        self.w = {}
        self.r = {}
        self.pr = {}


class Prog:
    def __init__(self, nc):
        self.nc = nc
        self.ops = {e: [] for e in ENGS}
        self.cnt = {e: 0 for e in ENGS}
        self.known = {e: {} for e in ENGS}
        self.dcnt = {}
        self.dnext = {q: 0 for q in NDMA}
        self.rr = 0
        self.floor = {}

    def barrier(self):
        for e in ENGS:
            if self.cnt[e] > 0:
                self.floor[e] = self.cnt[e]
        for sk, v in self.dcnt.items():
            self.floor[sk] = v

    def _deps(self, eng, reads, writes, extra=(), part=True):
        toks = {}

        def add(sk, v):
            if toks.get(sk, 0) < v:
                toks[sk] = v
        for r in reads:
            for sk, v in r.w.items():
                add(sk, v)
        for w in writes:
            if w.r or not part:
                for sk, v in w.w.items():
                    add(sk, v)
            for sk, v in w.r.items():
                add(sk, v)
            for sk, v in w.pr.items():
                add(sk, v)
        for sk, v in extra:
            add(sk, v)
        for sk, v in self.floor.items():
            add(sk, v)
        waits = []
        kn = self.known[eng]
        for sk, v in toks.items():
            if sk == eng and (eng == 'pe' or not SAME_ENG_SYNC):
                continue
            if kn.get(sk, 0) < v:
                kn[sk] = v
                waits.append((sk, v))
        return waits

    def _mark(self, tok, reads, writes, part=True):
        sk, v = tok
        for w in writes:
            if w.r or not part:
                pr = dict(w.r)
                for k2, v2 in w.w.items():
                    if pr.get(k2, 0) < v2:
                        pr[k2] = v2
                w.pr = pr
                w.w = {sk: v}
                w.r = {}
            elif w.w.get(sk, 0) < v:
                w.w[sk] = v
        for r in reads:
            if r in writes:
                continue
            if r.r.get(sk, 0) < v:
                r.r[sk] = v

    def op(self, eng, fn, reads=(), writes=(), inc=True):
        waits = self._deps(eng, reads, writes)
        if inc:
            self.cnt[eng] += 1
            tok = (eng, self.cnt[eng])
        else:
            tok = (eng, self.cnt[eng] + 1)
        self.ops[eng].append((waits, fn, eng if inc else None, 1))
        self._mark(tok, reads, writes)

    def dma(self, q, out, in_, reads=(), writes=(), **kw):
        if q is None:
            q = ('sp', 'sp', 'act')[self.rr % 3]
            self.rr += 1
        j = self.dnext[q]
        self.dnext[q] = (j + 1) % NDMA[q]
        sk = '%s_d%d' % (q, j)
        prev = self.dcnt.get(sk, 0)
        waits = self._deps(q, reads, writes, extra=[(sk, prev)] if prev else [])
        self.dcnt[sk] = prev + 16
        tok = (sk, prev + 16)
        self.ops[q].append((waits, lambda e: e.dma_start(out=out, in_=in_, **kw), sk, 16))
        self._mark(tok, reads, writes)

    def finish(self):
        nc = self.nc
        sems = {}
        for e in ENGS:
            sems[e] = nc.alloc_semaphore(name='s_' + e)
        for sk in self.dcnt:
            sems[sk] = nc.alloc_semaphore(name='s_' + sk)
        fin = [(e, self.cnt[e]) for e in ENGS if e != 'sp' and self.cnt[e] > 0]
        fin += [(sk, v) for sk, v in self.dcnt.items()]
        handles = {'pe': 'tensor', 'act': 'scalar', 'dve': 'vector', 'pool': 'gpsimd', 'sp': 'sync'}
        ops = self.ops

        def replay(eng):
            def body(e):
                for waits, fn, sk, n in ops[eng]:
                    for wk, wv in waits:
                        e.wait_ge(sems[wk], wv)
                    ins = fn(e)
                    if sk is not None:
                        ins.then_inc(sems[sk], n)
                if eng == 'sp':
                    for wk, wv in fin:
                        e.wait_ge(sems[wk], wv)
            return body

        with nc.Block() as block:
            for eng in ENGS:
                getattr(block, handles[eng])(replay(eng))


class Ops:
    def __init__(self, P):
        self.P = P

    def tt(self, eng, out, in0, in1, op, R, W):
        self.P.op(eng, lambda e: e.tensor_tensor(out=out, in0=in0, in1=in1, op=op), R, W)

    def ts(self, eng, out, in0, s1, s2, op0, op1, R, W):
        if op1 is None:
            self.P.op(eng, lambda e: e.tensor_scalar(out=out, in0=in0, scalar1=s1, scalar2=None, op0=op0), R, W)
        else:
            self.P.op(eng, lambda e: e.tensor_scalar(out=out, in0=in0, scalar1=s1, scalar2=s2, op0=op0, op1=op1), R, W)

    def stt(self, out, in0, scalar, in1, op0, op1, R, W):
        self.P.op('dve', lambda e: e.scalar_tensor_tensor(out=out, in0=in0, scalar=scalar, in1=in1, op0=op0, op1=op1), R, W)

    def act(self, out, in_, func, R, W, bias=None, scale=None, accum=None):
        kw = {}
        if bias is not None:
            kw['bias'] = bias
        if scale is not None:
            kw['scale'] = scale
        if accum is not None:
            kw['accum_out'] = accum
        self.P.op('act', lambda e: e.activation(out=out, in_=in_, func=func, **kw), R, W)

    def copy(self, eng, out, in_, R, W):
        if eng == 'act':
            self.P.op('act', lambda e: e.copy(out=out, in_=in_), R, W)
        else:
            self.P.op(eng, lambda e: e.tensor_copy(out=out, in_=in_), R, W)

    def memset(self, eng, ap, val, W):
        self.P.op(eng, lambda e: e.memset(ap, val), (), W)

    def recip(self, out, in_, R, W):
        self.P.op('dve', lambda e: e.reciprocal(out=out, in_=in_), R, W)

    def reduce(self, out, in_, op, R, W, absval=None):
        self.P.op('dve', lambda e: e.tensor_reduce(out=out, in_=in_, axis=AX.X, op=op, apply_absolute_value=absval), R, W)

    def mm(self, out, lhsT, rhs, start, stop, R, W, inc=True):
        self.P.op('pe', lambda e: e.matmul(out, lhsT, rhs, start=start, stop=stop, skip_group_check=True), R, W, inc=inc)

    def tr(self, out, in_, ident, R, W, inc=True):
        self.P.op('pe', lambda e: e.transpose(out=out, in_=in_, identity=ident), R, W, inc=inc)

    def dma(self, out, in_, R, W, q=None):
        self.P.dma(q, out, in_, R, W)


def bc_mid(ap, shape):
    return ap.unsqueeze(2).to_broadcast(shape)


def bc_lead(ap, shape):
    return ap.unsqueeze(1).to_broadcast(shape)


class B:
    def __init__(self, nc):
        self.nc = nc
        self.P = Prog(nc)
        self.uid = 0

    def name(self, n):
        self.uid += 1
        return '%s_%d' % (n, self.uid)

    def sb(self, es, n, shape, dt):
        t = es.enter_context(self.nc.sbuf_tensor(self.name(n), shape, dt))
        return t, Res(n)

    def ps(self, es, n, shape, dt=F32):
        t = es.enter_context(self.nc.psum_tensor(self.name(n), shape, dt))
        return t, Res(n)

    def dram(self, n, shape, dt, nres=1):
        t = self.nc.dram_tensor(self.name(n), shape, dt, kind="Internal").ap()
        return t, [Res(n) for _ in range(nres)]


def blk_of(t0, n):
    return list(range(t0 // 512, (t0 + n - 1) // 512 + 1))


NBLK = (NT + 511) // 512
TBLKS = [(0, 256)] + [(256 + 512 * i, 512) for i in range(16)]


def rs_of(rl, t0, n):
    return [rl[i] for i in blk_of(t0, n)]


C_XBC = 0
C_ZS = 1536
C_DT = 2560
C_Q = 2592
C_K = 3616
C_V = 4640
C_ZA = 5664
C_MQK = 6688
C_MV = 7712
C_MO = 8736
C_MZ = 9760
C_IG = 10784
C_FG = 10792
C_GT = 10800


def build_program(n_layers, dbg=None, only=None):
    dbg = dbg or ()
    scr_cache = {}
    nc = bass.Bass("TRN2", target_bir_lowering=False)
    b = B(nc)
    P = b.P
    L = n_layers

    def inp(n, shape, dt=F32):
        return nc.dram_tensor(n, shape, dt, kind="ExternalInput").ap()

    xin = inp("xin", [NT, D])
    cc = inp("cc", [128, 16])
    ident_in = inp("ident", [128, 128])
    masks_in = inp("masks", [128, 4, 128])
    cos_in = inp("cosb", [TL, 64])
    sin_in = inp("sinb", [TL, 64])
    lami_in = inp("lami", [128, L])
    w_mod = inp("w_mod", [L, D, 3 * D])
    bmod = inp("bmod", [128, L, 24])
    ng = inp("ng", [128, L, 8])
    w_in = inp("w_in", [L, D, D_IN])
    cw = inp("cw", [128, L, 12, 3])
    cb = inp("cb", [128, L, 12])
    alog = inp("alog", [128, L, 32])
    dtb = inp("dtb", [128, L, 32])
    dsk = inp("dsk", [128, L, 16])
    sng = inp("sng", [128, L, D])
    gq = inp("gq", [128, L, 64])
    gk = inp("gk", [128, L, 64])
    lamp = inp("lamp", [128, L, 4, 64])
    subg = inp("subg", [128, L, 8])
    mcw = inp("mcw", [128, L, 8, 3])
    mcb = inp("mcb", [128, L, 8])
    ib = inp("ib", [128, L, 8])
    fb = inp("fb", [128, L, 8])
    mng = inp("mng", [128, L, D])
    sel_in = inp("sel", [2, 2, 128])
    bgate = inp("bgate", [2, L, D])
    w_br = inp("w_br", [L, 3, D, D])
    w_out = inp("w_out", [L, D, D])
    xout = nc.dram_tensor("xout", [NT, D], F32, kind="ExternalOutput").ap()

    def scratch(n, shape, dt, nres=NBLK):
        if only is not None and n in only.get('inputs', ()):
            t = nc.dram_tensor("dbg_" + n, shape, dt, kind="ExternalInput").ap()
            return t, [Res(n) for _ in range(nres)]
        if n in dbg or INTERNAL_OK is None or (INTERNAL_OK != 'ALL' and n not in INTERNAL_OK):
            t = nc.dram_tensor("dbg_" + n, shape, dt, kind="ExternalOutput").ap()
            return t, [Res(n) for _ in range(nres)]
        return b.dram(n, shape, dt, nres)

    xbufs = [(xin, [Res('xin') for _ in range(NBLK)])]
    for l in range(1, L):
        xbufs.append(scratch("xres%d" % l, [NT, D], F32))
    xout_res = [Res('xout') for _ in range(NBLK)]

    hT, hT_r = scratch("hT", [8, 128, NT], BF16)
    xbcT, xbcT_r = scratch("xbcT", [12, 128, NT], BF16)
    mqkT, mqkT_r = scratch("mqkT", [8, 128, NT], BF16)
    zs, zs_r = scratch("zs", [NT, 1024], BF16)
    small, small_r = scratch("small", [NT, 48], F32)
    aqk, aqk_r = scratch("aqk", [NT, 2048], BF16)
    avz, avz_r = scratch("avz", [NT, 1024], BF16)
    azT, azT_r = scratch("azT", [8, 128, NT], BF16)
    mvo, mvo_r = scratch("mvo", [NT, 2048], BF16)
    mz, mz_r = scratch("mz", [NT, 1024], BF16)
    gtT, gtT_r = scratch("gtT", [24, 128, NT], BF16)
    yT, yT_r = scratch("yT", [3, 8, 128, NT], BF16)

    with ExitStack() as g_es:
        identf, identf_r = b.sb(g_es, "identf", [128, 128], F32)
        identb, identb_r = b.sb(g_es, "identb", [128, 128], BF16)
        masks, masks_r = b.sb(g_es, "masks", [128, 4, 128], F32)
        onesb, onesb_r = b.sb(g_es, "onesb", [128, 128], BF16)
        onesf, onesf_r = b.sb(g_es, "onesf", [128, 128], F32)
        gbc, gbc_r = b.sb(g_es, "gbc", [128, L, 2, D], F32)
        modt, modt_r = b.sb(g_es, "modt", [128, L, 24, 2], F32)
        a1t, a1t_r = b.sb(g_es, "a1t", [128, L, 8, 2], F32)
        P.dma('sp', identf[:], ident_in[:, :], writes=[identf_r])
        P.dma('sp', masks[:], masks_in[:, :, :], writes=[masks_r])
        P.op('dve', lambda e: e.tensor_copy(out=identb[:], in_=identf[:]), reads=[identf_r], writes=[identb_r])
        P.op('dve', lambda e: e.memset(onesb[:], 1.0), writes=[onesb_r])
        P.op('dve', lambda e: e.memset(onesf[:], 1.0), writes=[onesf_r])

        with ExitStack() as es:
            cct, cct_r = b.sb(es, "cct", [128, 16], F32)
            sct, sct_r = b.sb(es, "sct", [128, 16], F32)
            bmt, bmt_r = b.sb(es, "bmt", [128, L, 24], F32)
            ngt, ngt_r = b.sb(es, "ngt", [128, L, 8], F32)
            wm, wm_r = b.sb(es, "wm", [128, 8, 3 * D], F32)
            pm, pm_r = b.ps(es, "pm", [128, 512], F32)
            pgr, pgr_r = b.ps(es, "pgr", [128, 2, 512], F32)
            pgb, pgb_r = b.ps(es, "pgb", [128, 2, 512], F32)
            selt, selt_r = b.sb(es, "selt", [2, 2, 128], F32)
            bgt, bgt_r = b.sb(es, "bgt", [2, L, D], F32)
            grow, grow_r = b.sb(es, "grow", [2, D], F32)
            P.dma('sp', selt[:], sel_in[:, :, :], writes=[selt_r])
            P.dma('sp', bgt[:], bgate[:, :, :], writes=[bgt_r])
            P.dma('sp', cct[:], cc[:, :], writes=[cct_r])
            P.dma('sp', bmt[:], bmod[:, :, :], writes=[bmt_r])
            P.dma('sp', ngt[:], ng[:, :, :], writes=[ngt_r])
            P.op('act', lambda e: e.activation(out=sct[:], in_=cct[:], func=AF.Silu), reads=[cct_r], writes=[sct_r])
            if 'mod' in dbg:
                dsc = nc.dram_tensor("dbg_sct", [128, 16], F32, kind="ExternalOutput").ap()
                P.dma('sp', dsc[:, :], sct[:], reads=[sct_r], writes=[Res('x')])
            for l in range(L):
                for kc in range(8):
                    P.dma(None, wm[:, kc, :], w_mod[l, kc * 128:(kc + 1) * 128, :], writes=[wm_r])
                for fc in range(24):
                    for kc in range(8):
                        P.op('pe', lambda e, fc=fc, kc=kc: e.matmul(
                            pm[:, fc * 2:fc * 2 + 2], wm[:, kc, fc * 128:(fc + 1) * 128], sct[:, kc * 2:kc * 2 + 2],
                            start=(kc == 0), stop=(kc == 7)),
                            reads=[wm_r, sct_r], writes=[pm_r], inc=(fc == 23 and kc == 7))
                for hf in range(2):
                    for kc in range(8):
                        P.op('pe', lambda e, hf=hf, kc=kc: e.matmul(
                            pgr[0:2, hf, :], sct[:, kc * 2:kc * 2 + 2], wm[:, kc, 2 * D + hf * 512:2 * D + (hf + 1) * 512],
                            start=(kc == 0), stop=(kc == 7)), reads=[wm_r, sct_r], writes=[pgr_r], inc=(kc == 7))
                P.op('dve', lambda e, l=l: e.tensor_tensor(out=grow[:, :].rearrange("p (h f) -> p h f", h=2), in0=pgr[0:2, :, :],
                                                       in1=bgt[:, l, :].rearrange("p (h f) -> p h f", h=2), op=ALU.add),
                     reads=[pgr_r, bgt_r], writes=[grow_r])
                for j in range(2):
                    for hf in range(2):
                        P.op('pe', lambda e, j=j, hf=hf: e.matmul(pgb[:, hf, :], selt[:, j, :], grow[:, hf * 512:(hf + 1) * 512],
                                                                start=True, stop=True), reads=[selt_r, grow_r], writes=[pgb_r])
                    P.op('dve', lambda e, l=l, j=j: e.tensor_copy(out=gbc[:, l, j, :].rearrange("p (h f) -> p h f", h=2), in_=pgb[:, :, :]),
                         reads=[pgb_r], writes=[gbc_r])
                for j in range(2):
                    P.op('dve', lambda e, l=l, j=j: e.tensor_tensor(
                        out=modt[:, l, :, j], in0=pm[:, 0:48].rearrange("p (f j) -> p f j", j=2)[:, :, j],
                        in1=bmt[:, l, :], op=ALU.add), reads=[pm_r, bmt_r], writes=[modt_r])
                for j in range(2):
                    P.op('dve', lambda e, l=l, j=j: e.scalar_tensor_tensor(
                        out=a1t[:, l, :, j], in0=modt[:, l, 8:16, j], scalar=1.0, in1=ngt[:, l, :],
                        op0=ALU.add, op1=ALU.mult), reads=[modt_r, ngt_r], writes=[a1t_r])
            P.barrier()

        if 'mod' in dbg:
            dmod = nc.dram_tensor("dbg_mod", [128, L * 48], F32, kind="ExternalOutput").ap()
            da1 = nc.dram_tensor("dbg_a1", [128, L * 16], F32, kind="ExternalOutput").ap()
            P.dma('sp', dmod[:, :], modt[:].rearrange("p l f j -> p (l f j)"), reads=[modt_r], writes=[Res('x')])
            P.dma('sp', da1[:, :], a1t[:].rearrange("p l f j -> p (l f j)"), reads=[a1t_r], writes=[Res('x')])
        for l in range(L):
            xsrc, xsrc_r = xbufs[l]
            if l + 1 < L:
                xdst, xdst_r = xbufs[l + 1]
                dst_off = 0
            else:
                xdst, xdst_r = xout, xout_res
                dst_off = 0
            layer(b, l, L, locals())

        P.finish()
        global LAST_PROG
        LAST_PROG = P
    return nc


def layer(b, l, L, G):
    cache = G['scr_cache']

    def S(n, shape, dt, nres=NBLK):
        if n not in cache:
            cache[n] = G['scratch'](n, shape, dt, nres)
        return cache[n]
    G['scr'] = S
    only = G.get('only')
    with ExitStack() as es:
        C = layer_consts(b, es, l, G)
        if only is None or 'a' in only:
            phase_a1(b, l, G)
            phase_a2(b, l, G)
        if only is None or 'ssd' in only:
            phase_ssd(b, l, G, C)
        if only is None or 'ml' in only:
            phase_mlstm(b, l, G, C)
        if only is None or 'attn' in only:
            phase_attn(b, l, G, C)
        if only is None or 'merge' in only:
            phase_merge(b, l, G, C)
        b.P.barrier()


def phase_a1(b, l, G):
    P = b.P
    xsrc, xsrc_r = G['xsrc'], G['xsrc_r']
    hT, hT_r = G['hT'], G['hT_r']
    identb, identb_r = G['identb'], G['identb_r']
    modt, modt_r, a1t, a1t_r = G['modt'], G['modt_r'], G['a1t'], G['a1t_r']
    with ExitStack() as es:
        xt = [b.sb(es, "xt", [128, D], F32) for _ in range(2)]
        junk, junk_r = b.sb(es, "junk", [128, D], BF16)
        xnb = [b.sb(es, "xnb", [128, D], BF16) for _ in range(2)]
        ss = [b.sb(es, "ss", [128, 1], F32) for _ in range(2)]
        pT = [b.ps(es, "pT", [128, D], BF16) for _ in range(2)]
        hb = [b.sb(es, "hb", [128, 8, 512], BF16) for _ in range(2)]
        for bi, (t0, bw) in enumerate(TBLKS):
            hbt, hb_r = hb[bi % 2]
            j = 1 if t0 < TC else 0
            for ti in range(bw // 128):
                i = (t0 // 128 + ti)
                xtt, xt_r = xt[i % 2]
                xn, xn_r = xnb[i % 2]
                st, st_r = ss[i % 2]
                pt, pt_r = pT[i % 2]
                P.dma(None, xtt[:], xsrc[i * 128:(i + 1) * 128, :], reads=rs_of(xsrc_r, i * 128, 128), writes=[xt_r])
                P.op('act', lambda e, xtt=xtt, st=st: e.activation(out=junk[:], in_=xtt[:], func=AF.Square, accum_out=st[:]),
                     reads=[xt_r], writes=[junk_r, st_r])
                P.op('act', lambda e, st=st: e.activation(out=st[:], in_=st[:], func=AF.Sqrt, scale=1.0 / D, bias=EPS),
                     reads=[st_r], writes=[st_r])
                P.op('dve', lambda e, st=st: e.reciprocal(out=st[:], in_=st[:]), reads=[st_r], writes=[st_r])
                P.op('dve', lambda e, xn=xn, xtt=xtt, st=st: e.tensor_scalar(out=xn[:], in0=xtt[:], scalar1=st[:, 0:1], scalar2=None, op0=ALU.mult),
                     reads=[xt_r, st_r], writes=[xn_r])
                for kc in range(8):
                    P.op('pe', lambda e, kc=kc, pt=pt, xn=xn: e.transpose(out=pt[:, kc * 128:(kc + 1) * 128], in_=xn[:, kc * 128:(kc + 1) * 128], identity=identb[:]),
                         reads=[xn_r, identb_r], writes=[pt_r], inc=(kc == 7))
                for kc in range(8):
                    P.op('dve', lambda e, kc=kc, pt=pt, hbt=hbt, ti=ti, j=j: e.tensor_scalar(
                        out=hbt[:, kc, ti * 128:(ti + 1) * 128], in0=pt[:, kc * 128:(kc + 1) * 128],
                        scalar1=a1t[:, l, kc, j:j + 1], scalar2=modt[:, l, kc, j:j + 1], op0=ALU.mult, op1=ALU.add),
                        reads=[pt_r, a1t_r, modt_r], writes=[hb_r])
            P.dma(None, hT[:, :, t0:t0 + bw].rearrange("k p t -> p k t"), hbt[:, :, 0:bw],
                  reads=[hb_r], writes=rs_of(hT_r, t0, bw))
        P.barrier()


def gemm_jobs():
    jobs = []
    jobs.append(('F', C_XBC, 1536, 'xbcT'))
    jobs.append(('F', C_MQK, 1024, 'mqkT'))
    jobs.append(('T', C_ZS, 1024, 'zs', 0))
    jobs.append(('T', C_Q, 2048, 'aqk', 0))
    jobs.append(('T', C_V, 1024, 'avz', 0))
    jobs.append(('F', C_ZA, 1024, 'azT'))
    jobs.append(('T', C_MV, 2048, 'mvo', 0))
    jobs.append(('T', C_MZ, 1024, 'mz', 0))
    jobs.append(('F', C_GT, 2048, 'gtT', 0))
    jobs.append(('F', C_GT + 2048, 1024, 'gtT', 16))
    return jobs


def phase_a2(b, l, G):
    P = b.P
    w_in = G['w_in']
    hT, hT_r = G['hT'], G['hT_r']
    with ExitStack() as es:
        stg = [b.sb(es, "stg", [128, 2048], F32) for _ in range(2)]
        Wg = [b.sb(es, "Wg", [128, 8, 2048], BF16) for _ in range(2)]
        hb = [b.sb(es, "hb2", [128, 8, 512], BF16) for _ in range(2)]
        ob = [b.sb(es, "ob", [128, 2048], BF16) for _ in range(2)]
        obs = [b.sb(es, "obs", [128, 48], F32) for _ in range(2)]
        pg = [b.ps(es, "pg", [128, 512], F32) for _ in range(4)]
        ws, ws_r = b.sb(es, "ws", [128, 8, 48], BF16)
        cnt = {'stg': 0, 'w': 0, 'hb': 0, 'ob': 0, 'pg': 0, 'ev': 0}

        def load_w(c0, n):
            wt, wt_r = Wg[cnt['w'] % 2]
            cnt['w'] += 1
            for kc in range(8):
                st, st_r = stg[cnt['stg'] % 2]
                cnt['stg'] += 1
                P.dma(None, st[:, 0:n], w_in[l, kc * 128:(kc + 1) * 128, c0:c0 + n], writes=[st_r])
                P.op('pool', lambda e, wt=wt, st=st, kc=kc, n=n: e.tensor_copy(out=wt[:, kc, 0:n], in_=st[:, 0:n]),
                     reads=[st_r], writes=[wt_r])
            return wt, wt_r

        def load_h(t0, bw):
            ht, ht_r = hb[cnt['hb'] % 2]
            cnt['hb'] += 1
            P.dma(None, ht[:, :, 0:bw], hT[:, :, t0:t0 + bw].rearrange("k p t -> p k t"),
                  reads=rs_of(hT_r, t0, bw), writes=[ht_r])
            return ht, ht_r

        def evac(out_ap, in_ap, reads, writes):
            eng = 'act' if cnt['ev'] % 2 == 0 else 'dve'
            cnt['ev'] += 1
            if eng == 'act':
                P.op('act', lambda e: e.copy(out=out_ap, in_=in_ap), reads=reads, writes=writes)
            else:
                P.op('dve', lambda e: e.tensor_copy(out=out_ap, in_=in_ap), reads=reads, writes=writes)

        for job in gemm_jobs():
            mode, c0, n = job[0], job[1], job[2]
            dst, dst_r = G[job[3]], G[job[3] + '_r']
            wt, wt_r = load_w(c0, n)
            for (t0, bw) in TBLKS:
                ht, ht_r = load_h(t0, bw)
                if mode == 'F':
                    for ch in range(n // 128):
                        pt, pt_r = pg[cnt['pg'] % 4]
                        cnt['pg'] += 1
                        for kc in range(8):
                            P.op('pe', lambda e, pt=pt, wt=wt, ht=ht, kc=kc, ch=ch, bw=bw: e.matmul(
                                pt[:, 0:bw], wt[:, kc, ch * 128:(ch + 1) * 128], ht[:, kc, 0:bw],
                                start=(kc == 0), stop=(kc == 7)), reads=[wt_r, ht_r], writes=[pt_r], inc=(kc == 7))
                        if ch % 4 == 0:
                            ot, ot_r = ob[cnt['ob'] % 2]
                            cnt['ob'] += 1
                        evac(ot[:, (ch % 4) * 512:(ch % 4) * 512 + bw], pt[:, 0:bw], [pt_r], [ot_r])
                        if ch % 4 == 3 or ch == n // 128 - 1:
                            cb0 = ch - (ch % 4) + (job[4] if len(job) > 4 else 0)
                            nchk = ch % 4 + 1
                            P.dma(None, dst[cb0:cb0 + nchk, :, t0:t0 + bw].rearrange("c p t -> p c t"),
                                  ot[:, 0:nchk * 512].rearrange("p (c t) -> p c t", c=nchk)[:, :, 0:bw],
                                  reads=[ot_r], writes=rs_of(dst_r, t0, bw))
                else:
                    dc0 = job[4]
                    for ti in range(bw // 128):
                        ot, ot_r = ob[cnt['ob'] % 2]
                        cnt['ob'] += 1
                        for sg in range(n // 512):
                            pt, pt_r = pg[cnt['pg'] % 4]
                            cnt['pg'] += 1
                            for kc in range(8):
                                P.op('pe', lambda e, pt=pt, wt=wt, ht=ht, kc=kc, sg=sg, ti=ti: e.matmul(
                                    pt[:, :], ht[:, kc, ti * 128:(ti + 1) * 128], wt[:, kc, sg * 512:(sg + 1) * 512],
                                    start=(kc == 0), stop=(kc == 7)), reads=[wt_r, ht_r], writes=[pt_r], inc=(kc == 7))
                            evac(ot[:, sg * 512:(sg + 1) * 512], pt[:, :], [pt_r], [ot_r])
                        r0 = t0 + ti * 128
                        P.dma(None, dst[r0:r0 + 128, dc0:dc0 + n], ot[:, 0:n], reads=[ot_r], writes=rs_of(dst_r, r0, 128))

        small, small_r = G['small'], G['small_r']
        for kc in range(8):
            st, st_r = stg[cnt['stg'] % 2]
            cnt['stg'] += 1
            P.dma(None, st[:, 0:32], w_in[l, kc * 128:(kc + 1) * 128, C_DT:C_DT + 32], writes=[st_r])
            P.dma(None, st[:, 32:48], w_in[l, kc * 128:(kc + 1) * 128, C_IG:C_IG + 16], writes=[st_r])
            P.op('pool', lambda e, st=st, kc=kc: e.tensor_copy(out=ws[:, kc, :], in_=st[:, 0:48]), reads=[st_r], writes=[ws_r])
        for (t0, bw) in TBLKS:
            ht, ht_r = load_h(t0, bw)
            for ti in range(bw // 128):
                pt, pt_r = pg[cnt['pg'] % 4]
                cnt['pg'] += 1
                ot, ot_r = obs[cnt['ob'] % 2]
                cnt['ob'] += 1
                for kc in range(8):
                    P.op('pe', lambda e, pt=pt, ht=ht, kc=kc, ti=ti: e.matmul(
                        pt[:, 0:48], ht[:, kc, ti * 128:(ti + 1) * 128], ws[:, kc, :],
                        start=(kc == 0), stop=(kc == 7)), reads=[ws_r, ht_r], writes=[pt_r], inc=(kc == 7))
                evac(ot[:, :], pt[:, 0:48], [pt_r], [ot_r])
                r0 = t0 + ti * 128
                P.dma(None, small[r0:r0 + 128, :], ot[:, :], reads=[ot_r], writes=rs_of(small_r, r0, 128))
        P.barrier()


def rope_tables():
    t = np.arange(TL)
    row = (t // 64).astype(np.float32)
    col = (t % 64).astype(np.float32)
    half = 32
    inv_freq = (10000.0 ** (-np.arange(0, half, 2, dtype=np.float32) / half)).astype(np.float32)
    ang_r = row[:, None] * inv_freq
    ang_c = col[:, None] * inv_freq
    cos = np.concatenate([np.cos(ang_r), np.cos(ang_r), np.cos(ang_c), np.cos(ang_c)], -1).astype(np.float32)
    sin = np.concatenate([np.sin(ang_r), np.sin(ang_r), np.sin(ang_c), np.sin(ang_c)], -1).astype(np.float32)
    sgn = np.tile(np.concatenate([-np.ones(16), np.ones(16)]), 2).astype(np.float32)
    return cos, sin * sgn


def bc(a):
    a = np.asarray(a, np.float32)
    return np.ascontiguousarray(np.broadcast_to(a[None], (128,) + a.shape))


def pmaj(v, nchunk):
    v = np.asarray(v, np.float32)
    lead = v.shape[:-1]
    r = v.reshape(lead + (nchunk, 128))
    r = np.moveaxis(r, -1, 0)
    return np.ascontiguousarray(r)


def make_inputs(inputs, bidx, layers, xin=None):
    L = len(layers)
    f = lambda k: np.asarray(inputs[k], np.float32)
    d = {}
    d['xin'] = np.ascontiguousarray(np.concatenate([f('ctx')[bidx], f('x')[bidx]], 0)) if xin is None else np.ascontiguousarray(xin, np.float32)
    cc = np.stack([f('c')[bidx], f('c_ctx')], -1)
    d['cc'] = np.ascontiguousarray(cc.reshape(8, 128, 2).transpose(1, 0, 2).reshape(128, 16))
    d['ident'] = np.eye(128, dtype=np.float32)
    t = np.arange(128)
    U = (t[:, None] <= t[None, :]).astype(np.float32)
    Lw = (t[:, None] >= t[None, :]).astype(np.float32)
    Sg = (t[:, None] > t[None, :]).astype(np.float32)
    Sl = (t[:, None] < t[None, :]).astype(np.float32)
    d['masks'] = np.ascontiguousarray(np.stack([U, Lw, Sg, Sl], 1))
    cos, sin = rope_tables()
    d['cosb'] = cos
    d['sinb'] = sin
    d['lami'] = bc(np.array([0.8 - 0.6 * math.exp(-0.3 * l) for l in layers], np.float32))
    d['w_mod'] = np.ascontiguousarray(f('w_mod')[layers])
    d['bmod'] = pmaj(f('b_mod')[layers], 24)
    d['ng'] = pmaj(f('norm_g')[layers], 8)
    d['w_in'] = np.ascontiguousarray(f('w_in')[layers])
    d['cw'] = np.ascontiguousarray(pmaj(f('ssd_conv_w')[layers], 12).transpose(0, 1, 3, 2))
    d['cb'] = pmaj(f('ssd_conv_b')[layers], 12)
    d['alog'] = bc(f('ssd_a_log')[layers].reshape(L, 32))
    d['dtb'] = bc(f('ssd_dt_bias')[layers].reshape(L, 32))
    d['dsk'] = bc(f('ssd_d')[layers])
    d['sng'] = bc(f('ssd_norm_g')[layers])
    d['gq'] = bc(f('diff_qn_g')[layers])
    d['gk'] = bc(f('diff_kn_g')[layers])
    d['lamp'] = bc(f('diff_lambda')[layers])
    d['subg'] = np.ascontiguousarray(np.broadcast_to(pmaj(f('diff_subln_g')[layers], 1), (128, L, 8)))
    d['mcw'] = np.ascontiguousarray(pmaj(f('ml_conv_w')[layers], 8).transpose(0, 1, 3, 2))
    d['mcb'] = pmaj(f('ml_conv_b')[layers], 8)
    d['ib'] = bc(f('ml_i_bias')[layers].reshape(L, 8))
    d['fb'] = bc(f('ml_f_bias')[layers].reshape(L, 8))
    d['mng'] = bc(f('ml_norm_g')[layers])
    sel = np.zeros((2, 2, 128), np.float32)
    sel[0, 0] = 1.0
    sel[1, 1] = 1.0
    d['sel'] = sel
    d['bgate'] = np.ascontiguousarray(np.broadcast_to(f('b_mod')[layers][None, :, 2 * D:3 * D], (2, L, D)))
    d['w_br'] = np.ascontiguousarray(f('w_branch')[layers])
    d['w_out'] = np.ascontiguousarray(f('w_out')[layers])
    return d


_NC_CACHE = {}


FUSED = True


def kernel(**inputs):
    nb = 4
    if FUSED:
        if DEPTH not in _NC_CACHE:
            _NC_CACHE[DEPTH] = build_program(DEPTH)
        nc = _NC_CACHE[DEPTH]
        in_maps = [make_inputs(inputs, bidx, list(range(DEPTH))) for bidx in range(nb)]
        res = run_bass_kernel_spmd(nc, in_maps, core_ids=list(range(nb)))
        return np.stack([np.asarray(r["xout"], np.float32)[TC:] for r in res.results], 0)
    if 1 not in _NC_CACHE:
        _NC_CACHE[1] = build_program(1)
    nc = _NC_CACHE[1]
    xcur = [None] * nb
    for l in range(DEPTH):
        in_maps = [make_inputs(inputs, bidx, [l], xin=xcur[bidx]) for bidx in range(nb)]
        res = run_bass_kernel_spmd(nc, in_maps, core_ids=list(range(nb)))
        xcur = [np.asarray(r["xout"], np.float32) for r in res.results]
    return np.stack([x[TC:] for x in xcur], 0)


def layer_consts(b, es, l, G):
    P = b.P
    O = Ops(P)
    C = {}

    def ld(name, src, shape, **kw):
        t, r = b.sb(es, "c_" + name, shape, F32)
        P.dma(None, t[:], src, writes=[r], **kw)
        C[name] = (t, r)
    ld('cw', G['cw'][:, l, :, :], [128, 12, 3])
    ld('cb', G['cb'][:, l, :], [128, 12])
    ld('alog', G['alog'][:, l, :], [128, 32])
    ld('dtb', G['dtb'][:, l, :], [128, 32])
    ld('dsk', G['dsk'][:, l, :], [128, 16])
    ld('sng', G['sng'][:, l, :], [128, D])
    ld('gq', G['gq'][:, l, :], [128, 64])
    ld('gk', G['gk'][:, l, :], [128, 64])
    ld('lamp', G['lamp'][:, l, :, :], [128, 4, 64])
    ld('subg', G['subg'][:, l, :], [128, 8])
    ld('mcw', G['mcw'][:, l, :, :], [128, 8, 3])
    ld('mcb', G['mcb'][:, l, :], [128, 8])
    ld('ib', G['ib'][:, l, :], [128, 8])
    ld('fb', G['fb'][:, l, :], [128, 8])
    ld('mng', G['mng'][:, l, :], [128, D])
    ld('lami', G['lami_in'][:, l:l + 1], [128, 1], allow_slow_non_contiguous=True)
    at, ar = C['alog']
    O.act(at[:], at[:], AF.Exp, [ar], [ar])
    O.ts('dve', at[:], at[:], -1.0, None, ALU.mult, None, [ar], [ar])
    return C


def softplus(b, O, out, x, tmp1, tmp2, R, W):
    O.act(tmp1, x, AF.Abs, R + W, W)
    O.act(tmp1, tmp1, AF.Exp, R + W, W, scale=-1.0)
    O.act(tmp2, tmp1, AF.Ln, R + W, W, bias=1.0)
    O.stt(out, x, 0.0, tmp2, ALU.max, ALU.add, R + W, W)


def seq_bounds(t0, bw):
    return (t0 == 0 or t0 == TC), (t0 + bw == TC or t0 + bw == NT)


def conv_silu(b, O, l, xT, xT_r, t0, bw, nchunk, cwt, cw_r, cbt, cb_r, xh, xh_r, tmps, uT, uT_r):
    P = O.P
    at_start, at_end = seq_bounds(t0, bw)
    lo = t0 if at_start else t0 - 1
    hi = t0 + bw if at_end else t0 + bw + 1
    if at_start:
        O.memset('pool', xh[:, :, 0:1], 0.0, [xh_r])
    if at_end:
        O.memset('pool', xh[:, :, bw + 1:bw + 2], 0.0, [xh_r])
    O.dma(xh[:, :, lo - (t0 - 1):hi - (t0 - 1)], xT[:, :, lo:hi].rearrange("c p t -> p c t"),
          rs_of(xT_r, lo, hi - lo), [xh_r])
    for ch in range(nchunk):
        tm, tm_r = tmps[ch % 2]
        O.ts('dve', tm[:, 0:bw], xh[:, ch, 1:bw + 1], cwt[:, ch, 1:2], None, ALU.mult, None, [xh_r, cw_r], [tm_r])
        O.stt(tm[:, 0:bw], xh[:, ch, 0:bw], cwt[:, ch, 0:1], tm[:, 0:bw], ALU.mult, ALU.add, [xh_r, cw_r, tm_r], [tm_r])
        O.stt(tm[:, 0:bw], xh[:, ch, 2:bw + 2], cwt[:, ch, 2:3], tm[:, 0:bw], ALU.mult, ALU.add, [xh_r, cw_r, tm_r], [tm_r])
        O.act(uT[:, ch, 0:bw], tm[:, 0:bw], AF.Silu, [tm_r, cb_r], [uT_r], bias=cbt[:, ch:ch + 1])


def decay_scalars(b, O, c, a_t, a_r, NA, masks, masks_r, onesf, onesf_r, pc, pc_r, PT, PT_r, E1, E2, E_r,
                  rs, de, cdall, cd_r):
    n2 = 2 * NA
    O.mm(pc[:, 0:n2], masks[:, 0, :], a_t[:, 0:n2], True, True, [masks_r, a_r], [pc_r], inc=False)
    O.mm(pc[:, n2:2 * n2], onesf[:, :], a_t[:, 0:n2], True, True, [onesf_r, a_r], [pc_r])
    O.copy('dve', PT[:, 0:2 * n2], pc[:, 0:2 * n2], [pc_r], [PT_r])
    Pf, Pb = PT[:, 0:NA], PT[:, NA:n2]
    Tf, Tb = PT[:, n2:n2 + NA], PT[:, n2 + NA:2 * n2]
    O.tt('dve', E2[:, NA:n2], Pb, a_t[:, NA:n2], ALU.subtract, [PT_r, a_r], [E_r])
    O.tt('dve', E1[:, NA:n2], Tb, E2[:, NA:n2], ALU.subtract, [PT_r, E_r], [E_r])
    O.tt('dve', E2[:, 0:NA], Tf, Pf, ALU.subtract, [PT_r], [E_r])
    O.copy('dve', E1[:, 0:NA], Pf, [PT_r], [E_r])
    O.act(rs, E1[:, 0:n2], AF.Exp, [E_r], [E_r])
    O.act(de, E2[:, 0:n2], AF.Exp, [E_r], [E_r])
    O.act(cdall[:, c, 0:n2], PT[:, n2:2 * n2], AF.Exp, [PT_r], [cd_r])


def phase_ssd(b, l, G, C):
    P = b.P
    O = Ops(P)
    nc = b.nc
    S = G['scr']
    masks, masks_r, onesf, onesf_r = G['masks'], G['masks_r'], G['onesf'], G['onesf_r']
    identb, identb_r = G['identb'], G['identb_r']
    xbcT, xbcT_r = G['xbcT'], G['xbcT_r']
    small, small_r = G['small'], G['small_r']
    sBT, sBT_r = S('sBT', [2, 128, NT], BF16)
    sCT, sCT_r = S('sCT', [2, 128, NT], BF16)
    sVt, sVt_r = S('sVt', [2, NT, D], BF16)
    sxs, sxs_r = S('sxs', [NT, D], BF16)
    ssm, ssm_r = S('ssm', [NT, 64], F32)
    sprev, sprev_r = S('sprev', [2, NCH, 128, D], BF16, 2 * NCH)
    sLb, sLb_r = S('sLb', [NCH, 128, D], F32, NCH)
    cwt, cw_r = C['cw']
    cbt, cb_r = C['cb']
    At, A_r = C['alog']
    dtbt, dtb_r = C['dtb']

    with ExitStack() as es0:
        cdall, cd_r = b.sb(es0, "cdall", [128, NCH, 32], F32)
        with ExitStack() as es:
            xh, xh_r = b.sb(es, "xh", [128, 12, 514], BF16)
            tmps = [b.sb(es, "ctmp", [128, 512], F32) for _ in range(2)]
            uT, uT_r = b.sb(es, "uT", [128, 12, 512], BF16)
            pX, pX_r = b.ps(es, "pX", [128, D], BF16)
            pB, pB_r = b.ps(es, "pB", [128, 1024], BF16)
            pc, pc_r = b.ps(es, "pc", [128, 512], F32)
            pL, pL_r = b.ps(es, "pL", [128, 4, 512], F32)
            xsb = [b.sb(es, "xsb", [128, D], BF16) for _ in range(2)]
            Btm, Btm_r = b.sb(es, "Btm", [128, 256], BF16)
            sml, sml_r = b.sb(es, "sml", [128, 256], F32)
            smo = [b.sb(es, "smo", [128, 64], F32) for _ in range(2)]
            PT, PT_r = b.sb(es, "PT", [128, 128], F32)
            Et, E_r = b.sb(es, "Et", [128, 128], F32)
            Vt = [b.sb(es, "Vt", [128, 2, D], BF16) for _ in range(2)]
            Vw, Vw_r = b.sb(es, "Vw", [128, 2, D], BF16)
            Sf, Sf_r = b.sb(es, "Sf", [128, D], F32)
            Stmp, Stmp_r = b.sb(es, "Stmp", [128, D], F32)
            Sfb = [b.sb(es, "Sfb", [128, D], BF16) for _ in range(2)]
            Lb = [b.sb(es, "Lb", [128, D], F32) for _ in range(2)]
            O.memset('dve', Sf[:], 0.0, [Sf_r])
            for (t0, bw) in TBLKS:
                conv_silu(b, O, l, xbcT, xbcT_r, t0, bw, 12, cwt, cw_r, cbt, cb_r, xh, xh_r, tmps, uT, uT_r)
                O.dma(sBT[:, :, t0:t0 + bw].rearrange("g p t -> p g t"), uT[:, 8:10, 0:bw], [uT_r], rs_of(sBT_r, t0, bw))
                O.dma(sCT[:, :, t0:t0 + bw].rearrange("g p t -> p g t"), uT[:, 10:12, 0:bw], [uT_r], rs_of(sCT_r, t0, bw))
                for ti in range(bw // 128):
                    c = t0 // 128 + ti
                    r0 = c * 128
                    tsl = slice(ti * 128, (ti + 1) * 128)
                    for ch in range(8):
                        O.tr(pX[:, ch * 128:(ch + 1) * 128], uT[:, ch, tsl], identb[:], [uT_r, identb_r], [pX_r], inc=(ch == 7))
                    for g in range(2):
                        O.tr(pB[:, g * 128:(g + 1) * 128], uT[:, 8 + g, tsl], identb[:], [uT_r, identb_r], [pB_r], inc=(g == 1))
                    xs_t, xs_r = xsb[c % 2]
                    O.copy('act', xs_t[:], pX[:], [pX_r], [xs_r])
                    O.dma(sxs[r0:r0 + 128, :], xs_t[:], [xs_r], rs_of(sxs_r, r0, 128))
                    O.copy('act', Btm[:], pB[:, 0:256], [pB_r], [Btm_r])
                    so, so_r = smo[c % 2]
                    O.dma(sml[:, 0:32], small[r0:r0 + 128, 0:32], rs_of(small_r, r0, 128), [sml_r])
                    O.tt('dve', sml[:, 32:64], sml[:, 0:32], dtbt[:], ALU.add, [sml_r, dtb_r], [sml_r])
                    softplus(b, O, sml[:, 128:160], sml[:, 32:64], sml[:, 64:96], sml[:, 96:128], [], [sml_r])
                    dt = sml[:, 128:160]
                    O.tt('dve', so[:, 0:32], dt, At[:], ALU.mult, [sml_r, A_r], [so_r])
                    decay_scalars(b, O, c, so, so_r, 16, masks, masks_r, onesf, onesf_r, pc, pc_r, PT, PT_r,
                                  Et[:, 0:32], Et[:, 32:64], E_r, so[:, 32:64], Et[:, 64:96], cdall, cd_r)
                    O.dma(ssm[r0:r0 + 128, :], so[:, :], [so_r, E_r], rs_of(ssm_r, r0, 128))
                    O.tt('dve', Et[:, 96:128], dt, Et[:, 64:96], ALU.mult, [sml_r, E_r], [E_r])
                    vt, vt_r = Vt[c % 2]
                    pXv = pX[:].rearrange("p (h e) -> p h e", h=16)
                    for d in range(2):
                        O.tt('dve', vt[:, d, :].rearrange("p (h e) -> p h e", h=16), pXv,
                             bc_mid(sml[:, 128 + d * 16:128 + (d + 1) * 16], [128, 16, 64]), ALU.mult, [pX_r, sml_r], [vt_r])
                        O.tt('dve', Vw[:, d, :].rearrange("p (h e) -> p h e", h=16), pXv,
                             bc_mid(Et[:, 96 + d * 16:96 + (d + 1) * 16], [128, 16, 64]), ALU.mult, [pX_r, E_r], [Vw_r])
                        O.dma(sVt[d, r0:r0 + 128, :], vt[:, d, :], [vt_r], rs_of(sVt_r, r0, 128))
                    for d in range(2):
                        for g in range(2):
                            O.mm(pL[:, d * 2 + g, :], Btm[:, g * 128:(g + 1) * 128], Vw[:, d, g * 512:(g + 1) * 512], True, True,
                                 [Btm_r, Vw_r], [pL_r], inc=(d == 1 and g == 1))
                    sb_t, sb_r = Sfb[c % 2]
                    O.copy('pool', sb_t[:], Sf[:], [Sf_r], [sb_r])
                    O.dma(sprev[0, c, :, :], sb_t[:], [sb_r], [sprev_r[c]])
                    O.tt('dve', Stmp[:].rearrange("p (h e) -> p h e", h=16), Sf[:].rearrange("p (h e) -> p h e", h=16),
                         bc_mid(cdall[:, c, 0:16], [128, 16, 64]), ALU.mult, [Sf_r, cd_r], [Stmp_r])
                    O.tt('dve', Sf[:].rearrange("p (g n) -> p g n", g=2), Stmp[:].rearrange("p (g n) -> p g n", g=2),
                         pL[:, 0:2, :], ALU.add, [Stmp_r, pL_r], [Sf_r])
                    lb_t, lb_r = Lb[c % 2]
                    O.copy('act', lb_t[:].rearrange("p (g n) -> p g n", g=2), pL[:, 2:4, :], [pL_r], [lb_r])
                    O.dma(sLb[c, :, :], lb_t[:], [lb_r], [sLb_r[c]])
            O.memset('dve', Sf[:], 0.0, [Sf_r])
            for c in BWD_ORDER:
                lb_t, lb_r = Lb[c % 2]
                sb_t, sb_r = Sfb[c % 2]
                O.dma(lb_t[:], sLb[c, :, :], [sLb_r[c]], [lb_r])
                O.copy('pool', sb_t[:], Sf[:], [Sf_r], [sb_r])
                O.dma(sprev[1, c, :, :], sb_t[:], [sb_r], [sprev_r[NCH + c]])
                O.tt('dve', Stmp[:].rearrange("p (h e) -> p h e", h=16), Sf[:].rearrange("p (h e) -> p h e", h=16),
                     bc_mid(cdall[:, c, 16:32], [128, 16, 64]), ALU.mult, [Sf_r, cd_r], [Stmp_r])
                O.tt('dve', Sf[:], Stmp[:], lb_t[:], ALU.add, [Stmp_r, lb_r], [Sf_r])
            P.barrier()

        zs, zs_r = G['zs'], G['zs_r']
        dskt, dsk_r = C['dsk']
        sngt, sng_r = C['sng']
        yT, yT_r = G['yT'], G['yT_r']
        with ExitStack() as es:
            xs2 = [b.sb(es, "xs2", [128, D], BF16) for _ in range(2)]
            z2 = [b.sb(es, "z2", [128, D], BF16) for _ in range(2)]
            yacc, yacc_r = b.sb(es, "yacc", [128, D], F32)
            szt, sz_r = b.sb(es, "szt", [128, D], F32)
            ssq, ssq_r = b.sb(es, "ssq", [128, 2], F32)
            ybf, ybf_r = b.sb(es, "ybf", [128, D], BF16)
            yTs = [b.sb(es, "yTs", [128, 8, 128], BF16) for _ in range(2)]
            pYT, pYT_r = b.ps(es, "pYT", [128, D], BF16)

            def load_extra(c):
                r0 = c * 128
                x_t, x_r = xs2[c % 2]
                z_t, z_r = z2[c % 2]
                O.dma(x_t[:], sxs[r0:r0 + 128, :], rs_of(sxs_r, r0, 128), [x_r])
                O.dma(z_t[:], zs[r0:r0 + 128, :], rs_of(zs_r, r0, 128), [z_r])

            def finish(c, Yf, Yf_r, Yb, Yb_r):
                r0 = c * 128
                x_t, x_r = xs2[c % 2]
                z_t, z_r = z2[c % 2]
                O.tt('pool', yacc[:], Yf[:, 0:D], Yb[:, 0:D], ALU.add, [Yf_r, Yb_r], [yacc_r])
                O.tt('dve', szt[:].rearrange("p (h e) -> p h e", h=16), x_t[:].rearrange("p (h e) -> p h e", h=16),
                     bc_mid(dskt[:, :], [128, 16, 64]), ALU.mult, [x_r, dsk_r], [sz_r])
                O.tt('dve', yacc[:], yacc[:], szt[:], ALU.add, [yacc_r, sz_r], [yacc_r])
                O.act(szt[:], z_t[:], AF.Silu, [z_r, sz_r], [sz_r])
                O.tt('dve', yacc[:], yacc[:], szt[:], ALU.mult, [yacc_r, sz_r], [yacc_r])
                O.act(szt[:], yacc[:], AF.Square, [yacc_r, sz_r], [sz_r, ssq_r], accum=ssq[:, 0:1])
                O.act(ssq[:, 1:2], ssq[:, 0:1], AF.Sqrt, [ssq_r], [ssq_r], scale=1.0 / D, bias=EPS)
                O.recip(ssq[:, 1:2], ssq[:, 1:2], [ssq_r], [ssq_r])
                O.stt(ybf[:], yacc[:], ssq[:, 1:2], sngt[:], ALU.mult, ALU.mult, [yacc_r, ssq_r, sng_r], [ybf_r])
                yt, yt_r = yTs[c % 2]
                for k in range(8):
                    O.tr(pYT[:, k * 128:(k + 1) * 128], ybf[:, k * 128:(k + 1) * 128], identb[:], [ybf_r, identb_r], [pYT_r], inc=(k == 7))
                O.copy('act', yt[:].rearrange("p k t -> p (k t)"), pYT[:], [pYT_r], [yt_r])
                O.dma(yT[0, :, :, r0:r0 + 128].rearrange("k p t -> p k t"), yt[:], [yt_r], rs_of(yT_r, r0, 128))

            la_pass2(b, O, G, es, NG=2, NSUB=8, PV=64, QT=sCT, QT_r=sCT_r, KT=sBT, KT_r=sBT_r, g0=0,
                     Vt=sVt, Vt_r=sVt_r, vcol0=0, sm=ssm, sm_r=ssm_r, NA=16, a0=0, sprev=sprev, sprev_r=sprev_r,
                     load_extra=load_extra, finish=finish)
            P.barrier()


def la_pass2(b, O, G, es, NG, NSUB, PV, QT, QT_r, KT, KT_r, g0, Vt, Vt_r, vcol0, sm, sm_r, NA, a0, sprev, sprev_r,
             load_extra, finish):
    P = O.P
    masks, masks_r = G['masks'], G['masks_r']
    GW = NSUB * PV
    NH = NG * NSUB
    W = NG * GW
    qt = [b.sb(es, "qt", [128, NG, 128], BF16) for _ in range(2)]
    kt = [b.sb(es, "kt", [128, NG, 128], BF16) for _ in range(2)]
    vt = [b.sb(es, "vt2", [128, 2, W], BF16) for _ in range(2)]
    smt = [b.sb(es, "smt", [128, 4 * NA], F32) for _ in range(2)]
    spt = [b.sb(es, "spt", [128, 2, W], BF16) for _ in range(2)]
    Gm, Gm_r = b.sb(es, "Gm", [128, 2, NG, 128], F32)
    segL = [b.sb(es, "segL", [128, 4, 128], F32) for _ in range(2)]
    Ee = [b.sb(es, "Ee", [128, 4, 128], F32) for _ in range(2)]
    MT = [b.sb(es, "MT", [128, 4, 128], BF16) for _ in range(2)]
    ytmp, ytmp_r = b.sb(es, "ytmp", [128, NG, GW], F32)
    Yd = [b.sb(es, "Yd", [128, NG, GW], F32) for _ in range(2)]
    pG, pG_r = b.ps(es, "pG", [128, 512], F32)
    pS = [b.ps(es, "pS", [128, 4, 128], F32) for _ in range(2)]
    ydg, ydg_r = b.ps(es, "ydg", [128, NG, 512], F32)
    yof, yof_r = b.ps(es, "yof", [128, NG, 512], F32)
    nseg = 0
    for c in range(NCH):
        r0 = c * 128
        q_t, q_r = qt[c % 2]
        k_t, k_r = kt[c % 2]
        v_t, v_r = vt[c % 2]
        s_t, s_r = smt[c % 2]
        p_t, p_r = spt[c % 2]
        O.dma(q_t[:], QT[g0:g0 + NG, :, r0:r0 + 128].rearrange("g p t -> p g t"), rs_of(QT_r, r0, 128), [q_r])
        O.dma(k_t[:], KT[g0:g0 + NG, :, r0:r0 + 128].rearrange("g p t -> p g t"), rs_of(KT_r, r0, 128), [k_r])
        for d in range(2):
            O.dma(v_t[:, d, :], Vt[d, r0:r0 + 128, vcol0:vcol0 + W], rs_of(Vt_r, r0, 128), [v_r])
            O.dma(p_t[:, d, :], sprev[d, c, :, vcol0:vcol0 + W], [sprev_r[d * NCH + c]], [p_r])
        O.dma(s_t[:], sm[r0:r0 + 128, :], rs_of(sm_r, r0, 128), [s_r])
        load_extra(c)
        for g in range(NG):
            O.mm(pG[:, g * 128:(g + 1) * 128], k_t[:, g, :], q_t[:, g, :], True, True, [k_r, q_r], [pG_r], inc=(g == NG - 1))
        for d in range(2):
            O.tt('dve', Gm[:, d, :, :], pG[:, 0:NG * 128].rearrange("p (g l) -> p g l", g=NG),
                 bc_lead(masks[:, d, :], [128, NG, 128]), ALU.mult, [pG_r, masks_r], [Gm_r])
        for d in range(2):
            y_t, y_r = Yd[d]
            first_in_bank = [True] * NG
            for h0 in range(0, NH, 4):
                nh = min(4, NH - h0)
                sl_t, sl_r = segL[nseg % 2]
                e_t, e_r = Ee[nseg % 2]
                m_t, m_r = MT[nseg % 2]
                ps_t, ps_r = pS[nseg % 2]
                nseg += 1
                for i in range(nh):
                    h = h0 + i
                    col = d * NA + a0 + h
                    O.ts('pool', sl_t[:, i, :], masks[:, 2 + d, :], s_t[:, col:col + 1], None, ALU.mult, None, [masks_r, s_r], [sl_r])
                for i in range(nh):
                    O.mm(ps_t[:, i, :], sl_t[:, i, :], masks[:, d, :], True, True, [sl_r, masks_r], [ps_r], inc=(i == nh - 1))
                O.act(e_t[:, 0:nh, :], ps_t[:, 0:nh, :], AF.Exp, [ps_r], [e_r])
                g = h0 // NSUB
                if NSUB >= 4:
                    O.tt('dve', m_t[:, 0:nh, :], e_t[:, 0:nh, :], bc_lead(Gm[:, d, g, :], [128, nh, 128]), ALU.mult, [e_r, Gm_r], [m_r])
                else:
                    for i in range(nh):
                        gi = (h0 + i) // NSUB
                        O.tt('dve', m_t[:, i, :], e_t[:, i, :], Gm[:, d, gi, :], ALU.mult, [e_r, Gm_r], [m_r])
                for i in range(nh):
                    h = h0 + i
                    gi, sub = h // NSUB, h % NSUB
                    O.mm(ydg[:, gi, sub * PV:(sub + 1) * PV], m_t[:, i, :], v_t[:, d, h * PV:(h + 1) * PV],
                         first_in_bank[gi], True, [m_r, v_r], [ydg_r], inc=(i == nh - 1))
                    first_in_bank[gi] = False
            for g in range(NG):
                O.mm(yof[:, g, 0:GW], q_t[:, g, :], p_t[:, d, g * GW:(g + 1) * GW], True, True, [q_r, p_r], [yof_r], inc=(g == NG - 1))
            rsc = s_t[:, 2 * NA + d * NA + a0:2 * NA + d * NA + a0 + NH]
            O.tt('dve', ytmp[:].rearrange("p g (s e) -> p (g s) e", s=NSUB), yof_view(yof, NG, NSUB, PV),
                 bc_mid(rsc, [128, NH, PV]), ALU.mult, [yof_r, s_r], [ytmp_r])
            O.tt('dve', y_t[:], ytmp[:], ydg[:, :, 0:GW], ALU.add, [ytmp_r, ydg_r], [y_r])
        finish(c, Yd[0][0][:].rearrange("p g w -> p (g w)"), Yd[0][1], Yd[1][0][:].rearrange("p g w -> p (g w)"), Yd[1][1])


def yof_view(yof, NG, NSUB, PV):
    if NSUB * PV == 512:
        return yof[:].rearrange("p g (s e) -> p (g s) e", s=NSUB)
    assert NSUB == 1
    return yof[:, :, 0:PV]


MLW = 4 * 257


def phase_mlstm(b, l, G, C):
    P = b.P
    O = Ops(P)
    S = G['scr']
    masks, masks_r, onesf, onesf_r = G['masks'], G['masks_r'], G['onesf'], G['onesf_r']
    identb, identb_r = G['identb'], G['identb_r']
    mqkT, mqkT_r = G['mqkT'], G['mqkT_r']
    small, small_r = G['small'], G['small_r']
    mvo, mvo_r = G['mvo'], G['mvo_r']
    mz, mz_r = G['mz'], G['mz_r']
    mQT, mQT_r = S('mQT', [4, 128, NT], BF16)
    mKT, mKT_r = S('mKT', [4, 128, NT], BF16)
    mVt, mVt_r = S('mVt', [2, NT, MLW], BF16)
    msm, msm_r = S('msm', [NT, 16], F32)
    mprev, mprev_r = S('mprev', [2, NCH, 128, MLW], BF16, 2 * NCH)
    mLb, mLb_r = S('mLb', [NCH, 128, MLW], F32, NCH)
    mh, mh_r = S('mh', [NT, 512], F32)
    cwt, cw_r = C['mcw']
    cbt, cb_r = C['mcb']
    ibt, ib_r = C['ib']
    fbt, fb_r = C['fb']
    LNS = math.log(128.0 ** -0.5)

    with ExitStack() as es0:
        cdall, cd_r = b.sb(es0, "mcdall", [128, NCH, 8], F32)
        with ExitStack() as es:
            xh, xh_r = b.sb(es, "mxh", [128, 8, 514], BF16)
            tmps = [b.sb(es, "mctmp", [128, 512], F32) for _ in range(2)]
            uT, uT_r = b.sb(es, "muT", [128, 8, 512], BF16)
            pK, pK_r = b.ps(es, "pK", [128, 1024], BF16)
            pc, pc_r = b.ps(es, "mpc", [128, 512], F32)
            pL, pL_r = b.ps(es, "mpL", [128, 4, 512], F32)
            Ktm, Ktm_r = b.sb(es, "Ktm", [128, 512], BF16)
            vin = [b.sb(es, "vin", [128, D], BF16) for _ in range(2)]
            sml, sml_r = b.sb(es, "msml", [128, 128], F32)
            smo = [b.sb(es, "msmo", [128, 16], F32) for _ in range(2)]
            PT, PT_r = b.sb(es, "mPT", [128, 32], F32)
            Et, E_r = b.sb(es, "mEt", [128, 32], F32)
            Vt = [b.sb(es, "mVt", [128, 2, MLW], BF16) for _ in range(2)]
            Vw, Vw_r = b.sb(es, "mVw", [128, 2, MLW], BF16)
            Sf, Sf_r = b.sb(es, "mSf", [128, MLW], F32)
            Stmp, Stmp_r = b.sb(es, "mStmp", [128, MLW], F32)
            Sfb = [b.sb(es, "mSfb", [128, MLW], BF16) for _ in range(2)]
            Lb = [b.sb(es, "mLb", [128, MLW], F32) for _ in range(2)]
            O.memset('dve', Sf[:], 0.0, [Sf_r])
            for (t0, bw) in TBLKS:
                conv_silu(b, O, l, mqkT, mqkT_r, t0, bw, 8, cwt, cw_r, cbt, cb_r, xh, xh_r, tmps, uT, uT_r)
                O.dma(mQT[:, :, t0:t0 + bw].rearrange("g p t -> p g t"), uT[:, 0:4, 0:bw], [uT_r], rs_of(mQT_r, t0, bw))
                O.dma(mKT[:, :, t0:t0 + bw].rearrange("g p t -> p g t"), uT[:, 4:8, 0:bw], [uT_r], rs_of(mKT_r, t0, bw))
                for ti in range(bw // 128):
                    c = t0 // 128 + ti
                    r0 = c * 128
                    tsl = slice(ti * 128, (ti + 1) * 128)
                    for h in range(4):
                        O.tr(pK[:, h * 128:(h + 1) * 128], uT[:, 4 + h, tsl], identb[:], [uT_r, identb_r], [pK_r], inc=(h == 3))
                    O.copy('act', Ktm[:], pK[:, 0:512], [pK_r], [Ktm_r])
                    v_t, v_r = vin[c % 2]
                    O.dma(v_t[:], mvo[r0:r0 + 128, 0:D], rs_of(mvo_r, r0, 128), [v_r])
                    so, so_r = smo[c % 2]
                    O.dma(sml[:, 0:16], small[r0:r0 + 128, 32:48], rs_of(small_r, r0, 128), [sml_r])
                    O.tt('dve', sml[:, 16:24], sml[:, 0:8], ibt[:], ALU.add, [sml_r, ib_r], [sml_r])
                    O.act(sml[:, 16:24], sml[:, 16:24], AF.Exp, [sml_r], [sml_r])
                    O.ts('dve', sml[:, 16:24], sml[:, 16:24], 128.0 ** -0.5, None, ALU.mult, None, [sml_r], [sml_r])
                    O.tt('dve', sml[:, 24:32], sml[:, 8:16], fbt[:], ALU.add, [sml_r, fb_r], [sml_r])
                    O.ts('dve', sml[:, 24:32], sml[:, 24:32], -1.0, None, ALU.mult, None, [sml_r], [sml_r])
                    softplus(b, O, sml[:, 32:40], sml[:, 24:32], sml[:, 40:48], sml[:, 48:56], [], [sml_r])
                    O.ts('dve', so[:, 0:8], sml[:, 32:40], -1.0, None, ALU.mult, None, [sml_r], [so_r])
                    decay_scalars(b, O, c, so, so_r, 4, masks, masks_r, onesf, onesf_r, pc, pc_r, PT, PT_r,
                                  Et[:, 0:8], Et[:, 8:16], E_r, so[:, 8:16], Et[:, 16:24], cdall, cd_r)
                    O.dma(msm[r0:r0 + 128, :], so[:, :], [so_r, E_r], rs_of(msm_r, r0, 128))
                    O.tt('dve', Et[:, 24:32], sml[:, 16:24], Et[:, 16:24], ALU.mult, [sml_r, E_r], [E_r])
                    vt, vt_r = Vt[c % 2]
                    vv = v_t[:].rearrange("p (h e) -> p h e", h=4)
                    for d in range(2):
                        sw = sml[:, 16 + d * 4:16 + (d + 1) * 4]
                        ww = Et[:, 24 + d * 4:24 + (d + 1) * 4]
                        vo = vt[:, d, :].rearrange("p (h e) -> p h e", h=4)
                        wo = Vw[:, d, :].rearrange("p (h e) -> p h e", h=4)
                        O.tt('dve', vo[:, :, 0:256], vv, bc_mid(sw, [128, 4, 256]), ALU.mult, [v_r, sml_r], [vt_r])
                        O.copy('dve', vo[:, :, 256], sw, [sml_r], [vt_r])
                        O.tt('dve', wo[:, :, 0:256], vv, bc_mid(ww, [128, 4, 256]), ALU.mult, [v_r, E_r], [Vw_r])
                        O.copy('dve', wo[:, :, 256], ww, [E_r], [Vw_r])
                        O.dma(mVt[d, r0:r0 + 128, :], vt[:, d, :], [vt_r], rs_of(mVt_r, r0, 128))
                    sb_t, sb_r = Sfb[c % 2]
                    lb_t, lb_r = Lb[c % 2]
                    for d in range(2):
                        for h in range(4):
                            O.mm(pL[:, h, 0:257], Ktm[:, h * 128:(h + 1) * 128], Vw[:, d, h * 257:(h + 1) * 257], True, True,
                                 [Ktm_r, Vw_r], [pL_r], inc=(h == 3))
                        if d == 0:
                            O.copy('pool', sb_t[:], Sf[:], [Sf_r], [sb_r])
                            O.dma(mprev[0, c, :, :], sb_t[:], [sb_r], [mprev_r[c]])
                            O.tt('dve', Stmp[:].rearrange("p (h e) -> p h e", h=4), Sf[:].rearrange("p (h e) -> p h e", h=4),
                                 bc_mid(cdall[:, c, 0:4], [128, 4, 257]), ALU.mult, [Sf_r, cd_r], [Stmp_r])
                            O.tt('dve', Sf[:].rearrange("p (h e) -> p h e", h=4), Stmp[:].rearrange("p (h e) -> p h e", h=4),
                                 pL[:, :, 0:257], ALU.add, [Stmp_r, pL_r], [Sf_r])
                        else:
                            O.copy('act', lb_t[:].rearrange("p (h e) -> p h e", h=4), pL[:, :, 0:257], [pL_r], [lb_r])
                            O.dma(mLb[c, :, :], lb_t[:], [lb_r], [mLb_r[c]])
            O.memset('dve', Sf[:], 0.0, [Sf_r])
            for c in BWD_ORDER:
                lb_t, lb_r = Lb[c % 2]
                sb_t, sb_r = Sfb[c % 2]
                O.dma(lb_t[:], mLb[c, :, :], [mLb_r[c]], [lb_r])
                O.copy('pool', sb_t[:], Sf[:], [Sf_r], [sb_r])
                O.dma(mprev[1, c, :, :], sb_t[:], [sb_r], [mprev_r[NCH + c]])
                O.tt('dve', Stmp[:].rearrange("p (h e) -> p h e", h=4), Sf[:].rearrange("p (h e) -> p h e", h=4),
                     bc_mid(cdall[:, c, 4:8], [128, 4, 257]), ALU.mult, [Sf_r, cd_r], [Stmp_r])
                O.tt('dve', Sf[:], Stmp[:], lb_t[:], ALU.add, [Stmp_r, lb_r], [Sf_r])
            P.barrier()

        mngt, mng_r = C['mng']
        yT, yT_r = G['yT'], G['yT_r']
        for pair in range(2):
            with ExitStack() as es:
                dn, dn_r = b.sb(es, "dn", [128, 8], F32)
                hp, hp_r = b.sb(es, "hp", [128, 2, 256], F32)
                hq, hq_r = b.sb(es, "hq", [128, 2, 256], F32)
                hall = [b.sb(es, "hall", [128, D], F32) for _ in range(2)]
                o2 = [b.sb(es, "o2", [128, D], BF16) for _ in range(2)]
                z2 = [b.sb(es, "mz2", [128, D], BF16) for _ in range(2)]
                sg, sg_r = b.sb(es, "sg", [128, D], F32)
                st4, st4_r = b.sb(es, "st4", [128, 8], F32)
                ybf, ybf_r = b.sb(es, "mybf", [128, D], BF16)
                yTs = [b.sb(es, "myTs", [128, 8, 128], BF16) for _ in range(2)]
                pYT, pYT_r = b.ps(es, "mpYT", [128, D], BF16)

                def load_extra(c, pair=pair):
                    if pair == 0:
                        return
                    r0 = c * 128
                    h_t, h_r = hall[c % 2]
                    o_t, o_r = o2[c % 2]
                    z_t, z_r = z2[c % 2]
                    O.dma(h_t[:, 0:512], mh[r0:r0 + 128, :], rs_of(mh_r, r0, 128), [h_r])
                    O.dma(o_t[:], mvo[r0:r0 + 128, D:2 * D], rs_of(mvo_r, r0, 128), [o_r])
                    O.dma(z_t[:], mz[r0:r0 + 128, :], rs_of(mz_r, r0, 128), [z_r])

                def finish(c, Yf, Yf_r, Yb, Yb_r, pair=pair):
                    r0 = c * 128
                    h_t, h_r = hall[c % 2]
                    Yfv = Yf.rearrange("p (g w) -> p g w", g=2)
                    Ybv = Yb.rearrange("p (g w) -> p g w", g=2)
                    O.act(dn[:, 0:2], Yfv[:, :, 256], AF.Abs, [Yf_r], [dn_r])
                    O.act(dn[:, 2:4], Ybv[:, :, 256], AF.Abs, [Yb_r], [dn_r])
                    O.ts('dve', dn[:, 0:4], dn[:, 0:4], 1.0, None, ALU.max, None, [dn_r], [dn_r])
                    O.recip(dn[:, 4:8], dn[:, 0:4], [dn_r], [dn_r])
                    O.tt('dve', hp[:], Yfv[:, :, 0:256], bc_mid(dn[:, 4:6], [128, 2, 256]), ALU.mult, [Yf_r, dn_r], [hp_r])
                    O.tt('dve', hq[:], Ybv[:, :, 0:256], bc_mid(dn[:, 6:8], [128, 2, 256]), ALU.mult, [Yb_r, dn_r], [hq_r])
                    if pair == 0:
                        O.tt('pool', hp[:], hp[:], hq[:], ALU.add, [hp_r, hq_r], [hp_r])
                        O.dma(mh[r0:r0 + 128, :], hp[:].rearrange("p g e -> p (g e)"), [hp_r], rs_of(mh_r, r0, 128))
                        return
                    o_t, o_r = o2[c % 2]
                    z_t, z_r = z2[c % 2]
                    O.tt('pool', h_t[:, 512:1024], hp[:].rearrange("p g e -> p (g e)"), hq[:].rearrange("p g e -> p (g e)"),
                         ALU.add, [hp_r, hq_r], [h_r])
                    O.act(sg[:], o_t[:], AF.Sigmoid, [o_r], [sg_r])
                    O.tt('dve', h_t[:], h_t[:], sg[:], ALU.mult, [h_r, sg_r], [h_r])
                    O.tt('dve', sg[:], h_t[:], h_t[:], ALU.mult, [h_r, sg_r], [sg_r])
                    O.reduce(st4[:, 0:4], sg[:].rearrange("p (h e) -> p h e", h=4), ALU.add, [sg_r], [st4_r])
                    O.act(st4[:, 4:8], st4[:, 0:4], AF.Sqrt, [st4_r], [st4_r], scale=1.0 / 256, bias=EPS)
                    O.recip(st4[:, 4:8], st4[:, 4:8], [st4_r], [st4_r])
                    O.tt('dve', h_t[:].rearrange("p (h e) -> p h e", h=4), h_t[:].rearrange("p (h e) -> p h e", h=4),
                         bc_mid(st4[:, 4:8], [128, 4, 256]), ALU.mult, [h_r, st4_r], [h_r])
                    O.tt('dve', h_t[:], h_t[:], mngt[:], ALU.mult, [h_r, mng_r], [h_r])
                    O.act(sg[:], z_t[:], AF.Silu, [z_r, sg_r], [sg_r])
                    O.tt('dve', ybf[:], h_t[:], sg[:], ALU.mult, [h_r, sg_r], [ybf_r])
                    yt, yt_r = yTs[c % 2]
                    for k in range(8):
                        O.tr(pYT[:, k * 128:(k + 1) * 128], ybf[:, k * 128:(k + 1) * 128], identb[:], [ybf_r, identb_r], [pYT_r], inc=(k == 7))
                    O.copy('act', yt[:].rearrange("p k t -> p (k t)"), pYT[:], [pYT_r], [yt_r])
                    O.dma(yT[2, :, :, r0:r0 + 128].rearrange("k p t -> p k t"), yt[:], [yt_r], rs_of(yT_r, r0, 128))

                la_pass2(b, O, G, es, NG=2, NSUB=1, PV=257, QT=mQT, QT_r=mQT_r, KT=mKT, KT_r=mKT_r, g0=2 * pair,
                         Vt=mVt, Vt_r=mVt_r, vcol0=2 * pair * 257, sm=msm, sm_r=msm_r, NA=4, a0=2 * pair,
                         sprev=mprev, sprev_r=mprev_r, load_extra=load_extra, finish=finish)
                P.barrier()


def phase_attn(b, l, G, C):
    P = b.P
    O = Ops(P)
    S = G['scr']
    identb, identb_r = G['identb'], G['identb_r']
    onesb, onesb_r = G['onesb'], G['onesb_r']
    aqk, aqk_r = G['aqk'], G['aqk_r']
    avz, avz_r = G['avz'], G['avz_r']
    azT, azT_r = G['azT'], G['azT_r']
    yT, yT_r = G['yT'], G['yT_r']
    cos_in, sin_in = G['cos_in'], G['sin_in']
    aQT, aQT_r = S('aQT', [8, 128, NT], BF16)
    aKT, aKT_r = S('aKT', [8, 128, NT], BF16)
    gqt, gq_r = C['gq']
    gkt, gk_r = C['gk']
    lampt, lamp_r = C['lamp']
    subgt, subg_r = C['subg']
    lamit, lami_r = C['lami']

    with ExitStack() as es0:
        sc, sc_r = b.sb(es0, "asc", [128, 16], F32)
        ggt, gg_r = b.sb(es0, "ggt", [128, 2, 64], F32)
        tmp64, tmp64_r = b.sb(es0, "tmp64", [128, 64], F32)
        O.ts('dve', ggt[:, 0, :], gqt[:], 0.125, None, ALU.mult, None, [gq_r], [gg_r])
        O.copy('dve', ggt[:, 1, :], gkt[:], [gk_r], [gg_r])
        O.reduce(sc[:, 4:6], ggt[:], ALU.max, [gg_r], [sc_r], absval=True)
        O.tt('dve', sc[:, 6:7], sc[:, 4:5], sc[:, 5:6], ALU.mult, [sc_r], [sc_r])
        O.ts('dve', sc[:, 0:1], sc[:, 6:7], -64.0, None, ALU.mult, None, [sc_r], [sc_r])
        for i in range(2):
            O.tt('dve', tmp64[:], lampt[:, 2 * i, :], lampt[:, 2 * i + 1, :], ALU.mult, [lamp_r], [tmp64_r])
            O.reduce(sc[:, 7 + i:8 + i], tmp64[:], ALU.add, [tmp64_r], [sc_r])
        O.act(sc[:, 7:9], sc[:, 7:9], AF.Exp, [sc_r], [sc_r])
        O.tt('dve', sc[:, 9:10], sc[:, 7:8], sc[:, 8:9], ALU.subtract, [sc_r], [sc_r])
        O.tt('dve', sc[:, 1:2], sc[:, 9:10], lamit[:, 0:1], ALU.add, [sc_r, lami_r], [sc_r])
        O.ts('dve', sc[:, 2:3], sc[:, 1:2], -1.0, None, ALU.mult, None, [sc_r], [sc_r])
        O.ts('dve', sc[:, 10:11], lamit[:, 0:1], -1.0, 1.0, ALU.mult, ALU.add, [lami_r], [sc_r])
        O.tt('dve', sc[:, 3:4], sc[:, 10:11], subgt[:, 0:1], ALU.mult, [sc_r, subg_r], [sc_r])

        with ExitStack() as es:
            qk = [b.sb(es, "qkraw", [128, 2048], BF16) for _ in range(2)]
            cs = [b.sb(es, "cs", [128, 2, 64], F32) for _ in range(2)]
            sq, sq_r = b.sb(es, "sq", [128, 2048], F32)
            xn, xn_r = b.sb(es, "xn", [128, 2048], F32)
            t2, t2_r = b.sb(es, "t2", [128, 2048], F32)
            st, st_r = b.sb(es, "ast", [128, 64], F32)
            xr, xr_r = b.sb(es, "xr", [128, 2048], BF16)
            pQ = [b.ps(es, "pQ", [128, 1024], BF16) for _ in range(2)]
            qTt = [b.sb(es, "qTt", [128, 2, 8, 128], BF16) for _ in range(2)]
            for i in range(NCH):
                r0 = i * 128
                q_t, q_r = qk[i % 2]
                O.dma(q_t[:], aqk[r0:r0 + 128, :], rs_of(aqk_r, r0, 128), [q_r])
                O.tt('dve', sq[:], q_t[:], q_t[:], ALU.mult, [q_r], [sq_r])
                O.reduce(st[:, 0:32], sq[:].rearrange("p (g e) -> p g e", e=64), ALU.add, [sq_r], [st_r])
                O.act(st[:, 32:64], st[:, 0:32], AF.Sqrt, [st_r], [st_r], scale=1.0 / 64, bias=EPS)
                O.recip(st[:, 32:64], st[:, 32:64], [st_r], [st_r])
                O.tt('dve', xn[:].rearrange("p (g e) -> p g e", e=64), q_t[:].rearrange("p (g e) -> p g e", e=64),
                     bc_mid(st[:, 32:64], [128, 32, 64]), ALU.mult, [q_r, st_r], [xn_r])
                for hf in range(2):
                    xv = xn[:, hf * 1024:(hf + 1) * 1024].rearrange("p (g e) -> p g e", e=64)
                    O.tt('pool', xv, xv, bc_lead(ggt[:, hf, :], [128, 16, 64]), ALU.mult, [xn_r, gg_r], [xn_r])
                if i >= 2:
                    c_t, c_r = cs[i % 2]
                    lt = r0 - TC
                    O.dma(c_t[:, 0, :], cos_in[lt:lt + 128, :], [], [c_r])
                    O.dma(c_t[:, 1, :], sin_in[lt:lt + 128, :], [], [c_r])
                    xg = xn[:].rearrange("p (g r q e) -> p g r q e", g=32, r=2, q=2, e=16)
                    tg = t2[:].rearrange("p (g r q e) -> p g r q e", g=32, r=2, q=2, e=16)
                    sv = c_t[:, 1, :].rearrange("p (r q e) -> p r q e", r=2, q=2, e=16)
                    for qq in range(2):
                        O.tt('dve', tg[:, :, :, qq, :], xg[:, :, :, 1 - qq, :],
                             sv[:, :, qq, :].unsqueeze(1).to_broadcast([128, 32, 2, 16]), ALU.mult, [xn_r, c_r], [t2_r])
                    O.tt('pool', xn[:].rearrange("p (g e) -> p g e", e=64), xn[:].rearrange("p (g e) -> p g e", e=64),
                         bc_lead(c_t[:, 0, :], [128, 32, 64]), ALU.mult, [xn_r, c_r], [xn_r])
                    O.tt('dve', xr[:], xn[:], t2[:], ALU.add, [xn_r, t2_r], [xr_r])
                else:
                    O.copy('dve', xr[:], xn[:], [xn_r], [xr_r])
                qt_t, qt_r = qTt[i % 2]
                for w in range(2):
                    p_t, p_r = pQ[w]
                    for h in range(8):
                        O.tr(p_t[:, h * 128:(h + 1) * 128], xr[:, w * 1024 + h * 128:w * 1024 + (h + 1) * 128], identb[:],
                             [xr_r, identb_r], [p_r], inc=(h == 7))
                    O.copy('act', qt_t[:, w, :, :].rearrange("p h t -> p (h t)"), p_t[:], [p_r], [qt_r])
                O.dma(aQT[:, :, r0:r0 + 128].rearrange("h p t -> p h t"), qt_t[:, 0, :, :], [qt_r], rs_of(aQT_r, r0, 128))
                O.dma(aKT[:, :, r0:r0 + 128].rearrange("h p t -> p h t"), qt_t[:, 1, :, :], [qt_r], rs_of(aKT_r, r0, 128))
            P.barrier()

        with ExitStack() as es:
            KTs = [b.sb(es, "KTs", [128, NT], BF16) for _ in range(2)]
            Vh = [b.sb(es, "Vh", [128, NCH, 128], BF16) for _ in range(2)]
            qb = [b.sb(es, "qb", [128, 512], BF16) for _ in range(2)]
            zb = [b.sb(es, "zb", [128, 512], BF16) for _ in range(2)]
            Pt = [b.sb(es, "Pt", [128, 2, 512], BF16) for _ in range(3)]
            rc, rc_r = b.sb(es, "rc", [128, 2, 512], F32)
            o0, o0_r = b.sb(es, "o0", [128, 512], F32)
            o1, o1_r = b.sb(es, "o1", [128, 512], F32)
            osq, osq_r = b.sb(es, "osq", [128, 512], BF16)
            rst, rst_r = b.sb(es, "rst", [128, 512], F32)
            szt, sz_r = b.sb(es, "aszt", [128, 512], F32)
            yo = [b.sb(es, "yo", [128, 512], BF16) for _ in range(2)]
            pSc = [b.ps(es, "pSc", [128, 2, 512], F32) for _ in range(2)]
            pO, pO_r = b.ps(es, "pO", [128, 2, 512], F32)
            pSm, pSm_r = b.ps(es, "pSm", [128, 2, 512], F32)
            it = 0
            nb = 0
            for h in range(8):
                k_t, k_r = KTs[h % 2]
                v_t, v_r = Vh[h % 2]
                O.dma(k_t[:], aKT[h, :, :], aKT_r, [k_r])
                for q4 in range(0, NCH, 11):
                    O.dma(v_t[:, q4:q4 + 11, :], avz[q4 * 128:(q4 + 11) * 128, h * 128:(h + 1) * 128].rearrange("(c p) e -> p c e", p=128),
                          rs_of(avz_r, q4 * 128, 11 * 128), [v_r])
                for (t0, bw) in TBLKS:
                    nk = 2 if t0 < TC else NCH
                    q_t, q_r = qb[nb % 2]
                    z_t, z_r = zb[nb % 2]
                    y_t, y_r = yo[nb % 2]
                    nb += 1
                    O.dma(q_t[:, 0:bw], aQT[h, :, t0:t0 + bw], rs_of(aQT_r, t0, bw), [q_r])
                    O.dma(z_t[:, 0:bw], azT[h, :, t0:t0 + bw], rs_of(azT_r, t0, bw), [z_r])
                    def emit_qk(kc, slot):
                        ps_t, ps_r = pSc[slot % 2]
                        for cm in range(2):
                            O.mm(ps_t[:, cm, 0:bw], k_t[cm * 64:(cm + 1) * 64, kc * 128:(kc + 1) * 128],
                                 q_t[cm * 64:(cm + 1) * 64, 0:bw], True, True, [k_r, q_r], [ps_r], inc=(cm == 1))
                    emit_qk(0, it)
                    for kc in range(nk):
                        ps_t, ps_r = pSc[it % 2]
                        p_t, p_r = Pt[it % 3]
                        if kc + 1 < nk:
                            emit_qk(kc + 1, it + 1)
                        O.act(p_t[:, :, 0:bw], ps_t[:, :, 0:bw], AF.Exp, [ps_r, sc_r], [p_r], bias=sc[:, 0:1])
                        for cm in range(2):
                            O.mm(pO[:, cm, 0:bw], v_t[:, kc, :], p_t[:, cm, 0:bw], kc == 0, kc == nk - 1, [v_r, p_r], [pO_r], inc=False)
                            O.mm(pSm[:, cm, 0:bw], onesb[:, :], p_t[:, cm, 0:bw], kc == 0, kc == nk - 1, [onesb_r, p_r], [pSm_r], inc=(cm == 1))
                        it += 1
                    O.P.op('dve', lambda e, bw=bw: e.reciprocal(out=rc[:, :, 0:bw], in_=pSm[:, :, 0:bw]), [pSm_r], [rc_r])
                    O.tt('dve', o0[:, 0:bw], pO[:, 0, 0:bw], rc[:, 0, 0:bw], ALU.mult, [pO_r, rc_r], [o0_r])
                    O.tt('dve', o1[:, 0:bw], pO[:, 1, 0:bw], rc[:, 1, 0:bw], ALU.mult, [pO_r, rc_r], [o1_r])
                    O.stt(o0[:, 0:bw], o1[:, 0:bw], sc[:, 2:3], o0[:, 0:bw], ALU.mult, ALU.add, [o1_r, o0_r, sc_r], [o0_r])
                    O.act(osq[:, 0:bw], o0[:, 0:bw], AF.Square, [o0_r], [osq_r])
                    ps_t, ps_r = pSc[it % 2]
                    it += 1
                    O.mm(ps_t[:, 0, 0:bw], onesb[:, :], osq[:, 0:bw], True, True, [onesb_r, osq_r], [ps_r])
                    O.act(rst[:, 0:bw], ps_t[:, 0, 0:bw], AF.Sqrt, [ps_r], [rst_r], scale=1.0 / 128, bias=EPS)
                    O.recip(rst[:, 0:bw], rst[:, 0:bw], [rst_r], [rst_r])
                    O.act(szt[:, 0:bw], z_t[:, 0:bw], AF.Silu, [z_r], [sz_r])
                    O.stt(o0[:, 0:bw], o0[:, 0:bw], sc[:, 3:4], rst[:, 0:bw], ALU.mult, ALU.mult, [o0_r, sc_r, rst_r], [o0_r])
                    O.tt('dve', y_t[:, 0:bw], o0[:, 0:bw], szt[:, 0:bw], ALU.mult, [o0_r, sz_r], [y_r])
                    O.dma(yT[1, h, :, t0:t0 + bw], y_t[:, 0:bw], [y_r], rs_of(yT_r, t0, bw))
            P.barrier()


def phase_merge(b, l, G, C):
    P = b.P
    O = Ops(P)
    yT, yT_r = G['yT'], G['yT_r']
    gtT, gtT_r = G['gtT'], G['gtT_r']
    w_br, w_out = G['w_br'], G['w_out']
    xsrc, xsrc_r, xdst, xdst_r, dst_off = G['xsrc'], G['xsrc_r'], G['xdst'], G['xdst_r'], G['dst_off']
    gbc, gbc_r = G['gbc'], G['gbc_r']
    with ExitStack() as es:
        stg = [b.sb(es, "wstg", [128, D], F32) for _ in range(2)]
        Wb, Wb_r = b.sb(es, "Wb", [128, 3, 8, D], BF16)
        Wo, Wo_r = b.sb(es, "Wo", [128, 8, D], BF16)
        yb = [b.sb(es, "yb", [128, 3, 8, 256], BF16) for _ in range(2)]
        gb = [b.sb(es, "gb", [128, 8, 256], BF16) for _ in range(2)]
        sg, sg_r = b.sb(es, "msg", [128, 8, 256], F32)
        macc, macc_r = b.sb(es, "macc", [128, 8, 256], F32)
        mt = [b.sb(es, "mtmp", [128, 512], F32) for _ in range(2)]
        mxT, mxT_r = b.sb(es, "mxT", [128, 8, 256], BF16)
        xin_t = [b.sb(es, "xin_t", [128, D], F32) for _ in range(2)]
        xo_t = [b.sb(es, "xo_t", [128, D], F32) for _ in range(2)]
        pm_ = [b.ps(es, "pmg", [128, 512], F32) for _ in range(4)]
        n = 0
        for br in range(4):
            for kc in range(8):
                st, st_r = stg[n % 2]
                n += 1
                src = w_br[l, br, kc * 128:(kc + 1) * 128, :] if br < 3 else w_out[l, kc * 128:(kc + 1) * 128, :]
                O.dma(st[:], src, [], [st_r])
                dst = Wb[:, br, kc, :] if br < 3 else Wo[:, kc, :]
                O.copy('pool', dst, st[:], [st_r], [Wb_r if br < 3 else Wo_r])
        npm = 0
        nm = 0
        for bi, (t0, bw) in enumerate([(256 * i, 256) for i in range(NT // 256)]):
            y_t, y_r = yb[bi % 2]
            for br in range(3):
                O.dma(y_t[:, br, :, 0:bw], yT[br, :, :, t0:t0 + bw].rearrange("k p t -> p k t"), rs_of(yT_r, t0, bw), [y_r])
            for br in range(3):
                g_t, g_r = gb[(bi * 3 + br) % 2]
                O.dma(g_t[:, :, 0:bw], gtT[br * 8:(br + 1) * 8, :, t0:t0 + bw].rearrange("k p t -> p k t"), rs_of(gtT_r, t0, bw), [g_r])
                O.act(sg[:, :, 0:bw], g_t[:, :, 0:bw], AF.Sigmoid, [g_r], [sg_r])
                for fc in range(8):
                    p_t, p_r = pm_[npm % 4]
                    npm += 1
                    for kc in range(8):
                        O.mm(p_t[:, 0:bw], Wb[:, br, kc, fc * 128:(fc + 1) * 128], y_t[:, br, kc, 0:bw], kc == 0, kc == 7,
                             [Wb_r, y_r], [p_r], inc=(kc == 7))
                    if br == 0:
                        O.tt('dve', macc[:, fc, 0:bw], p_t[:, 0:bw], sg[:, fc, 0:bw], ALU.mult, [p_r, sg_r], [macc_r])
                    else:
                        m_t, m_r = mt[nm % 2]
                        nm += 1
                        O.tt('dve', m_t[:, 0:bw], p_t[:, 0:bw], sg[:, fc, 0:bw], ALU.mult, [p_r, sg_r], [m_r])
                        if br == 1:
                            O.tt('pool', macc[:, fc, 0:bw], macc[:, fc, 0:bw], m_t[:, 0:bw], ALU.add, [macc_r, m_r], [macc_r])
                        else:
                            O.tt('pool', mxT[:, fc, 0:bw], macc[:, fc, 0:bw], m_t[:, 0:bw], ALU.add, [macc_r, m_r], [mxT_r])
            j = 1 if t0 < TC else 0
            for ti in range(bw // 128):
                r0 = t0 + ti * 128
                if r0 < dst_off:
                    continue
                xi, xi_r = xin_t[ti % 2]
                xo, xo_r = xo_t[ti % 2]
                O.dma(xi[:], xsrc[r0:r0 + 128, :], rs_of(xsrc_r, r0, 128), [xi_r])
                for hf in range(2):
                    p_t, p_r = pm_[npm % 4]
                    npm += 1
                    for kc in range(8):
                        O.mm(p_t[:, :], mxT[:, kc, ti * 128:(ti + 1) * 128], Wo[:, kc, hf * 512:(hf + 1) * 512], kc == 0, kc == 7,
                             [mxT_r, Wo_r], [p_r], inc=(kc == 7))
                    O.tt('dve', xo[:, hf * 512:(hf + 1) * 512], p_t[:, :], gbc[:, l, j, hf * 512:(hf + 1) * 512], ALU.mult,
                         [p_r, gbc_r], [xo_r])
                O.tt('pool', xo[:], xo[:], xi[:], ALU.add, [xo_r, xi_r], [xo_r])
                O.dma(xdst[r0 - dst_off:r0 - dst_off + 128, :], xo[:], [xo_r], rs_of(xdst_r, r0, 128))
        P.barrier()
```

```python
import math
import numpy as np
import ml_dtypes
from contextlib import ExitStack
import concourse.bass as bass
import concourse.mybir as mybir
from concourse.bass_utils import run_bass_kernel_spmd

F32 = mybir.dt.float32
BF16 = mybir.dt.bfloat16
AF = mybir.ActivationFunctionType
ALU = mybir.AluOpType
AX = mybir.AxisListType

ENGS = ['pe', 'act', 'dve', 'pool', 'sp']
NDMA = {'sp': 12, 'act': 6, 'pool': 6}
SAME_ENG_SYNC = True
INTERNAL_OK = 'ALL'

DEPTH = 4
D = 1024
TC = 256
TL = 8192
NT = TC + TL
NCH = NT // 128
D_IN = 13872
EPS = 1e-6
FWD_ORDER = list(range(NCH))
BWD_ORDER = [1, 0] + list(range(NCH - 1, 1, -1))


class Res:
    __slots__ = ('name', 'w', 'r', 'pr')

    def __init__(self, name=''):
        self.name = name
        self.w = {}
        self.r = {}
        self.pr = {}


class Prog:
    def __init__(self, nc):
        self.nc = nc
        self.ops = {e: [] for e in ENGS}
        self.cnt = {e: 0 for e in ENGS}
        self.known = {e: {} for e in ENGS}
        self.dcnt = {}
        self.dnext = {q: 0 for q in NDMA}
        self.rr = 0
        self.floor = {}

    def barrier(self):
        for e in ENGS:
            if self.cnt[e] > 0:
                self.floor[e] = self.cnt[e]
        for sk, v in self.dcnt.items():
            self.floor[sk] = v

    def _deps(self, eng, reads, writes, extra=(), part=True):
        toks = {}

        def add(sk, v):
            if toks.get(sk, 0) < v:
                toks[sk] = v
        for r in reads:
            for sk, v in r.w.items():
                add(sk, v)
        for w in writes:
            if w.r or not part:
                for sk, v in w.w.items():
                    add(sk, v)
            for sk, v in w.r.items():
                add(sk, v)
            for sk, v in w.pr.items():
                add(sk, v)
        for sk, v in extra:
            add(sk, v)
        for sk, v in self.floor.items():
            add(sk, v)
        waits = []
        kn = self.known[eng]
        for sk, v in toks.items():
            if sk == eng and (eng == 'pe' or not SAME_ENG_SYNC):
                continue
            if kn.get(sk, 0) < v:
                kn[sk] = v
                waits.append((sk, v))
        return waits

    def _mark(self, tok, reads, writes, part=True):
        sk, v = tok
        for w in writes:
            if w.r or not part:
                pr = dict(w.r)
                for k2, v2 in w.w.items():
                    if pr.get(k2, 0) < v2:
                        pr[k2] = v2
                w.pr = pr
                w.w = {sk: v}
                w.r = {}
            elif w.w.get(sk, 0) < v:
                w.w[sk] = v
        for r in reads:
            if r in writes:
                continue
            if r.r.get(sk, 0) < v:
                r.r[sk] = v

    def op(self, eng, fn, reads=(), writes=(), inc=True):
        waits = self._deps(eng, reads, writes)
        if inc:
            self.cnt[eng] += 1
            tok = (eng, self.cnt[eng])
        else:
            tok = (eng, self.cnt[eng] + 1)
        self.ops[eng].append((waits, fn, eng if inc else None, 1))
        self._mark(tok, reads, writes)

    def dma(self, q, out, in_, reads=(), writes=(), **kw):
        if q is None:
            q = ('sp', 'sp', 'act')[self.rr % 3]
            self.rr += 1
        j = self.dnext[q]
        self.dnext[q] = (j + 1) % NDMA[q]
        sk = '%s_d%d' % (q, j)
        prev = self.dcnt.get(sk, 0)
        waits = self._deps(q, reads, writes, extra=[(sk, prev)] if prev else [])
        self.dcnt[sk] = prev + 16
        tok = (sk, prev + 16)
        self.ops[q].append((waits, lambda e: e.dma_start(out=out, in_=in_, **kw), sk, 16))
        self._mark(tok, reads, writes)

    def finish(self):
        nc = self.nc
        sems = {}
        for e in ENGS:
            sems[e] = nc.alloc_semaphore(name='s_' + e)
        for sk in self.dcnt:
            sems[sk] = nc.alloc_semaphore(name='s_' + sk)
        fin = [(e, self.cnt[e]) for e in ENGS if e != 'sp' and self.cnt[e] > 0]
        fin += [(sk, v) for sk, v in self.dcnt.items()]
        handles = {'pe': 'tensor', 'act': 'scalar', 'dve': 'vector', 'pool': 'gpsimd', 'sp': 'sync'}
        ops = self.ops

        def replay(eng):
            def body(e):
                for waits, fn, sk, n in ops[eng]:
                    for wk, wv in waits:
                        e.wait_ge(sems[wk], wv)
                    ins = fn(e)
                    if sk is not None:
                        ins.then_inc(sems[sk], n)
                if eng == 'sp':
                    for wk, wv in fin:
                        e.wait_ge(sems[wk], wv)
            return body

        with nc.Block() as block:
            for eng in ENGS:
                getattr(block, handles[eng])(replay(eng))


class Ops:
    def __init__(self, P):
        self.P = P

    def tt(self, eng, out, in0, in1, op, R, W):
        self.P.op(eng, lambda e: e.tensor_tensor(out=out, in0=in0, in1=in1, op=op), R, W)

    def ts(self, eng, out, in0, s1, s2, op0, op1, R, W):
        if op1 is None:
            self.P.op(eng, lambda e: e.tensor_scalar(out=out, in0=in0, scalar1=s1, scalar2=None, op0=op0), R, W)
        else:
            self.P.op(eng, lambda e: e.tensor_scalar(out=out, in0=in0, scalar1=s1, scalar2=s2, op0=op0, op1=op1), R, W)

    def stt(self, out, in0, scalar, in1, op0, op1, R, W):
        self.P.op('dve', lambda e: e.scalar_tensor_tensor(out=out, in0=in0, scalar=scalar, in1=in1, op0=op0, op1=op1), R, W)

    def act(self, out, in_, func, R, W, bias=None, scale=None, accum=None):
        kw = {}
        if bias is not None:
            kw['bias'] = bias
        if scale is not None:
            kw['scale'] = scale
        if accum is not None:
            kw['accum_out'] = accum
        self.P.op('act', lambda e: e.activation(out=out, in_=in_, func=func, **kw), R, W)

    def copy(self, eng, out, in_, R, W):
        if eng == 'act':
            self.P.op('act', lambda e: e.copy(out=out, in_=in_), R, W)
        else:
            self.P.op(eng, lambda e: e.tensor_copy(out=out, in_=in_), R, W)

    def memset(self, eng, ap, val, W):
        self.P.op(eng, lambda e: e.memset(ap, val), (), W)

    def recip(self, out, in_, R, W):
        self.P.op('dve', lambda e: e.reciprocal(out=out, in_=in_), R, W)

    def reduce(self, out, in_, op, R, W, absval=None):
        self.P.op('dve', lambda e: e.tensor_reduce(out=out, in_=in_, axis=AX.X, op=op, apply_absolute_value=absval), R, W)

    def mm(self, out, lhsT, rhs, start, stop, R, W, inc=True):
        self.P.op('pe', lambda e: e.matmul(out, lhsT, rhs, start=start, stop=stop, skip_group_check=True), R, W, inc=inc)

    def tr(self, out, in_, ident, R, W, inc=True):
        self.P.op('pe', lambda e: e.transpose(out=out, in_=in_, identity=ident), R, W, inc=inc)

    def dma(self, out, in_, R, W, q=None):
        self.P.dma(q, out, in_, R, W)


def bc_mid(ap, shape):
    return ap.unsqueeze(2).to_broadcast(shape)


def bc_lead(ap, shape):
    return ap.unsqueeze(1).to_broadcast(shape)


class B:
    def __init__(self, nc):
        self.nc = nc
        self.P = Prog(nc)
        self.uid = 0

    def name(self, n):
        self.uid += 1
        return '%s_%d' % (n, self.uid)

    def sb(self, es, n, shape, dt):
        t = es.enter_context(self.nc.sbuf_tensor(self.name(n), shape, dt))
        return t, Res(n)

    def ps(self, es, n, shape, dt=F32):
        t = es.enter_context(self.nc.psum_tensor(self.name(n), shape, dt))
        return t, Res(n)

    def dram(self, n, shape, dt, nres=1):
        t = self.nc.dram_tensor(self.name(n), shape, dt, kind="Internal").ap()
        return t, [Res(n) for _ in range(nres)]


def blk_of(t0, n):
    return list(range(t0 // 512, (t0 + n - 1) // 512 + 1))


NBLK = (NT + 511) // 512
TBLKS = [(0, 256)] + [(256 + 512 * i, 512) for i in range(16)]


def rs_of(rl, t0, n):
    return [rl[i] for i in blk_of(t0, n)]


C_XBC = 0
C_ZS = 1536
C_DT = 2560
C_Q = 2592
C_K = 3616
C_V = 4640
C_ZA = 5664
C_MQK = 6688
C_MV = 7712
C_MO = 8736
C_MZ = 9760
C_IG = 10784
C_FG = 10792
C_GT = 10800


def build_program(n_layers, dbg=None, only=None):
    dbg = dbg or ()
    scr_cache = {}
    nc = bass.Bass("TRN2", target_bir_lowering=False)
    b = B(nc)
    P = b.P
    L = n_layers

    def inp(n, shape, dt=F32):
        return nc.dram_tensor(n, shape, dt, kind="ExternalInput").ap()

    xin = inp("xin", [NT, D])
    cc = inp("cc", [128, 16])
    ident_in = inp("ident", [128, 128])
    masks_in = inp("masks", [128, 4, 128])
    cos_in = inp("cosb", [TL, 64])
    sin_in = inp("sinb", [TL, 64])
    lami_in = inp("lami", [128, L])
    w_mod = inp("w_mod", [L, D, 3 * D])
    bmod = inp("bmod", [128, L, 24])
    ng = inp("ng", [128, L, 8])
    w_in = inp("w_in", [L, D, D_IN])
    cw = inp("cw", [128, L, 12, 3])
    cb = inp("cb", [128, L, 12])
    alog = inp("alog", [128, L, 32])
    dtb = inp("dtb", [128, L, 32])
    dsk = inp("dsk", [128, L, 16])
    sng = inp("sng", [128, L, D])
    gq = inp("gq", [128, L, 64])
    gk = inp("gk", [128, L, 64])
    lamp = inp("lamp", [128, L, 4, 64])
    subg = inp("subg", [128, L, 8])
    mcw = inp("mcw", [128, L, 8, 3])
    mcb = inp("mcb", [128, L, 8])
    ib = inp("ib", [128, L, 8])
    fb = inp("fb", [128, L, 8])
    mng = inp("mng", [128, L, D])
    sel_in = inp("sel", [2, 2, 128])
    bgate = inp("bgate", [2, L, D])
    w_br = inp("w_br", [L, 3, D, D])
    w_out = inp("w_out", [L, D, D])
    xout = nc.dram_tensor("xout", [NT, D], F32, kind="ExternalOutput").ap()

    def scratch(n, shape, dt, nres=NBLK):
        if only is not None and n in only.get('inputs', ()):
            t = nc.dram_tensor("dbg_" + n, shape, dt, kind="ExternalInput").ap()
            return t, [Res(n) for _ in range(nres)]
        if n in dbg or INTERNAL_OK is None or (INTERNAL_OK != 'ALL' and n not in INTERNAL_OK):
            t = nc.dram_tensor("dbg_" + n, shape, dt, kind="ExternalOutput").ap()
            return t, [Res(n) for _ in range(nres)]
        return b.dram(n, shape, dt, nres)

    xbufs = [(xin, [Res('xin') for _ in range(NBLK)])]
    for l in range(1, L):
        xbufs.append(scratch("xres%d" % l, [NT, D], F32))
    xout_res = [Res('xout') for _ in range(NBLK)]

    hT, hT_r = scratch("hT", [8, 128, NT], BF16)
    xbcT, xbcT_r = scratch("xbcT", [12, 128, NT], BF16)
    mqkT, mqkT_r = scratch("mqkT", [8, 128, NT], BF16)
    zs, zs_r = scratch("zs", [NT, 1024], BF16)
    small, small_r = scratch("small", [NT, 48], F32)
    aqk, aqk_r = scratch("aqk", [NT, 2048], BF16)
    avz, avz_r = scratch("avz", [NT, 1024], BF16)
    azT, azT_r = scratch("azT", [8, 128, NT], BF16)
    mvo, mvo_r = scratch("mvo", [NT, 2048], BF16)
    mz, mz_r = scratch("mz", [NT, 1024], BF16)
    gtT, gtT_r = scratch("gtT", [24, 128, NT], BF16)
    yT, yT_r = scratch("yT", [3, 8, 128, NT], BF16)

    with ExitStack() as g_es:
        identf, identf_r = b.sb(g_es, "identf", [128, 128], F32)
        identb, identb_r = b.sb(g_es, "identb", [128, 128], BF16)
        masks, masks_r = b.sb(g_es, "masks", [128, 4, 128], F32)
        onesb, onesb_r = b.sb(g_es, "onesb", [128, 128], BF16)
        onesf, onesf_r = b.sb(g_es, "onesf", [128, 128], F32)
        gbc, gbc_r = b.sb(g_es, "gbc", [128, L, 2, D], F32)
        modt, modt_r = b.sb(g_es, "modt", [128, L, 24, 2], F32)
        a1t, a1t_r = b.sb(g_es, "a1t", [128, L, 8, 2], F32)
        P.dma('sp', identf[:], ident_in[:, :], writes=[identf_r])
        P.dma('sp', masks[:], masks_in[:, :, :], writes=[masks_r])
        P.op('dve', lambda e: e.tensor_copy(out=identb[:], in_=identf[:]), reads=[identf_r], writes=[identb_r])
        P.op('dve', lambda e: e.memset(onesb[:], 1.0), writes=[onesb_r])
        P.op('dve', lambda e: e.memset(onesf[:], 1.0), writes=[onesf_r])

        with ExitStack() as es:
            cct, cct_r = b.sb(es, "cct", [128, 16], F32)
            sct, sct_r = b.sb(es, "sct", [128, 16], F32)
            bmt, bmt_r = b.sb(es, "bmt", [128, L, 24], F32)
            ngt, ngt_r = b.sb(es, "ngt", [128, L, 8], F32)
            wm, wm_r = b.sb(es, "wm", [128, 8, 3 * D], F32)
            pm, pm_r = b.ps(es, "pm", [128, 512], F32)
            pgr, pgr_r = b.ps(es, "pgr", [128, 2, 512], F32)
            pgb, pgb_r = b.ps(es, "pgb", [128, 2, 512], F32)
            selt, selt_r = b.sb(es, "selt", [2, 2, 128], F32)
            bgt, bgt_r = b.sb(es, "bgt", [2, L, D], F32)
            grow, grow_r = b.sb(es, "grow", [2, D], F32)
            P.dma('sp', selt[:], sel_in[:, :, :], writes=[selt_r])
            P.dma('sp', bgt[:], bgate[:, :, :], writes=[bgt_r])
            P.dma('sp', cct[:], cc[:, :], writes=[cct_r])
            P.dma('sp', bmt[:], bmod[:, :, :], writes=[bmt_r])
            P.dma('sp', ngt[:], ng[:, :, :], writes=[ngt_r])
            P.op('act', lambda e: e.activation(out=sct[:], in_=cct[:], func=AF.Silu), reads=[cct_r], writes=[sct_r])
            if 'mod' in dbg:
                dsc = nc.dram_tensor("dbg_sct", [128, 16], F32, kind="ExternalOutput").ap()
                P.dma('sp', dsc[:, :], sct[:], reads=[sct_r], writes=[Res('x')])
            for l in range(L):
                for kc in range(8):
                    P.dma(None, wm[:, kc, :], w_mod[l, kc * 128:(kc + 1) * 128, :], writes=[wm_r])
                for fc in range(24):
                    for kc in range(8):
                        P.op('pe', lambda e, fc=fc, kc=kc: e.matmul(
                            pm[:, fc * 2:fc * 2 + 2], wm[:, kc, fc * 128:(fc + 1) * 128], sct[:, kc * 2:kc * 2 + 2],
                            start=(kc == 0), stop=(kc == 7)),
                            reads=[wm_r, sct_r], writes=[pm_r], inc=(fc == 23 and kc == 7))
                for hf in range(2):
                    for kc in range(8):
                        P.op('pe', lambda e, hf=hf, kc=kc: e.matmul(
                            pgr[0:2, hf, :], sct[:, kc * 2:kc * 2 + 2], wm[:, kc, 2 * D + hf * 512:2 * D + (hf + 1) * 512],
                            start=(kc == 0), stop=(kc == 7)), reads=[wm_r, sct_r], writes=[pgr_r], inc=(kc == 7))
                P.op('dve', lambda e, l=l: e.tensor_tensor(out=grow[:, :].rearrange("p (h f) -> p h f", h=2), in0=pgr[0:2, :, :],
                                                       in1=bgt[:, l, :].rearrange("p (h f) -> p h f", h=2), op=ALU.add),
                     reads=[pgr_r, bgt_r], writes=[grow_r])
                for j in range(2):
                    for hf in range(2):
                        P.op('pe', lambda e, j=j, hf=hf: e.matmul(pgb[:, hf, :], selt[:, j, :], grow[:, hf * 512:(hf + 1) * 512],
                                                                start=True, stop=True), reads=[selt_r, grow_r], writes=[pgb_r])
                    P.op('dve', lambda e, l=l, j=j: e.tensor_copy(out=gbc[:, l, j, :].rearrange("p (h f) -> p h f", h=2), in_=pgb[:, :, :]),
                         reads=[pgb_r], writes=[gbc_r])
                for j in range(2):
                    P.op('dve', lambda e, l=l, j=j: e.tensor_tensor(
                        out=modt[:, l, :, j], in0=pm[:, 0:48].rearrange("p (f j) -> p f j", j=2)[:, :, j],
                        in1=bmt[:, l, :], op=ALU.add), reads=[pm_r, bmt_r], writes=[modt_r])
                for j in range(2):
                    P.op('dve', lambda e, l=l, j=j: e.scalar_tensor_tensor(
                        out=a1t[:, l, :, j], in0=modt[:, l, 8:16, j], scalar=1.0, in1=ngt[:, l, :],
                        op0=ALU.add, op1=ALU.mult), reads=[modt_r, ngt_r], writes=[a1t_r])
            P.barrier()

        if 'mod' in dbg:
            dmod = nc.dram_tensor("dbg_mod", [128, L * 48], F32, kind="ExternalOutput").ap()
            da1 = nc.dram_tensor("dbg_a1", [128, L * 16], F32, kind="ExternalOutput").ap()
            P.dma('sp', dmod[:, :], modt[:].rearrange("p l f j -> p (l f j)"), reads=[modt_r], writes=[Res('x')])
            P.dma('sp', da1[:, :], a1t[:].rearrange("p l f j -> p (l f j)"), reads=[a1t_r], writes=[Res('x')])
        for l in range(L):
            xsrc, xsrc_r = xbufs[l]
            if l + 1 < L:
                xdst, xdst_r = xbufs[l + 1]
                dst_off = 0
            else:
                xdst, xdst_r = xout, xout_res
                dst_off = 0
            layer(b, l, L, locals())

        P.finish()
        global LAST_PROG
        LAST_PROG = P
    return nc


def layer(b, l, L, G):
    cache = G['scr_cache']

    def S(n, shape, dt, nres=NBLK):
        if n not in cache:
            cache[n] = G['scratch'](n, shape, dt, nres)
        return cache[n]
    G['scr'] = S
    only = G.get('only')
    with ExitStack() as es:
        C = layer_consts(b, es, l, G)
        if only is None or 'a' in only:
            phase_a1(b, l, G)
            phase_a2(b, l, G)
        if only is None or 'ssd' in only:
            phase_ssd(b, l, G, C)
        if only is None or 'ml' in only:
            phase_mlstm(b, l, G, C)
        if only is None or 'attn' in only:
            phase_attn(b, l, G, C)
        if only is None or 'merge' in only:
            phase_merge(b, l, G, C)
        b.P.barrier()


def phase_a1(b, l, G):
    P = b.P
    xsrc, xsrc_r = G['xsrc'], G['xsrc_r']
    hT, hT_r = G['hT'], G['hT_r']
    identb, identb_r = G['identb'], G['identb_r']
    modt, modt_r, a1t, a1t_r = G['modt'], G['modt_r'], G['a1t'], G['a1t_r']
    with ExitStack() as es:
        xt = [b.sb(es, "xt", [128, D], F32) for _ in range(2)]
        junk, junk_r = b.sb(es, "junk", [128, D], BF16)
        xnb = [b.sb(es, "xnb", [128, D], BF16) for _ in range(2)]
        ss = [b.sb(es, "ss", [128, 1], F32) for _ in range(2)]
        pT = [b.ps(es, "pT", [128, D], BF16) for _ in range(2)]
        hb = [b.sb(es, "hb", [128, 8, 512], BF16) for _ in range(2)]
        for bi, (t0, bw) in enumerate(TBLKS):
            hbt, hb_r = hb[bi % 2]
            j = 1 if t0 < TC else 0
            for ti in range(bw // 128):
                i = (t0 // 128 + ti)
                xtt, xt_r = xt[i % 2]
                xn, xn_r = xnb[i % 2]
                st, st_r = ss[i % 2]
                pt, pt_r = pT[i % 2]
                P.dma(None, xtt[:], xsrc[i * 128:(i + 1) * 128, :], reads=rs_of(xsrc_r, i * 128, 128), writes=[xt_r])
                P.op('act', lambda e, xtt=xtt, st=st: e.activation(out=junk[:], in_=xtt[:], func=AF.Square, accum_out=st[:]),
                     reads=[xt_r], writes=[junk_r, st_r])
                P.op('act', lambda e, st=st: e.activation(out=st[:], in_=st[:], func=AF.Sqrt, scale=1.0 / D, bias=EPS),
                     reads=[st_r], writes=[st_r])
                P.op('dve', lambda e, st=st: e.reciprocal(out=st[:], in_=st[:]), reads=[st_r], writes=[st_r])
                P.op('dve', lambda e, xn=xn, xtt=xtt, st=st: e.tensor_scalar(out=xn[:], in0=xtt[:], scalar1=st[:, 0:1], scalar2=None, op0=ALU.mult),
                     reads=[xt_r, st_r], writes=[xn_r])
                for kc in range(8):
                    P.op('pe', lambda e, kc=kc, pt=pt, xn=xn: e.transpose(out=pt[:, kc * 128:(kc + 1) * 128], in_=xn[:, kc * 128:(kc + 1) * 128], identity=identb[:]),
                         reads=[xn_r, identb_r], writes=[pt_r], inc=(kc == 7))
                for kc in range(8):
                    P.op('dve', lambda e, kc=kc, pt=pt, hbt=hbt, ti=ti, j=j: e.tensor_scalar(
                        out=hbt[:, kc, ti * 128:(ti + 1) * 128], in0=pt[:, kc * 128:(kc + 1) * 128],
                        scalar1=a1t[:, l, kc, j:j + 1], scalar2=modt[:, l, kc, j:j + 1], op0=ALU.mult, op1=ALU.add),
                        reads=[pt_r, a1t_r, modt_r], writes=[hb_r])
            P.dma(None, hT[:, :, t0:t0 + bw].rearrange("k p t -> p k t"), hbt[:, :, 0:bw],
                  reads=[hb_r], writes=rs_of(hT_r, t0, bw))
        P.barrier()


def gemm_jobs():
    jobs = []
    jobs.append(('F', C_XBC, 1536, 'xbcT'))
    jobs.append(('F', C_MQK, 1024, 'mqkT'))
    jobs.append(('T', C_ZS, 1024, 'zs', 0))
    jobs.append(('T', C_Q, 2048, 'aqk', 0))
    jobs.append(('T', C_V, 1024, 'avz', 0))
    jobs.append(('F', C_ZA, 1024, 'azT'))
    jobs.append(('T', C_MV, 2048, 'mvo', 0))
    jobs.append(('T', C_MZ, 1024, 'mz', 0))
    jobs.append(('F', C_GT, 2048, 'gtT', 0))
    jobs.append(('F', C_GT + 2048, 1024, 'gtT', 16))
    return jobs


def phase_a2(b, l, G):
    P = b.P
    w_in = G['w_in']
    hT, hT_r = G['hT'], G['hT_r']
    with ExitStack() as es:
        stg = [b.sb(es, "stg", [128, 2048], F32) for _ in range(2)]
        Wg = [b.sb(es, "Wg", [128, 8, 2048], BF16) for _ in range(2)]
        hb = [b.sb(es, "hb2", [128, 8, 512], BF16) for _ in range(2)]
        ob = [b.sb(es, "ob", [128, 2048], BF16) for _ in range(2)]
        obs = [b.sb(es, "obs", [128, 48], F32) for _ in range(2)]
        pg = [b.ps(es, "pg", [128, 512], F32) for _ in range(4)]
        ws, ws_r = b.sb(es, "ws", [128, 8, 48], BF16)
        cnt = {'stg': 0, 'w': 0, 'hb': 0, 'ob': 0, 'pg': 0, 'ev': 0}

        def load_w(c0, n):
            wt, wt_r = Wg[cnt['w'] % 2]
            cnt['w'] += 1
            for kc in range(8):
                st, st_r = stg[cnt['stg'] % 2]
                cnt['stg'] += 1
                P.dma(None, st[:, 0:n], w_in[l, kc * 128:(kc + 1) * 128, c0:c0 + n], writes=[st_r])
                P.op('pool', lambda e, wt=wt, st=st, kc=kc, n=n: e.tensor_copy(out=wt[:, kc, 0:n], in_=st[:, 0:n]),
                     reads=[st_r], writes=[wt_r])
            return wt, wt_r

        def load_h(t0, bw):
            ht, ht_r = hb[cnt['hb'] % 2]
            cnt['hb'] += 1
            P.dma(None, ht[:, :, 0:bw], hT[:, :, t0:t0 + bw].rearrange("k p t -> p k t"),
                  reads=rs_of(hT_r, t0, bw), writes=[ht_r])
            return ht, ht_r

        def evac(out_ap, in_ap, reads, writes):
            eng = 'act' if cnt['ev'] % 2 == 0 else 'dve'
            cnt['ev'] += 1
            if eng == 'act':
                P.op('act', lambda e: e.copy(out=out_ap, in_=in_ap), reads=reads, writes=writes)
            else:
                P.op('dve', lambda e: e.tensor_copy(out=out_ap, in_=in_ap), reads=reads, writes=writes)

        for job in gemm_jobs():
            mode, c0, n = job[0], job[1], job[2]
            dst, dst_r = G[job[3]], G[job[3] + '_r']
            wt, wt_r = load_w(c0, n)
            for (t0, bw) in TBLKS:
                ht, ht_r = load_h(t0, bw)
                if mode == 'F':
                    for ch in range(n // 128):
                        pt, pt_r = pg[cnt['pg'] % 4]
                        cnt['pg'] += 1
                        for kc in range(8):
                            P.op('pe', lambda e, pt=pt, wt=wt, ht=ht, kc=kc, ch=ch, bw=bw: e.matmul(
                                pt[:, 0:bw], wt[:, kc, ch * 128:(ch + 1) * 128], ht[:, kc, 0:bw],
                                start=(kc == 0), stop=(kc == 7)), reads=[wt_r, ht_r], writes=[pt_r], inc=(kc == 7))
                        if ch % 4 == 0:
                            ot, ot_r = ob[cnt['ob'] % 2]
                            cnt['ob'] += 1
                        evac(ot[:, (ch % 4) * 512:(ch % 4) * 512 + bw], pt[:, 0:bw], [pt_r], [ot_r])
                        if ch % 4 == 3 or ch == n // 128 - 1:
                            cb0 = ch - (ch % 4) + (job[4] if len(job) > 4 else 0)
                            nchk = ch % 4 + 1
                            P.dma(None, dst[cb0:cb0 + nchk, :, t0:t0 + bw].rearrange("c p t -> p c t"),
                                  ot[:, 0:nchk * 512].rearrange("p (c t) -> p c t", c=nchk)[:, :, 0:bw],
                                  reads=[ot_r], writes=rs_of(dst_r, t0, bw))
                else:
                    dc0 = job[4]
                    for ti in range(bw // 128):
                        ot, ot_r = ob[cnt['ob'] % 2]
                        cnt['ob'] += 1
                        for sg in range(n // 512):
                            pt, pt_r = pg[cnt['pg'] % 4]
                            cnt['pg'] += 1
                            for kc in range(8):
                                P.op('pe', lambda e, pt=pt, wt=wt, ht=ht, kc=kc, sg=sg, ti=ti: e.matmul(
                                    pt[:, :], ht[:, kc, ti * 128:(ti + 1) * 128], wt[:, kc, sg * 512:(sg + 1) * 512],
                                    start=(kc == 0), stop=(kc == 7)), reads=[wt_r, ht_r], writes=[pt_r], inc=(kc == 7))
                            evac(ot[:, sg * 512:(sg + 1) * 512], pt[:, :], [pt_r], [ot_r])
                        r0 = t0 + ti * 128
                        P.dma(None, dst[r0:r0 + 128, dc0:dc0 + n], ot[:, 0:n], reads=[ot_r], writes=rs_of(dst_r, r0, 128))

        small, small_r = G['small'], G['small_r']
        for kc in range(8):
            st, st_r = stg[cnt['stg'] % 2]
            cnt['stg'] += 1
            P.dma(None, st[:, 0:32], w_in[l, kc * 128:(kc + 1) * 128, C_DT:C_DT + 32], writes=[st_r])
            P.dma(None, st[:, 32:48], w_in[l, kc * 128:(kc + 1) * 128, C_IG:C_IG + 16], writes=[st_r])
            P.op('pool', lambda e, st=st, kc=kc: e.tensor_copy(out=ws[:, kc, :], in_=st[:, 0:48]), reads=[st_r], writes=[ws_r])
        for (t0, bw) in TBLKS:
            ht, ht_r = load_h(t0, bw)
            for ti in range(bw // 128):
                pt, pt_r = pg[cnt['pg'] % 4]
                cnt['pg'] += 1
                ot, ot_r = obs[cnt['ob'] % 2]
                cnt['ob'] += 1
                for kc in range(8):
                    P.op('pe', lambda e, pt=pt, ht=ht, kc=kc, ti=ti: e.matmul(
                        pt[:, 0:48], ht[:, kc, ti * 128:(ti + 1) * 128], ws[:, kc, :],
                        start=(kc == 0), stop=(kc == 7)), reads=[ws_r, ht_r], writes=[pt_r], inc=(kc == 7))
                evac(ot[:, :], pt[:, 0:48], [pt_r], [ot_r])
                r0 = t0 + ti * 128
                P.dma(None, small[r0:r0 + 128, :], ot[:, :], reads=[ot_r], writes=rs_of(small_r, r0, 128))
        P.barrier()


def rope_tables():
    t = np.arange(TL)
    row = (t // 64).astype(np.float32)
    col = (t % 64).astype(np.float32)
    half = 32
    inv_freq = (10000.0 ** (-np.arange(0, half, 2, dtype=np.float32) / half)).astype(np.float32)
    ang_r = row[:, None] * inv_freq
    ang_c = col[:, None] * inv_freq
    cos = np.concatenate([np.cos(ang_r), np.cos(ang_r), np.cos(ang_c), np.cos(ang_c)], -1).astype(np.float32)
    sin = np.concatenate([np.sin(ang_r), np.sin(ang_r), np.sin(ang_c), np.sin(ang_c)], -1).astype(np.float32)
    sgn = np.tile(np.concatenate([-np.ones(16), np.ones(16)]), 2).astype(np.float32)
    return cos, sin * sgn


def bc(a):
    a = np.asarray(a, np.float32)
    return np.ascontiguousarray(np.broadcast_to(a[None], (128,) + a.shape))


def pmaj(v, nchunk):
    v = np.asarray(v, np.float32)
    lead = v.shape[:-1]
    r = v.reshape(lead + (nchunk, 128))
    r = np.moveaxis(r, -1, 0)
    return np.ascontiguousarray(r)


def make_inputs(inputs, bidx, layers, xin=None):
    L = len(layers)
    f = lambda k: np.asarray(inputs[k], np.float32)
    d = {}
    d['xin'] = np.ascontiguousarray(np.concatenate([f('ctx')[bidx], f('x')[bidx]], 0)) if xin is None else np.ascontiguousarray(xin, np.float32)
    cc = np.stack([f('c')[bidx], f('c_ctx')], -1)
    d['cc'] = np.ascontiguousarray(cc.reshape(8, 128, 2).transpose(1, 0, 2).reshape(128, 16))
    d['ident'] = np.eye(128, dtype=np.float32)
    t = np.arange(128)
    U = (t[:, None] <= t[None, :]).astype(np.float32)
    Lw = (t[:, None] >= t[None, :]).astype(np.float32)
    Sg = (t[:, None] > t[None, :]).astype(np.float32)
    Sl = (t[:, None] < t[None, :]).astype(np.float32)
    d['masks'] = np.ascontiguousarray(np.stack([U, Lw, Sg, Sl], 1))
    cos, sin = rope_tables()
    d['cosb'] = cos
    d['sinb'] = sin
    d['lami'] = bc(np.array([0.8 - 0.6 * math.exp(-0.3 * l) for l in layers], np.float32))
    d['w_mod'] = np.ascontiguousarray(f('w_mod')[layers])
    d['bmod'] = pmaj(f('b_mod')[layers], 24)
    d['ng'] = pmaj(f('norm_g')[layers], 8)
    d['w_in'] = np.ascontiguousarray(f('w_in')[layers])
    d['cw'] = np.ascontiguousarray(pmaj(f('ssd_conv_w')[layers], 12).transpose(0, 1, 3, 2))
    d['cb'] = pmaj(f('ssd_conv_b')[layers], 12)
    d['alog'] = bc(f('ssd_a_log')[layers].reshape(L, 32))
    d['dtb'] = bc(f('ssd_dt_bias')[layers].reshape(L, 32))
    d['dsk'] = bc(f('ssd_d')[layers])
    d['sng'] = bc(f('ssd_norm_g')[layers])
    d['gq'] = bc(f('diff_qn_g')[layers])
    d['gk'] = bc(f('diff_kn_g')[layers])
    d['lamp'] = bc(f('diff_lambda')[layers])
    d['subg'] = np.ascontiguousarray(np.broadcast_to(pmaj(f('diff_subln_g')[layers], 1), (128, L, 8)))
    d['mcw'] = np.ascontiguousarray(pmaj(f('ml_conv_w')[layers], 8).transpose(0, 1, 3, 2))
    d['mcb'] = pmaj(f('ml_conv_b')[layers], 8)
    d['ib'] = bc(f('ml_i_bias')[layers].reshape(L, 8))
    d['fb'] = bc(f('ml_f_bias')[layers].reshape(L, 8))
    d['mng'] = bc(f('ml_norm_g')[layers])
    sel = np.zeros((2, 2, 128), np.float32)
    sel[0, 0] = 1.0
    sel[1, 1] = 1.0
    d['sel'] = sel
    d['bgate'] = np.ascontiguousarray(np.broadcast_to(f('b_mod')[layers][None, :, 2 * D:3 * D], (2, L, D)))
    d['w_br'] = np.ascontiguousarray(f('w_branch')[layers])
    d['w_out'] = np.ascontiguousarray(f('w_out')[layers])
    return d


_NC_CACHE = {}


FUSED = True


def kernel(**inputs):
    nb = 4
    if FUSED:
        if DEPTH not in _NC_CACHE:
            _NC_CACHE[DEPTH] = build_program(DEPTH)
        nc = _NC_CACHE[DEPTH]
        in_maps = [make_inputs(inputs, bidx, list(range(DEPTH))) for bidx in range(nb)]
        res = run_bass_kernel_spmd(nc, in_maps, core_ids=list(range(nb)))
        return np.stack([np.asarray(r["xout"], np.float32)[TC:] for r in res.results], 0)
    if 1 not in _NC_CACHE:
        _NC_CACHE[1] = build_program(1)
    nc = _NC_CACHE[1]
    xcur = [None] * nb
    for l in range(DEPTH):
        in_maps = [make_inputs(inputs, bidx, [l], xin=xcur[bidx]) for bidx in range(nb)]
        res = run_bass_kernel_spmd(nc, in_maps, core_ids=list(range(nb)))
        xcur = [np.asarray(r["xout"], np.float32) for r in res.results]
    return np.stack([x[TC:] for x in xcur], 0)


def layer_consts(b, es, l, G):
    P = b.P
    O = Ops(P)
    C = {}

    def ld(name, src, shape, **kw):
        t, r = b.sb(es, "c_" + name, shape, F32)
        P.dma(None, t[:], src, writes=[r], **kw)
        C[name] = (t, r)
    ld('cw', G['cw'][:, l, :, :], [128, 12, 3])
    ld('cb', G['cb'][:, l, :], [128, 12])
    ld('alog', G['alog'][:, l, :], [128, 32])
    ld('dtb', G['dtb'][:, l, :], [128, 32])
    ld('dsk', G['dsk'][:, l, :], [128, 16])
    ld('sng', G['sng'][:, l, :], [128, D])
    ld('gq', G['gq'][:, l, :], [128, 64])
    ld('gk', G['gk'][:, l, :], [128, 64])
    ld('lamp', G['lamp'][:, l, :, :], [128, 4, 64])
    ld('subg', G['subg'][:, l, :], [128, 8])
    ld('mcw', G['mcw'][:, l, :, :], [128, 8, 3])
    ld('mcb', G['mcb'][:, l, :], [128, 8])
    ld('ib', G['ib'][:, l, :], [128, 8])
    ld('fb', G['fb'][:, l, :], [128, 8])
    ld('mng', G['mng'][:, l, :], [128, D])
    ld('lami', G['lami_in'][:, l:l + 1], [128, 1], allow_slow_non_contiguous=True)
    at, ar = C['alog']
    O.act(at[:], at[:], AF.Exp, [ar], [ar])
    O.ts('dve', at[:], at[:], -1.0, None, ALU.mult, None, [ar], [ar])
    return C


def softplus(b, O, out, x, tmp1, tmp2, R, W):
    O.act(tmp1, x, AF.Abs, R + W, W)
    O.act(tmp1, tmp1, AF.Exp, R + W, W, scale=-1.0)
    O.act(tmp2, tmp1, AF.Ln, R + W, W, bias=1.0)
    O.stt(out, x, 0.0, tmp2, ALU.max, ALU.add, R + W, W)


def seq_bounds(t0, bw):
    return (t0 == 0 or t0 == TC), (t0 + bw == TC or t0 + bw == NT)


def conv_silu(b, O, l, xT, xT_r, t0, bw, nchunk, cwt, cw_r, cbt, cb_r, xh, xh_r, tmps, uT, uT_r):
    P = O.P
    at_start, at_end = seq_bounds(t0, bw)
    lo = t0 if at_start else t0 - 1
    hi = t0 + bw if at_end else t0 + bw + 1
    if at_start:
        O.memset('pool', xh[:, :, 0:1], 0.0, [xh_r])
    if at_end:
        O.memset('pool', xh[:, :, bw + 1:bw + 2], 0.0, [xh_r])
    O.dma(xh[:, :, lo - (t0 - 1):hi - (t0 - 1)], xT[:, :, lo:hi].rearrange("c p t -> p c t"),
          rs_of(xT_r, lo, hi - lo), [xh_r])
    for ch in range(nchunk):
        tm, tm_r = tmps[ch % 2]
        O.ts('dve', tm[:, 0:bw], xh[:, ch, 1:bw + 1], cwt[:, ch, 1:2], None, ALU.mult, None, [xh_r, cw_r], [tm_r])
        O.stt(tm[:, 0:bw], xh[:, ch, 0:bw], cwt[:, ch, 0:1], tm[:, 0:bw], ALU.mult, ALU.add, [xh_r, cw_r, tm_r], [tm_r])
        O.stt(tm[:, 0:bw], xh[:, ch, 2:bw + 2], cwt[:, ch, 2:3], tm[:, 0:bw], ALU.mult, ALU.add, [xh_r, cw_r, tm_r], [tm_r])
        O.act(uT[:, ch, 0:bw], tm[:, 0:bw], AF.Silu, [tm_r, cb_r], [uT_r], bias=cbt[:, ch:ch + 1])


def decay_scalars(b, O, c, a_t, a_r, NA, masks, masks_r, onesf, onesf_r, pc, pc_r, PT, PT_r, E1, E2, E_r,
                  rs, de, cdall, cd_r):
    n2 = 2 * NA
    O.mm(pc[:, 0:n2], masks[:, 0, :], a_t[:, 0:n2], True, True, [masks_r, a_r], [pc_r], inc=False)
    O.mm(pc[:, n2:2 * n2], onesf[:, :], a_t[:, 0:n2], True, True, [onesf_r, a_r], [pc_r])
    O.copy('dve', PT[:, 0:2 * n2], pc[:, 0:2 * n2], [pc_r], [PT_r])
    Pf, Pb = PT[:, 0:NA], PT[:, NA:n2]
    Tf, Tb = PT[:, n2:n2 + NA], PT[:, n2 + NA:2 * n2]
    O.tt('dve', E2[:, NA:n2], Pb, a_t[:, NA:n2], ALU.subtract, [PT_r, a_r], [E_r])
    O.tt('dve', E1[:, NA:n2], Tb, E2[:, NA:n2], ALU.subtract, [PT_r, E_r], [E_r])
    O.tt('dve', E2[:, 0:NA], Tf, Pf, ALU.subtract, [PT_r], [E_r])
    O.copy('dve', E1[:, 0:NA], Pf, [PT_r], [E_r])
    O.act(rs, E1[:, 0:n2], AF.Exp, [E_r], [E_r])
    O.act(de, E2[:, 0:n2], AF.Exp, [E_r], [E_r])
    O.act(cdall[:, c, 0:n2], PT[:, n2:2 * n2], AF.Exp, [PT_r], [cd_r])


def phase_ssd(b, l, G, C):
    P = b.P
    O = Ops(P)
    nc = b.nc
    S = G['scr']
    masks, masks_r, onesf, onesf_r = G['masks'], G['masks_r'], G['onesf'], G['onesf_r']
    identb, identb_r = G['identb'], G['identb_r']
    xbcT, xbcT_r = G['xbcT'], G['xbcT_r']
    small, small_r = G['small'], G['small_r']
    sBT, sBT_r = S('sBT', [2, 128, NT], BF16)
    sCT, sCT_r = S('sCT', [2, 128, NT], BF16)
    sVt, sVt_r = S('sVt', [2, NT, D], BF16)
    sxs, sxs_r = S('sxs', [NT, D], BF16)
    ssm, ssm_r = S('ssm', [NT, 64], F32)
    sprev, sprev_r = S('sprev', [2, NCH, 128, D], BF16, 2 * NCH)
    sLb, sLb_r = S('sLb', [NCH, 128, D], F32, NCH)
    cwt, cw_r = C['cw']
    cbt, cb_r = C['cb']
    At, A_r = C['alog']
    dtbt, dtb_r = C['dtb']

    with ExitStack() as es0:
        cdall, cd_r = b.sb(es0, "cdall", [128, NCH, 32], F32)
        with ExitStack() as es:
            xh, xh_r = b.sb(es, "xh", [128, 12, 514], BF16)
            tmps = [b.sb(es, "ctmp", [128, 512], F32) for _ in range(2)]
            uT, uT_r = b.sb(es, "uT", [128, 12, 512], BF16)
            pX, pX_r = b.ps(es, "pX", [128, D], BF16)
            pB, pB_r = b.ps(es, "pB", [128, 1024], BF16)
            pc, pc_r = b.ps(es, "pc", [128, 512], F32)
            pL, pL_r = b.ps(es, "pL", [128, 4, 512], F32)
            xsb = [b.sb(es, "xsb", [128, D], BF16) for _ in range(2)]
            Btm, Btm_r = b.sb(es, "Btm", [128, 256], BF16)
            sml, sml_r = b.sb(es, "sml", [128, 256], F32)
            smo = [b.sb(es, "smo", [128, 64], F32) for _ in range(2)]
            PT, PT_r = b.sb(es, "PT", [128, 128], F32)
            Et, E_r = b.sb(es, "Et", [128, 128], F32)
            Vt = [b.sb(es, "Vt", [128, 2, D], BF16) for _ in range(2)]
            Vw, Vw_r = b.sb(es, "Vw", [128, 2, D], BF16)
            Sf, Sf_r = b.sb(es, "Sf", [128, D], F32)
            Stmp, Stmp_r = b.sb(es, "Stmp", [128, D], F32)
            Sfb = [b.sb(es, "Sfb", [128, D], BF16) for _ in range(2)]
            Lb = [b.sb(es, "Lb", [128, D], F32) for _ in range(2)]
            O.memset('dve', Sf[:], 0.0, [Sf_r])
            for (t0, bw) in TBLKS:
                conv_silu(b, O, l, xbcT, xbcT_r, t0, bw, 12, cwt, cw_r, cbt, cb_r, xh, xh_r, tmps, uT, uT_r)
                O.dma(sBT[:, :, t0:t0 + bw].rearrange("g p t -> p g t"), uT[:, 8:10, 0:bw], [uT_r], rs_of(sBT_r, t0, bw))
                O.dma(sCT[:, :, t0:t0 + bw].rearrange("g p t -> p g t"), uT[:, 10:12, 0:bw], [uT_r], rs_of(sCT_r, t0, bw))
                for ti in range(bw // 128):
                    c = t0 // 128 + ti
                    r0 = c * 128
                    tsl = slice(ti * 128, (ti + 1) * 128)
                    for ch in range(8):
                        O.tr(pX[:, ch * 128:(ch + 1) * 128], uT[:, ch, tsl], identb[:], [uT_r, identb_r], [pX_r], inc=(ch == 7))
                    for g in range(2):
                        O.tr(pB[:, g * 128:(g + 1) * 128], uT[:, 8 + g, tsl], identb[:], [uT_r, identb_r], [pB_r], inc=(g == 1))
                    xs_t, xs_r = xsb[c % 2]
                    O.copy('act', xs_t[:], pX[:], [pX_r], [xs_r])
                    O.dma(sxs[r0:r0 + 128, :], xs_t[:], [xs_r], rs_of(sxs_r, r0, 128))
                    O.copy('act', Btm[:], pB[:, 0:256], [pB_r], [Btm_r])
                    so, so_r = smo[c % 2]
                    O.dma(sml[:, 0:32], small[r0:r0 + 128, 0:32], rs_of(small_r, r0, 128), [sml_r])
                    O.tt('dve', sml[:, 32:64], sml[:, 0:32], dtbt[:], ALU.add, [sml_r, dtb_r], [sml_r])
                    softplus(b, O, sml[:, 128:160], sml[:, 32:64], sml[:, 64:96], sml[:, 96:128], [], [sml_r])
                    dt = sml[:, 128:160]
                    O.tt('dve', so[:, 0:32], dt, At[:], ALU.mult, [sml_r, A_r], [so_r])
                    decay_scalars(b, O, c, so, so_r, 16, masks, masks_r, onesf, onesf_r, pc, pc_r, PT, PT_r,
                                  Et[:, 0:32], Et[:, 32:64], E_r, so[:, 32:64], Et[:, 64:96], cdall, cd_r)
                    O.dma(ssm[r0:r0 + 128, :], so[:, :], [so_r, E_r], rs_of(ssm_r, r0, 128))
                    O.tt('dve', Et[:, 96:128], dt, Et[:, 64:96], ALU.mult, [sml_r, E_r], [E_r])
                    vt, vt_r = Vt[c % 2]
                    pXv = pX[:].rearrange("p (h e) -> p h e", h=16)
                    for d in range(2):
                        O.tt('dve', vt[:, d, :].rearrange("p (h e) -> p h e", h=16), pXv,
                             bc_mid(sml[:, 128 + d * 16:128 + (d + 1) * 16], [128, 16, 64]), ALU.mult, [pX_r, sml_r], [vt_r])
                        O.tt('dve', Vw[:, d, :].rearrange("p (h e) -> p h e", h=16), pXv,
                             bc_mid(Et[:, 96 + d * 16:96 + (d + 1) * 16], [128, 16, 64]), ALU.mult, [pX_r, E_r], [Vw_r])
                        O.dma(sVt[d, r0:r0 + 128, :], vt[:, d, :], [vt_r], rs_of(sVt_r, r0, 128))
                    for d in range(2):
                        for g in range(2):
                            O.mm(pL[:, d * 2 + g, :], Btm[:, g * 128:(g + 1) * 128], Vw[:, d, g * 512:(g + 1) * 512], True, True,
                                 [Btm_r, Vw_r], [pL_r], inc=(d == 1 and g == 1))
                    sb_t, sb_r = Sfb[c % 2]
                    O.copy('act', sb_t[:], Sf[:], [Sf_r], [sb_r])
                    O.dma(sprev[0, c, :, :], sb_t[:], [sb_r], [sprev_r[c]])
                    O.tt('dve', Stmp[:].rearrange("p (h e) -> p h e", h=16), Sf[:].rearrange("p (h e) -> p h e", h=16),
                         bc_mid(cdall[:, c, 0:16], [128, 16, 64]), ALU.mult, [Sf_r, cd_r], [Stmp_r])
                    O.tt('dve', Sf[:].rearrange("p (g n) -> p g n", g=2), Stmp[:].rearrange("p (g n) -> p g n", g=2),
                         pL[:, 0:2, :], ALU.add, [Stmp_r, pL_r], [Sf_r])
                    lb_t, lb_r = Lb[c % 2]
                    O.copy('act', lb_t[:].rearrange("p (g n) -> p g n", g=2), pL[:, 2:4, :], [pL_r], [lb_r])
                    O.dma(sLb[c, :, :], lb_t[:], [lb_r], [sLb_r[c]])
            O.memset('dve', Sf[:], 0.0, [Sf_r])
            for c in BWD_ORDER:
                lb_t, lb_r = Lb[c % 2]
                sb_t, sb_r = Sfb[c % 2]
                O.dma(lb_t[:], sLb[c, :, :], [sLb_r[c]], [lb_r])
                O.copy('act', sb_t[:], Sf[:], [Sf_r], [sb_r])
                O.dma(sprev[1, c, :, :], sb_t[:], [sb_r], [sprev_r[NCH + c]])
                O.tt('dve', Stmp[:].rearrange("p (h e) -> p h e", h=16), Sf[:].rearrange("p (h e) -> p h e", h=16),
                     bc_mid(cdall[:, c, 16:32], [128, 16, 64]), ALU.mult, [Sf_r, cd_r], [Stmp_r])
                O.tt('dve', Sf[:], Stmp[:], lb_t[:], ALU.add, [Stmp_r, lb_r], [Sf_r])
            P.barrier()

        zs, zs_r = G['zs'], G['zs_r']
        dskt, dsk_r = C['dsk']
        sngt, sng_r = C['sng']
        yT, yT_r = G['yT'], G['yT_r']
        with ExitStack() as es:
            xs2 = [b.sb(es, "xs2", [128, D], BF16) for _ in range(2)]
            z2 = [b.sb(es, "z2", [128, D], BF16) for _ in range(2)]
            yacc, yacc_r = b.sb(es, "yacc", [128, D], F32)
            szt, sz_r = b.sb(es, "szt", [128, D], F32)
            ssq, ssq_r = b.sb(es, "ssq", [128, 2], F32)
            ybf, ybf_r = b.sb(es, "ybf", [128, D], BF16)
            yTs = [b.sb(es, "yTs", [128, 8, 128], BF16) for _ in range(2)]
            pYT, pYT_r = b.ps(es, "pYT", [128, D], BF16)

            def load_extra(c):
                r0 = c * 128
                x_t, x_r = xs2[c % 2]
                z_t, z_r = z2[c % 2]
                O.dma(x_t[:], sxs[r0:r0 + 128, :], rs_of(sxs_r, r0, 128), [x_r])
                O.dma(z_t[:], zs[r0:r0 + 128, :], rs_of(zs_r, r0, 128), [z_r])

            def finish(c, Yf, Yf_r, Yb, Yb_r):
                r0 = c * 128
                x_t, x_r = xs2[c % 2]
                z_t, z_r = z2[c % 2]
                O.tt('pool', yacc[:], Yf[:, 0:D], Yb[:, 0:D], ALU.add, [Yf_r, Yb_r], [yacc_r])
                O.tt('dve', szt[:].rearrange("p (h e) -> p h e", h=16), x_t[:].rearrange("p (h e) -> p h e", h=16),
                     bc_mid(dskt[:, :], [128, 16, 64]), ALU.mult, [x_r, dsk_r], [sz_r])
                O.tt('dve', yacc[:], yacc[:], szt[:], ALU.add, [yacc_r, sz_r], [yacc_r])
                O.act(szt[:], z_t[:], AF.Silu, [z_r, sz_r], [sz_r])
                O.tt('dve', yacc[:], yacc[:], szt[:], ALU.mult, [yacc_r, sz_r], [yacc_r])
                O.act(szt[:], yacc[:], AF.Square, [yacc_r, sz_r], [sz_r, ssq_r], accum=ssq[:, 0:1])
                O.act(ssq[:, 1:2], ssq[:, 0:1], AF.Sqrt, [ssq_r], [ssq_r], scale=1.0 / D, bias=EPS)
                O.recip(ssq[:, 1:2], ssq[:, 1:2], [ssq_r], [ssq_r])
                O.stt(ybf[:], yacc[:], ssq[:, 1:2], sngt[:], ALU.mult, ALU.mult, [yacc_r, ssq_r, sng_r], [ybf_r])
                yt, yt_r = yTs[c % 2]
                for k in range(8):
                    O.tr(pYT[:, k * 128:(k + 1) * 128], ybf[:, k * 128:(k + 1) * 128], identb[:], [ybf_r, identb_r], [pYT_r], inc=(k == 7))
                O.copy('act', yt[:].rearrange("p k t -> p (k t)"), pYT[:], [pYT_r], [yt_r])
                O.dma(yT[0, :, :, r0:r0 + 128].rearrange("k p t -> p k t"), yt[:], [yt_r], rs_of(yT_r, r0, 128))

            la_pass2(b, O, G, es, NG=2, NSUB=8, PV=64, QT=sCT, QT_r=sCT_r, KT=sBT, KT_r=sBT_r, g0=0,
                     Vt=sVt, Vt_r=sVt_r, vcol0=0, sm=ssm, sm_r=ssm_r, NA=16, a0=0, sprev=sprev, sprev_r=sprev_r,
                     load_extra=load_extra, finish=finish)
            P.barrier()


def la_pass2(b, O, G, es, NG, NSUB, PV, QT, QT_r, KT, KT_r, g0, Vt, Vt_r, vcol0, sm, sm_r, NA, a0, sprev, sprev_r,
             load_extra, finish):
    P = O.P
    masks, masks_r = G['masks'], G['masks_r']
    GW = NSUB * PV
    NH = NG * NSUB
    W = NG * GW
    qt = [b.sb(es, "qt", [128, NG, 128], BF16) for _ in range(2)]
    kt = [b.sb(es, "kt", [128, NG, 128], BF16) for _ in range(2)]
    vt = [b.sb(es, "vt2", [128, 2, W], BF16) for _ in range(2)]
    smt = [b.sb(es, "smt", [128, 4 * NA], F32) for _ in range(2)]
    spt = [b.sb(es, "spt", [128, 2, W], BF16) for _ in range(2)]
    Gm, Gm_r = b.sb(es, "Gm", [128, 2, NG, 128], F32)
    segL = [b.sb(es, "segL", [128, 4, 128], F32) for _ in range(2)]
    Ee = [b.sb(es, "Ee", [128, 4, 128], F32) for _ in range(2)]
    MT = [b.sb(es, "MT", [128, 4, 128], BF16) for _ in range(2)]
    ytmp, ytmp_r = b.sb(es, "ytmp", [128, NG, GW], F32)
    Yd = [b.sb(es, "Yd", [128, NG, GW], F32) for _ in range(2)]
    pG, pG_r = b.ps(es, "pG", [128, 512], F32)
    pS = [b.ps(es, "pS", [128, 4, 128], F32) for _ in range(2)]
    ydg, ydg_r = b.ps(es, "ydg", [128, NG, 512], F32)
    yof, yof_r = b.ps(es, "yof", [128, NG, 512], F32)
    nseg = 0
    for c in range(NCH):
        r0 = c * 128
        q_t, q_r = qt[c % 2]
        k_t, k_r = kt[c % 2]
        v_t, v_r = vt[c % 2]
        s_t, s_r = smt[c % 2]
        p_t, p_r = spt[c % 2]
        O.dma(q_t[:], QT[g0:g0 + NG, :, r0:r0 + 128].rearrange("g p t -> p g t"), rs_of(QT_r, r0, 128), [q_r])
        O.dma(k_t[:], KT[g0:g0 + NG, :, r0:r0 + 128].rearrange("g p t -> p g t"), rs_of(KT_r, r0, 128), [k_r])
        for d in range(2):
            O.dma(v_t[:, d, :], Vt[d, r0:r0 + 128, vcol0:vcol0 + W], rs_of(Vt_r, r0, 128), [v_r])
            O.dma(p_t[:, d, :], sprev[d, c, :, vcol0:vcol0 + W], [sprev_r[d * NCH + c]], [p_r])
        O.dma(s_t[:], sm[r0:r0 + 128, :], rs_of(sm_r, r0, 128), [s_r])
        load_extra(c)
        for g in range(NG):
            O.mm(pG[:, g * 128:(g + 1) * 128], k_t[:, g, :], q_t[:, g, :], True, True, [k_r, q_r], [pG_r], inc=(g == NG - 1))
        for d in range(2):
            O.tt('dve', Gm[:, d, :, :], pG[:, 0:NG * 128].rearrange("p (g l) -> p g l", g=NG),
                 bc_lead(masks[:, d, :], [128, NG, 128]), ALU.mult, [pG_r, masks_r], [Gm_r])
        rounds = [(d, h0, min(4, NH - h0)) for d in range(2) for h0 in range(0, NH, 4)]
        first_in_bank = {0: [True] * NG, 1: [True] * NG}

        def stage_a(ri, slot):
            d, h0, nh = rounds[ri]
            sl_t, sl_r = segL[slot % 2]
            ps_t, ps_r = pS[slot % 2]
            col = d * NA + a0 + h0
            O.tt('dve', sl_t[:, 0:nh, :], bc_lead(masks[:, 2 + d, :], [128, nh, 128]),
                 bc_mid(s_t[:, col:col + nh], [128, nh, 128]), ALU.mult, [masks_r, s_r], [sl_r])
            for i in range(nh):
                O.mm(ps_t[:, i, :], sl_t[:, i, :], masks[:, d, :], True, True, [sl_r, masks_r], [ps_r], inc=(i == nh - 1))

        def stage_b(ri, slot):
            d, h0, nh = rounds[ri]
            y_t, y_r = Yd[d]
            e_t, e_r = Ee[slot % 2]
            m_t, m_r = MT[slot % 2]
            ps_t, ps_r = pS[slot % 2]
            O.act(e_t[:, 0:nh, :], ps_t[:, 0:nh, :], AF.Exp, [ps_r], [e_r])
            g = h0 // NSUB
            if NSUB >= 4:
                O.tt('dve', m_t[:, 0:nh, :], e_t[:, 0:nh, :], bc_lead(Gm[:, d, g, :], [128, nh, 128]), ALU.mult, [e_r, Gm_r], [m_r])
            else:
                for i in range(nh):
                    gi = (h0 + i) // NSUB
                    O.tt('dve', m_t[:, i, :], e_t[:, i, :], Gm[:, d, gi, :], ALU.mult, [e_r, Gm_r], [m_r])
            for i in range(nh):
                h = h0 + i
                gi, sub = h // NSUB, h % NSUB
                O.mm(ydg[:, gi, sub * PV:(sub + 1) * PV], m_t[:, i, :], v_t[:, d, h * PV:(h + 1) * PV],
                     first_in_bank[d][gi], True, [m_r, v_r], [ydg_r], inc=(i == nh - 1))
                first_in_bank[d][gi] = False
            if h0 + nh >= NH:
                for g2 in range(NG):
                    O.mm(yof[:, g2, 0:GW], q_t[:, g2, :], p_t[:, d, g2 * GW:(g2 + 1) * GW], True, True, [q_r, p_r], [yof_r], inc=(g2 == NG - 1))
                rsc = s_t[:, 2 * NA + d * NA + a0:2 * NA + d * NA + a0 + NH]
                O.tt('dve', ytmp[:].rearrange("p g (s e) -> p (g s) e", s=NSUB), yof_view(yof, NG, NSUB, PV),
                     bc_mid(rsc, [128, NH, PV]), ALU.mult, [yof_r, s_r], [ytmp_r])
                O.tt('dve', y_t[:], ytmp[:], ydg[:, :, 0:GW], ALU.add, [ytmp_r, ydg_r], [y_r])

        stage_a(0, nseg)
        for ri in range(len(rounds)):
            if ri + 1 < len(rounds):
                stage_a(ri + 1, nseg + 1)
            stage_b(ri, nseg)
            nseg += 1
        finish(c, Yd[0][0][:].rearrange("p g w -> p (g w)"), Yd[0][1], Yd[1][0][:].rearrange("p g w -> p (g w)"), Yd[1][1])


def yof_view(yof, NG, NSUB, PV):
    if NSUB * PV == 512:
        return yof[:].rearrange("p g (s e) -> p (g s) e", s=NSUB)
    assert NSUB == 1
    return yof[:, :, 0:PV]


MLW = 4 * 257


def phase_mlstm(b, l, G, C):
    P = b.P
    O = Ops(P)
    S = G['scr']
    masks, masks_r, onesf, onesf_r = G['masks'], G['masks_r'], G['onesf'], G['onesf_r']
    identb, identb_r = G['identb'], G['identb_r']
    mqkT, mqkT_r = G['mqkT'], G['mqkT_r']
    small, small_r = G['small'], G['small_r']
    mvo, mvo_r = G['mvo'], G['mvo_r']
    mz, mz_r = G['mz'], G['mz_r']
    mQT, mQT_r = S('mQT', [4, 128, NT], BF16)
    mKT, mKT_r = S('mKT', [4, 128, NT], BF16)
    mVt, mVt_r = S('mVt', [2, NT, MLW], BF16)
    msm, msm_r = S('msm', [NT, 16], F32)
    mprev, mprev_r = S('mprev', [2, NCH, 128, MLW], BF16, 2 * NCH)
    mLb, mLb_r = S('mLb', [NCH, 128, MLW], F32, NCH)
    mh, mh_r = S('mh', [NT, 512], F32)
    cwt, cw_r = C['mcw']
    cbt, cb_r = C['mcb']
    ibt, ib_r = C['ib']
    fbt, fb_r = C['fb']
    LNS = math.log(128.0 ** -0.5)

    with ExitStack() as es0:
        cdall, cd_r = b.sb(es0, "mcdall", [128, NCH, 8], F32)
        with ExitStack() as es:
            xh, xh_r = b.sb(es, "mxh", [128, 8, 514], BF16)
            tmps = [b.sb(es, "mctmp", [128, 512], F32) for _ in range(2)]
            uT, uT_r = b.sb(es, "muT", [128, 8, 512], BF16)
            pK, pK_r = b.ps(es, "pK", [128, 1024], BF16)
            pc, pc_r = b.ps(es, "mpc", [128, 512], F32)
            pL, pL_r = b.ps(es, "mpL", [128, 4, 512], F32)
            Ktm, Ktm_r = b.sb(es, "Ktm", [128, 512], BF16)
            vin = [b.sb(es, "vin", [128, D], BF16) for _ in range(2)]
            sml, sml_r = b.sb(es, "msml", [128, 128], F32)
            smo = [b.sb(es, "msmo", [128, 16], F32) for _ in range(2)]
            PT, PT_r = b.sb(es, "mPT", [128, 32], F32)
            Et, E_r = b.sb(es, "mEt", [128, 32], F32)
            Vt = [b.sb(es, "mVt", [128, 2, MLW], BF16) for _ in range(2)]
            Vw, Vw_r = b.sb(es, "mVw", [128, 2, MLW], BF16)
            Sf, Sf_r = b.sb(es, "mSf", [128, MLW], F32)
            Stmp, Stmp_r = b.sb(es, "mStmp", [128, MLW], F32)
            Sfb = [b.sb(es, "mSfb", [128, MLW], BF16) for _ in range(2)]
            Lb = [b.sb(es, "mLb", [128, MLW], F32) for _ in range(2)]
            O.memset('dve', Sf[:], 0.0, [Sf_r])
            for (t0, bw) in TBLKS:
                conv_silu(b, O, l, mqkT, mqkT_r, t0, bw, 8, cwt, cw_r, cbt, cb_r, xh, xh_r, tmps, uT, uT_r)
                O.dma(mQT[:, :, t0:t0 + bw].rearrange("g p t -> p g t"), uT[:, 0:4, 0:bw], [uT_r], rs_of(mQT_r, t0, bw))
                O.dma(mKT[:, :, t0:t0 + bw].rearrange("g p t -> p g t"), uT[:, 4:8, 0:bw], [uT_r], rs_of(mKT_r, t0, bw))
                for ti in range(bw // 128):
                    c = t0 // 128 + ti
                    r0 = c * 128
                    tsl = slice(ti * 128, (ti + 1) * 128)
                    for h in range(4):
                        O.tr(pK[:, h * 128:(h + 1) * 128], uT[:, 4 + h, tsl], identb[:], [uT_r, identb_r], [pK_r], inc=(h == 3))
                    O.copy('act', Ktm[:], pK[:, 0:512], [pK_r], [Ktm_r])
                    v_t, v_r = vin[c % 2]
                    O.dma(v_t[:], mvo[r0:r0 + 128, 0:D], rs_of(mvo_r, r0, 128), [v_r])
                    so, so_r = smo[c % 2]
                    O.dma(sml[:, 0:16], small[r0:r0 + 128, 32:48], rs_of(small_r, r0, 128), [sml_r])
                    O.tt('dve', sml[:, 16:24], sml[:, 0:8], ibt[:], ALU.add, [sml_r, ib_r], [sml_r])
                    O.act(sml[:, 16:24], sml[:, 16:24], AF.Exp, [sml_r], [sml_r])
                    O.ts('dve', sml[:, 16:24], sml[:, 16:24], 128.0 ** -0.5, None, ALU.mult, None, [sml_r], [sml_r])
                    O.tt('dve', sml[:, 24:32], sml[:, 8:16], fbt[:], ALU.add, [sml_r, fb_r], [sml_r])
                    O.ts('dve', sml[:, 24:32], sml[:, 24:32], -1.0, None, ALU.mult, None, [sml_r], [sml_r])
                    softplus(b, O, sml[:, 32:40], sml[:, 24:32], sml[:, 40:48], sml[:, 48:56], [], [sml_r])
                    O.ts('dve', so[:, 0:8], sml[:, 32:40], -1.0, None, ALU.mult, None, [sml_r], [so_r])
                    decay_scalars(b, O, c, so, so_r, 4, masks, masks_r, onesf, onesf_r, pc, pc_r, PT, PT_r,
                                  Et[:, 0:8], Et[:, 8:16], E_r, so[:, 8:16], Et[:, 16:24], cdall, cd_r)
                    O.dma(msm[r0:r0 + 128, :], so[:, :], [so_r, E_r], rs_of(msm_r, r0, 128))
                    O.tt('dve', Et[:, 24:32], sml[:, 16:24], Et[:, 16:24], ALU.mult, [sml_r, E_r], [E_r])
                    vt, vt_r = Vt[c % 2]
                    vv = v_t[:].rearrange("p (h e) -> p h e", h=4)
                    for d in range(2):
                        sw = sml[:, 16 + d * 4:16 + (d + 1) * 4]
                        ww = Et[:, 24 + d * 4:24 + (d + 1) * 4]
                        vo = vt[:, d, :].rearrange("p (h e) -> p h e", h=4)
                        wo = Vw[:, d, :].rearrange("p (h e) -> p h e", h=4)
                        O.tt('dve', vo[:, :, 0:256], vv, bc_mid(sw, [128, 4, 256]), ALU.mult, [v_r, sml_r], [vt_r])
                        O.copy('dve', vo[:, :, 256], sw, [sml_r], [vt_r])
                        O.tt('dve', wo[:, :, 0:256], vv, bc_mid(ww, [128, 4, 256]), ALU.mult, [v_r, E_r], [Vw_r])
                        O.copy('dve', wo[:, :, 256], ww, [E_r], [Vw_r])
                        O.dma(mVt[d, r0:r0 + 128, :], vt[:, d, :], [vt_r], rs_of(mVt_r, r0, 128))
                    sb_t, sb_r = Sfb[c % 2]
                    lb_t, lb_r = Lb[c % 2]
                    for d in range(2):
                        for h in range(4):
                            O.mm(pL[:, h, 0:257], Ktm[:, h * 128:(h + 1) * 128], Vw[:, d, h * 257:(h + 1) * 257], True, True,
                                 [Ktm_r, Vw_r], [pL_r], inc=(h == 3))
                        if d == 0:
                            O.copy('act', sb_t[:], Sf[:], [Sf_r], [sb_r])
                            O.dma(mprev[0, c, :, :], sb_t[:], [sb_r], [mprev_r[c]])
                            O.tt('dve', Stmp[:].rearrange("p (h e) -> p h e", h=4), Sf[:].rearrange("p (h e) -> p h e", h=4),
                                 bc_mid(cdall[:, c, 0:4], [128, 4, 257]), ALU.mult, [Sf_r, cd_r], [Stmp_r])
                            O.tt('dve', Sf[:].rearrange("p (h e) -> p h e", h=4), Stmp[:].rearrange("p (h e) -> p h e", h=4),
                                 pL[:, :, 0:257], ALU.add, [Stmp_r, pL_r], [Sf_r])
                        else:
                            O.copy('act', lb_t[:].rearrange("p (h e) -> p h e", h=4), pL[:, :, 0:257], [pL_r], [lb_r])
                            O.dma(mLb[c, :, :], lb_t[:], [lb_r], [mLb_r[c]])
            O.memset('dve', Sf[:], 0.0, [Sf_r])
            for c in BWD_ORDER:
                lb_t, lb_r = Lb[c % 2]
                sb_t, sb_r = Sfb[c % 2]
                O.dma(lb_t[:], mLb[c, :, :], [mLb_r[c]], [lb_r])
                O.copy('act', sb_t[:], Sf[:], [Sf_r], [sb_r])
                O.dma(mprev[1, c, :, :], sb_t[:], [sb_r], [mprev_r[NCH + c]])
                O.tt('dve', Stmp[:].rearrange("p (h e) -> p h e", h=4), Sf[:].rearrange("p (h e) -> p h e", h=4),
                     bc_mid(cdall[:, c, 4:8], [128, 4, 257]), ALU.mult, [Sf_r, cd_r], [Stmp_r])
                O.tt('dve', Sf[:], Stmp[:], lb_t[:], ALU.add, [Stmp_r, lb_r], [Sf_r])
            P.barrier()

        mngt, mng_r = C['mng']
        yT, yT_r = G['yT'], G['yT_r']
        for pair in range(2):
            with ExitStack() as es:
                dn, dn_r = b.sb(es, "dn", [128, 8], F32)
                hp, hp_r = b.sb(es, "hp", [128, 2, 256], F32)
                hq, hq_r = b.sb(es, "hq", [128, 2, 256], F32)
                hall = [b.sb(es, "hall", [128, D], F32) for _ in range(2)]
                o2 = [b.sb(es, "o2", [128, D], BF16) for _ in range(2)]
                z2 = [b.sb(es, "mz2", [128, D], BF16) for _ in range(2)]
                sg, sg_r = b.sb(es, "sg", [128, D], F32)
                st4, st4_r = b.sb(es, "st4", [128, 8], F32)
                ybf, ybf_r = b.sb(es, "mybf", [128, D], BF16)
                yTs = [b.sb(es, "myTs", [128, 8, 128], BF16) for _ in range(2)]
                pYT, pYT_r = b.ps(es, "mpYT", [128, D], BF16)

                def load_extra(c, pair=pair):
                    if pair == 0:
                        return
                    r0 = c * 128
                    h_t, h_r = hall[c % 2]
                    o_t, o_r = o2[c % 2]
                    z_t, z_r = z2[c % 2]
                    O.dma(h_t[:, 0:512], mh[r0:r0 + 128, :], rs_of(mh_r, r0, 128), [h_r])
                    O.dma(o_t[:], mvo[r0:r0 + 128, D:2 * D], rs_of(mvo_r, r0, 128), [o_r])
                    O.dma(z_t[:], mz[r0:r0 + 128, :], rs_of(mz_r, r0, 128), [z_r])

                def finish(c, Yf, Yf_r, Yb, Yb_r, pair=pair):
                    r0 = c * 128
                    h_t, h_r = hall[c % 2]
                    Yfv = Yf.rearrange("p (g w) -> p g w", g=2)
                    Ybv = Yb.rearrange("p (g w) -> p g w", g=2)
                    O.act(dn[:, 0:2], Yfv[:, :, 256], AF.Abs, [Yf_r], [dn_r])
                    O.act(dn[:, 2:4], Ybv[:, :, 256], AF.Abs, [Yb_r], [dn_r])
                    O.ts('dve', dn[:, 0:4], dn[:, 0:4], 1.0, None, ALU.max, None, [dn_r], [dn_r])
                    O.recip(dn[:, 4:8], dn[:, 0:4], [dn_r], [dn_r])
                    O.tt('dve', hp[:], Yfv[:, :, 0:256], bc_mid(dn[:, 4:6], [128, 2, 256]), ALU.mult, [Yf_r, dn_r], [hp_r])
                    O.tt('dve', hq[:], Ybv[:, :, 0:256], bc_mid(dn[:, 6:8], [128, 2, 256]), ALU.mult, [Yb_r, dn_r], [hq_r])
                    if pair == 0:
                        O.tt('pool', hp[:], hp[:], hq[:], ALU.add, [hp_r, hq_r], [hp_r])
                        O.dma(mh[r0:r0 + 128, :], hp[:].rearrange("p g e -> p (g e)"), [hp_r], rs_of(mh_r, r0, 128))
                        return
                    o_t, o_r = o2[c % 2]
                    z_t, z_r = z2[c % 2]
                    O.tt('pool', h_t[:, 512:1024], hp[:].rearrange("p g e -> p (g e)"), hq[:].rearrange("p g e -> p (g e)"),
                         ALU.add, [hp_r, hq_r], [h_r])
                    O.act(sg[:], o_t[:], AF.Sigmoid, [o_r], [sg_r])
                    O.tt('dve', h_t[:], h_t[:], sg[:], ALU.mult, [h_r, sg_r], [h_r])
                    O.tt('dve', sg[:], h_t[:], h_t[:], ALU.mult, [h_r, sg_r], [sg_r])
                    O.reduce(st4[:, 0:4], sg[:].rearrange("p (h e) -> p h e", h=4), ALU.add, [sg_r], [st4_r])
                    O.act(st4[:, 4:8], st4[:, 0:4], AF.Sqrt, [st4_r], [st4_r], scale=1.0 / 256, bias=EPS)
                    O.recip(st4[:, 4:8], st4[:, 4:8], [st4_r], [st4_r])
                    O.tt('dve', h_t[:].rearrange("p (h e) -> p h e", h=4), h_t[:].rearrange("p (h e) -> p h e", h=4),
                         bc_mid(st4[:, 4:8], [128, 4, 256]), ALU.mult, [h_r, st4_r], [h_r])
                    O.tt('dve', h_t[:], h_t[:], mngt[:], ALU.mult, [h_r, mng_r], [h_r])
                    O.act(sg[:], z_t[:], AF.Silu, [z_r, sg_r], [sg_r])
                    O.tt('dve', ybf[:], h_t[:], sg[:], ALU.mult, [h_r, sg_r], [ybf_r])
                    yt, yt_r = yTs[c % 2]
                    for k in range(8):
                        O.tr(pYT[:, k * 128:(k + 1) * 128], ybf[:, k * 128:(k + 1) * 128], identb[:], [ybf_r, identb_r], [pYT_r], inc=(k == 7))
                    O.copy('act', yt[:].rearrange("p k t -> p (k t)"), pYT[:], [pYT_r], [yt_r])
                    O.dma(yT[2, :, :, r0:r0 + 128].rearrange("k p t -> p k t"), yt[:], [yt_r], rs_of(yT_r, r0, 128))

                la_pass2(b, O, G, es, NG=2, NSUB=1, PV=257, QT=mQT, QT_r=mQT_r, KT=mKT, KT_r=mKT_r, g0=2 * pair,
                         Vt=mVt, Vt_r=mVt_r, vcol0=2 * pair * 257, sm=msm, sm_r=msm_r, NA=4, a0=2 * pair,
                         sprev=mprev, sprev_r=mprev_r, load_extra=load_extra, finish=finish)
                P.barrier()


def phase_attn(b, l, G, C):
    P = b.P
    O = Ops(P)
    S = G['scr']
    identb, identb_r = G['identb'], G['identb_r']
    onesb, onesb_r = G['onesb'], G['onesb_r']
    aqk, aqk_r = G['aqk'], G['aqk_r']
    avz, avz_r = G['avz'], G['avz_r']
    azT, azT_r = G['azT'], G['azT_r']
    yT, yT_r = G['yT'], G['yT_r']
    cos_in, sin_in = G['cos_in'], G['sin_in']
    aQT, aQT_r = S('aQT', [8, 128, NT], BF16)
    aKT, aKT_r = S('aKT', [8, 128, NT], BF16)
    gqt, gq_r = C['gq']
    gkt, gk_r = C['gk']
    lampt, lamp_r = C['lamp']
    subgt, subg_r = C['subg']
    lamit, lami_r = C['lami']

    with ExitStack() as es0:
        sc, sc_r = b.sb(es0, "asc", [128, 16], F32)
        ggt, gg_r = b.sb(es0, "ggt", [128, 2, 64], F32)
        tmp64, tmp64_r = b.sb(es0, "tmp64", [128, 64], F32)
        O.ts('dve', ggt[:, 0, :], gqt[:], 0.125, None, ALU.mult, None, [gq_r], [gg_r])
        O.copy('dve', ggt[:, 1, :], gkt[:], [gk_r], [gg_r])
        O.reduce(sc[:, 4:6], ggt[:], ALU.max, [gg_r], [sc_r], absval=True)
        O.tt('dve', sc[:, 6:7], sc[:, 4:5], sc[:, 5:6], ALU.mult, [sc_r], [sc_r])
        O.ts('dve', sc[:, 0:1], sc[:, 6:7], -64.0, None, ALU.mult, None, [sc_r], [sc_r])
        for i in range(2):
            O.tt('dve', tmp64[:], lampt[:, 2 * i, :], lampt[:, 2 * i + 1, :], ALU.mult, [lamp_r], [tmp64_r])
            O.reduce(sc[:, 7 + i:8 + i], tmp64[:], ALU.add, [tmp64_r], [sc_r])
        O.act(sc[:, 7:9], sc[:, 7:9], AF.Exp, [sc_r], [sc_r])
        O.tt('dve', sc[:, 9:10], sc[:, 7:8], sc[:, 8:9], ALU.subtract, [sc_r], [sc_r])
        O.tt('dve', sc[:, 1:2], sc[:, 9:10], lamit[:, 0:1], ALU.add, [sc_r, lami_r], [sc_r])
        O.ts('dve', sc[:, 2:3], sc[:, 1:2], -1.0, None, ALU.mult, None, [sc_r], [sc_r])
        O.ts('dve', sc[:, 10:11], lamit[:, 0:1], -1.0, 1.0, ALU.mult, ALU.add, [lami_r], [sc_r])
        O.tt('dve', sc[:, 3:4], sc[:, 10:11], subgt[:, 0:1], ALU.mult, [sc_r, subg_r], [sc_r])

        with ExitStack() as es:
            qk = [b.sb(es, "qkraw", [128, 2048], BF16) for _ in range(2)]
            cs = [b.sb(es, "cs", [128, 2, 64], F32) for _ in range(2)]
            sq, sq_r = b.sb(es, "sq", [128, 2048], F32)
            xn, xn_r = b.sb(es, "xn", [128, 2048], F32)
            t2, t2_r = b.sb(es, "t2", [128, 2048], F32)
            st, st_r = b.sb(es, "ast", [128, 64], F32)
            xr, xr_r = b.sb(es, "xr", [128, 2048], BF16)
            pQ = [b.ps(es, "pQ", [128, 1024], BF16) for _ in range(2)]
            qTt = [b.sb(es, "qTt", [128, 2, 8, 128], BF16) for _ in range(2)]
            for i in range(NCH):
                r0 = i * 128
                q_t, q_r = qk[i % 2]
                O.dma(q_t[:], aqk[r0:r0 + 128, :], rs_of(aqk_r, r0, 128), [q_r])
                O.tt('dve', sq[:], q_t[:], q_t[:], ALU.mult, [q_r], [sq_r])
                O.reduce(st[:, 0:32], sq[:].rearrange("p (g e) -> p g e", e=64), ALU.add, [sq_r], [st_r])
                O.act(st[:, 32:64], st[:, 0:32], AF.Sqrt, [st_r], [st_r], scale=1.0 / 64, bias=EPS)
                O.recip(st[:, 32:64], st[:, 32:64], [st_r], [st_r])
                O.tt('dve', xn[:].rearrange("p (g e) -> p g e", e=64), q_t[:].rearrange("p (g e) -> p g e", e=64),
                     bc_mid(st[:, 32:64], [128, 32, 64]), ALU.mult, [q_r, st_r], [xn_r])
                for hf in range(2):
                    xv = xn[:, hf * 1024:(hf + 1) * 1024].rearrange("p (g e) -> p g e", e=64)
                    O.tt('pool', xv, xv, bc_lead(ggt[:, hf, :], [128, 16, 64]), ALU.mult, [xn_r, gg_r], [xn_r])
                if i >= 2:
                    c_t, c_r = cs[i % 2]
                    lt = r0 - TC
                    O.dma(c_t[:, 0, :], cos_in[lt:lt + 128, :], [], [c_r])
                    O.dma(c_t[:, 1, :], sin_in[lt:lt + 128, :], [], [c_r])
                    xg = xn[:].rearrange("p (g r q e) -> p g r q e", g=32, r=2, q=2, e=16)
                    tg = t2[:].rearrange("p (g r q e) -> p g r q e", g=32, r=2, q=2, e=16)
                    sv = c_t[:, 1, :].rearrange("p (r q e) -> p r q e", r=2, q=2, e=16)
                    for qq in range(2):
                        O.tt('dve', tg[:, :, :, qq, :], xg[:, :, :, 1 - qq, :],
                             sv[:, :, qq, :].unsqueeze(1).to_broadcast([128, 32, 2, 16]), ALU.mult, [xn_r, c_r], [t2_r])
                    O.tt('pool', xn[:].rearrange("p (g e) -> p g e", e=64), xn[:].rearrange("p (g e) -> p g e", e=64),
                         bc_lead(c_t[:, 0, :], [128, 32, 64]), ALU.mult, [xn_r, c_r], [xn_r])
                    O.tt('dve', xr[:], xn[:], t2[:], ALU.add, [xn_r, t2_r], [xr_r])
                else:
                    O.copy('dve', xr[:], xn[:], [xn_r], [xr_r])
                qt_t, qt_r = qTt[i % 2]
                for w in range(2):
                    p_t, p_r = pQ[w]
                    for h in range(8):
                        O.tr(p_t[:, h * 128:(h + 1) * 128], xr[:, w * 1024 + h * 128:w * 1024 + (h + 1) * 128], identb[:],
                             [xr_r, identb_r], [p_r], inc=(h == 7))
                    O.copy('act', qt_t[:, w, :, :].rearrange("p h t -> p (h t)"), p_t[:], [p_r], [qt_r])
                O.dma(aQT[:, :, r0:r0 + 128].rearrange("h p t -> p h t"), qt_t[:, 0, :, :], [qt_r], rs_of(aQT_r, r0, 128))
                O.dma(aKT[:, :, r0:r0 + 128].rearrange("h p t -> p h t"), qt_t[:, 1, :, :], [qt_r], rs_of(aKT_r, r0, 128))
            P.barrier()

        with ExitStack() as es:
            KTs = [b.sb(es, "KTs", [128, NT], BF16) for _ in range(2)]
            Vh = [b.sb(es, "Vh", [128, NCH, 128], BF16) for _ in range(2)]
            qb = [b.sb(es, "qb", [128, 512], BF16) for _ in range(2)]
            zb = [b.sb(es, "zb", [128, 512], BF16) for _ in range(2)]
            Pt = [b.sb(es, "Pt", [128, 2, 512], BF16) for _ in range(3)]
            rc, rc_r = b.sb(es, "rc", [128, 2, 512], F32)
            accs, accs_r = b.sb(es, "accs", [128, 2, 512], F32)
            o0, o0_r = b.sb(es, "o0", [128, 512], F32)
            o1, o1_r = b.sb(es, "o1", [128, 512], F32)
            osq, osq_r = b.sb(es, "osq", [128, 512], BF16)
            rst, rst_r = b.sb(es, "rst", [128, 512], F32)
            szt, sz_r = b.sb(es, "aszt", [128, 512], F32)
            yo = [b.sb(es, "yo", [128, 512], BF16) for _ in range(2)]
            pSc = [b.ps(es, "pSc", [128, 2, 512], F32) for _ in range(2)]
            pO, pO_r = b.ps(es, "pO", [128, 2, 512], F32)
            pSm, pSm_r = b.ps(es, "pSm", [128, 2, 512], F32)
            pSmA_r = Res('pSmA')
            it = 0
            nb = 0
            for h in range(8):
                k_t, k_r = KTs[h % 2]
                v_t, v_r = Vh[h % 2]
                O.dma(k_t[:], aKT[h, :, :], aKT_r, [k_r])
                for q4 in range(0, NCH, 11):
                    O.dma(v_t[:, q4:q4 + 11, :], avz[q4 * 128:(q4 + 11) * 128, h * 128:(h + 1) * 128].rearrange("(c p) e -> p c e", p=128),
                          rs_of(avz_r, q4 * 128, 11 * 128), [v_r])
                for (t0, bw) in TBLKS:
                    nk = 2 if t0 < TC else NCH
                    q_t, q_r = qb[nb % 2]
                    z_t, z_r = zb[nb % 2]
                    y_t, y_r = yo[nb % 2]
                    nb += 1
                    O.dma(q_t[:, 0:bw], aQT[h, :, t0:t0 + bw], rs_of(aQT_r, t0, bw), [q_r])
                    O.dma(z_t[:, 0:bw], azT[h, :, t0:t0 + bw], rs_of(azT_r, t0, bw), [z_r])
                    def emit_qk(kc, slot):
                        ps_t, ps_r = pSc[slot % 2]
                        for cm in range(2):
                            O.mm(ps_t[:, cm, 0:bw], k_t[cm * 64:(cm + 1) * 64, kc * 128:(kc + 1) * 128],
                                 q_t[cm * 64:(cm + 1) * 64, 0:bw], True, True, [k_r, q_r], [ps_r], inc=(cm == 1))
                    emit_qk(0, it)
                    for kc in range(nk):
                        ps_t, ps_r = pSc[it % 2]
                        p_t, p_r = Pt[it % 3]
                        if kc + 1 < nk:
                            emit_qk(kc + 1, it + 1)
                        O.act(p_t[:, :, 0:bw], ps_t[:, :, 0:bw], AF.Exp, [ps_r, sc_r], [p_r], bias=sc[:, 0:1])
                        for cm in range(2):
                            O.mm(pO[:, cm, 0:bw], v_t[:, kc, :], p_t[:, cm, 0:bw], kc == 0, kc == nk - 1, [v_r, p_r], [pO_r], inc=(cm == 1))
                        O.mm(pSm[:, 0, 0:bw], onesb[:, :], p_t[:, 0, 0:bw], kc == 0, kc == nk - 1, [onesb_r, p_r], [pSmA_r])
                        if kc == 0:
                            O.copy('dve', pSm[:, 1, 0:bw], p_t[:, 1, 0:bw], [p_r], [pSm_r])
                        else:
                            O.tt('dve', pSm[:, 1, 0:bw], pSm[:, 1, 0:bw], p_t[:, 1, 0:bw], ALU.add, [pSm_r, p_r], [pSm_r])
                        it += 1
                    O.copy('dve', accs[:, 1, 0:bw], pSm[:, 1, 0:bw], [pSm_r], [accs_r])
                    O.mm(pSm[:, 1, 0:bw], G['onesf'][:, :], accs[:, 1, 0:bw], True, True, [G['onesf_r'], accs_r], [pSm_r])
                    O.P.op('dve', lambda e, bw=bw: e.reciprocal(out=rc[:, :, 0:bw], in_=pSm[:, :, 0:bw]), [pSm_r, pSmA_r], [rc_r])
                    O.tt('dve', o0[:, 0:bw], pO[:, 0, 0:bw], rc[:, 0, 0:bw], ALU.mult, [pO_r, rc_r], [o0_r])
                    O.tt('dve', o1[:, 0:bw], pO[:, 1, 0:bw], rc[:, 1, 0:bw], ALU.mult, [pO_r, rc_r], [o1_r])
                    O.stt(o0[:, 0:bw], o1[:, 0:bw], sc[:, 2:3], o0[:, 0:bw], ALU.mult, ALU.add, [o1_r, o0_r, sc_r], [o0_r])
                    O.act(osq[:, 0:bw], o0[:, 0:bw], AF.Square, [o0_r], [osq_r])
                    ps_t, ps_r = pSc[it % 2]
                    it += 1
                    O.mm(ps_t[:, 0, 0:bw], onesb[:, :], osq[:, 0:bw], True, True, [onesb_r, osq_r], [ps_r])
                    O.act(rst[:, 0:bw], ps_t[:, 0, 0:bw], AF.Sqrt, [ps_r], [rst_r], scale=1.0 / 128, bias=EPS)
                    O.recip(rst[:, 0:bw], rst[:, 0:bw], [rst_r], [rst_r])
                    O.act(szt[:, 0:bw], z_t[:, 0:bw], AF.Silu, [z_r], [sz_r])
                    O.stt(o0[:, 0:bw], o0[:, 0:bw], sc[:, 3:4], rst[:, 0:bw], ALU.mult, ALU.mult, [o0_r, sc_r, rst_r], [o0_r])
                    O.tt('dve', y_t[:, 0:bw], o0[:, 0:bw], szt[:, 0:bw], ALU.mult, [o0_r, sz_r], [y_r])
                    O.dma(yT[1, h, :, t0:t0 + bw], y_t[:, 0:bw], [y_r], rs_of(yT_r, t0, bw))
            P.barrier()


def phase_merge(b, l, G, C):
    P = b.P
    O = Ops(P)
    yT, yT_r = G['yT'], G['yT_r']
    gtT, gtT_r = G['gtT'], G['gtT_r']
    w_br, w_out = G['w_br'], G['w_out']
    xsrc, xsrc_r, xdst, xdst_r, dst_off = G['xsrc'], G['xsrc_r'], G['xdst'], G['xdst_r'], G['dst_off']
    gbc, gbc_r = G['gbc'], G['gbc_r']
    with ExitStack() as es:
        stg = [b.sb(es, "wstg", [128, D], F32) for _ in range(2)]
        Wb, Wb_r = b.sb(es, "Wb", [128, 3, 8, D], BF16)
        Wo, Wo_r = b.sb(es, "Wo", [128, 8, D], BF16)
        yb = [b.sb(es, "yb", [128, 3, 8, 256], BF16) for _ in range(2)]
        gb = [b.sb(es, "gb", [128, 8, 256], BF16) for _ in range(2)]
        sg, sg_r = b.sb(es, "msg", [128, 8, 256], F32)
        macc, macc_r = b.sb(es, "macc", [128, 8, 256], F32)
        mt = [b.sb(es, "mtmp", [128, 512], F32) for _ in range(2)]
        mxT, mxT_r = b.sb(es, "mxT", [128, 8, 256], BF16)
        xin_t = [b.sb(es, "xin_t", [128, D], F32) for _ in range(2)]
        xo_t = [b.sb(es, "xo_t", [128, D], F32) for _ in range(2)]
        pm_ = [b.ps(es, "pmg", [128, 512], F32) for _ in range(4)]
        n = 0
        for br in range(4):
            for kc in range(8):
                st, st_r = stg[n % 2]
                n += 1
                src = w_br[l, br, kc * 128:(kc + 1) * 128, :] if br < 3 else w_out[l, kc * 128:(kc + 1) * 128, :]
                O.dma(st[:], src, [], [st_r])
                dst = Wb[:, br, kc, :] if br < 3 else Wo[:, kc, :]
                O.copy('pool', dst, st[:], [st_r], [Wb_r if br < 3 else Wo_r])
        npm = 0
        nm = 0
        for bi, (t0, bw) in enumerate([(256 * i, 256) for i in range(NT // 256)]):
            y_t, y_r = yb[bi % 2]
            for br in range(3):
                O.dma(y_t[:, br, :, 0:bw], yT[br, :, :, t0:t0 + bw].rearrange("k p t -> p k t"), rs_of(yT_r, t0, bw), [y_r])
            for br in range(3):
                g_t, g_r = gb[(bi * 3 + br) % 2]
                O.dma(g_t[:, :, 0:bw], gtT[br * 8:(br + 1) * 8, :, t0:t0 + bw].rearrange("k p t -> p k t"), rs_of(gtT_r, t0, bw), [g_r])
                O.act(sg[:, :, 0:bw], g_t[:, :, 0:bw], AF.Sigmoid, [g_r], [sg_r])
                for fc in range(8):
                    p_t, p_r = pm_[npm % 4]
                    npm += 1
                    for kc in range(8):
                        O.mm(p_t[:, 0:bw], Wb[:, br, kc, fc * 128:(fc + 1) * 128], y_t[:, br, kc, 0:bw], kc == 0, kc == 7,
                             [Wb_r, y_r], [p_r], inc=(kc == 7))
                    if br == 0:
                        O.tt('dve', macc[:, fc, 0:bw], p_t[:, 0:bw], sg[:, fc, 0:bw], ALU.mult, [p_r, sg_r], [macc_r])
                    else:
                        m_t, m_r = mt[nm % 2]
                        nm += 1
                        O.tt('dve', m_t[:, 0:bw], p_t[:, 0:bw], sg[:, fc, 0:bw], ALU.mult, [p_r, sg_r], [m_r])
                        if br == 1:
                            O.tt('pool', macc[:, fc, 0:bw], macc[:, fc, 0:bw], m_t[:, 0:bw], ALU.add, [macc_r, m_r], [macc_r])
                        else:
                            O.tt('pool', mxT[:, fc, 0:bw], macc[:, fc, 0:bw], m_t[:, 0:bw], ALU.add, [macc_r, m_r], [mxT_r])
            j = 1 if t0 < TC else 0
            for ti in range(bw // 128):
                r0 = t0 + ti * 128
                if r0 < dst_off:
                    continue
                xi, xi_r = xin_t[ti % 2]
                xo, xo_r = xo_t[ti % 2]
                O.dma(xi[:], xsrc[r0:r0 + 128, :], rs_of(xsrc_r, r0, 128), [xi_r])
                for hf in range(2):
                    p_t, p_r = pm_[npm % 4]
                    npm += 1
                    for kc in range(8):
                        O.mm(p_t[:, :], mxT[:, kc, ti * 128:(ti + 1) * 128], Wo[:, kc, hf * 512:(hf + 1) * 512], kc == 0, kc == 7,
                             [mxT_r, Wo_r], [p_r], inc=(kc == 7))
                    O.tt('dve', xo[:, hf * 512:(hf + 1) * 512], p_t[:, :], gbc[:, l, j, hf * 512:(hf + 1) * 512], ALU.mult,
                         [p_r, gbc_r], [xo_r])
                O.tt('pool', xo[:], xo[:], xi[:], ALU.add, [xo_r, xi_r], [xo_r])
                O.dma(xdst[r0 - dst_off:r0 - dst_off + 128, :], xo[:], [xo_r], rs_of(xdst_r, r0, 128))
        P.barrier()
```

```python
import math
import numpy as np
import ml_dtypes
from contextlib import ExitStack
import concourse.bass as bass
import concourse.mybir as mybir
from concourse.bass_utils import run_bass_kernel_spmd

F32 = mybir.dt.float32
BF16 = mybir.dt.bfloat16
AF = mybir.ActivationFunctionType
ALU = mybir.AluOpType
AX = mybir.AxisListType

ENGS = ['pe', 'act', 'dve', 'pool', 'sp']
NDMA = {'sp': 12, 'act': 6, 'pool': 6}
SAME_ENG_SYNC = True
INTERNAL_OK = 'ALL'

DEPTH = 4
D = 1024
TC = 256
TL = 8192
NT = TC + TL
NCH = NT // 128
D_IN = 13872
EPS = 1e-6
FWD_ORDER = list(range(NCH))
BWD_ORDER = [1, 0] + list(range(NCH - 1, 1, -1))


class Res:
    __slots__ = ('name', 'w', 'r', 'pr', 'excl')

    def __init__(self, name='', excl=False):
        self.name = name
        self.excl = excl
        self.w = {}
        self.r = {}
        self.pr = {}


class Prog:
    def __init__(self, nc):
        self.nc = nc
        self.ops = {e: [] for e in ENGS}
        self.cnt = {e: 0 for e in ENGS}
        self.known = {e: {} for e in ENGS}
        self.dcnt = {}
        self.dnext = {q: 0 for q in NDMA}
        self.rr = 0
        self.floor = {}

    def barrier(self):
        for e in ENGS:
            if self.cnt[e] > 0:
                self.floor[e] = self.cnt[e]
        for sk, v in self.dcnt.items():
            self.floor[sk] = v

    def _deps(self, eng, reads, writes, extra=(), part=True):
        toks = {}

        def add(sk, v):
            if toks.get(sk, 0) < v:
                toks[sk] = v
        for r in reads:
            for sk, v in r.w.items():
                add(sk, v)
        for w in writes:
            if w.r or not part or w.excl:
                for sk, v in w.w.items():
                    add(sk, v)
            for sk, v in w.r.items():
                add(sk, v)
            for sk, v in w.pr.items():
                add(sk, v)
        for sk, v in extra:
            add(sk, v)
        for sk, v in self.floor.items():
            add(sk, v)
        waits = []
        kn = self.known[eng]
        for sk, v in toks.items():
            if sk == eng and (eng == 'pe' or not SAME_ENG_SYNC):
                continue
            if kn.get(sk, 0) < v:
                kn[sk] = v
                waits.append((sk, v))
        return waits

    def _mark(self, tok, reads, writes, part=True):
        sk, v = tok
        for w in writes:
            if w.r or not part or w.excl:
                pr = dict(w.r)
                for k2, v2 in w.w.items():
                    if pr.get(k2, 0) < v2:
                        pr[k2] = v2
                w.pr = pr
                w.w = {sk: v}
                w.r = {}
            elif w.w.get(sk, 0) < v:
                w.w[sk] = v
        for r in reads:
            if r in writes:
                continue
            if r.r.get(sk, 0) < v:
                r.r[sk] = v

    def op(self, eng, fn, reads=(), writes=(), inc=True):
        xr = [r for r in reads if r.excl and r not in writes]
        if xr:
            writes = list(writes) + xr
        waits = self._deps(eng, reads, writes)
        if inc:
            self.cnt[eng] += 1
            tok = (eng, self.cnt[eng])
        else:
            tok = (eng, self.cnt[eng] + 1)
        self.ops[eng].append((waits, fn, eng if inc else None, 1))
        self._mark(tok, reads, writes)

    def dma(self, q, out, in_, reads=(), writes=(), **kw):
        if q is None:
            q = ('sp', 'sp', 'act')[self.rr % 3]
            self.rr += 1
        j = self.dnext[q]
        self.dnext[q] = (j + 1) % NDMA[q]
        sk = '%s_d%d' % (q, j)
        prev = self.dcnt.get(sk, 0)
        waits = self._deps(q, reads, writes, extra=[(sk, prev)] if prev else [])
        self.dcnt[sk] = prev + 16
        tok = (sk, prev + 16)
        self.ops[q].append((waits, lambda e: e.dma_start(out=out, in_=in_, **kw), sk, 16))
        self._mark(tok, reads, writes)

    def finish(self):
        nc = self.nc
        sems = {}
        for e in ENGS:
            sems[e] = nc.alloc_semaphore(name='s_' + e)
        for sk in self.dcnt:
            sems[sk] = nc.alloc_semaphore(name='s_' + sk)
        fin = [(e, self.cnt[e]) for e in ENGS if e != 'sp' and self.cnt[e] > 0]
        fin += [(sk, v) for sk, v in self.dcnt.items()]
        handles = {'pe': 'tensor', 'act': 'scalar', 'dve': 'vector', 'pool': 'gpsimd', 'sp': 'sync'}
        ops = self.ops

        def replay(eng):
            def body(e):
                for waits, fn, sk, n in ops[eng]:
                    for wk, wv in waits:
                        e.wait_ge(sems[wk], wv)
                    ins = fn(e)
                    if sk is not None:
                        ins.then_inc(sems[sk], n)
                if eng == 'sp':
                    for wk, wv in fin:
                        e.wait_ge(sems[wk], wv)
            return body

        with nc.Block() as block:
            for eng in ENGS:
                getattr(block, handles[eng])(replay(eng))


class Ops:
    def __init__(self, P):
        self.P = P

    def tt(self, eng, out, in0, in1, op, R, W):
        self.P.op(eng, lambda e: e.tensor_tensor(out=out, in0=in0, in1=in1, op=op), R, W)

    def ts(self, eng, out, in0, s1, s2, op0, op1, R, W):
        if op1 is None:
            self.P.op(eng, lambda e: e.tensor_scalar(out=out, in0=in0, scalar1=s1, scalar2=None, op0=op0), R, W)
        else:
            self.P.op(eng, lambda e: e.tensor_scalar(out=out, in0=in0, scalar1=s1, scalar2=s2, op0=op0, op1=op1), R, W)

    def stt(self, out, in0, scalar, in1, op0, op1, R, W):
        self.P.op('dve', lambda e: e.scalar_tensor_tensor(out=out, in0=in0, scalar=scalar, in1=in1, op0=op0, op1=op1), R, W)

    def act(self, out, in_, func, R, W, bias=None, scale=None, accum=None):
        kw = {}
        if bias is not None:
            kw['bias'] = bias
        if scale is not None:
            kw['scale'] = scale
        if accum is not None:
            kw['accum_out'] = accum
        self.P.op('act', lambda e: e.activation(out=out, in_=in_, func=func, **kw), R, W)

    def copy(self, eng, out, in_, R, W):
        if eng == 'act':
            self.P.op('act', lambda e: e.copy(out=out, in_=in_), R, W)
        else:
            self.P.op(eng, lambda e: e.tensor_copy(out=out, in_=in_), R, W)

    def memset(self, eng, ap, val, W):
        self.P.op(eng, lambda e: e.memset(ap, val), (), W)

    def recip(self, out, in_, R, W):
        self.P.op('dve', lambda e: e.reciprocal(out=out, in_=in_), R, W)

    def reduce(self, out, in_, op, R, W, absval=None):
        self.P.op('dve', lambda e: e.tensor_reduce(out=out, in_=in_, axis=AX.X, op=op, apply_absolute_value=absval), R, W)

    def mm(self, out, lhsT, rhs, start, stop, R, W, inc=True):
        self.P.op('pe', lambda e: e.matmul(out, lhsT, rhs, start=start, stop=stop, skip_group_check=True), R, W, inc=inc)

    def tr(self, out, in_, ident, R, W, inc=True):
        self.P.op('pe', lambda e: e.transpose(out=out, in_=in_, identity=ident), R, W, inc=inc)

    def dma(self, out, in_, R, W, q=None):
        self.P.dma(q, out, in_, R, W)


def bc_mid(ap, shape):
    return ap.unsqueeze(2).to_broadcast(shape)


def bc_lead(ap, shape):
    return ap.unsqueeze(1).to_broadcast(shape)


class B:
    def __init__(self, nc):
        self.nc = nc
        self.P = Prog(nc)
        self.uid = 0

    def name(self, n):
        self.uid += 1
        return '%s_%d' % (n, self.uid)

    def sb(self, es, n, shape, dt):
        t = es.enter_context(self.nc.sbuf_tensor(self.name(n), shape, dt))
        return t, Res(n)

    def ps(self, es, n, shape, dt=F32):
        t = es.enter_context(self.nc.psum_tensor(self.name(n), shape, dt))
        return t, Res(n, excl=True)

    def dram(self, n, shape, dt, nres=1):
        t = self.nc.dram_tensor(self.name(n), shape, dt, kind="Internal").ap()
        return t, [Res(n) for _ in range(nres)]


def blk_of(t0, n):
    return list(range(t0 // 512, (t0 + n - 1) // 512 + 1))


NBLK = (NT + 511) // 512
TBLKS = [(0, 256)] + [(256 + 512 * i, 512) for i in range(16)]


def rs_of(rl, t0, n):
    return [rl[i] for i in blk_of(t0, n)]


C_XBC = 0
C_ZS = 1536
C_DT = 2560
C_Q = 2592
C_K = 3616
C_V = 4640
C_ZA = 5664
C_MQK = 6688
C_MV = 7712
C_MO = 8736
C_MZ = 9760
C_IG = 10784
C_FG = 10792
C_GT = 10800


def build_program(n_layers, dbg=None, only=None):
    dbg = dbg or ()
    scr_cache = {}
    nc = bass.Bass("TRN2", target_bir_lowering=False)
    b = B(nc)
    P = b.P
    L = n_layers

    def inp(n, shape, dt=F32):
        return nc.dram_tensor(n, shape, dt, kind="ExternalInput").ap()

    xin = inp("xin", [NT, D])
    cc = inp("cc", [128, 16])
    ident_in = inp("ident", [128, 128])
    masks_in = inp("masks", [128, 4, 128])
    cos_in = inp("cosb", [TL, 64])
    sin_in = inp("sinb", [TL, 64])
    lami_in = inp("lami", [128, L])
    w_mod = inp("w_mod", [L, D, 3 * D])
    bmod = inp("bmod", [128, L, 24])
    ng = inp("ng", [128, L, 8])
    w_in = inp("w_in", [L, D, D_IN])
    cw = inp("cw", [128, L, 12, 3])
    cb = inp("cb", [128, L, 12])
    alog = inp("alog", [128, L, 32])
    dtb = inp("dtb", [128, L, 32])
    dsk = inp("dsk", [128, L, 16])
    sng = inp("sng", [128, L, D])
    gq = inp("gq", [128, L, 64])
    gk = inp("gk", [128, L, 64])
    lamp = inp("lamp", [128, L, 4, 64])
    subg = inp("subg", [128, L, 8])
    mcw = inp("mcw", [128, L, 8, 3])
    mcb = inp("mcb", [128, L, 8])
    ib = inp("ib", [128, L, 8])
    fb = inp("fb", [128, L, 8])
    mng = inp("mng", [128, L, D])
    sel_in = inp("sel", [2, 2, 128])
    bgate = inp("bgate", [2, L, D])
    w_br = inp("w_br", [L, 3, D, D])
    w_out = inp("w_out", [L, D, D])
    xout = nc.dram_tensor("xout", [NT, D], F32, kind="ExternalOutput").ap()

    def scratch(n, shape, dt, nres=NBLK):
        if only is not None and n in only.get('inputs', ()):
            t = nc.dram_tensor("dbg_" + n, shape, dt, kind="ExternalInput").ap()
            return t, [Res(n) for _ in range(nres)]
        if n in dbg or INTERNAL_OK is None or (INTERNAL_OK != 'ALL' and n not in INTERNAL_OK):
            t = nc.dram_tensor("dbg_" + n, shape, dt, kind="ExternalOutput").ap()
            return t, [Res(n) for _ in range(nres)]
        return b.dram(n, shape, dt, nres)

    xbufs = [(xin, [Res('xin') for _ in range(NBLK)])]
    for l in range(1, L):
        xbufs.append(scratch("xres%d" % l, [NT, D], F32))
    xout_res = [Res('xout') for _ in range(NBLK)]

    hT, hT_r = scratch("hT", [8, 128, NT], BF16)
    xbcT, xbcT_r = scratch("xbcT", [12, 128, NT], BF16)
    mqkT, mqkT_r = scratch("mqkT", [8, 128, NT], BF16)
    zs, zs_r = scratch("zs", [NT, 1024], BF16)
    small, small_r = scratch("small", [NT, 48], F32)
    aqk, aqk_r = scratch("aqk", [NT, 2048], BF16)
    avz, avz_r = scratch("avz", [NT, 1024], BF16)
    azT, azT_r = scratch("azT", [8, 128, NT], BF16)
    mvo, mvo_r = scratch("mvo", [NT, 2048], BF16)
    mz, mz_r = scratch("mz", [NT, 1024], BF16)
    gtT, gtT_r = scratch("gtT", [24, 128, NT], BF16)
    yT, yT_r = scratch("yT", [3, 8, 128, NT], BF16)

    with ExitStack() as g_es:
        identf, identf_r = b.sb(g_es, "identf", [128, 128], F32)
        identb, identb_r = b.sb(g_es, "identb", [128, 128], BF16)
        masks, masks_r = b.sb(g_es, "masks", [128, 4, 128], F32)
        onesb, onesb_r = b.sb(g_es, "onesb", [128, 128], BF16)
        onesf, onesf_r = b.sb(g_es, "onesf", [128, 128], F32)
        gbc, gbc_r = b.sb(g_es, "gbc", [128, L, 2, D], F32)
        modt, modt_r = b.sb(g_es, "modt", [128, L, 24, 2], F32)
        a1t, a1t_r = b.sb(g_es, "a1t", [128, L, 8, 2], F32)
        P.dma('sp', identf[:], ident_in[:, :], writes=[identf_r])
        P.dma('sp', masks[:], masks_in[:, :, :], writes=[masks_r])
        P.op('dve', lambda e: e.tensor_copy(out=identb[:], in_=identf[:]), reads=[identf_r], writes=[identb_r])
        P.op('dve', lambda e: e.memset(onesb[:], 1.0), writes=[onesb_r])
        P.op('dve', lambda e: e.memset(onesf[:], 1.0), writes=[onesf_r])

        with ExitStack() as es:
            cct, cct_r = b.sb(es, "cct", [128, 16], F32)
            sct, sct_r = b.sb(es, "sct", [128, 16], F32)
            bmt, bmt_r = b.sb(es, "bmt", [128, L, 24], F32)
            ngt, ngt_r = b.sb(es, "ngt", [128, L, 8], F32)
            wm, wm_r = b.sb(es, "wm", [128, 8, 3 * D], F32)
            pm, pm_r = b.ps(es, "pm", [128, 512], F32)
            pgr, pgr_r = b.ps(es, "pgr", [128, 2, 512], F32)
            pgb, pgb_r = b.ps(es, "pgb", [128, 2, 512], F32)
            selt, selt_r = b.sb(es, "selt", [2, 2, 128], F32)
            bgt, bgt_r = b.sb(es, "bgt", [2, L, D], F32)
            grow, grow_r = b.sb(es, "grow", [2, D], F32)
            P.dma('sp', selt[:], sel_in[:, :, :], writes=[selt_r])
            P.dma('sp', bgt[:], bgate[:, :, :], writes=[bgt_r])
            P.dma('sp', cct[:], cc[:, :], writes=[cct_r])
            P.dma('sp', bmt[:], bmod[:, :, :], writes=[bmt_r])
            P.dma('sp', ngt[:], ng[:, :, :], writes=[ngt_r])
            P.op('act', lambda e: e.activation(out=sct[:], in_=cct[:], func=AF.Silu), reads=[cct_r], writes=[sct_r])
            if 'mod' in dbg:
                dsc = nc.dram_tensor("dbg_sct", [128, 16], F32, kind="ExternalOutput").ap()
                P.dma('sp', dsc[:, :], sct[:], reads=[sct_r], writes=[Res('x')])
            for l in range(L):
                for kc in range(8):
                    P.dma(None, wm[:, kc, :], w_mod[l, kc * 128:(kc + 1) * 128, :], writes=[wm_r])
                for fc in range(24):
                    for kc in range(8):
                        P.op('pe', lambda e, fc=fc, kc=kc: e.matmul(
                            pm[:, fc * 2:fc * 2 + 2], wm[:, kc, fc * 128:(fc + 1) * 128], sct[:, kc * 2:kc * 2 + 2],
                            start=(kc == 0), stop=(kc == 7)),
                            reads=[wm_r, sct_r], writes=[pm_r], inc=(fc == 23 and kc == 7))
                for hf in range(2):
                    for kc in range(8):
                        P.op('pe', lambda e, hf=hf, kc=kc: e.matmul(
                            pgr[0:2, hf, :], sct[:, kc * 2:kc * 2 + 2], wm[:, kc, 2 * D + hf * 512:2 * D + (hf + 1) * 512],
                            start=(kc == 0), stop=(kc == 7)), reads=[wm_r, sct_r], writes=[pgr_r], inc=(kc == 7))
                P.op('dve', lambda e, l=l: e.tensor_tensor(out=grow[:, :].rearrange("p (h f) -> p h f", h=2), in0=pgr[0:2, :, :],
                                                       in1=bgt[:, l, :].rearrange("p (h f) -> p h f", h=2), op=ALU.add),
                     reads=[pgr_r, bgt_r], writes=[grow_r])
                for j in range(2):
                    for hf in range(2):
                        P.op('pe', lambda e, j=j, hf=hf: e.matmul(pgb[:, hf, :], selt[:, j, :], grow[:, hf * 512:(hf + 1) * 512],
                                                                start=True, stop=True), reads=[selt_r, grow_r], writes=[pgb_r])
                    P.op('dve', lambda e, l=l, j=j: e.tensor_copy(out=gbc[:, l, j, :].rearrange("p (h f) -> p h f", h=2), in_=pgb[:, :, :]),
                         reads=[pgb_r], writes=[gbc_r])
                for j in range(2):
                    P.op('dve', lambda e, l=l, j=j: e.tensor_tensor(
                        out=modt[:, l, :, j], in0=pm[:, 0:48].rearrange("p (f j) -> p f j", j=2)[:, :, j],
                        in1=bmt[:, l, :], op=ALU.add), reads=[pm_r, bmt_r], writes=[modt_r])
                for j in range(2):
                    P.op('dve', lambda e, l=l, j=j: e.scalar_tensor_tensor(
                        out=a1t[:, l, :, j], in0=modt[:, l, 8:16, j], scalar=1.0, in1=ngt[:, l, :],
                        op0=ALU.add, op1=ALU.mult), reads=[modt_r, ngt_r], writes=[a1t_r])
            P.barrier()

        if 'mod' in dbg:
            dmod = nc.dram_tensor("dbg_mod", [128, L * 48], F32, kind="ExternalOutput").ap()
            da1 = nc.dram_tensor("dbg_a1", [128, L * 16], F32, kind="ExternalOutput").ap()
            P.dma('sp', dmod[:, :], modt[:].rearrange("p l f j -> p (l f j)"), reads=[modt_r], writes=[Res('x')])
            P.dma('sp', da1[:, :], a1t[:].rearrange("p l f j -> p (l f j)"), reads=[a1t_r], writes=[Res('x')])
        for l in range(L):
            xsrc, xsrc_r = xbufs[l]
            if l + 1 < L:
                xdst, xdst_r = xbufs[l + 1]
                dst_off = 0
            else:
                xdst, xdst_r = xout, xout_res
                dst_off = 0
            layer(b, l, L, locals())

        P.finish()
        global LAST_PROG
        LAST_PROG = P
    return nc


def layer(b, l, L, G):
    cache = G['scr_cache']

    def S(n, shape, dt, nres=NBLK):
        if n not in cache:
            cache[n] = G['scratch'](n, shape, dt, nres)
        return cache[n]
    G['scr'] = S
    only = G.get('only')
    with ExitStack() as es:
        C = layer_consts(b, es, l, G)
        if only is None or 'a' in only:
            phase_a1(b, l, G)
            phase_a2(b, l, G)
        if only is None or 'ssd' in only:
            phase_ssd(b, l, G, C)
        if only is None or 'ml' in only:
            phase_mlstm(b, l, G, C)
        if only is None or 'attn' in only:
            phase_attn(b, l, G, C)
        if only is None or 'merge' in only:
            phase_merge(b, l, G, C)
        b.P.barrier()


def phase_a1(b, l, G):
    P = b.P
    xsrc, xsrc_r = G['xsrc'], G['xsrc_r']
    hT, hT_r = G['hT'], G['hT_r']
    identb, identb_r = G['identb'], G['identb_r']
    modt, modt_r, a1t, a1t_r = G['modt'], G['modt_r'], G['a1t'], G['a1t_r']
    with ExitStack() as es:
        xt = [b.sb(es, "xt", [128, D], F32) for _ in range(2)]
        junk, junk_r = b.sb(es, "junk", [128, D], BF16)
        xnb = [b.sb(es, "xnb", [128, D], BF16) for _ in range(2)]
        ss = [b.sb(es, "ss", [128, 1], F32) for _ in range(2)]
        pT = [b.ps(es, "pT", [128, D], BF16) for _ in range(2)]
        hb = [b.sb(es, "hb", [128, 8, 512], BF16) for _ in range(2)]
        for bi, (t0, bw) in enumerate(TBLKS):
            hbt, hb_r = hb[bi % 2]
            j = 1 if t0 < TC else 0
            for ti in range(bw // 128):
                i = (t0 // 128 + ti)
                xtt, xt_r = xt[i % 2]
                xn, xn_r = xnb[i % 2]
                st, st_r = ss[i % 2]
                pt, pt_r = pT[i % 2]
                P.dma(None, xtt[:], xsrc[i * 128:(i + 1) * 128, :], reads=rs_of(xsrc_r, i * 128, 128), writes=[xt_r])
                P.op('act', lambda e, xtt=xtt, st=st: e.activation(out=junk[:], in_=xtt[:], func=AF.Square, accum_out=st[:]),
                     reads=[xt_r], writes=[junk_r, st_r])
                P.op('act', lambda e, st=st: e.activation(out=st[:], in_=st[:], func=AF.Sqrt, scale=1.0 / D, bias=EPS),
                     reads=[st_r], writes=[st_r])
                P.op('dve', lambda e, st=st: e.reciprocal(out=st[:], in_=st[:]), reads=[st_r], writes=[st_r])
                P.op('dve', lambda e, xn=xn, xtt=xtt, st=st: e.tensor_scalar(out=xn[:], in0=xtt[:], scalar1=st[:, 0:1], scalar2=None, op0=ALU.mult),
                     reads=[xt_r, st_r], writes=[xn_r])
                for kc in range(8):
                    P.op('pe', lambda e, kc=kc, pt=pt, xn=xn: e.transpose(out=pt[:, kc * 128:(kc + 1) * 128], in_=xn[:, kc * 128:(kc + 1) * 128], identity=identb[:]),
                         reads=[xn_r, identb_r], writes=[pt_r], inc=(kc == 7))
                for kc in range(8):
                    P.op('dve', lambda e, kc=kc, pt=pt, hbt=hbt, ti=ti, j=j: e.tensor_scalar(
                        out=hbt[:, kc, ti * 128:(ti + 1) * 128], in0=pt[:, kc * 128:(kc + 1) * 128],
                        scalar1=a1t[:, l, kc, j:j + 1], scalar2=modt[:, l, kc, j:j + 1], op0=ALU.mult, op1=ALU.add),
                        reads=[pt_r, a1t_r, modt_r], writes=[hb_r])
            P.dma(None, hT[:, :, t0:t0 + bw].rearrange("k p t -> p k t"), hbt[:, :, 0:bw],
                  reads=[hb_r], writes=rs_of(hT_r, t0, bw))
        P.barrier()


def gemm_jobs():
    jobs = []
    jobs.append(('F', C_XBC, 1536, 'xbcT'))
    jobs.append(('F', C_MQK, 1024, 'mqkT'))
    jobs.append(('T', C_ZS, 1024, 'zs', 0))
    jobs.append(('T', C_Q, 2048, 'aqk', 0))
    jobs.append(('T', C_V, 1024, 'avz', 0))
    jobs.append(('F', C_ZA, 1024, 'azT'))
    jobs.append(('T', C_MV, 2048, 'mvo', 0))
    jobs.append(('T', C_MZ, 1024, 'mz', 0))
    jobs.append(('F', C_GT, 2048, 'gtT', 0))
    jobs.append(('F', C_GT + 2048, 1024, 'gtT', 16))
    return jobs


def phase_a2(b, l, G):
    P = b.P
    w_in = G['w_in']
    hT, hT_r = G['hT'], G['hT_r']
    with ExitStack() as es:
        stg = [b.sb(es, "stg", [128, 2048], F32) for _ in range(2)]
        Wg = [b.sb(es, "Wg", [128, 8, 2048], BF16) for _ in range(2)]
        hb = [b.sb(es, "hb2", [128, 8, 512], BF16) for _ in range(2)]
        ob = [b.sb(es, "ob", [128, 2048], BF16) for _ in range(2)]
        obs = [b.sb(es, "obs", [128, 48], F32) for _ in range(2)]
        pg = [b.ps(es, "pg", [128, 512], F32) for _ in range(4)]
        ws, ws_r = b.sb(es, "ws", [128, 8, 48], BF16)
        cnt = {'stg': 0, 'w': 0, 'hb': 0, 'ob': 0, 'pg': 0, 'ev': 0}

        def load_w(c0, n):
            wt, wt_r = Wg[cnt['w'] % 2]
            cnt['w'] += 1
            for kc in range(8):
                st, st_r = stg[cnt['stg'] % 2]
                cnt['stg'] += 1
                P.dma(None, st[:, 0:n], w_in[l, kc * 128:(kc + 1) * 128, c0:c0 + n], writes=[st_r])
                P.op('pool', lambda e, wt=wt, st=st, kc=kc, n=n: e.tensor_copy(out=wt[:, kc, 0:n], in_=st[:, 0:n]),
                     reads=[st_r], writes=[wt_r])
            return wt, wt_r

        def load_h(t0, bw):
            ht, ht_r = hb[cnt['hb'] % 2]
            cnt['hb'] += 1
            P.dma(None, ht[:, :, 0:bw], hT[:, :, t0:t0 + bw].rearrange("k p t -> p k t"),
                  reads=rs_of(hT_r, t0, bw), writes=[ht_r])
            return ht, ht_r

        def evac(out_ap, in_ap, reads, writes):
            eng = 'act' if cnt['ev'] % 2 == 0 else 'dve'
            cnt['ev'] += 1
            if eng == 'act':
                P.op('act', lambda e: e.copy(out=out_ap, in_=in_ap), reads=reads, writes=writes)
            else:
                P.op('dve', lambda e: e.tensor_copy(out=out_ap, in_=in_ap), reads=reads, writes=writes)

        for job in gemm_jobs():
            mode, c0, n = job[0], job[1], job[2]
            dst, dst_r = G[job[3]], G[job[3] + '_r']
            wt, wt_r = load_w(c0, n)
            for (t0, bw) in TBLKS:
                ht, ht_r = load_h(t0, bw)
                if mode == 'F':
                    for ch in range(n // 128):
                        pt, pt_r = pg[cnt['pg'] % 4]
                        cnt['pg'] += 1
                        for kc in range(8):
                            P.op('pe', lambda e, pt=pt, wt=wt, ht=ht, kc=kc, ch=ch, bw=bw: e.matmul(
                                pt[:, 0:bw], wt[:, kc, ch * 128:(ch + 1) * 128], ht[:, kc, 0:bw],
                                start=(kc == 0), stop=(kc == 7)), reads=[wt_r, ht_r], writes=[pt_r], inc=(kc == 7))
                        if ch % 4 == 0:
                            ot, ot_r = ob[cnt['ob'] % 2]
                            cnt['ob'] += 1
                        evac(ot[:, (ch % 4) * 512:(ch % 4) * 512 + bw], pt[:, 0:bw], [pt_r], [ot_r])
                        if ch % 4 == 3 or ch == n // 128 - 1:
                            cb0 = ch - (ch % 4) + (job[4] if len(job) > 4 else 0)
                            nchk = ch % 4 + 1
                            P.dma(None, dst[cb0:cb0 + nchk, :, t0:t0 + bw].rearrange("c p t -> p c t"),
                                  ot[:, 0:nchk * 512].rearrange("p (c t) -> p c t", c=nchk)[:, :, 0:bw],
                                  reads=[ot_r], writes=rs_of(dst_r, t0, bw))
                else:
                    dc0 = job[4]
                    for ti in range(bw // 128):
                        ot, ot_r = ob[cnt['ob'] % 2]
                        cnt['ob'] += 1
                        for sg in range(n // 512):
                            pt, pt_r = pg[cnt['pg'] % 4]
                            cnt['pg'] += 1
                            for kc in range(8):
                                P.op('pe', lambda e, pt=pt, wt=wt, ht=ht, kc=kc, sg=sg, ti=ti: e.matmul(
                                    pt[:, :], ht[:, kc, ti * 128:(ti + 1) * 128], wt[:, kc, sg * 512:(sg + 1) * 512],
                                    start=(kc == 0), stop=(kc == 7)), reads=[wt_r, ht_r], writes=[pt_r], inc=(kc == 7))
                            evac(ot[:, sg * 512:(sg + 1) * 512], pt[:, :], [pt_r], [ot_r])
                        r0 = t0 + ti * 128
                        P.dma(None, dst[r0:r0 + 128, dc0:dc0 + n], ot[:, 0:n], reads=[ot_r], writes=rs_of(dst_r, r0, 128))

        small, small_r = G['small'], G['small_r']
        for kc in range(8):
            st, st_r = stg[cnt['stg'] % 2]
            cnt['stg'] += 1
            P.dma(None, st[:, 0:32], w_in[l, kc * 128:(kc + 1) * 128, C_DT:C_DT + 32], writes=[st_r])
            P.dma(None, st[:, 32:48], w_in[l, kc * 128:(kc + 1) * 128, C_IG:C_IG + 16], writes=[st_r])
            P.op('pool', lambda e, st=st, kc=kc: e.tensor_copy(out=ws[:, kc, :], in_=st[:, 0:48]), reads=[st_r], writes=[ws_r])
        for (t0, bw) in TBLKS:
            ht, ht_r = load_h(t0, bw)
            for ti in range(bw // 128):
                pt, pt_r = pg[cnt['pg'] % 4]
                cnt['pg'] += 1
                ot, ot_r = obs[cnt['ob'] % 2]
                cnt['ob'] += 1
                for kc in range(8):
                    P.op('pe', lambda e, pt=pt, ht=ht, kc=kc, ti=ti: e.matmul(
                        pt[:, 0:48], ht[:, kc, ti * 128:(ti + 1) * 128], ws[:, kc, :],
                        start=(kc == 0), stop=(kc == 7)), reads=[ws_r, ht_r], writes=[pt_r], inc=(kc == 7))
                evac(ot[:, :], pt[:, 0:48], [pt_r], [ot_r])
                r0 = t0 + ti * 128
                P.dma(None, small[r0:r0 + 128, :], ot[:, :], reads=[ot_r], writes=rs_of(small_r, r0, 128))
        P.barrier()


def rope_tables():
    t = np.arange(TL)
    row = (t // 64).astype(np.float32)
    col = (t % 64).astype(np.float32)
    half = 32
    inv_freq = (10000.0 ** (-np.arange(0, half, 2, dtype=np.float32) / half)).astype(np.float32)
    ang_r = row[:, None] * inv_freq
    ang_c = col[:, None] * inv_freq
    cos = np.concatenate([np.cos(ang_r), np.cos(ang_r), np.cos(ang_c), np.cos(ang_c)], -1).astype(np.float32)
    sin = np.concatenate([np.sin(ang_r), np.sin(ang_r), np.sin(ang_c), np.sin(ang_c)], -1).astype(np.float32)
    sgn = np.tile(np.concatenate([-np.ones(16), np.ones(16)]), 2).astype(np.float32)
    return cos, sin * sgn


def bc(a):
    a = np.asarray(a, np.float32)
    return np.ascontiguousarray(np.broadcast_to(a[None], (128,) + a.shape))


def pmaj(v, nchunk):
    v = np.asarray(v, np.float32)
    lead = v.shape[:-1]
    r = v.reshape(lead + (nchunk, 128))
    r = np.moveaxis(r, -1, 0)
    return np.ascontiguousarray(r)


def make_inputs(inputs, bidx, layers, xin=None):
    L = len(layers)
    f = lambda k: np.asarray(inputs[k], np.float32)
    d = {}
    d['xin'] = np.ascontiguousarray(np.concatenate([f('ctx')[bidx], f('x')[bidx]], 0)) if xin is None else np.ascontiguousarray(xin, np.float32)
    cc = np.stack([f('c')[bidx], f('c_ctx')], -1)
    d['cc'] = np.ascontiguousarray(cc.reshape(8, 128, 2).transpose(1, 0, 2).reshape(128, 16))
    d['ident'] = np.eye(128, dtype=np.float32)
    t = np.arange(128)
    U = (t[:, None] <= t[None, :]).astype(np.float32)
    Lw = (t[:, None] >= t[None, :]).astype(np.float32)
    Sg = (t[:, None] > t[None, :]).astype(np.float32)
    Sl = (t[:, None] < t[None, :]).astype(np.float32)
    d['masks'] = np.ascontiguousarray(np.stack([U, Lw, Sg, Sl], 1))
    cos, sin = rope_tables()
    d['cosb'] = cos
    d['sinb'] = sin
    d['lami'] = bc(np.array([0.8 - 0.6 * math.exp(-0.3 * l) for l in layers], np.float32))
    d['w_mod'] = np.ascontiguousarray(f('w_mod')[layers])
    d['bmod'] = pmaj(f('b_mod')[layers], 24)
    d['ng'] = pmaj(f('norm_g')[layers], 8)
    d['w_in'] = np.ascontiguousarray(f('w_in')[layers])
    d['cw'] = np.ascontiguousarray(pmaj(f('ssd_conv_w')[layers], 12).transpose(0, 1, 3, 2))
    d['cb'] = pmaj(f('ssd_conv_b')[layers], 12)
    d['alog'] = bc(f('ssd_a_log')[layers].reshape(L, 32))
    d['dtb'] = bc(f('ssd_dt_bias')[layers].reshape(L, 32))
    d['dsk'] = bc(f('ssd_d')[layers])
    d['sng'] = bc(f('ssd_norm_g')[layers])
    d['gq'] = bc(f('diff_qn_g')[layers])
    d['gk'] = bc(f('diff_kn_g')[layers])
    d['lamp'] = bc(f('diff_lambda')[layers])
    d['subg'] = np.ascontiguousarray(np.broadcast_to(pmaj(f('diff_subln_g')[layers], 1), (128, L, 8)))
    d['mcw'] = np.ascontiguousarray(pmaj(f('ml_conv_w')[layers], 8).transpose(0, 1, 3, 2))
    d['mcb'] = pmaj(f('ml_conv_b')[layers], 8)
    d['ib'] = bc(f('ml_i_bias')[layers].reshape(L, 8))
    d['fb'] = bc(f('ml_f_bias')[layers].reshape(L, 8))
    d['mng'] = bc(f('ml_norm_g')[layers])
    sel = np.zeros((2, 2, 128), np.float32)
    sel[0, 0] = 1.0
    sel[1, 1] = 1.0
    d['sel'] = sel
    d['bgate'] = np.ascontiguousarray(np.broadcast_to(f('b_mod')[layers][None, :, 2 * D:3 * D], (2, L, D)))
    d['w_br'] = np.ascontiguousarray(f('w_branch')[layers])
    d['w_out'] = np.ascontiguousarray(f('w_out')[layers])
    return d


_NC_CACHE = {}


FUSED = True


def kernel(**inputs):
    nb = 4
    if FUSED:
        if DEPTH not in _NC_CACHE:
            _NC_CACHE[DEPTH] = build_program(DEPTH)
        nc = _NC_CACHE[DEPTH]
        in_maps = [make_inputs(inputs, bidx, list(range(DEPTH))) for bidx in range(nb)]
        res = run_bass_kernel_spmd(nc, in_maps, core_ids=list(range(nb)))
        return np.stack([np.asarray(r["xout"], np.float32)[TC:] for r in res.results], 0)
    if 1 not in _NC_CACHE:
        _NC_CACHE[1] = build_program(1)
    nc = _NC_CACHE[1]
    xcur = [None] * nb
    for l in range(DEPTH):
        in_maps = [make_inputs(inputs, bidx, [l], xin=xcur[bidx]) for bidx in range(nb)]
        res = run_bass_kernel_spmd(nc, in_maps, core_ids=list(range(nb)))
        xcur = [np.asarray(r["xout"], np.float32) for r in res.results]
    return np.stack([x[TC:] for x in xcur], 0)


def layer_consts(b, es, l, G):
    P = b.P
    O = Ops(P)
    C = {}

    def ld(name, src, shape, **kw):
        t, r = b.sb(es, "c_" + name, shape, F32)
        P.dma(None, t[:], src, writes=[r], **kw)
        C[name] = (t, r)
    ld('cw', G['cw'][:, l, :, :], [128, 12, 3])
    ld('cb', G['cb'][:, l, :], [128, 12])
    ld('alog', G['alog'][:, l, :], [128, 32])
    ld('dtb', G['dtb'][:, l, :], [128, 32])
    ld('dsk', G['dsk'][:, l, :], [128, 16])
    ld('sng', G['sng'][:, l, :], [128, D])
    ld('gq', G['gq'][:, l, :], [128, 64])
    ld('gk', G['gk'][:, l, :], [128, 64])
    ld('lamp', G['lamp'][:, l, :, :], [128, 4, 64])
    ld('subg', G['subg'][:, l, :], [128, 8])
    ld('mcw', G['mcw'][:, l, :, :], [128, 8, 3])
    ld('mcb', G['mcb'][:, l, :], [128, 8])
    ld('ib', G['ib'][:, l, :], [128, 8])
    ld('fb', G['fb'][:, l, :], [128, 8])
    ld('mng', G['mng'][:, l, :], [128, D])
    ld('lami', G['lami_in'][:, l:l + 1], [128, 1], allow_slow_non_contiguous=True)
    at, ar = C['alog']
    O.act(at[:], at[:], AF.Exp, [ar], [ar])
    O.ts('dve', at[:], at[:], -1.0, None, ALU.mult, None, [ar], [ar])
    return C


def softplus(b, O, out, x, tmp1, tmp2, R, W):
    O.act(tmp1, x, AF.Abs, R + W, W)
    O.act(tmp1, tmp1, AF.Exp, R + W, W, scale=-1.0)
    O.act(tmp2, tmp1, AF.Ln, R + W, W, bias=1.0)
    O.stt(out, x, 0.0, tmp2, ALU.max, ALU.add, R + W, W)


def seq_bounds(t0, bw):
    return (t0 == 0 or t0 == TC), (t0 + bw == TC or t0 + bw == NT)


def conv_silu(b, O, l, xT, xT_r, t0, bw, nchunk, cwt, cw_r, cbt, cb_r, xh, xh_r, tmps, uT, uT_r):
    P = O.P
    at_start, at_end = seq_bounds(t0, bw)
    lo = t0 if at_start else t0 - 1
    hi = t0 + bw if at_end else t0 + bw + 1
    if at_start:
        O.memset('pool', xh[:, :, 0:1], 0.0, [xh_r])
    if at_end:
        O.memset('pool', xh[:, :, bw + 1:bw + 2], 0.0, [xh_r])
    O.dma(xh[:, :, lo - (t0 - 1):hi - (t0 - 1)], xT[:, :, lo:hi].rearrange("c p t -> p c t"),
          rs_of(xT_r, lo, hi - lo), [xh_r])
    for ch in range(nchunk):
        tm, tm_r = tmps[ch % 2]
        O.ts('dve', tm[:, 0:bw], xh[:, ch, 1:bw + 1], cwt[:, ch, 1:2], None, ALU.mult, None, [xh_r, cw_r], [tm_r])
        O.stt(tm[:, 0:bw], xh[:, ch, 0:bw], cwt[:, ch, 0:1], tm[:, 0:bw], ALU.mult, ALU.add, [xh_r, cw_r, tm_r], [tm_r])
        O.stt(tm[:, 0:bw], xh[:, ch, 2:bw + 2], cwt[:, ch, 2:3], tm[:, 0:bw], ALU.mult, ALU.add, [xh_r, cw_r, tm_r], [tm_r])
        O.act(uT[:, ch, 0:bw], tm[:, 0:bw], AF.Silu, [tm_r, cb_r], [uT_r], bias=cbt[:, ch:ch + 1])


def decay_scalars(b, O, c, a_t, a_r, NA, masks, masks_r, onesf, onesf_r, pc, pc_r, PT, PT_r, E1, E2, E_r,
                  rs, de, cdall, cd_r):
    n2 = 2 * NA
    O.mm(pc[:, 0:n2], masks[:, 0, :], a_t[:, 0:n2], True, True, [masks_r, a_r], [pc_r], inc=False)
    O.mm(pc[:, n2:2 * n2], onesf[:, :], a_t[:, 0:n2], True, True, [onesf_r, a_r], [pc_r])
    O.copy('dve', PT[:, 0:2 * n2], pc[:, 0:2 * n2], [pc_r], [PT_r])
    Pf, Pb = PT[:, 0:NA], PT[:, NA:n2]
    Tf, Tb = PT[:, n2:n2 + NA], PT[:, n2 + NA:2 * n2]
    O.tt('dve', E2[:, NA:n2], Pb, a_t[:, NA:n2], ALU.subtract, [PT_r, a_r], [E_r])
    O.tt('dve', E1[:, NA:n2], Tb, E2[:, NA:n2], ALU.subtract, [PT_r, E_r], [E_r])
    O.tt('dve', E2[:, 0:NA], Tf, Pf, ALU.subtract, [PT_r], [E_r])
    O.copy('dve', E1[:, 0:NA], Pf, [PT_r], [E_r])
    O.act(rs, E1[:, 0:n2], AF.Exp, [E_r], [E_r])
    O.act(de, E2[:, 0:n2], AF.Exp, [E_r], [E_r])
    O.act(cdall[:, c, 0:n2], PT[:, n2:2 * n2], AF.Exp, [PT_r], [cd_r])


def block_scalars(b, O, G, t0, bw, xa, x_r, n2, NA, Abc, A_r, pc, pc_r, T, T_r, cdall, cd_r, sm_dram, sm_dram_r):
    masks, masks_r, onesf, onesf_r = G['masks'], G['masks_r'], G['onesf'], G['onesf_r']
    nt = bw // 128
    c0 = t0 // 128
    a = xa[:, 0:nt, 0, :]
    O.copy('dve', T[:, 0, 0:nt, :], a, [x_r], [T_r])
    O.mm(pc[:, 0:nt * n2], masks[:, 0, :], T[:, 0, 0:nt, :].rearrange("p t c -> p (t c)"), True, True, [masks_r, T_r], [pc_r], inc=False)
    O.mm(pc[:, 256:256 + nt * n2], onesf[:, :], T[:, 0, 0:nt, :].rearrange("p t c -> p (t c)"), True, True, [onesf_r, T_r], [pc_r])
    O.copy('dve', xa[:, 0:nt, 1, :], pc[:, 0:nt * n2].rearrange("p (t c) -> p t c", t=nt), [pc_r], [x_r])
    O.copy('dve', xa[:, 0:nt, 2, :], pc[:, 256:256 + nt * n2].rearrange("p (t c) -> p t c", t=nt), [pc_r], [x_r])
    Pm, Tm, E1, E2 = xa[:, 0:nt, 1, :], xa[:, 0:nt, 2, :], xa[:, 0:nt, 3, :], xa[:, 0:nt, 4, :]
    O.tt('dve', E2[:, :, NA:n2], Pm[:, :, NA:n2], a[:, :, NA:n2], ALU.subtract, [x_r], [x_r])
    O.tt('dve', E1[:, :, NA:n2], Tm[:, :, NA:n2], E2[:, :, NA:n2], ALU.subtract, [x_r], [x_r])
    O.tt('dve', E2[:, :, 0:NA], Tm[:, :, 0:NA], Pm[:, :, 0:NA], ALU.subtract, [x_r], [x_r])
    O.copy('dve', E1[:, :, 0:NA], Pm[:, :, 0:NA], [x_r], [x_r])
    O.act(xa[:, 0:nt, 5, :], E1, AF.Exp, [x_r], [x_r])
    O.act(xa[:, 0:nt, 6, :], E2, AF.Exp, [x_r], [x_r])
    O.act(cdall[:, c0:c0 + nt, 0:n2], Tm, AF.Exp, [x_r], [cd_r])
    O.dma(sm_dram[t0:t0 + bw, 0:n2].rearrange("(t p) c -> p t c", p=128), a, [x_r], rs_of(sm_dram_r, t0, bw))
    O.dma(sm_dram[t0:t0 + bw, n2:2 * n2].rearrange("(t p) c -> p t c", p=128), xa[:, 0:nt, 5, :], [x_r], rs_of(sm_dram_r, t0, bw))


def phase_ssd(b, l, G, C):
    P = b.P
    O = Ops(P)
    nc = b.nc
    S = G['scr']
    masks, masks_r, onesf, onesf_r = G['masks'], G['masks_r'], G['onesf'], G['onesf_r']
    identb, identb_r = G['identb'], G['identb_r']
    xbcT, xbcT_r = G['xbcT'], G['xbcT_r']
    small, small_r = G['small'], G['small_r']
    sBT, sBT_r = S('sBT', [2, 128, NT], BF16)
    sCT, sCT_r = S('sCT', [2, 128, NT], BF16)
    sVt, sVt_r = S('sVt', [2, NT, D], BF16)
    sxs, sxs_r = S('sxs', [NT, D], BF16)
    ssm, ssm_r = S('ssm', [NT, 64], F32)
    sprev, sprev_r = S('sprev', [2, NCH, 128, D], BF16, 2 * NCH)
    sLb, sLb_r = S('sLb', [NCH, 128, D], F32, NCH)
    cwt, cw_r = C['cw']
    cbt, cb_r = C['cb']
    At, A_r = C['alog']
    dtbt, dtb_r = C['dtb']

    with ExitStack() as es0:
        cdall, cd_r = b.sb(es0, "cdall", [128, NCH, 32], F32)
        with ExitStack() as es:
            xh, xh_r = b.sb(es, "xh", [128, 12, 514], BF16)
            tmps = [b.sb(es, "ctmp", [128, 512], F32) for _ in range(2)]
            uT, uT_r = b.sb(es, "uT", [128, 12, 512], BF16)
            pX, pX_r = b.ps(es, "pX", [128, D], BF16)
            pB, pB_r = b.ps(es, "pB", [128, 1024], BF16)
            pc, pc_r = b.ps(es, "pc", [128, 512], F32)
            pL, pL_r = b.ps(es, "pL", [128, 4, 512], F32)
            xsb = [b.sb(es, "xsb", [128, D], BF16) for _ in range(2)]
            Btm, Btm_r = b.sb(es, "Btm", [128, 256], BF16)
            sml, sml_r = b.sb(es, "sml", [128, 256], F32)
            smo = [b.sb(es, "smo", [128, 64], F32) for _ in range(2)]
            PT, PT_r = b.sb(es, "PT", [128, 128], F32)
            Et, E_r = b.sb(es, "Et", [128, 128], F32)
            Vt = [b.sb(es, "Vt", [128, 2, D], BF16) for _ in range(2)]
            Vw, Vw_r = b.sb(es, "Vw", [128, 2, D], BF16)
            xab = [b.sb(es, "xab", [128, 4, 8, 32], F32) for _ in range(2)]
            xwb = [b.sb(es, "xwb", [128, 4, 32], F32) for _ in range(2)]
            Tsc, Tsc_r = b.sb(es, "Tsc", [128, 1, 4, 32], F32)
            Sf, Sf_r = b.sb(es, "Sf", [128, D], F32)
            Stmp, Stmp_r = b.sb(es, "Stmp", [128, D], F32)
            Sfb = [b.sb(es, "Sfb", [128, D], BF16) for _ in range(2)]
            Lb = [b.sb(es, "Lb", [128, D], F32) for _ in range(2)]
            O.memset('dve', Sf[:], 0.0, [Sf_r])
            for (t0, bw) in TBLKS:
                conv_silu(b, O, l, xbcT, xbcT_r, t0, bw, 12, cwt, cw_r, cbt, cb_r, xh, xh_r, tmps, uT, uT_r)
                O.dma(sBT[:, :, t0:t0 + bw].rearrange("g p t -> p g t"), uT[:, 8:10, 0:bw], [uT_r], rs_of(sBT_r, t0, bw))
                O.dma(sCT[:, :, t0:t0 + bw].rearrange("g p t -> p g t"), uT[:, 10:12, 0:bw], [uT_r], rs_of(sCT_r, t0, bw))
                nt = bw // 128
                bslot = (t0 // 512) % 2
                xa, x_r = xab[bslot]
                xw, xw_r = xwb[bslot]
                O.dma(xa[:, 0:nt, 1, :], small[t0:t0 + bw, 0:32].rearrange("(t p) c -> p t c", p=128), rs_of(small_r, t0, bw), [x_r])
                O.tt('dve', xa[:, 0:nt, 2, :], xa[:, 0:nt, 1, :], bc_lead(dtbt[:], [128, nt, 32]), ALU.add, [x_r, dtb_r], [x_r])
                softplus(b, O, xa[:, 0:nt, 7, :], xa[:, 0:nt, 2, :], xa[:, 0:nt, 3, :], xa[:, 0:nt, 4, :], [], [x_r])
                O.tt('dve', xa[:, 0:nt, 0, :], xa[:, 0:nt, 7, :], bc_lead(At[:], [128, nt, 32]), ALU.mult, [x_r, A_r], [x_r])
                block_scalars(b, O, G, t0, bw, xa, x_r, 32, 16, At, A_r, pc, pc_r, Tsc, Tsc_r, cdall, cd_r, ssm, ssm_r)
                O.tt('dve', xw[:, 0:nt, :], xa[:, 0:nt, 7, :], xa[:, 0:nt, 6, :], ALU.mult, [x_r], [xw_r])
                for ti in range(bw // 128):
                    c = t0 // 128 + ti
                    r0 = c * 128
                    tsl = slice(ti * 128, (ti + 1) * 128)
                    for ch in range(8):
                        O.tr(pX[:, ch * 128:(ch + 1) * 128], uT[:, ch, tsl], identb[:], [uT_r, identb_r], [pX_r], inc=(ch == 7))
                    for g in range(2):
                        O.tr(pB[:, g * 128:(g + 1) * 128], uT[:, 8 + g, tsl], identb[:], [uT_r, identb_r], [pB_r], inc=(g == 1))
                    xs_t, xs_r = xsb[c % 2]
                    O.copy('act', xs_t[:], pX[:], [pX_r], [xs_r])
                    O.dma(sxs[r0:r0 + 128, :], xs_t[:], [xs_r], rs_of(sxs_r, r0, 128))
                    O.copy('act', Btm[:], pB[:, 0:256], [pB_r], [Btm_r])
                    dt = xa[:, ti, 7, :]
                    dew = xw[:, ti, :]
                    vt, vt_r = Vt[c % 2]
                    pXv = pX[:].rearrange("p (h e) -> p h e", h=16)
                    for d in range(2):
                        O.tt('dve', vt[:, d, :].rearrange("p (h e) -> p h e", h=16), pXv,
                             bc_mid(dt[:, d * 16:(d + 1) * 16], [128, 16, 64]), ALU.mult, [pX_r, x_r], [vt_r])
                        O.tt('dve', Vw[:, d, :].rearrange("p (h e) -> p h e", h=16), pXv,
                             bc_mid(dew[:, d * 16:(d + 1) * 16], [128, 16, 64]), ALU.mult, [pX_r, xw_r], [Vw_r])
                        O.dma(sVt[d, r0:r0 + 128, :], vt[:, d, :], [vt_r], rs_of(sVt_r, r0, 128))
                    for d in range(2):
                        for g in range(2):
                            O.mm(pL[:, d * 2 + g, :], Btm[:, g * 128:(g + 1) * 128], Vw[:, d, g * 512:(g + 1) * 512], True, True,
                                 [Btm_r, Vw_r], [pL_r], inc=(d == 1 and g == 1))
                    sb_t, sb_r = Sfb[c % 2]
                    O.copy('act', sb_t[:], Sf[:], [Sf_r], [sb_r])
                    O.dma(sprev[0, c, :, :], sb_t[:], [sb_r], [sprev_r[c]])
                    O.tt('dve', Stmp[:].rearrange("p (h e) -> p h e", h=16), Sf[:].rearrange("p (h e) -> p h e", h=16),
                         bc_mid(cdall[:, c, 0:16], [128, 16, 64]), ALU.mult, [Sf_r, cd_r], [Stmp_r])
                    O.tt('dve', Sf[:].rearrange("p (g n) -> p g n", g=2), Stmp[:].rearrange("p (g n) -> p g n", g=2),
                         pL[:, 0:2, :], ALU.add, [Stmp_r, pL_r], [Sf_r])
                    lb_t, lb_r = Lb[c % 2]
                    O.copy('act', lb_t[:].rearrange("p (g n) -> p g n", g=2), pL[:, 2:4, :], [pL_r], [lb_r])
                    O.dma(sLb[c, :, :], lb_t[:], [lb_r], [sLb_r[c]])
            O.memset('dve', Sf[:], 0.0, [Sf_r])
            for c in BWD_ORDER:
                lb_t, lb_r = Lb[c % 2]
                sb_t, sb_r = Sfb[c % 2]
                O.dma(lb_t[:], sLb[c, :, :], [sLb_r[c]], [lb_r])
                O.copy('act', sb_t[:], Sf[:], [Sf_r], [sb_r])
                O.dma(sprev[1, c, :, :], sb_t[:], [sb_r], [sprev_r[NCH + c]])
                O.tt('dve', Stmp[:].rearrange("p (h e) -> p h e", h=16), Sf[:].rearrange("p (h e) -> p h e", h=16),
                     bc_mid(cdall[:, c, 16:32], [128, 16, 64]), ALU.mult, [Sf_r, cd_r], [Stmp_r])
                O.tt('dve', Sf[:], Stmp[:], lb_t[:], ALU.add, [Stmp_r, lb_r], [Sf_r])
            P.barrier()

        zs, zs_r = G['zs'], G['zs_r']
        dskt, dsk_r = C['dsk']
        sngt, sng_r = C['sng']
        yT, yT_r = G['yT'], G['yT_r']
        with ExitStack() as es:
            xs2 = [b.sb(es, "xs2", [128, D], BF16) for _ in range(2)]
            z2 = [b.sb(es, "z2", [128, D], BF16) for _ in range(2)]
            yacc, yacc_r = b.sb(es, "yacc", [128, D], F32)
            szt, sz_r = b.sb(es, "szt", [128, D], F32)
            ssq, ssq_r = b.sb(es, "ssq", [128, 2], F32)
            ybf, ybf_r = b.sb(es, "ybf", [128, D], BF16)
            yTs = [b.sb(es, "yTs", [128, 8, 128], BF16) for _ in range(2)]
            pYT, pYT_r = b.ps(es, "pYT", [128, D], BF16)

            def load_extra(c):
                r0 = c * 128
                x_t, x_r = xs2[c % 2]
                z_t, z_r = z2[c % 2]
                O.dma(x_t[:], sxs[r0:r0 + 128, :], rs_of(sxs_r, r0, 128), [x_r])
                O.dma(z_t[:], zs[r0:r0 + 128, :], rs_of(zs_r, r0, 128), [z_r])

            def finish(c, Yf, Yf_r, Yb, Yb_r):
                r0 = c * 128
                x_t, x_r = xs2[c % 2]
                z_t, z_r = z2[c % 2]
                O.tt('pool', yacc[:], Yf[:, 0:D], Yb[:, 0:D], ALU.add, [Yf_r, Yb_r], [yacc_r])
                O.tt('dve', szt[:].rearrange("p (h e) -> p h e", h=16), x_t[:].rearrange("p (h e) -> p h e", h=16),
                     bc_mid(dskt[:, :], [128, 16, 64]), ALU.mult, [x_r, dsk_r], [sz_r])
                O.tt('dve', yacc[:], yacc[:], szt[:], ALU.add, [yacc_r, sz_r], [yacc_r])
                O.act(szt[:], z_t[:], AF.Silu, [z_r, sz_r], [sz_r])
                O.tt('dve', yacc[:], yacc[:], szt[:], ALU.mult, [yacc_r, sz_r], [yacc_r])
                O.act(szt[:], yacc[:], AF.Square, [yacc_r, sz_r], [sz_r, ssq_r], accum=ssq[:, 0:1])
                O.act(ssq[:, 1:2], ssq[:, 0:1], AF.Sqrt, [ssq_r], [ssq_r], scale=1.0 / D, bias=EPS)
                O.recip(ssq[:, 1:2], ssq[:, 1:2], [ssq_r], [ssq_r])
                O.stt(ybf[:], yacc[:], ssq[:, 1:2], sngt[:], ALU.mult, ALU.mult, [yacc_r, ssq_r, sng_r], [ybf_r])
                yt, yt_r = yTs[c % 2]
                for k in range(8):
                    O.tr(pYT[:, k * 128:(k + 1) * 128], ybf[:, k * 128:(k + 1) * 128], identb[:], [ybf_r, identb_r], [pYT_r], inc=(k == 7))
                O.copy('act', yt[:].rearrange("p k t -> p (k t)"), pYT[:], [pYT_r], [yt_r])
                O.dma(yT[0, :, :, r0:r0 + 128].rearrange("k p t -> p k t"), yt[:], [yt_r], rs_of(yT_r, r0, 128))

            la_pass2(b, O, G, es, NG=2, NSUB=8, PV=64, QT=sCT, QT_r=sCT_r, KT=sBT, KT_r=sBT_r, g0=0,
                     Vt=sVt, Vt_r=sVt_r, vcol0=0, sm=ssm, sm_r=ssm_r, NA=16, a0=0, sprev=sprev, sprev_r=sprev_r,
                     load_extra=load_extra, finish=finish)
            P.barrier()


def la_pass2(b, O, G, es, NG, NSUB, PV, QT, QT_r, KT, KT_r, g0, Vt, Vt_r, vcol0, sm, sm_r, NA, a0, sprev, sprev_r,
             load_extra, finish):
    P = O.P
    masks, masks_r = G['masks'], G['masks_r']
    GW = NSUB * PV
    NH = NG * NSUB
    W = NG * GW
    qt = [b.sb(es, "qt", [128, NG, 128], BF16) for _ in range(2)]
    kt = [b.sb(es, "kt", [128, NG, 128], BF16) for _ in range(2)]
    vt = [b.sb(es, "vt2", [128, 2, W], BF16) for _ in range(2)]
    smt = [b.sb(es, "smt", [128, 4 * NA], F32) for _ in range(2)]
    spt = [b.sb(es, "spt", [128, 2, W], BF16) for _ in range(2)]
    Gm, Gm_r = b.sb(es, "Gm", [128, 2, NG, 128], F32)
    segL = [b.sb(es, "segL", [128, 4, 128], F32) for _ in range(2)]
    Ee = [b.sb(es, "Ee", [128, 4, 128], F32) for _ in range(2)]
    MT = [b.sb(es, "MT", [128, 4, 128], BF16) for _ in range(2)]
    ytmp, ytmp_r = b.sb(es, "ytmp", [128, NG, GW], F32)
    Yd = [b.sb(es, "Yd", [128, NG, GW], F32) for _ in range(2)]
    pG, pG_r = b.ps(es, "pG", [128, 512], F32)
    pS = [b.ps(es, "pS", [128, 4, 128], F32) for _ in range(2)]
    ydg, ydg_r = b.ps(es, "ydg", [128, NG, 512], F32)
    yof, yof_r = b.ps(es, "yof", [128, NG, 512], F32)
    nseg = 0
    for c in range(NCH):
        r0 = c * 128
        q_t, q_r = qt[c % 2]
        k_t, k_r = kt[c % 2]
        v_t, v_r = vt[c % 2]
        s_t, s_r = smt[c % 2]
        p_t, p_r = spt[c % 2]
        O.dma(q_t[:], QT[g0:g0 + NG, :, r0:r0 + 128].rearrange("g p t -> p g t"), rs_of(QT_r, r0, 128), [q_r])
        O.dma(k_t[:], KT[g0:g0 + NG, :, r0:r0 + 128].rearrange("g p t -> p g t"), rs_of(KT_r, r0, 128), [k_r])
        for d in range(2):
            O.dma(v_t[:, d, :], Vt[d, r0:r0 + 128, vcol0:vcol0 + W], rs_of(Vt_r, r0, 128), [v_r])
            O.dma(p_t[:, d, :], sprev[d, c, :, vcol0:vcol0 + W], [sprev_r[d * NCH + c]], [p_r])
        O.dma(s_t[:], sm[r0:r0 + 128, :], rs_of(sm_r, r0, 128), [s_r])
        load_extra(c)
        for g in range(NG):
            O.mm(pG[:, g * 128:(g + 1) * 128], k_t[:, g, :], q_t[:, g, :], True, True, [k_r, q_r], [pG_r], inc=(g == NG - 1))
        for d in range(2):
            O.tt('dve', Gm[:, d, :, :], pG[:, 0:NG * 128].rearrange("p (g l) -> p g l", g=NG),
                 bc_lead(masks[:, d, :], [128, NG, 128]), ALU.mult, [pG_r, masks_r], [Gm_r])
        rounds = [(d, h0, min(4, NH - h0)) for d in range(2) for h0 in range(0, NH, 4)]
        first_in_bank = {0: [True] * NG, 1: [True] * NG}

        def stage_a(ri, slot):
            d, h0, nh = rounds[ri]
            sl_t, sl_r = segL[slot % 2]
            ps_t, ps_r = pS[slot % 2]
            col = d * NA + a0 + h0
            O.tt('dve', sl_t[:, 0:nh, :], bc_lead(masks[:, 2 + d, :], [128, nh, 128]),
                 bc_mid(s_t[:, col:col + nh], [128, nh, 128]), ALU.mult, [masks_r, s_r], [sl_r])
            for i in range(nh):
                O.mm(ps_t[:, i, :], sl_t[:, i, :], masks[:, d, :], True, True, [sl_r, masks_r], [ps_r], inc=(i == nh - 1))

        def stage_b(ri, slot):
            d, h0, nh = rounds[ri]
            y_t, y_r = Yd[d]
            e_t, e_r = Ee[slot % 2]
            m_t, m_r = MT[slot % 2]
            ps_t, ps_r = pS[slot % 2]
            O.act(e_t[:, 0:nh, :], ps_t[:, 0:nh, :], AF.Exp, [ps_r], [e_r])
            g = h0 // NSUB
            if NSUB >= 4:
                O.tt('dve', m_t[:, 0:nh, :], e_t[:, 0:nh, :], bc_lead(Gm[:, d, g, :], [128, nh, 128]), ALU.mult, [e_r, Gm_r], [m_r])
            else:
                for i in range(nh):
                    gi = (h0 + i) // NSUB
                    O.tt('dve', m_t[:, i, :], e_t[:, i, :], Gm[:, d, gi, :], ALU.mult, [e_r, Gm_r], [m_r])
            for i in range(nh):
                h = h0 + i
                gi, sub = h // NSUB, h % NSUB
                O.mm(ydg[:, gi, sub * PV:(sub + 1) * PV], m_t[:, i, :], v_t[:, d, h * PV:(h + 1) * PV],
                     first_in_bank[d][gi], True, [m_r, v_r], [ydg_r], inc=(i == nh - 1))
                first_in_bank[d][gi] = False
            if h0 + nh >= NH:
                for g2 in range(NG):
                    O.mm(yof[:, g2, 0:GW], q_t[:, g2, :], p_t[:, d, g2 * GW:(g2 + 1) * GW], True, True, [q_r, p_r], [yof_r], inc=(g2 == NG - 1))
                rsc = s_t[:, 2 * NA + d * NA + a0:2 * NA + d * NA + a0 + NH]
                O.tt('dve', ytmp[:].rearrange("p g (s e) -> p (g s) e", s=NSUB), yof_view(yof, NG, NSUB, PV),
                     bc_mid(rsc, [128, NH, PV]), ALU.mult, [yof_r, s_r], [ytmp_r])
                O.tt('dve', y_t[:], ytmp[:], ydg[:, :, 0:GW], ALU.add, [ytmp_r, ydg_r], [y_r])

        stage_a(0, nseg)
        for ri in range(len(rounds)):
            if ri + 1 < len(rounds):
                stage_a(ri + 1, nseg + 1)
            stage_b(ri, nseg)
            nseg += 1
        finish(c, Yd[0][0][:].rearrange("p g w -> p (g w)"), Yd[0][1], Yd[1][0][:].rearrange("p g w -> p (g w)"), Yd[1][1])


def yof_view(yof, NG, NSUB, PV):
    if NSUB * PV == 512:
        return yof[:].rearrange("p g (s e) -> p (g s) e", s=NSUB)
    assert NSUB == 1
    return yof[:, :, 0:PV]


MLW = 4 * 257


def phase_mlstm(b, l, G, C):
    P = b.P
    O = Ops(P)
    S = G['scr']
    masks, masks_r, onesf, onesf_r = G['masks'], G['masks_r'], G['onesf'], G['onesf_r']
    identb, identb_r = G['identb'], G['identb_r']
    mqkT, mqkT_r = G['mqkT'], G['mqkT_r']
    small, small_r = G['small'], G['small_r']
    mvo, mvo_r = G['mvo'], G['mvo_r']
    mz, mz_r = G['mz'], G['mz_r']
    mQT, mQT_r = S('mQT', [4, 128, NT], BF16)
    mKT, mKT_r = S('mKT', [4, 128, NT], BF16)
    mVt, mVt_r = S('mVt', [2, NT, MLW], BF16)
    msm, msm_r = S('msm', [NT, 16], F32)
    mprev, mprev_r = S('mprev', [2, NCH, 128, MLW], BF16, 2 * NCH)
    mLb, mLb_r = S('mLb', [NCH, 128, MLW], F32, NCH)
    mh, mh_r = S('mh', [NT, 512], F32)
    cwt, cw_r = C['mcw']
    cbt, cb_r = C['mcb']
    ibt, ib_r = C['ib']
    fbt, fb_r = C['fb']
    LNS = math.log(128.0 ** -0.5)

    with ExitStack() as es0:
        cdall, cd_r = b.sb(es0, "mcdall", [128, NCH, 8], F32)
        with ExitStack() as es:
            xh, xh_r = b.sb(es, "mxh", [128, 8, 514], BF16)
            tmps = [b.sb(es, "mctmp", [128, 512], F32) for _ in range(2)]
            uT, uT_r = b.sb(es, "muT", [128, 8, 512], BF16)
            pK, pK_r = b.ps(es, "pK", [128, 1024], BF16)
            pc, pc_r = b.ps(es, "mpc", [128, 512], F32)
            pL, pL_r = b.ps(es, "mpL", [128, 4, 512], F32)
            Ktm, Ktm_r = b.sb(es, "Ktm", [128, 512], BF16)
            vin = [b.sb(es, "vin", [128, D], BF16) for _ in range(2)]
            sml, sml_r = b.sb(es, "msml", [128, 128], F32)
            smo = [b.sb(es, "msmo", [128, 16], F32) for _ in range(2)]
            PT, PT_r = b.sb(es, "mPT", [128, 32], F32)
            Et, E_r = b.sb(es, "mEt", [128, 32], F32)
            Vt = [b.sb(es, "mVt", [128, 2, MLW], BF16) for _ in range(2)]
            Vw, Vw_r = b.sb(es, "mVw", [128, 2, MLW], BF16)
            xab = [b.sb(es, "mxab", [128, 4, 8, 8], F32) for _ in range(2)]
            xwb = [b.sb(es, "mxwb", [128, 4, 8], F32) for _ in range(2)]
            swbb = [b.sb(es, "mswb", [128, 4, 8], F32) for _ in range(2)]
            Tsc, Tsc_r = b.sb(es, "mTsc", [128, 1, 4, 8], F32)
            Sf, Sf_r = b.sb(es, "mSf", [128, MLW], F32)
            Stmp, Stmp_r = b.sb(es, "mStmp", [128, MLW], F32)
            Sfb = [b.sb(es, "mSfb", [128, MLW], BF16) for _ in range(2)]
            Lb = [b.sb(es, "mLb", [128, MLW], F32) for _ in range(2)]
            O.memset('dve', Sf[:], 0.0, [Sf_r])
            for (t0, bw) in TBLKS:
                conv_silu(b, O, l, mqkT, mqkT_r, t0, bw, 8, cwt, cw_r, cbt, cb_r, xh, xh_r, tmps, uT, uT_r)
                O.dma(mQT[:, :, t0:t0 + bw].rearrange("g p t -> p g t"), uT[:, 0:4, 0:bw], [uT_r], rs_of(mQT_r, t0, bw))
                O.dma(mKT[:, :, t0:t0 + bw].rearrange("g p t -> p g t"), uT[:, 4:8, 0:bw], [uT_r], rs_of(mKT_r, t0, bw))
                nt = bw // 128
                bslot = (t0 // 512) % 2
                xa, x_r = xab[bslot]
                xw, xw_r = xwb[bslot]
                swb, sw_r = swbb[bslot]
                O.dma(xa[:, 0:nt, 1:3, :], small[t0:t0 + bw, 32:48].rearrange("(t p) (s c) -> p t s c", p=128, s=2), rs_of(small_r, t0, bw), [x_r])
                O.tt('dve', swb[:, 0:nt, :], xa[:, 0:nt, 1, :], bc_lead(ibt[:], [128, nt, 8]), ALU.add, [x_r, ib_r], [sw_r])
                O.act(swb[:, 0:nt, :], swb[:, 0:nt, :], AF.Exp, [sw_r], [sw_r])
                O.ts('dve', swb[:, 0:nt, :], swb[:, 0:nt, :], 128.0 ** -0.5, None, ALU.mult, None, [sw_r], [sw_r])
                O.tt('dve', xa[:, 0:nt, 3, :], xa[:, 0:nt, 2, :], bc_lead(fbt[:], [128, nt, 8]), ALU.add, [x_r, fb_r], [x_r])
                O.ts('dve', xa[:, 0:nt, 3, :], xa[:, 0:nt, 3, :], -1.0, None, ALU.mult, None, [x_r], [x_r])
                softplus(b, O, xa[:, 0:nt, 7, :], xa[:, 0:nt, 3, :], xa[:, 0:nt, 4, :], xa[:, 0:nt, 5, :], [], [x_r])
                O.ts('dve', xa[:, 0:nt, 0, :], xa[:, 0:nt, 7, :], -1.0, None, ALU.mult, None, [x_r], [x_r])
                block_scalars(b, O, G, t0, bw, xa, x_r, 8, 4, None, None, pc, pc_r, Tsc, Tsc_r, cdall, cd_r, msm, msm_r)
                O.tt('dve', xw[:, 0:nt, :], swb[:, 0:nt, :], xa[:, 0:nt, 6, :], ALU.mult, [x_r, sw_r], [xw_r])
                for ti in range(bw // 128):
                    c = t0 // 128 + ti
                    r0 = c * 128
                    tsl = slice(ti * 128, (ti + 1) * 128)
                    for h in range(4):
                        O.tr(pK[:, h * 128:(h + 1) * 128], uT[:, 4 + h, tsl], identb[:], [uT_r, identb_r], [pK_r], inc=(h == 3))
                    O.copy('act', Ktm[:], pK[:, 0:512], [pK_r], [Ktm_r])
                    v_t, v_r = vin[c % 2]
                    O.dma(v_t[:], mvo[r0:r0 + 128, 0:D], rs_of(mvo_r, r0, 128), [v_r])
                    vt, vt_r = Vt[c % 2]
                    vv = v_t[:].rearrange("p (h e) -> p h e", h=4)
                    for d in range(2):
                        sw = swb[:, ti, d * 4:(d + 1) * 4]
                        ww = xw[:, ti, d * 4:(d + 1) * 4]
                        vo = vt[:, d, :].rearrange("p (h e) -> p h e", h=4)
                        wo = Vw[:, d, :].rearrange("p (h e) -> p h e", h=4)
                        O.tt('dve', vo[:, :, 0:256], vv, bc_mid(sw, [128, 4, 256]), ALU.mult, [v_r, sw_r], [vt_r])
                        O.copy('dve', vo[:, :, 256], sw, [sw_r], [vt_r])
                        O.tt('dve', wo[:, :, 0:256], vv, bc_mid(ww, [128, 4, 256]), ALU.mult, [v_r, xw_r], [Vw_r])
                        O.copy('dve', wo[:, :, 256], ww, [xw_r], [Vw_r])
                        O.dma(mVt[d, r0:r0 + 128, :], vt[:, d, :], [vt_r], rs_of(mVt_r, r0, 128))
                    sb_t, sb_r = Sfb[c % 2]
                    lb_t, lb_r = Lb[c % 2]
                    for d in range(2):
                        for h in range(4):
                            O.mm(pL[:, h, 0:257], Ktm[:, h * 128:(h + 1) * 128], Vw[:, d, h * 257:(h + 1) * 257], True, True,
                                 [Ktm_r, Vw_r], [pL_r], inc=(h == 3))
                        if d == 0:
                            O.copy('act', sb_t[:], Sf[:], [Sf_r], [sb_r])
                            O.dma(mprev[0, c, :, :], sb_t[:], [sb_r], [mprev_r[c]])
                            O.tt('dve', Stmp[:].rearrange("p (h e) -> p h e", h=4), Sf[:].rearrange("p (h e) -> p h e", h=4),
                                 bc_mid(cdall[:, c, 0:4], [128, 4, 257]), ALU.mult, [Sf_r, cd_r], [Stmp_r])
                            O.tt('dve', Sf[:].rearrange("p (h e) -> p h e", h=4), Stmp[:].rearrange("p (h e) -> p h e", h=4),
                                 pL[:, :, 0:257], ALU.add, [Stmp_r, pL_r], [Sf_r])
                        else:
                            O.copy('act', lb_t[:].rearrange("p (h e) -> p h e", h=4), pL[:, :, 0:257], [pL_r], [lb_r])
                            O.dma(mLb[c, :, :], lb_t[:], [lb_r], [mLb_r[c]])
            O.memset('dve', Sf[:], 0.0, [Sf_r])
            for c in BWD_ORDER:
                lb_t, lb_r = Lb[c % 2]
                sb_t, sb_r = Sfb[c % 2]
                O.dma(lb_t[:], mLb[c, :, :], [mLb_r[c]], [lb_r])
                O.copy('act', sb_t[:], Sf[:], [Sf_r], [sb_r])
                O.dma(mprev[1, c, :, :], sb_t[:], [sb_r], [mprev_r[NCH + c]])
                O.tt('dve', Stmp[:].rearrange("p (h e) -> p h e", h=4), Sf[:].rearrange("p (h e) -> p h e", h=4),
                     bc_mid(cdall[:, c, 4:8], [128, 4, 257]), ALU.mult, [Sf_r, cd_r], [Stmp_r])
                O.tt('dve', Sf[:], Stmp[:], lb_t[:], ALU.add, [Stmp_r, lb_r], [Sf_r])
            P.barrier()

        mngt, mng_r = C['mng']
        yT, yT_r = G['yT'], G['yT_r']
        for pair in range(2):
            with ExitStack() as es:
                dn, dn_r = b.sb(es, "dn", [128, 8], F32)
                hp, hp_r = b.sb(es, "hp", [128, 2, 256], F32)
                hq, hq_r = b.sb(es, "hq", [128, 2, 256], F32)
                hall = [b.sb(es, "hall", [128, D], F32) for _ in range(2)]
                o2 = [b.sb(es, "o2", [128, D], BF16) for _ in range(2)]
                z2 = [b.sb(es, "mz2", [128, D], BF16) for _ in range(2)]
                sg, sg_r = b.sb(es, "sg", [128, D], F32)
                st4, st4_r = b.sb(es, "st4", [128, 8], F32)
                ybf, ybf_r = b.sb(es, "mybf", [128, D], BF16)
                yTs = [b.sb(es, "myTs", [128, 8, 128], BF16) for _ in range(2)]
                pYT, pYT_r = b.ps(es, "mpYT", [128, D], BF16)

                def load_extra(c, pair=pair):
                    if pair == 0:
                        return
                    r0 = c * 128
                    h_t, h_r = hall[c % 2]
                    o_t, o_r = o2[c % 2]
                    z_t, z_r = z2[c % 2]
                    O.dma(h_t[:, 0:512], mh[r0:r0 + 128, :], rs_of(mh_r, r0, 128), [h_r])
                    O.dma(o_t[:], mvo[r0:r0 + 128, D:2 * D], rs_of(mvo_r, r0, 128), [o_r])
                    O.dma(z_t[:], mz[r0:r0 + 128, :], rs_of(mz_r, r0, 128), [z_r])

                def finish(c, Yf, Yf_r, Yb, Yb_r, pair=pair):
                    r0 = c * 128
                    h_t, h_r = hall[c % 2]
                    Yfv = Yf.rearrange("p (g w) -> p g w", g=2)
                    Ybv = Yb.rearrange("p (g w) -> p g w", g=2)
                    O.act(dn[:, 0:2], Yfv[:, :, 256], AF.Abs, [Yf_r], [dn_r])
                    O.act(dn[:, 2:4], Ybv[:, :, 256], AF.Abs, [Yb_r], [dn_r])
                    O.ts('dve', dn[:, 0:4], dn[:, 0:4], 1.0, None, ALU.max, None, [dn_r], [dn_r])
                    O.recip(dn[:, 4:8], dn[:, 0:4], [dn_r], [dn_r])
                    O.tt('dve', hp[:], Yfv[:, :, 0:256], bc_mid(dn[:, 4:6], [128, 2, 256]), ALU.mult, [Yf_r, dn_r], [hp_r])
                    O.tt('dve', hq[:], Ybv[:, :, 0:256], bc_mid(dn[:, 6:8], [128, 2, 256]), ALU.mult, [Yb_r, dn_r], [hq_r])
                    if pair == 0:
                        O.tt('pool', hp[:], hp[:], hq[:], ALU.add, [hp_r, hq_r], [hp_r])
                        O.dma(mh[r0:r0 + 128, :], hp[:].rearrange("p g e -> p (g e)"), [hp_r], rs_of(mh_r, r0, 128))
                        return
                    o_t, o_r = o2[c % 2]
                    z_t, z_r = z2[c % 2]
                    O.tt('pool', h_t[:, 512:1024], hp[:].rearrange("p g e -> p (g e)"), hq[:].rearrange("p g e -> p (g e)"),
                         ALU.add, [hp_r, hq_r], [h_r])
                    O.act(sg[:], o_t[:], AF.Sigmoid, [o_r], [sg_r])
                    O.tt('dve', h_t[:], h_t[:], sg[:], ALU.mult, [h_r, sg_r], [h_r])
                    O.tt('dve', sg[:], h_t[:], h_t[:], ALU.mult, [h_r, sg_r], [sg_r])
                    O.reduce(st4[:, 0:4], sg[:].rearrange("p (h e) -> p h e", h=4), ALU.add, [sg_r], [st4_r])
                    O.act(st4[:, 4:8], st4[:, 0:4], AF.Sqrt, [st4_r], [st4_r], scale=1.0 / 256, bias=EPS)
                    O.recip(st4[:, 4:8], st4[:, 4:8], [st4_r], [st4_r])
                    O.tt('dve', h_t[:].rearrange("p (h e) -> p h e", h=4), h_t[:].rearrange("p (h e) -> p h e", h=4),
                         bc_mid(st4[:, 4:8], [128, 4, 256]), ALU.mult, [h_r, st4_r], [h_r])
                    O.tt('dve', h_t[:], h_t[:], mngt[:], ALU.mult, [h_r, mng_r], [h_r])
                    O.act(sg[:], z_t[:], AF.Silu, [z_r, sg_r], [sg_r])
                    O.tt('dve', ybf[:], h_t[:], sg[:], ALU.mult, [h_r, sg_r], [ybf_r])
                    yt, yt_r = yTs[c % 2]
                    for k in range(8):
                        O.tr(pYT[:, k * 128:(k + 1) * 128], ybf[:, k * 128:(k + 1) * 128], identb[:], [ybf_r, identb_r], [pYT_r], inc=(k == 7))
                    O.copy('act', yt[:].rearrange("p k t -> p (k t)"), pYT[:], [pYT_r], [yt_r])
                    O.dma(yT[2, :, :, r0:r0 + 128].rearrange("k p t -> p k t"), yt[:], [yt_r], rs_of(yT_r, r0, 128))

                la_pass2(b, O, G, es, NG=2, NSUB=1, PV=257, QT=mQT, QT_r=mQT_r, KT=mKT, KT_r=mKT_r, g0=2 * pair,
                         Vt=mVt, Vt_r=mVt_r, vcol0=2 * pair * 257, sm=msm, sm_r=msm_r, NA=4, a0=2 * pair,
                         sprev=mprev, sprev_r=mprev_r, load_extra=load_extra, finish=finish)
                P.barrier()


def phase_attn(b, l, G, C):
    P = b.P
    O = Ops(P)
    S = G['scr']
    identb, identb_r = G['identb'], G['identb_r']
    onesb, onesb_r = G['onesb'], G['onesb_r']
    aqk, aqk_r = G['aqk'], G['aqk_r']
    avz, avz_r = G['avz'], G['avz_r']
    azT, azT_r = G['azT'], G['azT_r']
    yT, yT_r = G['yT'], G['yT_r']
    cos_in, sin_in = G['cos_in'], G['sin_in']
    aQT, aQT_r = S('aQT', [8, 128, NT], BF16)
    aKT, aKT_r = S('aKT', [8, 128, NT], BF16)
    gqt, gq_r = C['gq']
    gkt, gk_r = C['gk']
    lampt, lamp_r = C['lamp']
    subgt, subg_r = C['subg']
    lamit, lami_r = C['lami']

    with ExitStack() as es0:
        sc, sc_r = b.sb(es0, "asc", [128, 16], F32)
        ggt, gg_r = b.sb(es0, "ggt", [128, 2, 64], F32)
        tmp64, tmp64_r = b.sb(es0, "tmp64", [128, 64], F32)
        O.ts('dve', ggt[:, 0, :], gqt[:], 0.125, None, ALU.mult, None, [gq_r], [gg_r])
        O.copy('dve', ggt[:, 1, :], gkt[:], [gk_r], [gg_r])
        O.reduce(sc[:, 4:6], ggt[:], ALU.max, [gg_r], [sc_r], absval=True)
        O.tt('dve', sc[:, 6:7], sc[:, 4:5], sc[:, 5:6], ALU.mult, [sc_r], [sc_r])
        O.ts('dve', sc[:, 0:1], sc[:, 6:7], -64.0, None, ALU.mult, None, [sc_r], [sc_r])
        for i in range(2):
            O.tt('dve', tmp64[:], lampt[:, 2 * i, :], lampt[:, 2 * i + 1, :], ALU.mult, [lamp_r], [tmp64_r])
            O.reduce(sc[:, 7 + i:8 + i], tmp64[:], ALU.add, [tmp64_r], [sc_r])
        O.act(sc[:, 7:9], sc[:, 7:9], AF.Exp, [sc_r], [sc_r])
        O.tt('dve', sc[:, 9:10], sc[:, 7:8], sc[:, 8:9], ALU.subtract, [sc_r], [sc_r])
        O.tt('dve', sc[:, 1:2], sc[:, 9:10], lamit[:, 0:1], ALU.add, [sc_r, lami_r], [sc_r])
        O.ts('dve', sc[:, 2:3], sc[:, 1:2], -1.0, None, ALU.mult, None, [sc_r], [sc_r])
        O.ts('dve', sc[:, 10:11], lamit[:, 0:1], -1.0, 1.0, ALU.mult, ALU.add, [lami_r], [sc_r])
        O.tt('dve', sc[:, 3:4], sc[:, 10:11], subgt[:, 0:1], ALU.mult, [sc_r, subg_r], [sc_r])

        with ExitStack() as es:
            qk = [b.sb(es, "qkraw", [128, 2048], BF16) for _ in range(2)]
            cs = [b.sb(es, "cs", [128, 2, 64], F32) for _ in range(2)]
            sq, sq_r = b.sb(es, "sq", [128, 2048], F32)
            xn, xn_r = b.sb(es, "xn", [128, 2048], F32)
            t2, t2_r = b.sb(es, "t2", [128, 2048], F32)
            st, st_r = b.sb(es, "ast", [128, 64], F32)
            xr, xr_r = b.sb(es, "xr", [128, 2048], BF16)
            pQ = [b.ps(es, "pQ", [128, 1024], BF16) for _ in range(2)]
            qTt = [b.sb(es, "qTt", [128, 2, 8, 128], BF16) for _ in range(2)]
            for i in range(NCH):
                r0 = i * 128
                q_t, q_r = qk[i % 2]
                O.dma(q_t[:], aqk[r0:r0 + 128, :], rs_of(aqk_r, r0, 128), [q_r])
                O.tt('dve', sq[:], q_t[:], q_t[:], ALU.mult, [q_r], [sq_r])
                O.reduce(st[:, 0:32], sq[:].rearrange("p (g e) -> p g e", e=64), ALU.add, [sq_r], [st_r])
                O.act(st[:, 32:64], st[:, 0:32], AF.Sqrt, [st_r], [st_r], scale=1.0 / 64, bias=EPS)
                O.recip(st[:, 32:64], st[:, 32:64], [st_r], [st_r])
                O.tt('dve', xn[:].rearrange("p (g e) -> p g e", e=64), q_t[:].rearrange("p (g e) -> p g e", e=64),
                     bc_mid(st[:, 32:64], [128, 32, 64]), ALU.mult, [q_r, st_r], [xn_r])
                for hf in range(2):
                    xv = xn[:, hf * 1024:(hf + 1) * 1024].rearrange("p (g e) -> p g e", e=64)
                    O.tt('pool', xv, xv, bc_lead(ggt[:, hf, :], [128, 16, 64]), ALU.mult, [xn_r, gg_r], [xn_r])
                if i >= 2:
                    c_t, c_r = cs[i % 2]
                    lt = r0 - TC
                    O.dma(c_t[:, 0, :], cos_in[lt:lt + 128, :], [], [c_r])
                    O.dma(c_t[:, 1, :], sin_in[lt:lt + 128, :], [], [c_r])
                    xg = xn[:].rearrange("p (g r q e) -> p g r q e", g=32, r=2, q=2, e=16)
                    tg = t2[:].rearrange("p (g r q e) -> p g r q e", g=32, r=2, q=2, e=16)
                    sv = c_t[:, 1, :].rearrange("p (r q e) -> p r q e", r=2, q=2, e=16)
                    for qq in range(2):
                        O.tt('dve', tg[:, :, :, qq, :], xg[:, :, :, 1 - qq, :],
                             sv[:, :, qq, :].unsqueeze(1).to_broadcast([128, 32, 2, 16]), ALU.mult, [xn_r, c_r], [t2_r])
                    O.tt('pool', xn[:].rearrange("p (g e) -> p g e", e=64), xn[:].rearrange("p (g e) -> p g e", e=64),
                         bc_lead(c_t[:, 0, :], [128, 32, 64]), ALU.mult, [xn_r, c_r], [xn_r])
                    O.tt('dve', xr[:], xn[:], t2[:], ALU.add, [xn_r, t2_r], [xr_r])
                else:
                    O.copy('dve', xr[:], xn[:], [xn_r], [xr_r])
                qt_t, qt_r = qTt[i % 2]
                for w in range(2):
                    p_t, p_r = pQ[w]
                    for h in range(8):
                        O.tr(p_t[:, h * 128:(h + 1) * 128], xr[:, w * 1024 + h * 128:w * 1024 + (h + 1) * 128], identb[:],
                             [xr_r, identb_r], [p_r], inc=(h == 7))
                    O.copy('act', qt_t[:, w, :, :].rearrange("p h t -> p (h t)"), p_t[:], [p_r], [qt_r])
                O.dma(aQT[:, :, r0:r0 + 128].rearrange("h p t -> p h t"), qt_t[:, 0, :, :], [qt_r], rs_of(aQT_r, r0, 128))
                O.dma(aKT[:, :, r0:r0 + 128].rearrange("h p t -> p h t"), qt_t[:, 1, :, :], [qt_r], rs_of(aKT_r, r0, 128))
            P.barrier()

        with ExitStack() as es:
            KTs = [b.sb(es, "KTs", [128, NT], BF16) for _ in range(2)]
            Vh = [b.sb(es, "Vh", [128, NCH, 128], BF16) for _ in range(2)]
            qb = [b.sb(es, "qb", [128, 512], BF16) for _ in range(2)]
            zb = [b.sb(es, "zb", [128, 512], BF16) for _ in range(2)]
            Pt = [b.sb(es, "Pt", [128, 2, 512], BF16) for _ in range(3)]
            rc, rc_r = b.sb(es, "rc", [128, 2, 512], F32)
            accs, accs_r = b.sb(es, "accs", [128, 2, 512], F32)
            o0, o0_r = b.sb(es, "o0", [128, 512], F32)
            o1, o1_r = b.sb(es, "o1", [128, 512], F32)
            osq, osq_r = b.sb(es, "osq", [128, 512], BF16)
            rst, rst_r = b.sb(es, "rst", [128, 512], F32)
            szt, sz_r = b.sb(es, "aszt", [128, 512], F32)
            yo = [b.sb(es, "yo", [128, 512], BF16) for _ in range(2)]
            pSc = [b.ps(es, "pSc", [128, 2, 512], F32) for _ in range(2)]
            pO, pO_r = b.ps(es, "pO", [128, 2, 512], F32)
            pSm, pSm_r = b.ps(es, "pSm", [128, 2, 512], F32)
            pSmA_r = Res('pSmA')
            it = 0
            nb = 0
            for h in range(8):
                k_t, k_r = KTs[h % 2]
                v_t, v_r = Vh[h % 2]
                O.dma(k_t[:], aKT[h, :, :], aKT_r, [k_r])
                for q4 in range(0, NCH, 11):
                    O.dma(v_t[:, q4:q4 + 11, :], avz[q4 * 128:(q4 + 11) * 128, h * 128:(h + 1) * 128].rearrange("(c p) e -> p c e", p=128),
                          rs_of(avz_r, q4 * 128, 11 * 128), [v_r])
                for (t0, bw) in TBLKS:
                    nk = 2 if t0 < TC else NCH
                    q_t, q_r = qb[nb % 2]
                    z_t, z_r = zb[nb % 2]
                    y_t, y_r = yo[nb % 2]
                    nb += 1
                    O.dma(q_t[:, 0:bw], aQT[h, :, t0:t0 + bw], rs_of(aQT_r, t0, bw), [q_r])
                    O.dma(z_t[:, 0:bw], azT[h, :, t0:t0 + bw], rs_of(azT_r, t0, bw), [z_r])
                    def emit_qk(kc, slot):
                        ps_t, ps_r = pSc[slot % 2]
                        for cm in range(2):
                            O.mm(ps_t[:, cm, 0:bw], k_t[cm * 64:(cm + 1) * 64, kc * 128:(kc + 1) * 128],
                                 q_t[cm * 64:(cm + 1) * 64, 0:bw], True, True, [k_r, q_r], [ps_r], inc=(cm == 1))
                    emit_qk(0, it)
                    for kc in range(nk):
                        ps_t, ps_r = pSc[it % 2]
                        p_t, p_r = Pt[it % 3]
                        if kc + 1 < nk:
                            emit_qk(kc + 1, it + 1)
                        O.act(p_t[:, :, 0:bw], ps_t[:, :, 0:bw], AF.Exp, [ps_r, sc_r], [p_r], bias=sc[:, 0:1])
                        for cm in range(2):
                            O.mm(pO[:, cm, 0:bw], v_t[:, kc, :], p_t[:, cm, 0:bw], kc == 0, kc == nk - 1, [v_r, p_r], [pO_r], inc=(cm == 1))
                        O.mm(pSm[:, 0, 0:bw], onesb[:, :], p_t[:, 0, 0:bw], kc == 0, kc == nk - 1, [onesb_r, p_r], [pSmA_r])
                        if kc == 0:
                            O.copy('dve', pSm[:, 1, 0:bw], p_t[:, 1, 0:bw], [p_r], [pSm_r])
                        else:
                            O.tt('dve', pSm[:, 1, 0:bw], pSm[:, 1, 0:bw], p_t[:, 1, 0:bw], ALU.add, [pSm_r, p_r], [pSm_r])
                        it += 1
                    O.copy('dve', accs[:, 1, 0:bw], pSm[:, 1, 0:bw], [pSm_r], [accs_r])
                    O.mm(pSm[:, 1, 0:bw], G['onesf'][:, :], accs[:, 1, 0:bw], True, True, [G['onesf_r'], accs_r], [pSm_r])
                    O.P.op('dve', lambda e, bw=bw: e.reciprocal(out=rc[:, :, 0:bw], in_=pSm[:, :, 0:bw]), [pSm_r, pSmA_r], [rc_r])
                    O.tt('dve', o0[:, 0:bw], pO[:, 0, 0:bw], rc[:, 0, 0:bw], ALU.mult, [pO_r, rc_r], [o0_r])
                    O.tt('dve', o1[:, 0:bw], pO[:, 1, 0:bw], rc[:, 1, 0:bw], ALU.mult, [pO_r, rc_r], [o1_r])
                    O.stt(o0[:, 0:bw], o1[:, 0:bw], sc[:, 2:3], o0[:, 0:bw], ALU.mult, ALU.add, [o1_r, o0_r, sc_r], [o0_r])
                    O.act(osq[:, 0:bw], o0[:, 0:bw], AF.Square, [o0_r], [osq_r])
                    ps_t, ps_r = pSc[it % 2]
                    it += 1
                    O.mm(ps_t[:, 0, 0:bw], onesb[:, :], osq[:, 0:bw], True, True, [onesb_r, osq_r], [ps_r])
                    O.act(rst[:, 0:bw], ps_t[:, 0, 0:bw], AF.Sqrt, [ps_r], [rst_r], scale=1.0 / 128, bias=EPS)
                    O.recip(rst[:, 0:bw], rst[:, 0:bw], [rst_r], [rst_r])
                    O.act(szt[:, 0:bw], z_t[:, 0:bw], AF.Silu, [z_r], [sz_r])
                    O.stt(o0[:, 0:bw], o0[:, 0:bw], sc[:, 3:4], rst[:, 0:bw], ALU.mult, ALU.mult, [o0_r, sc_r, rst_r], [o0_r])
                    O.tt('dve', y_t[:, 0:bw], o0[:, 0:bw], szt[:, 0:bw], ALU.mult, [o0_r, sz_r], [y_r])
                    O.dma(yT[1, h, :, t0:t0 + bw], y_t[:, 0:bw], [y_r], rs_of(yT_r, t0, bw))
            P.barrier()


def phase_merge(b, l, G, C):
    P = b.P
    O = Ops(P)
    yT, yT_r = G['yT'], G['yT_r']
    gtT, gtT_r = G['gtT'], G['gtT_r']
    w_br, w_out = G['w_br'], G['w_out']
    xsrc, xsrc_r, xdst, xdst_r, dst_off = G['xsrc'], G['xsrc_r'], G['xdst'], G['xdst_r'], G['dst_off']
    gbc, gbc_r = G['gbc'], G['gbc_r']
    with ExitStack() as es:
        stg = [b.sb(es, "wstg", [128, D], F32) for _ in range(2)]
        Wb, Wb_r = b.sb(es, "Wb", [128, 3, 8, D], BF16)
        Wo, Wo_r = b.sb(es, "Wo", [128, 8, D], BF16)
        yb = [b.sb(es, "yb", [128, 3, 8, 256], BF16) for _ in range(2)]
        gb = [b.sb(es, "gb", [128, 8, 256], BF16) for _ in range(2)]
        sg, sg_r = b.sb(es, "msg", [128, 8, 256], F32)
        macc, macc_r = b.sb(es, "macc", [128, 8, 256], F32)
        mt = [b.sb(es, "mtmp", [128, 512], F32) for _ in range(2)]
        mxT, mxT_r = b.sb(es, "mxT", [128, 8, 256], BF16)
        xin_t = [b.sb(es, "xin_t", [128, D], F32) for _ in range(2)]
        xo_t = [b.sb(es, "xo_t", [128, D], F32) for _ in range(2)]
        pm_ = [b.ps(es, "pmg", [128, 512], F32) for _ in range(4)]
        n = 0
        for br in range(4):
            for kc in range(8):
                st, st_r = stg[n % 2]
                n += 1
                src = w_br[l, br, kc * 128:(kc + 1) * 128, :] if br < 3 else w_out[l, kc * 128:(kc + 1) * 128, :]
                O.dma(st[:], src, [], [st_r])
                dst = Wb[:, br, kc, :] if br < 3 else Wo[:, kc, :]
                O.copy('pool', dst, st[:], [st_r], [Wb_r if br < 3 else Wo_r])
        npm = 0
        nm = 0
        for bi, (t0, bw) in enumerate([(256 * i, 256) for i in range(NT // 256)]):
            y_t, y_r = yb[bi % 2]
            for br in range(3):
                O.dma(y_t[:, br, :, 0:bw], yT[br, :, :, t0:t0 + bw].rearrange("k p t -> p k t"), rs_of(yT_r, t0, bw), [y_r])
            for br in range(3):
                g_t, g_r = gb[(bi * 3 + br) % 2]
                O.dma(g_t[:, :, 0:bw], gtT[br * 8:(br + 1) * 8, :, t0:t0 + bw].rearrange("k p t -> p k t"), rs_of(gtT_r, t0, bw), [g_r])
                O.act(sg[:, :, 0:bw], g_t[:, :, 0:bw], AF.Sigmoid, [g_r], [sg_r])
                for fc in range(8):
                    p_t, p_r = pm_[npm % 4]
                    npm += 1
                    for kc in range(8):
                        O.mm(p_t[:, 0:bw], Wb[:, br, kc, fc * 128:(fc + 1) * 128], y_t[:, br, kc, 0:bw], kc == 0, kc == 7,
                             [Wb_r, y_r], [p_r], inc=(kc == 7))
                    if br == 0:
                        O.tt('dve', macc[:, fc, 0:bw], p_t[:, 0:bw], sg[:, fc, 0:bw], ALU.mult, [p_r, sg_r], [macc_r])
                    else:
                        m_t, m_r = mt[nm % 2]
                        nm += 1
                        O.tt('dve', m_t[:, 0:bw], p_t[:, 0:bw], sg[:, fc, 0:bw], ALU.mult, [p_r, sg_r], [m_r])
                        if br == 1:
                            O.tt('pool', macc[:, fc, 0:bw], macc[:, fc, 0:bw], m_t[:, 0:bw], ALU.add, [macc_r, m_r], [macc_r])
                        else:
                            O.tt('pool', mxT[:, fc, 0:bw], macc[:, fc, 0:bw], m_t[:, 0:bw], ALU.add, [macc_r, m_r], [mxT_r])
            j = 1 if t0 < TC else 0
            for ti in range(bw // 128):
                r0 = t0 + ti * 128
                if r0 < dst_off:
                    continue
                xi, xi_r = xin_t[ti % 2]
                xo, xo_r = xo_t[ti % 2]
                O.dma(xi[:], xsrc[r0:r0 + 128, :], rs_of(xsrc_r, r0, 128), [xi_r])
                for hf in range(2):
                    p_t, p_r = pm_[npm % 4]
                    npm += 1
                    for kc in range(8):
                        O.mm(p_t[:, :], mxT[:, kc, ti * 128:(ti + 1) * 128], Wo[:, kc, hf * 512:(hf + 1) * 512], kc == 0, kc == 7,
                             [mxT_r, Wo_r], [p_r], inc=(kc == 7))
                    O.tt('dve', xo[:, hf * 512:(hf + 1) * 512], p_t[:, :], gbc[:, l, j, hf * 512:(hf + 1) * 512], ALU.mult,
                         [p_r, gbc_r], [xo_r])
                O.tt('pool', xo[:], xo[:], xi[:], ALU.add, [xo_r, xi_r], [xo_r])
                O.dma(xdst[r0 - dst_off:r0 - dst_off + 128, :], xo[:], [xo_r], rs_of(xdst_r, r0, 128))
        P.barrier()
```

```python
import math
import numpy as np
import ml_dtypes
from contextlib import ExitStack
import concourse.bass as bass
import concourse.mybir as mybir
from concourse.bass_utils import run_bass_kernel_spmd

F32 = mybir.dt.float32
BF16 = mybir.dt.bfloat16
AF = mybir.ActivationFunctionType
ALU = mybir.AluOpType
AX = mybir.AxisListType

ENGS = ['pe', 'act', 'dve', 'pool', 'sp']
NDMA = {'sp': 12, 'act': 6, 'pool': 6}
SAME_ENG_SYNC = True
INTERNAL_OK = 'ALL'

DEPTH = 4
D = 1024
TC = 256
TL = 8192
NT = TC + TL
NCH = NT // 128
D_IN = 13872
EPS = 1e-6
FWD_ORDER = list(range(NCH))
BWD_ORDER = [1, 0] + list(range(NCH - 1, 1, -1))


class Res:
    __slots__ = ('name', 'w', 'r', 'pr', 'excl')

    def __init__(self, name='', excl=False):
        self.name = name
        self.excl = excl
        self.w = {}
        self.r = {}
        self.pr = {}


class Prog:
    def __init__(self, nc):
        self.nc = nc
        self.ops = {e: [] for e in ENGS}
        self.cnt = {e: 0 for e in ENGS}
        self.known = {e: {} for e in ENGS}
        self.dcnt = {}
        self.dnext = {q: 0 for q in NDMA}
        self.rr = 0
        self.floor = {}

    def barrier(self):
        for e in ENGS:
            if self.cnt[e] > 0:
                self.floor[e] = self.cnt[e]
        for sk, v in self.dcnt.items():
            self.floor[sk] = v

    def _deps(self, eng, reads, writes, extra=(), part=True):
        toks = {}

        def add(sk, v):
            if toks.get(sk, 0) < v:
                toks[sk] = v
        for r in reads:
            for sk, v in r.w.items():
                add(sk, v)
        for w in writes:
            if w.r or not part or w.excl:
                for sk, v in w.w.items():
                    add(sk, v)
            for sk, v in w.r.items():
                add(sk, v)
            for sk, v in w.pr.items():
                add(sk, v)
        for sk, v in extra:
            add(sk, v)
        for sk, v in self.floor.items():
            add(sk, v)
        waits = []
        kn = self.known[eng]
        for sk, v in toks.items():
            if sk == eng and (eng == 'pe' or not SAME_ENG_SYNC):
                continue
            if kn.get(sk, 0) < v:
                kn[sk] = v
                waits.append((sk, v))
        return waits

    def _mark(self, tok, reads, writes, part=True):
        sk, v = tok
        for w in writes:
            if w.r or not part or w.excl:
                pr = dict(w.r)
                for k2, v2 in w.w.items():
                    if pr.get(k2, 0) < v2:
                        pr[k2] = v2
                w.pr = pr
                w.w = {sk: v}
                w.r = {}
            elif w.w.get(sk, 0) < v:
                w.w[sk] = v
        for r in reads:
            if r in writes:
                continue
            if r.r.get(sk, 0) < v:
                r.r[sk] = v

    def op(self, eng, fn, reads=(), writes=(), inc=True):
        xr = [r for r in reads if r.excl and r not in writes]
        if xr:
            writes = list(writes) + xr
        waits = self._deps(eng, reads, writes)
        if inc:
            self.cnt[eng] += 1
            tok = (eng, self.cnt[eng])
        else:
            tok = (eng, self.cnt[eng] + 1)
        self.ops[eng].append((waits, fn, eng if inc else None, 1))
        self._mark(tok, reads, writes)

    def dma(self, q, out, in_, reads=(), writes=(), **kw):
        if q is None:
            q = ('sp', 'sp', 'act')[self.rr % 3]
            self.rr += 1
        j = self.dnext[q]
        self.dnext[q] = (j + 1) % NDMA[q]
        sk = '%s_d%d' % (q, j)
        prev = self.dcnt.get(sk, 0)
        waits = self._deps(q, reads, writes, extra=[(sk, prev)] if prev else [])
        self.dcnt[sk] = prev + 16
        tok = (sk, prev + 16)
        self.ops[q].append((waits, lambda e: e.dma_start(out=out, in_=in_, **kw), sk, 16))
        self._mark(tok, reads, writes)

    def finish(self):
        nc = self.nc
        sems = {}
        for e in ENGS:
            sems[e] = nc.alloc_semaphore(name='s_' + e)
        for sk in self.dcnt:
            sems[sk] = nc.alloc_semaphore(name='s_' + sk)
        fin = [(e, self.cnt[e]) for e in ENGS if e != 'sp' and self.cnt[e] > 0]
        fin += [(sk, v) for sk, v in self.dcnt.items()]
        handles = {'pe': 'tensor', 'act': 'scalar', 'dve': 'vector', 'pool': 'gpsimd', 'sp': 'sync'}
        ops = self.ops

        def replay(eng):
            def body(e):
                for waits, fn, sk, n in ops[eng]:
                    for wk, wv in waits:
                        e.wait_ge(sems[wk], wv)
                    ins = fn(e)
                    if sk is not None:
                        ins.then_inc(sems[sk], n)
                if eng == 'sp':
                    for wk, wv in fin:
                        e.wait_ge(sems[wk], wv)
            return body

        with nc.Block() as block:
            for eng in ENGS:
                getattr(block, handles[eng])(replay(eng))


class Ops:
    def __init__(self, P):
        self.P = P

    def tt(self, eng, out, in0, in1, op, R, W):
        self.P.op(eng, lambda e: e.tensor_tensor(out=out, in0=in0, in1=in1, op=op), R, W)

    def ts(self, eng, out, in0, s1, s2, op0, op1, R, W):
        if op1 is None:
            self.P.op(eng, lambda e: e.tensor_scalar(out=out, in0=in0, scalar1=s1, scalar2=None, op0=op0), R, W)
        else:
            self.P.op(eng, lambda e: e.tensor_scalar(out=out, in0=in0, scalar1=s1, scalar2=s2, op0=op0, op1=op1), R, W)

    def stt(self, out, in0, scalar, in1, op0, op1, R, W):
        self.P.op('dve', lambda e: e.scalar_tensor_tensor(out=out, in0=in0, scalar=scalar, in1=in1, op0=op0, op1=op1), R, W)

    def act(self, out, in_, func, R, W, bias=None, scale=None, accum=None):
        kw = {}
        if bias is not None:
            kw['bias'] = bias
        if scale is not None:
            kw['scale'] = scale
        if accum is not None:
            kw['accum_out'] = accum
        self.P.op('act', lambda e: e.activation(out=out, in_=in_, func=func, **kw), R, W)

    def copy(self, eng, out, in_, R, W):
        if eng == 'act':
            self.P.op('act', lambda e: e.copy(out=out, in_=in_), R, W)
        else:
            self.P.op(eng, lambda e: e.tensor_copy(out=out, in_=in_), R, W)

    def memset(self, eng, ap, val, W):
        self.P.op(eng, lambda e: e.memset(ap, val), (), W)

    def recip(self, out, in_, R, W):
        self.P.op('dve', lambda e: e.reciprocal(out=out, in_=in_), R, W)

    def reduce(self, out, in_, op, R, W, absval=None):
        self.P.op('dve', lambda e: e.tensor_reduce(out=out, in_=in_, axis=AX.X, op=op, apply_absolute_value=absval), R, W)

    def mm(self, out, lhsT, rhs, start, stop, R, W, inc=True):
        self.P.op('pe', lambda e: e.matmul(out, lhsT, rhs, start=start, stop=stop, skip_group_check=True), R, W, inc=inc)

    def tr(self, out, in_, ident, R, W, inc=True):
        self.P.op('pe', lambda e: e.transpose(out=out, in_=in_, identity=ident), R, W, inc=inc)

    def dma(self, out, in_, R, W, q=None):
        self.P.dma(q, out, in_, R, W)


def bc_mid(ap, shape):
    return ap.unsqueeze(2).to_broadcast(shape)


def bc_lead(ap, shape):
    return ap.unsqueeze(1).to_broadcast(shape)


class B:
    def __init__(self, nc):
        self.nc = nc
        self.P = Prog(nc)
        self.uid = 0

    def name(self, n):
        self.uid += 1
        return '%s_%d' % (n, self.uid)

    def sb(self, es, n, shape, dt):
        t = es.enter_context(self.nc.sbuf_tensor(self.name(n), shape, dt))
        return t, Res(n)

    def ps(self, es, n, shape, dt=F32):
        t = es.enter_context(self.nc.psum_tensor(self.name(n), shape, dt))
        return t, Res(n, excl=True)

    def dram(self, n, shape, dt, nres=1):
        t = self.nc.dram_tensor(self.name(n), shape, dt, kind="Internal").ap()
        return t, [Res(n) for _ in range(nres)]


def blk_of(t0, n):
    return list(range(t0 // 512, (t0 + n - 1) // 512 + 1))


NBLK = (NT + 511) // 512
TBLKS = [(0, 256)] + [(256 + 512 * i, 512) for i in range(16)]


def rs_of(rl, t0, n):
    return [rl[i] for i in blk_of(t0, n)]


C_XBC = 0
C_ZS = 1536
C_DT = 2560
C_Q = 2592
C_K = 3616
C_V = 4640
C_ZA = 5664
C_MQK = 6688
C_MV = 7712
C_MO = 8736
C_MZ = 9760
C_IG = 10784
C_FG = 10792
C_GT = 10800


def build_program(n_layers, dbg=None, only=None):
    dbg = dbg or ()
    scr_cache = {}
    nc = bass.Bass("TRN2", target_bir_lowering=False)
    b = B(nc)
    P = b.P
    L = n_layers

    def inp(n, shape, dt=F32):
        return nc.dram_tensor(n, shape, dt, kind="ExternalInput").ap()

    xin = inp("xin", [NT, D])
    cc = inp("cc", [128, 16])
    ident_in = inp("ident", [128, 128])
    masks_in = inp("masks", [128, 4, 128])
    cos_in = inp("cosb", [TL, 64])
    sin_in = inp("sinb", [TL, 64])
    lami_in = inp("lami", [128, L])
    w_mod = inp("w_mod", [L, D, 3 * D])
    bmod = inp("bmod", [128, L, 24])
    ng = inp("ng", [128, L, 8])
    w_in = inp("w_in", [L, D, D_IN])
    cw = inp("cw", [128, L, 12, 3])
    cb = inp("cb", [128, L, 12])
    alog = inp("alog", [128, L, 32])
    dtb = inp("dtb", [128, L, 32])
    dsk = inp("dsk", [128, L, 16])
    sng = inp("sng", [128, L, D])
    gq = inp("gq", [128, L, 64])
    gk = inp("gk", [128, L, 64])
    lamp = inp("lamp", [128, L, 4, 64])
    subg = inp("subg", [128, L, 8])
    mcw = inp("mcw", [128, L, 8, 3])
    mcb = inp("mcb", [128, L, 8])
    ib = inp("ib", [128, L, 8])
    fb = inp("fb", [128, L, 8])
    mng = inp("mng", [128, L, D])
    sel_in = inp("sel", [2, 2, 128])
    bgate = inp("bgate", [2, L, D])
    w_br = inp("w_br", [L, 3, D, D])
    w_out = inp("w_out", [L, D, D])
    xout = nc.dram_tensor("xout", [NT, D], F32, kind="ExternalOutput").ap()

    def scratch(n, shape, dt, nres=NBLK):
        if only is not None and n in only.get('inputs', ()):
            t = nc.dram_tensor("dbg_" + n, shape, dt, kind="ExternalInput").ap()
            return t, [Res(n) for _ in range(nres)]
        if n in dbg or INTERNAL_OK is None or (INTERNAL_OK != 'ALL' and n not in INTERNAL_OK):
            t = nc.dram_tensor("dbg_" + n, shape, dt, kind="ExternalOutput").ap()
            return t, [Res(n) for _ in range(nres)]
        return b.dram(n, shape, dt, nres)

    xbufs = [(xin, [Res('xin') for _ in range(NBLK)])]
    for l in range(1, L):
        xbufs.append(scratch("xres%d" % l, [NT, D], F32))
    xout_res = [Res('xout') for _ in range(NBLK)]

    hT, hT_r = scratch("hT", [8, 128, NT], BF16)
    xbcT, xbcT_r = scratch("xbcT", [12, 128, NT], BF16)
    mqkT, mqkT_r = scratch("mqkT", [8, 128, NT], BF16)
    zs, zs_r = scratch("zs", [NT, 1024], BF16)
    small, small_r = scratch("small", [NT, 48], F32)
    aqk, aqk_r = scratch("aqk", [NT, 2048], BF16)
    avz, avz_r = scratch("avz", [NT, 1024], BF16)
    azT, azT_r = scratch("azT", [8, 128, NT], BF16)
    mvo, mvo_r = scratch("mvo", [NT, 2048], BF16)
    mz, mz_r = scratch("mz", [NT, 1024], BF16)
    gtT, gtT_r = scratch("gtT", [24, 128, NT], BF16)
    yT, yT_r = scratch("yT", [3, 8, 128, NT], BF16)

    with ExitStack() as g_es:
        identf, identf_r = b.sb(g_es, "identf", [128, 128], F32)
        identb, identb_r = b.sb(g_es, "identb", [128, 128], BF16)
        masks, masks_r = b.sb(g_es, "masks", [128, 4, 128], F32)
        onesb, onesb_r = b.sb(g_es, "onesb", [128, 128], BF16)
        onesf, onesf_r = b.sb(g_es, "onesf", [128, 128], F32)
        gbc, gbc_r = b.sb(g_es, "gbc", [128, L, 2, D], F32)
        modt, modt_r = b.sb(g_es, "modt", [128, L, 24, 2], F32)
        a1t, a1t_r = b.sb(g_es, "a1t", [128, L, 8, 2], F32)
        P.dma('sp', identf[:], ident_in[:, :], writes=[identf_r])
        P.dma('sp', masks[:], masks_in[:, :, :], writes=[masks_r])
        P.op('dve', lambda e: e.tensor_copy(out=identb[:], in_=identf[:]), reads=[identf_r], writes=[identb_r])
        P.op('dve', lambda e: e.memset(onesb[:], 1.0), writes=[onesb_r])
        P.op('dve', lambda e: e.memset(onesf[:], 1.0), writes=[onesf_r])

        with ExitStack() as es:
            cct, cct_r = b.sb(es, "cct", [128, 16], F32)
            sct, sct_r = b.sb(es, "sct", [128, 16], F32)
            bmt, bmt_r = b.sb(es, "bmt", [128, L, 24], F32)
            ngt, ngt_r = b.sb(es, "ngt", [128, L, 8], F32)
            wm, wm_r = b.sb(es, "wm", [128, 8, 3 * D], F32)
            pm, pm_r = b.ps(es, "pm", [128, 512], F32)
            pgr, pgr_r = b.ps(es, "pgr", [128, 2, 512], F32)
            pgb, pgb_r = b.ps(es, "pgb", [128, 2, 512], F32)
            selt, selt_r = b.sb(es, "selt", [2, 2, 128], F32)
            bgt, bgt_r = b.sb(es, "bgt", [2, L, D], F32)
            grow, grow_r = b.sb(es, "grow", [2, D], F32)
            P.dma('sp', selt[:], sel_in[:, :, :], writes=[selt_r])
            P.dma('sp', bgt[:], bgate[:, :, :], writes=[bgt_r])
            P.dma('sp', cct[:], cc[:, :], writes=[cct_r])
            P.dma('sp', bmt[:], bmod[:, :, :], writes=[bmt_r])
            P.dma('sp', ngt[:], ng[:, :, :], writes=[ngt_r])
            P.op('act', lambda e: e.activation(out=sct[:], in_=cct[:], func=AF.Silu), reads=[cct_r], writes=[sct_r])
            if 'mod' in dbg:
                dsc = nc.dram_tensor("dbg_sct", [128, 16], F32, kind="ExternalOutput").ap()
                P.dma('sp', dsc[:, :], sct[:], reads=[sct_r], writes=[Res('x')])
            for l in range(L):
                for kc in range(8):
                    P.dma(None, wm[:, kc, :], w_mod[l, kc * 128:(kc + 1) * 128, :], writes=[wm_r])
                for fc in range(24):
                    for kc in range(8):
                        P.op('pe', lambda e, fc=fc, kc=kc: e.matmul(
                            pm[:, fc * 2:fc * 2 + 2], wm[:, kc, fc * 128:(fc + 1) * 128], sct[:, kc * 2:kc * 2 + 2],
                            start=(kc == 0), stop=(kc == 7)),
                            reads=[wm_r, sct_r], writes=[pm_r], inc=(fc == 23 and kc == 7))
                for hf in range(2):
                    for kc in range(8):
                        P.op('pe', lambda e, hf=hf, kc=kc: e.matmul(
                            pgr[0:2, hf, :], sct[:, kc * 2:kc * 2 + 2], wm[:, kc, 2 * D + hf * 512:2 * D + (hf + 1) * 512],
                            start=(kc == 0), stop=(kc == 7)), reads=[wm_r, sct_r], writes=[pgr_r], inc=(kc == 7))
                P.op('dve', lambda e, l=l: e.tensor_tensor(out=grow[:, :].rearrange("p (h f) -> p h f", h=2), in0=pgr[0:2, :, :],
                                                       in1=bgt[:, l, :].rearrange("p (h f) -> p h f", h=2), op=ALU.add),
                     reads=[pgr_r, bgt_r], writes=[grow_r])
                for j in range(2):
                    for hf in range(2):
                        P.op('pe', lambda e, j=j, hf=hf: e.matmul(pgb[:, hf, :], selt[:, j, :], grow[:, hf * 512:(hf + 1) * 512],
                                                                start=True, stop=True), reads=[selt_r, grow_r], writes=[pgb_r])
                    P.op('dve', lambda e, l=l, j=j: e.tensor_copy(out=gbc[:, l, j, :].rearrange("p (h f) -> p h f", h=2), in_=pgb[:, :, :]),
                         reads=[pgb_r], writes=[gbc_r])
                for j in range(2):
                    P.op('dve', lambda e, l=l, j=j: e.tensor_tensor(
                        out=modt[:, l, :, j], in0=pm[:, 0:48].rearrange("p (f j) -> p f j", j=2)[:, :, j],
                        in1=bmt[:, l, :], op=ALU.add), reads=[pm_r, bmt_r], writes=[modt_r])
                for j in range(2):
                    P.op('dve', lambda e, l=l, j=j: e.scalar_tensor_tensor(
                        out=a1t[:, l, :, j], in0=modt[:, l, 8:16, j], scalar=1.0, in1=ngt[:, l, :],
                        op0=ALU.add, op1=ALU.mult), reads=[modt_r, ngt_r], writes=[a1t_r])
            P.barrier()

        if 'mod' in dbg:
            dmod = nc.dram_tensor("dbg_mod", [128, L * 48], F32, kind="ExternalOutput").ap()
            da1 = nc.dram_tensor("dbg_a1", [128, L * 16], F32, kind="ExternalOutput").ap()
            P.dma('sp', dmod[:, :], modt[:].rearrange("p l f j -> p (l f j)"), reads=[modt_r], writes=[Res('x')])
            P.dma('sp', da1[:, :], a1t[:].rearrange("p l f j -> p (l f j)"), reads=[a1t_r], writes=[Res('x')])
        for l in range(L):
            xsrc, xsrc_r = xbufs[l]
            if l + 1 < L:
                xdst, xdst_r = xbufs[l + 1]
                dst_off = 0
            else:
                xdst, xdst_r = xout, xout_res
                dst_off = 0
            layer(b, l, L, locals())

        P.finish()
        global LAST_PROG
        LAST_PROG = P
    return nc


def layer(b, l, L, G):
    cache = G['scr_cache']

    def S(n, shape, dt, nres=NBLK):
        if n not in cache:
            cache[n] = G['scratch'](n, shape, dt, nres)
        return cache[n]
    G['scr'] = S
    only = G.get('only')
    with ExitStack() as es:
        C = layer_consts(b, es, l, G)
        if only is None or 'a' in only:
            phase_a1(b, l, G)
            phase_a2(b, l, G)
        if only is None or 'ssd' in only:
            phase_ssd(b, l, G, C)
        if only is None or 'ml' in only:
            phase_mlstm(b, l, G, C)
        if only is None or 'attn' in only:
            phase_attn(b, l, G, C)
        if only is None or 'merge' in only:
            phase_merge(b, l, G, C)
        b.P.barrier()


def phase_a1(b, l, G):
    P = b.P
    xsrc, xsrc_r = G['xsrc'], G['xsrc_r']
    hT, hT_r = G['hT'], G['hT_r']
    identb, identb_r = G['identb'], G['identb_r']
    modt, modt_r, a1t, a1t_r = G['modt'], G['modt_r'], G['a1t'], G['a1t_r']
    with ExitStack() as es:
        xt = [b.sb(es, "xt", [128, D], F32) for _ in range(2)]
        junk, junk_r = b.sb(es, "junk", [128, D], BF16)
        xnb = [b.sb(es, "xnb", [128, D], BF16) for _ in range(2)]
        ss = [b.sb(es, "ss", [128, 1], F32) for _ in range(2)]
        pT = [b.ps(es, "pT", [128, D], BF16) for _ in range(2)]
        hb = [b.sb(es, "hb", [128, 8, 512], BF16) for _ in range(2)]
        for bi, (t0, bw) in enumerate(TBLKS):
            hbt, hb_r = hb[bi % 2]
            j = 1 if t0 < TC else 0
            for ti in range(bw // 128):
                i = (t0 // 128 + ti)
                xtt, xt_r = xt[i % 2]
                xn, xn_r = xnb[i % 2]
                st, st_r = ss[i % 2]
                pt, pt_r = pT[i % 2]
                P.dma(None, xtt[:], xsrc[i * 128:(i + 1) * 128, :], reads=rs_of(xsrc_r, i * 128, 128), writes=[xt_r])
                P.op('act', lambda e, xtt=xtt, st=st: e.activation(out=junk[:], in_=xtt[:], func=AF.Square, accum_out=st[:]),
                     reads=[xt_r], writes=[junk_r, st_r])
                P.op('act', lambda e, st=st: e.activation(out=st[:], in_=st[:], func=AF.Sqrt, scale=1.0 / D, bias=EPS),
                     reads=[st_r], writes=[st_r])
                P.op('dve', lambda e, st=st: e.reciprocal(out=st[:], in_=st[:]), reads=[st_r], writes=[st_r])
                P.op('dve', lambda e, xn=xn, xtt=xtt, st=st: e.tensor_scalar(out=xn[:], in0=xtt[:], scalar1=st[:, 0:1], scalar2=None, op0=ALU.mult),
                     reads=[xt_r, st_r], writes=[xn_r])
                for kc in range(8):
                    P.op('pe', lambda e, kc=kc, pt=pt, xn=xn: e.transpose(out=pt[:, kc * 128:(kc + 1) * 128], in_=xn[:, kc * 128:(kc + 1) * 128], identity=identb[:]),
                         reads=[xn_r, identb_r], writes=[pt_r], inc=(kc == 7))
                for kc in range(8):
                    P.op('dve', lambda e, kc=kc, pt=pt, hbt=hbt, ti=ti, j=j: e.tensor_scalar(
                        out=hbt[:, kc, ti * 128:(ti + 1) * 128], in0=pt[:, kc * 128:(kc + 1) * 128],
                        scalar1=a1t[:, l, kc, j:j + 1], scalar2=modt[:, l, kc, j:j + 1], op0=ALU.mult, op1=ALU.add),
                        reads=[pt_r, a1t_r, modt_r], writes=[hb_r])
            P.dma(None, hT[:, :, t0:t0 + bw].rearrange("k p t -> p k t"), hbt[:, :, 0:bw],
                  reads=[hb_r], writes=rs_of(hT_r, t0, bw))
        P.barrier()


def gemm_jobs():
    jobs = []
    jobs.append(('F', C_XBC, 1536, 'xbcT'))
    jobs.append(('F', C_MQK, 1024, 'mqkT'))
    jobs.append(('T', C_ZS, 1024, 'zs', 0))
    jobs.append(('T', C_Q, 2048, 'aqk', 0))
    jobs.append(('T', C_V, 1024, 'avz', 0))
    jobs.append(('F', C_ZA, 1024, 'azT'))
    jobs.append(('T', C_MV, 2048, 'mvo', 0))
    jobs.append(('T', C_MZ, 1024, 'mz', 0))
    jobs.append(('F', C_GT, 2048, 'gtT', 0))
    jobs.append(('F', C_GT + 2048, 1024, 'gtT', 16))
    return jobs


def phase_a2(b, l, G):
    P = b.P
    w_in = G['w_in']
    hT, hT_r = G['hT'], G['hT_r']
    with ExitStack() as es:
        stg = [b.sb(es, "stg", [128, 2048], F32) for _ in range(2)]
        Wg = [b.sb(es, "Wg", [128, 8, 2048], BF16) for _ in range(2)]
        hb = [b.sb(es, "hb2", [128, 8, 512], BF16) for _ in range(2)]
        ob = [b.sb(es, "ob", [128, 2048], BF16) for _ in range(2)]
        obs = [b.sb(es, "obs", [128, 48], F32) for _ in range(2)]
        pg = [b.ps(es, "pg", [128, 512], F32) for _ in range(4)]
        ws, ws_r = b.sb(es, "ws", [128, 8, 48], BF16)
        cnt = {'stg': 0, 'w': 0, 'hb': 0, 'ob': 0, 'pg': 0, 'ev': 0}

        def load_w(c0, n):
            wt, wt_r = Wg[cnt['w'] % 2]
            cnt['w'] += 1
            for kc in range(8):
                st, st_r = stg[cnt['stg'] % 2]
                cnt['stg'] += 1
                P.dma(None, st[:, 0:n], w_in[l, kc * 128:(kc + 1) * 128, c0:c0 + n], writes=[st_r])
                P.op('pool', lambda e, wt=wt, st=st, kc=kc, n=n: e.tensor_copy(out=wt[:, kc, 0:n], in_=st[:, 0:n]),
                     reads=[st_r], writes=[wt_r])
            return wt, wt_r

        def load_h(t0, bw):
            ht, ht_r = hb[cnt['hb'] % 2]
            cnt['hb'] += 1
            P.dma(None, ht[:, :, 0:bw], hT[:, :, t0:t0 + bw].rearrange("k p t -> p k t"),
                  reads=rs_of(hT_r, t0, bw), writes=[ht_r])
            return ht, ht_r

        def evac(out_ap, in_ap, reads, writes):
            eng = 'act' if cnt['ev'] % 2 == 0 else 'dve'
            cnt['ev'] += 1
            if eng == 'act':
                P.op('act', lambda e: e.copy(out=out_ap, in_=in_ap), reads=reads, writes=writes)
            else:
                P.op('dve', lambda e: e.tensor_copy(out=out_ap, in_=in_ap), reads=reads, writes=writes)

        for job in gemm_jobs():
            mode, c0, n = job[0], job[1], job[2]
            dst, dst_r = G[job[3]], G[job[3] + '_r']
            wt, wt_r = load_w(c0, n)
            for (t0, bw) in TBLKS:
                ht, ht_r = load_h(t0, bw)
                if mode == 'F':
                    for ch in range(n // 128):
                        pt, pt_r = pg[cnt['pg'] % 4]
                        cnt['pg'] += 1
                        for kc in range(8):
                            P.op('pe', lambda e, pt=pt, wt=wt, ht=ht, kc=kc, ch=ch, bw=bw: e.matmul(
                                pt[:, 0:bw], wt[:, kc, ch * 128:(ch + 1) * 128], ht[:, kc, 0:bw],
                                start=(kc == 0), stop=(kc == 7)), reads=[wt_r, ht_r], writes=[pt_r], inc=(kc == 7))
                        if ch % 4 == 0:
                            ot, ot_r = ob[cnt['ob'] % 2]
                            cnt['ob'] += 1
                        evac(ot[:, (ch % 4) * 512:(ch % 4) * 512 + bw], pt[:, 0:bw], [pt_r], [ot_r])
                        if ch % 4 == 3 or ch == n // 128 - 1:
                            cb0 = ch - (ch % 4) + (job[4] if len(job) > 4 else 0)
                            nchk = ch % 4 + 1
                            P.dma(None, dst[cb0:cb0 + nchk, :, t0:t0 + bw].rearrange("c p t -> p c t"),
                                  ot[:, 0:nchk * 512].rearrange("p (c t) -> p c t", c=nchk)[:, :, 0:bw],
                                  reads=[ot_r], writes=rs_of(dst_r, t0, bw))
                else:
                    dc0 = job[4]
                    for ti in range(bw // 128):
                        ot, ot_r = ob[cnt['ob'] % 2]
                        cnt['ob'] += 1
                        for sg in range(n // 512):
                            pt, pt_r = pg[cnt['pg'] % 4]
                            cnt['pg'] += 1
                            for kc in range(8):
                                P.op('pe', lambda e, pt=pt, wt=wt, ht=ht, kc=kc, sg=sg, ti=ti: e.matmul(
                                    pt[:, :], ht[:, kc, ti * 128:(ti + 1) * 128], wt[:, kc, sg * 512:(sg + 1) * 512],
                                    start=(kc == 0), stop=(kc == 7)), reads=[wt_r, ht_r], writes=[pt_r], inc=(kc == 7))
                            evac(ot[:, sg * 512:(sg + 1) * 512], pt[:, :], [pt_r], [ot_r])
                        r0 = t0 + ti * 128
                        P.dma(None, dst[r0:r0 + 128, dc0:dc0 + n], ot[:, 0:n], reads=[ot_r], writes=rs_of(dst_r, r0, 128))

        small, small_r = G['small'], G['small_r']
        for kc in range(8):
            st, st_r = stg[cnt['stg'] % 2]
            cnt['stg'] += 1
            P.dma(None, st[:, 0:32], w_in[l, kc * 128:(kc + 1) * 128, C_DT:C_DT + 32], writes=[st_r])
            P.dma(None, st[:, 32:48], w_in[l, kc * 128:(kc + 1) * 128, C_IG:C_IG + 16], writes=[st_r])
            P.op('pool', lambda e, st=st, kc=kc: e.tensor_copy(out=ws[:, kc, :], in_=st[:, 0:48]), reads=[st_r], writes=[ws_r])
        for (t0, bw) in TBLKS:
            ht, ht_r = load_h(t0, bw)
            for ti in range(bw // 128):
                pt, pt_r = pg[cnt['pg'] % 4]
                cnt['pg'] += 1
                ot, ot_r = obs[cnt['ob'] % 2]
                cnt['ob'] += 1
                for kc in range(8):
                    P.op('pe', lambda e, pt=pt, ht=ht, kc=kc, ti=ti: e.matmul(
                        pt[:, 0:48], ht[:, kc, ti * 128:(ti + 1) * 128], ws[:, kc, :],
                        start=(kc == 0), stop=(kc == 7)), reads=[ws_r, ht_r], writes=[pt_r], inc=(kc == 7))
                evac(ot[:, :], pt[:, 0:48], [pt_r], [ot_r])
                r0 = t0 + ti * 128
                P.dma(None, small[r0:r0 + 128, :], ot[:, :], reads=[ot_r], writes=rs_of(small_r, r0, 128))
        P.barrier()


def rope_tables():
    t = np.arange(TL)
    row = (t // 64).astype(np.float32)
    col = (t % 64).astype(np.float32)
    half = 32
    inv_freq = (10000.0 ** (-np.arange(0, half, 2, dtype=np.float32) / half)).astype(np.float32)
    ang_r = row[:, None] * inv_freq
    ang_c = col[:, None] * inv_freq
    cos = np.concatenate([np.cos(ang_r), np.cos(ang_r), np.cos(ang_c), np.cos(ang_c)], -1).astype(np.float32)
    sin = np.concatenate([np.sin(ang_r), np.sin(ang_r), np.sin(ang_c), np.sin(ang_c)], -1).astype(np.float32)
    sgn = np.tile(np.concatenate([-np.ones(16), np.ones(16)]), 2).astype(np.float32)
    return cos, sin * sgn


def bc(a):
    a = np.asarray(a, np.float32)
    return np.ascontiguousarray(np.broadcast_to(a[None], (128,) + a.shape))


def pmaj(v, nchunk):
    v = np.asarray(v, np.float32)
    lead = v.shape[:-1]
    r = v.reshape(lead + (nchunk, 128))
    r = np.moveaxis(r, -1, 0)
    return np.ascontiguousarray(r)


def make_inputs(inputs, bidx, layers, xin=None):
    L = len(layers)
    f = lambda k: np.asarray(inputs[k], np.float32)
    d = {}
    d['xin'] = np.ascontiguousarray(np.concatenate([f('ctx')[bidx], f('x')[bidx]], 0)) if xin is None else np.ascontiguousarray(xin, np.float32)
    cc = np.stack([f('c')[bidx], f('c_ctx')], -1)
    d['cc'] = np.ascontiguousarray(cc.reshape(8, 128, 2).transpose(1, 0, 2).reshape(128, 16))
    d['ident'] = np.eye(128, dtype=np.float32)
    t = np.arange(128)
    U = (t[:, None] <= t[None, :]).astype(np.float32)
    Lw = (t[:, None] >= t[None, :]).astype(np.float32)
    Sg = (t[:, None] > t[None, :]).astype(np.float32)
    Sl = (t[:, None] < t[None, :]).astype(np.float32)
    d['masks'] = np.ascontiguousarray(np.stack([U, Lw, Sg, Sl], 1))
    cos, sin = rope_tables()
    d['cosb'] = cos
    d['sinb'] = sin
    d['lami'] = bc(np.array([0.8 - 0.6 * math.exp(-0.3 * l) for l in layers], np.float32))
    d['w_mod'] = np.ascontiguousarray(f('w_mod')[layers])
    d['bmod'] = pmaj(f('b_mod')[layers], 24)
    d['ng'] = pmaj(f('norm_g')[layers], 8)
    d['w_in'] = np.ascontiguousarray(f('w_in')[layers])
    d['cw'] = np.ascontiguousarray(pmaj(f('ssd_conv_w')[layers], 12).transpose(0, 1, 3, 2))
    d['cb'] = pmaj(f('ssd_conv_b')[layers], 12)
    d['alog'] = bc(f('ssd_a_log')[layers].reshape(L, 32))
    d['dtb'] = bc(f('ssd_dt_bias')[layers].reshape(L, 32))
    d['dsk'] = bc(f('ssd_d')[layers])
    d['sng'] = bc(f('ssd_norm_g')[layers])
    d['gq'] = bc(f('diff_qn_g')[layers])
    d['gk'] = bc(f('diff_kn_g')[layers])
    d['lamp'] = bc(f('diff_lambda')[layers])
    d['subg'] = np.ascontiguousarray(np.broadcast_to(pmaj(f('diff_subln_g')[layers], 1), (128, L, 8)))
    d['mcw'] = np.ascontiguousarray(pmaj(f('ml_conv_w')[layers], 8).transpose(0, 1, 3, 2))
    d['mcb'] = pmaj(f('ml_conv_b')[layers], 8)
    d['ib'] = bc(f('ml_i_bias')[layers].reshape(L, 8))
    d['fb'] = bc(f('ml_f_bias')[layers].reshape(L, 8))
    d['mng'] = bc(f('ml_norm_g')[layers])
    sel = np.zeros((2, 2, 128), np.float32)
    sel[0, 0] = 1.0
    sel[1, 1] = 1.0
    d['sel'] = sel
    d['bgate'] = np.ascontiguousarray(np.broadcast_to(f('b_mod')[layers][None, :, 2 * D:3 * D], (2, L, D)))
    d['w_br'] = np.ascontiguousarray(f('w_branch')[layers])
    d['w_out'] = np.ascontiguousarray(f('w_out')[layers])
    return d


_NC_CACHE = {}


FUSED = True


def kernel(**inputs):
    nb = 4
    if FUSED:
        if DEPTH not in _NC_CACHE:
            _NC_CACHE[DEPTH] = build_program(DEPTH)
        nc = _NC_CACHE[DEPTH]
        in_maps = [make_inputs(inputs, bidx, list(range(DEPTH))) for bidx in range(nb)]
        res = run_bass_kernel_spmd(nc, in_maps, core_ids=list(range(nb)))
        return np.stack([np.asarray(r["xout"], np.float32)[TC:] for r in res.results], 0)
    if 1 not in _NC_CACHE:
        _NC_CACHE[1] = build_program(1)
    nc = _NC_CACHE[1]
    xcur = [None] * nb
    for l in range(DEPTH):
        in_maps = [make_inputs(inputs, bidx, [l], xin=xcur[bidx]) for bidx in range(nb)]
        res = run_bass_kernel_spmd(nc, in_maps, core_ids=list(range(nb)))
        xcur = [np.asarray(r["xout"], np.float32) for r in res.results]
    return np.stack([x[TC:] for x in xcur], 0)


def layer_consts(b, es, l, G):
    P = b.P
    O = Ops(P)
    C = {}

    def ld(name, src, shape, **kw):
        t, r = b.sb(es, "c_" + name, shape, F32)
        P.dma(None, t[:], src, writes=[r], **kw)
        C[name] = (t, r)
    ld('cw', G['cw'][:, l, :, :], [128, 12, 3])
    ld('cb', G['cb'][:, l, :], [128, 12])
    ld('alog', G['alog'][:, l, :], [128, 32])
    ld('dtb', G['dtb'][:, l, :], [128, 32])
    ld('dsk', G['dsk'][:, l, :], [128, 16])
    ld('sng', G['sng'][:, l, :], [128, D])
    ld('gq', G['gq'][:, l, :], [128, 64])
    ld('gk', G['gk'][:, l, :], [128, 64])
    ld('lamp', G['lamp'][:, l, :, :], [128, 4, 64])
    ld('subg', G['subg'][:, l, :], [128, 8])
    ld('mcw', G['mcw'][:, l, :, :], [128, 8, 3])
    ld('mcb', G['mcb'][:, l, :], [128, 8])
    ld('ib', G['ib'][:, l, :], [128, 8])
    ld('fb', G['fb'][:, l, :], [128, 8])
    ld('mng', G['mng'][:, l, :], [128, D])
    ld('lami', G['lami_in'][:, l:l + 1], [128, 1], allow_slow_non_contiguous=True)
    at, ar = C['alog']
    O.act(at[:], at[:], AF.Exp, [ar], [ar])
    O.ts('dve', at[:], at[:], -1.0, None, ALU.mult, None, [ar], [ar])
    return C


def softplus(b, O, out, x, tmp1, tmp2, R, W):
    O.act(tmp1, x, AF.Abs, R + W, W)
    O.act(tmp1, tmp1, AF.Exp, R + W, W, scale=-1.0)
    O.act(tmp2, tmp1, AF.Ln, R + W, W, bias=1.0)
    O.stt(out, x, 0.0, tmp2, ALU.max, ALU.add, R + W, W)


def seq_bounds(t0, bw):
    return (t0 == 0 or t0 == TC), (t0 + bw == TC or t0 + bw == NT)


def conv_silu(b, O, l, xT, xT_r, t0, bw, nchunk, cwt, cw_r, cbt, cb_r, xh, xh_r, tmps, uT, uT_r):
    P = O.P
    at_start, at_end = seq_bounds(t0, bw)
    lo = t0 if at_start else t0 - 1
    hi = t0 + bw if at_end else t0 + bw + 1
    if at_start:
        O.memset('pool', xh[:, :, 0:1], 0.0, [xh_r])
    if at_end:
        O.memset('pool', xh[:, :, bw + 1:bw + 2], 0.0, [xh_r])
    O.dma(xh[:, :, lo - (t0 - 1):hi - (t0 - 1)], xT[:, :, lo:hi].rearrange("c p t -> p c t"),
          rs_of(xT_r, lo, hi - lo), [xh_r])
    for ch in range(nchunk):
        tm, tm_r = tmps[ch % 2]
        O.ts('dve', tm[:, 0:bw], xh[:, ch, 1:bw + 1], cwt[:, ch, 1:2], None, ALU.mult, None, [xh_r, cw_r], [tm_r])
        O.stt(tm[:, 0:bw], xh[:, ch, 0:bw], cwt[:, ch, 0:1], tm[:, 0:bw], ALU.mult, ALU.add, [xh_r, cw_r, tm_r], [tm_r])
        O.stt(tm[:, 0:bw], xh[:, ch, 2:bw + 2], cwt[:, ch, 2:3], tm[:, 0:bw], ALU.mult, ALU.add, [xh_r, cw_r, tm_r], [tm_r])
        O.act(uT[:, ch, 0:bw], tm[:, 0:bw], AF.Silu, [tm_r, cb_r], [uT_r], bias=cbt[:, ch:ch + 1])


def decay_scalars(b, O, c, a_t, a_r, NA, masks, masks_r, onesf, onesf_r, pc, pc_r, PT, PT_r, E1, E2, E_r,
                  rs, de, cdall, cd_r):
    n2 = 2 * NA
    O.mm(pc[:, 0:n2], masks[:, 0, :], a_t[:, 0:n2], True, True, [masks_r, a_r], [pc_r], inc=False)
    O.mm(pc[:, n2:2 * n2], onesf[:, :], a_t[:, 0:n2], True, True, [onesf_r, a_r], [pc_r])
    O.copy('dve', PT[:, 0:2 * n2], pc[:, 0:2 * n2], [pc_r], [PT_r])
    Pf, Pb = PT[:, 0:NA], PT[:, NA:n2]
    Tf, Tb = PT[:, n2:n2 + NA], PT[:, n2 + NA:2 * n2]
    O.tt('dve', E2[:, NA:n2], Pb, a_t[:, NA:n2], ALU.subtract, [PT_r, a_r], [E_r])
    O.tt('dve', E1[:, NA:n2], Tb, E2[:, NA:n2], ALU.subtract, [PT_r, E_r], [E_r])
    O.tt('dve', E2[:, 0:NA], Tf, Pf, ALU.subtract, [PT_r], [E_r])
    O.copy('dve', E1[:, 0:NA], Pf, [PT_r], [E_r])
    O.act(rs, E1[:, 0:n2], AF.Exp, [E_r], [E_r])
    O.act(de, E2[:, 0:n2], AF.Exp, [E_r], [E_r])
    O.act(cdall[:, c, 0:n2], PT[:, n2:2 * n2], AF.Exp, [PT_r], [cd_r])


def block_scalars(b, O, G, t0, bw, xa, x_r, n2, NA, Abc, A_r, pc, pc_r, T, T_r, cdall, cd_r, sm_dram, sm_dram_r):
    masks, masks_r, onesf, onesf_r = G['masks'], G['masks_r'], G['onesf'], G['onesf_r']
    nt = bw // 128
    c0 = t0 // 128
    a = xa[:, 0:nt, 0, :]
    O.copy('dve', T[:, 0, 0:nt, :], a, [x_r], [T_r])
    O.mm(pc[:, 0:nt * n2], masks[:, 0, :], T[:, 0, 0:nt, :].rearrange("p t c -> p (t c)"), True, True, [masks_r, T_r], [pc_r], inc=False)
    O.mm(pc[:, 256:256 + nt * n2], onesf[:, :], T[:, 0, 0:nt, :].rearrange("p t c -> p (t c)"), True, True, [onesf_r, T_r], [pc_r])
    O.copy('dve', xa[:, 0:nt, 1, :], pc[:, 0:nt * n2].rearrange("p (t c) -> p t c", t=nt), [pc_r], [x_r])
    O.copy('dve', xa[:, 0:nt, 2, :], pc[:, 256:256 + nt * n2].rearrange("p (t c) -> p t c", t=nt), [pc_r], [x_r])
    Pm, Tm, E1, E2 = xa[:, 0:nt, 1, :], xa[:, 0:nt, 2, :], xa[:, 0:nt, 3, :], xa[:, 0:nt, 4, :]
    O.tt('dve', E2[:, :, NA:n2], Pm[:, :, NA:n2], a[:, :, NA:n2], ALU.subtract, [x_r], [x_r])
    O.tt('dve', E1[:, :, NA:n2], Tm[:, :, NA:n2], E2[:, :, NA:n2], ALU.subtract, [x_r], [x_r])
    O.tt('dve', E2[:, :, 0:NA], Tm[:, :, 0:NA], Pm[:, :, 0:NA], ALU.subtract, [x_r], [x_r])
    O.copy('dve', E1[:, :, 0:NA], Pm[:, :, 0:NA], [x_r], [x_r])
    O.act(xa[:, 0:nt, 5, :], E1, AF.Exp, [x_r], [x_r])
    O.act(xa[:, 0:nt, 6, :], E2, AF.Exp, [x_r], [x_r])
    O.act(cdall[:, c0:c0 + nt, 0:n2], Tm, AF.Exp, [x_r], [cd_r])
    O.dma(sm_dram[t0:t0 + bw, 0:n2].rearrange("(t p) c -> p t c", p=128), a, [x_r], rs_of(sm_dram_r, t0, bw))
    O.dma(sm_dram[t0:t0 + bw, n2:2 * n2].rearrange("(t p) c -> p t c", p=128), xa[:, 0:nt, 5, :], [x_r], rs_of(sm_dram_r, t0, bw))


def phase_ssd(b, l, G, C):
    P = b.P
    O = Ops(P)
    nc = b.nc
    S = G['scr']
    masks, masks_r, onesf, onesf_r = G['masks'], G['masks_r'], G['onesf'], G['onesf_r']
    identb, identb_r = G['identb'], G['identb_r']
    xbcT, xbcT_r = G['xbcT'], G['xbcT_r']
    small, small_r = G['small'], G['small_r']
    sBT, sBT_r = S('sBT', [2, 128, NT], BF16)
    sCT, sCT_r = S('sCT', [2, 128, NT], BF16)
    sVt, sVt_r = S('sVt', [2, NT, D], BF16)
    sxs, sxs_r = S('sxs', [NT, D], BF16)
    ssm, ssm_r = S('ssm', [NT, 64], F32)
    sprev, sprev_r = S('sprev', [2, NCH, 128, D], BF16, 2 * NCH)
    sLb, sLb_r = S('sLb', [NCH, 128, D], F32, NCH)
    cwt, cw_r = C['cw']
    cbt, cb_r = C['cb']
    At, A_r = C['alog']
    dtbt, dtb_r = C['dtb']

    with ExitStack() as es0:
        cdall, cd_r = b.sb(es0, "cdall", [128, NCH, 32], F32)
        with ExitStack() as es:
            xh, xh_r = b.sb(es, "xh", [128, 12, 514], BF16)
            tmps = [b.sb(es, "ctmp", [128, 512], F32) for _ in range(2)]
            uT, uT_r = b.sb(es, "uT", [128, 12, 512], BF16)
            pX, pX_r = b.ps(es, "pX", [128, D], BF16)
            pB, pB_r = b.ps(es, "pB", [128, 1024], BF16)
            pc, pc_r = b.ps(es, "pc", [128, 512], F32)
            pL, pL_r = b.ps(es, "pL", [128, 4, 512], F32)
            xsb = [b.sb(es, "xsb", [128, D], BF16) for _ in range(2)]
            Btm, Btm_r = b.sb(es, "Btm", [128, 256], BF16)
            sml, sml_r = b.sb(es, "sml", [128, 256], F32)
            smo = [b.sb(es, "smo", [128, 64], F32) for _ in range(2)]
            PT, PT_r = b.sb(es, "PT", [128, 128], F32)
            Et, E_r = b.sb(es, "Et", [128, 128], F32)
            Vt = [b.sb(es, "Vt", [128, 2, D], BF16) for _ in range(2)]
            Vw, Vw_r = b.sb(es, "Vw", [128, 2, D], BF16)
            xab = [b.sb(es, "xab", [128, 4, 8, 32], F32) for _ in range(2)]
            xwb = [b.sb(es, "xwb", [128, 4, 32], F32) for _ in range(2)]
            Tsc, Tsc_r = b.sb(es, "Tsc", [128, 1, 4, 32], F32)
            Sf, Sf_r = b.sb(es, "Sf", [128, D], F32)
            Stmp, Stmp_r = b.sb(es, "Stmp", [128, D], F32)
            Sfb = [b.sb(es, "Sfb", [128, D], BF16) for _ in range(2)]
            Lb = [b.sb(es, "Lb", [128, D], F32) for _ in range(2)]
            O.memset('dve', Sf[:], 0.0, [Sf_r])
            for (t0, bw) in TBLKS:
                conv_silu(b, O, l, xbcT, xbcT_r, t0, bw, 12, cwt, cw_r, cbt, cb_r, xh, xh_r, tmps, uT, uT_r)
                O.dma(sBT[:, :, t0:t0 + bw].rearrange("g p t -> p g t"), uT[:, 8:10, 0:bw], [uT_r], rs_of(sBT_r, t0, bw))
                O.dma(sCT[:, :, t0:t0 + bw].rearrange("g p t -> p g t"), uT[:, 10:12, 0:bw], [uT_r], rs_of(sCT_r, t0, bw))
                nt = bw // 128
                bslot = (t0 // 512) % 2
                xa, x_r = xab[bslot]
                xw, xw_r = xwb[bslot]
                O.dma(xa[:, 0:nt, 1, :], small[t0:t0 + bw, 0:32].rearrange("(t p) c -> p t c", p=128), rs_of(small_r, t0, bw), [x_r])
                O.tt('dve', xa[:, 0:nt, 2, :], xa[:, 0:nt, 1, :], bc_lead(dtbt[:], [128, nt, 32]), ALU.add, [x_r, dtb_r], [x_r])
                softplus(b, O, xa[:, 0:nt, 7, :], xa[:, 0:nt, 2, :], xa[:, 0:nt, 3, :], xa[:, 0:nt, 4, :], [], [x_r])
                O.tt('dve', xa[:, 0:nt, 0, :], xa[:, 0:nt, 7, :], bc_lead(At[:], [128, nt, 32]), ALU.mult, [x_r, A_r], [x_r])
                block_scalars(b, O, G, t0, bw, xa, x_r, 32, 16, At, A_r, pc, pc_r, Tsc, Tsc_r, cdall, cd_r, ssm, ssm_r)
                O.tt('dve', xw[:, 0:nt, :], xa[:, 0:nt, 7, :], xa[:, 0:nt, 6, :], ALU.mult, [x_r], [xw_r])
                for ti in range(bw // 128):
                    c = t0 // 128 + ti
                    r0 = c * 128
                    tsl = slice(ti * 128, (ti + 1) * 128)
                    for ch in range(8):
                        O.tr(pX[:, ch * 128:(ch + 1) * 128], uT[:, ch, tsl], identb[:], [uT_r, identb_r], [pX_r], inc=(ch == 7))
                    for g in range(2):
                        O.tr(pB[:, g * 128:(g + 1) * 128], uT[:, 8 + g, tsl], identb[:], [uT_r, identb_r], [pB_r], inc=(g == 1))
                    xs_t, xs_r = xsb[c % 2]
                    O.copy('act', xs_t[:], pX[:], [pX_r], [xs_r])
                    O.dma(sxs[r0:r0 + 128, :], xs_t[:], [xs_r], rs_of(sxs_r, r0, 128))
                    O.copy('act', Btm[:], pB[:, 0:256], [pB_r], [Btm_r])
                    dt = xa[:, ti, 7, :]
                    dew = xw[:, ti, :]
                    vt, vt_r = Vt[c % 2]
                    pXv = pX[:].rearrange("p (h e) -> p h e", h=16)
                    for d in range(2):
                        O.tt('dve', vt[:, d, :].rearrange("p (h e) -> p h e", h=16), pXv,
                             bc_mid(dt[:, d * 16:(d + 1) * 16], [128, 16, 64]), ALU.mult, [pX_r, x_r], [vt_r])
                        O.tt('dve', Vw[:, d, :].rearrange("p (h e) -> p h e", h=16), pXv,
                             bc_mid(dew[:, d * 16:(d + 1) * 16], [128, 16, 64]), ALU.mult, [pX_r, xw_r], [Vw_r])
                        O.dma(sVt[d, r0:r0 + 128, :], vt[:, d, :], [vt_r], rs_of(sVt_r, r0, 128))
                    for d in range(2):
                        for g in range(2):
                            O.mm(pL[:, d * 2 + g, :], Btm[:, g * 128:(g + 1) * 128], Vw[:, d, g * 512:(g + 1) * 512], True, True,
                                 [Btm_r, Vw_r], [pL_r], inc=(d == 1 and g == 1))
                    sb_t, sb_r = Sfb[c % 2]
                    O.copy('act', sb_t[:], Sf[:], [Sf_r], [sb_r])
                    O.dma(sprev[0, c, :, :], sb_t[:], [sb_r], [sprev_r[c]])
                    O.tt('dve', Stmp[:].rearrange("p (h e) -> p h e", h=16), Sf[:].rearrange("p (h e) -> p h e", h=16),
                         bc_mid(cdall[:, c, 0:16], [128, 16, 64]), ALU.mult, [Sf_r, cd_r], [Stmp_r])
                    O.tt('dve', Sf[:].rearrange("p (g n) -> p g n", g=2), Stmp[:].rearrange("p (g n) -> p g n", g=2),
                         pL[:, 0:2, :], ALU.add, [Stmp_r, pL_r], [Sf_r])
                    lb_t, lb_r = Lb[c % 2]
                    O.copy('act', lb_t[:].rearrange("p (g n) -> p g n", g=2), pL[:, 2:4, :], [pL_r], [lb_r])
                    O.dma(sLb[c, :, :], lb_t[:], [lb_r], [sLb_r[c]])
            O.memset('dve', Sf[:], 0.0, [Sf_r])
            for c in BWD_ORDER:
                lb_t, lb_r = Lb[c % 2]
                sb_t, sb_r = Sfb[c % 2]
                O.dma(lb_t[:], sLb[c, :, :], [sLb_r[c]], [lb_r])
                O.copy('act', sb_t[:], Sf[:], [Sf_r], [sb_r])
                O.dma(sprev[1, c, :, :], sb_t[:], [sb_r], [sprev_r[NCH + c]])
                O.tt('dve', Stmp[:].rearrange("p (h e) -> p h e", h=16), Sf[:].rearrange("p (h e) -> p h e", h=16),
                     bc_mid(cdall[:, c, 16:32], [128, 16, 64]), ALU.mult, [Sf_r, cd_r], [Stmp_r])
                O.tt('dve', Sf[:], Stmp[:], lb_t[:], ALU.add, [Stmp_r, lb_r], [Sf_r])
            P.barrier()

        zs, zs_r = G['zs'], G['zs_r']
        dskt, dsk_r = C['dsk']
        sngt, sng_r = C['sng']
        yT, yT_r = G['yT'], G['yT_r']
        with ExitStack() as es:
            xs2 = [b.sb(es, "xs2", [128, D], BF16) for _ in range(2)]
            z2 = [b.sb(es, "z2", [128, D], BF16) for _ in range(2)]
            yacc, yacc_r = b.sb(es, "yacc", [128, D], F32)
            szt, sz_r = b.sb(es, "szt", [128, D], F32)
            ssq, ssq_r = b.sb(es, "ssq", [128, 2], F32)
            ybf, ybf_r = b.sb(es, "ybf", [128, D], BF16)
            yTs = [b.sb(es, "yTs", [128, 8, 128], BF16) for _ in range(2)]
            pYT, pYT_r = b.ps(es, "pYT", [128, D], BF16)

            def load_extra(c):
                r0 = c * 128
                x_t, x_r = xs2[c % 2]
                z_t, z_r = z2[c % 2]
                O.dma(x_t[:], sxs[r0:r0 + 128, :], rs_of(sxs_r, r0, 128), [x_r])
                O.dma(z_t[:], zs[r0:r0 + 128, :], rs_of(zs_r, r0, 128), [z_r])

            def finish(c, Yf, Yf_r, Yb, Yb_r):
                r0 = c * 128
                x_t, x_r = xs2[c % 2]
                z_t, z_r = z2[c % 2]
                O.tt('pool', yacc[:], Yf[:, 0:D], Yb[:, 0:D], ALU.add, [Yf_r, Yb_r], [yacc_r])
                O.tt('dve', szt[:].rearrange("p (h e) -> p h e", h=16), x_t[:].rearrange("p (h e) -> p h e", h=16),
                     bc_mid(dskt[:, :], [128, 16, 64]), ALU.mult, [x_r, dsk_r], [sz_r])
                O.tt('dve', yacc[:], yacc[:], szt[:], ALU.add, [yacc_r, sz_r], [yacc_r])
                O.act(szt[:], z_t[:], AF.Silu, [z_r, sz_r], [sz_r])
                O.tt('dve', yacc[:], yacc[:], szt[:], ALU.mult, [yacc_r, sz_r], [yacc_r])
                O.act(szt[:], yacc[:], AF.Square, [yacc_r, sz_r], [sz_r, ssq_r], accum=ssq[:, 0:1])
                O.act(ssq[:, 1:2], ssq[:, 0:1], AF.Sqrt, [ssq_r], [ssq_r], scale=1.0 / D, bias=EPS)
                O.recip(ssq[:, 1:2], ssq[:, 1:2], [ssq_r], [ssq_r])
                O.stt(ybf[:], yacc[:], ssq[:, 1:2], sngt[:], ALU.mult, ALU.mult, [yacc_r, ssq_r, sng_r], [ybf_r])
                yt, yt_r = yTs[c % 2]
                for k in range(8):
                    O.tr(pYT[:, k * 128:(k + 1) * 128], ybf[:, k * 128:(k + 1) * 128], identb[:], [ybf_r, identb_r], [pYT_r], inc=(k == 7))
                O.copy('act', yt[:].rearrange("p k t -> p (k t)"), pYT[:], [pYT_r], [yt_r])
                O.dma(yT[0, :, :, r0:r0 + 128].rearrange("k p t -> p k t"), yt[:], [yt_r], rs_of(yT_r, r0, 128))

            la_pass2(b, O, G, es, NG=2, NSUB=8, PV=64, QT=sCT, QT_r=sCT_r, KT=sBT, KT_r=sBT_r, g0=0,
                     Vt=sVt, Vt_r=sVt_r, vcol0=0, sm=ssm, sm_r=ssm_r, NA=16, a0=0, sprev=sprev, sprev_r=sprev_r,
                     load_extra=load_extra, finish=finish)
            P.barrier()


def la_pass2(b, O, G, es, NG, NSUB, PV, QT, QT_r, KT, KT_r, g0, Vt, Vt_r, vcol0, sm, sm_r, NA, a0, sprev, sprev_r,
             load_extra, finish):
    P = O.P
    masks, masks_r = G['masks'], G['masks_r']
    GW = NSUB * PV
    NH = NG * NSUB
    W = NG * GW
    qt = [b.sb(es, "qt", [128, NG, 128], BF16) for _ in range(2)]
    kt = [b.sb(es, "kt", [128, NG, 128], BF16) for _ in range(2)]
    vt = [b.sb(es, "vt2", [128, 2, W], BF16) for _ in range(2)]
    smt = [b.sb(es, "smt", [128, 4 * NA], F32) for _ in range(2)]
    spt = [b.sb(es, "spt", [128, 2, W], BF16) for _ in range(2)]
    Gm, Gm_r = b.sb(es, "Gm", [128, 2, NG, 128], F32)
    segL = [b.sb(es, "segL", [128, 4, 128], F32) for _ in range(2)]
    Ee = [b.sb(es, "Ee", [128, 4, 128], F32) for _ in range(2)]
    MT = [b.sb(es, "MT", [128, 4, 128], BF16) for _ in range(2)]
    ytmp, ytmp_r = b.sb(es, "ytmp", [128, NG, GW], F32)
    Yd2 = [[b.sb(es, "Yd", [128, NG, GW], F32) for _ in range(2)] for _ in range(2)]
    pending = []
    pG, pG_r = b.ps(es, "pG", [128, 512], F32)
    pS = [b.ps(es, "pS", [128, 4, 128], F32) for _ in range(2)]
    ydg, ydg_r = b.ps(es, "ydg", [128, NG, 512], F32)
    yof, yof_r = b.ps(es, "yof", [128, NG, 512], F32)
    nseg = 0
    for c in range(NCH):
        r0 = c * 128
        Yd = Yd2[c % 2]
        q_t, q_r = qt[c % 2]
        k_t, k_r = kt[c % 2]
        v_t, v_r = vt[c % 2]
        s_t, s_r = smt[c % 2]
        p_t, p_r = spt[c % 2]
        O.dma(q_t[:], QT[g0:g0 + NG, :, r0:r0 + 128].rearrange("g p t -> p g t"), rs_of(QT_r, r0, 128), [q_r])
        O.dma(k_t[:], KT[g0:g0 + NG, :, r0:r0 + 128].rearrange("g p t -> p g t"), rs_of(KT_r, r0, 128), [k_r])
        for d in range(2):
            O.dma(v_t[:, d, :], Vt[d, r0:r0 + 128, vcol0:vcol0 + W], rs_of(Vt_r, r0, 128), [v_r])
            O.dma(p_t[:, d, :], sprev[d, c, :, vcol0:vcol0 + W], [sprev_r[d * NCH + c]], [p_r])
        O.dma(s_t[:], sm[r0:r0 + 128, :], rs_of(sm_r, r0, 128), [s_r])
        load_extra(c)
        for g in range(NG):
            O.mm(pG[:, g * 128:(g + 1) * 128], k_t[:, g, :], q_t[:, g, :], True, True, [k_r, q_r], [pG_r], inc=(g == NG - 1))
        for d in range(2):
            O.tt('dve', Gm[:, d, :, :], pG[:, 0:NG * 128].rearrange("p (g l) -> p g l", g=NG),
                 bc_lead(masks[:, d, :], [128, NG, 128]), ALU.mult, [pG_r, masks_r], [Gm_r])
        rounds = [(d, h0, min(4, NH - h0)) for d in range(2) for h0 in range(0, NH, 4)]
        first_in_bank = {0: [True] * NG, 1: [True] * NG}

        def stage_a(ri, slot):
            d, h0, nh = rounds[ri]
            sl_t, sl_r = segL[slot % 2]
            ps_t, ps_r = pS[slot % 2]
            col = d * NA + a0 + h0
            O.tt('dve', sl_t[:, 0:nh, :], bc_lead(masks[:, 2 + d, :], [128, nh, 128]),
                 bc_mid(s_t[:, col:col + nh], [128, nh, 128]), ALU.mult, [masks_r, s_r], [sl_r])
            for i in range(nh):
                O.mm(ps_t[:, i, :], sl_t[:, i, :], masks[:, d, :], True, True, [sl_r, masks_r], [ps_r], inc=(i == nh - 1))

        def stage_b(ri, slot):
            d, h0, nh = rounds[ri]
            y_t, y_r = Yd[d]
            e_t, e_r = Ee[slot % 2]
            m_t, m_r = MT[slot % 2]
            ps_t, ps_r = pS[slot % 2]
            O.act(e_t[:, 0:nh, :], ps_t[:, 0:nh, :], AF.Exp, [ps_r], [e_r])
            g = h0 // NSUB
            if NSUB >= 4:
                O.tt('dve', m_t[:, 0:nh, :], e_t[:, 0:nh, :], bc_lead(Gm[:, d, g, :], [128, nh, 128]), ALU.mult, [e_r, Gm_r], [m_r])
            else:
                for i in range(nh):
                    gi = (h0 + i) // NSUB
                    O.tt('dve', m_t[:, i, :], e_t[:, i, :], Gm[:, d, gi, :], ALU.mult, [e_r, Gm_r], [m_r])
            for i in range(nh):
                h = h0 + i
                gi, sub = h // NSUB, h % NSUB
                O.mm(ydg[:, gi, sub * PV:(sub + 1) * PV], m_t[:, i, :], v_t[:, d, h * PV:(h + 1) * PV],
                     first_in_bank[d][gi], True, [m_r, v_r], [ydg_r], inc=(i == nh - 1))
                first_in_bank[d][gi] = False
            if h0 + nh >= NH:
                for g2 in range(NG):
                    O.mm(yof[:, g2, 0:GW], q_t[:, g2, :], p_t[:, d, g2 * GW:(g2 + 1) * GW], True, True, [q_r, p_r], [yof_r], inc=(g2 == NG - 1))
                rsc = s_t[:, 2 * NA + d * NA + a0:2 * NA + d * NA + a0 + NH]
                O.tt('dve', ytmp[:].rearrange("p g (s e) -> p (g s) e", s=NSUB), yof_view(yof, NG, NSUB, PV),
                     bc_mid(rsc, [128, NH, PV]), ALU.mult, [yof_r, s_r], [ytmp_r])
                O.tt('dve', y_t[:], ytmp[:], ydg[:, :, 0:GW], ALU.add, [ytmp_r, ydg_r], [y_r])

        stage_a(0, nseg)
        for ri in range(len(rounds)):
            if ri + 1 < len(rounds):
                stage_a(ri + 1, nseg + 1)
            stage_b(ri, nseg)
            nseg += 1
            if ri == 0 and pending:
                pending.pop()()
        pending.append(lambda c=c, Yd=Yd: finish(c, Yd[0][0][:].rearrange("p g w -> p (g w)"), Yd[0][1],
                                                 Yd[1][0][:].rearrange("p g w -> p (g w)"), Yd[1][1]))
    while pending:
        pending.pop()()


def yof_view(yof, NG, NSUB, PV):
    if NSUB * PV == 512:
        return yof[:].rearrange("p g (s e) -> p (g s) e", s=NSUB)
    assert NSUB == 1
    return yof[:, :, 0:PV]


MLW = 4 * 257


def phase_mlstm(b, l, G, C):
    P = b.P
    O = Ops(P)
    S = G['scr']
    masks, masks_r, onesf, onesf_r = G['masks'], G['masks_r'], G['onesf'], G['onesf_r']
    identb, identb_r = G['identb'], G['identb_r']
    mqkT, mqkT_r = G['mqkT'], G['mqkT_r']
    small, small_r = G['small'], G['small_r']
    mvo, mvo_r = G['mvo'], G['mvo_r']
    mz, mz_r = G['mz'], G['mz_r']
    mQT, mQT_r = S('mQT', [4, 128, NT], BF16)
    mKT, mKT_r = S('mKT', [4, 128, NT], BF16)
    mVt, mVt_r = S('mVt', [2, NT, MLW], BF16)
    msm, msm_r = S('msm', [NT, 16], F32)
    mprev, mprev_r = S('mprev', [2, NCH, 128, MLW], BF16, 2 * NCH)
    mLb, mLb_r = S('mLb', [NCH, 128, MLW], F32, NCH)
    mh, mh_r = S('mh', [NT, 512], F32)
    cwt, cw_r = C['mcw']
    cbt, cb_r = C['mcb']
    ibt, ib_r = C['ib']
    fbt, fb_r = C['fb']
    LNS = math.log(128.0 ** -0.5)

    with ExitStack() as es0:
        cdall, cd_r = b.sb(es0, "mcdall", [128, NCH, 8], F32)
        with ExitStack() as es:
            xh, xh_r = b.sb(es, "mxh", [128, 8, 514], BF16)
            tmps = [b.sb(es, "mctmp", [128, 512], F32) for _ in range(2)]
            uT, uT_r = b.sb(es, "muT", [128, 8, 512], BF16)
            pK, pK_r = b.ps(es, "pK", [128, 1024], BF16)
            pc, pc_r = b.ps(es, "mpc", [128, 512], F32)
            pL, pL_r = b.ps(es, "mpL", [128, 4, 512], F32)
            Ktm, Ktm_r = b.sb(es, "Ktm", [128, 512], BF16)
            vin = [b.sb(es, "vin", [128, D], BF16) for _ in range(2)]
            sml, sml_r = b.sb(es, "msml", [128, 128], F32)
            smo = [b.sb(es, "msmo", [128, 16], F32) for _ in range(2)]
            PT, PT_r = b.sb(es, "mPT", [128, 32], F32)
            Et, E_r = b.sb(es, "mEt", [128, 32], F32)
            Vt = [b.sb(es, "mVt", [128, 2, MLW], BF16) for _ in range(2)]
            Vw, Vw_r = b.sb(es, "mVw", [128, 2, MLW], BF16)
            xab = [b.sb(es, "mxab", [128, 4, 8, 8], F32) for _ in range(2)]
            xwb = [b.sb(es, "mxwb", [128, 4, 8], F32) for _ in range(2)]
            swbb = [b.sb(es, "mswb", [128, 4, 8], F32) for _ in range(2)]
            Tsc, Tsc_r = b.sb(es, "mTsc", [128, 1, 4, 8], F32)
            Sf, Sf_r = b.sb(es, "mSf", [128, MLW], F32)
            Stmp, Stmp_r = b.sb(es, "mStmp", [128, MLW], F32)
            Sfb = [b.sb(es, "mSfb", [128, MLW], BF16) for _ in range(2)]
            Lb = [b.sb(es, "mLb", [128, MLW], F32) for _ in range(2)]
            O.memset('dve', Sf[:], 0.0, [Sf_r])
            for (t0, bw) in TBLKS:
                conv_silu(b, O, l, mqkT, mqkT_r, t0, bw, 8, cwt, cw_r, cbt, cb_r, xh, xh_r, tmps, uT, uT_r)
                O.dma(mQT[:, :, t0:t0 + bw].rearrange("g p t -> p g t"), uT[:, 0:4, 0:bw], [uT_r], rs_of(mQT_r, t0, bw))
                O.dma(mKT[:, :, t0:t0 + bw].rearrange("g p t -> p g t"), uT[:, 4:8, 0:bw], [uT_r], rs_of(mKT_r, t0, bw))
                nt = bw // 128
                bslot = (t0 // 512) % 2
                xa, x_r = xab[bslot]
                xw, xw_r = xwb[bslot]
                swb, sw_r = swbb[bslot]
                O.dma(xa[:, 0:nt, 1:3, :], small[t0:t0 + bw, 32:48].rearrange("(t p) (s c) -> p t s c", p=128, s=2), rs_of(small_r, t0, bw), [x_r])
                O.tt('dve', swb[:, 0:nt, :], xa[:, 0:nt, 1, :], bc_lead(ibt[:], [128, nt, 8]), ALU.add, [x_r, ib_r], [sw_r])
                O.act(swb[:, 0:nt, :], swb[:, 0:nt, :], AF.Exp, [sw_r], [sw_r])
                O.ts('dve', swb[:, 0:nt, :], swb[:, 0:nt, :], 128.0 ** -0.5, None, ALU.mult, None, [sw_r], [sw_r])
                O.tt('dve', xa[:, 0:nt, 3, :], xa[:, 0:nt, 2, :], bc_lead(fbt[:], [128, nt, 8]), ALU.add, [x_r, fb_r], [x_r])
                O.ts('dve', xa[:, 0:nt, 3, :], xa[:, 0:nt, 3, :], -1.0, None, ALU.mult, None, [x_r], [x_r])
                softplus(b, O, xa[:, 0:nt, 7, :], xa[:, 0:nt, 3, :], xa[:, 0:nt, 4, :], xa[:, 0:nt, 5, :], [], [x_r])
                O.ts('dve', xa[:, 0:nt, 0, :], xa[:, 0:nt, 7, :], -1.0, None, ALU.mult, None, [x_r], [x_r])
                block_scalars(b, O, G, t0, bw, xa, x_r, 8, 4, None, None, pc, pc_r, Tsc, Tsc_r, cdall, cd_r, msm, msm_r)
                O.tt('dve', xw[:, 0:nt, :], swb[:, 0:nt, :], xa[:, 0:nt, 6, :], ALU.mult, [x_r, sw_r], [xw_r])
                for ti in range(bw // 128):
                    c = t0 // 128 + ti
                    r0 = c * 128
                    tsl = slice(ti * 128, (ti + 1) * 128)
                    for h in range(4):
                        O.tr(pK[:, h * 128:(h + 1) * 128], uT[:, 4 + h, tsl], identb[:], [uT_r, identb_r], [pK_r], inc=(h == 3))
                    O.copy('act', Ktm[:], pK[:, 0:512], [pK_r], [Ktm_r])
                    v_t, v_r = vin[c % 2]
                    O.dma(v_t[:], mvo[r0:r0 + 128, 0:D], rs_of(mvo_r, r0, 128), [v_r])
                    vt, vt_r = Vt[c % 2]
                    vv = v_t[:].rearrange("p (h e) -> p h e", h=4)
                    for d in range(2):
                        sw = swb[:, ti, d * 4:(d + 1) * 4]
                        ww = xw[:, ti, d * 4:(d + 1) * 4]
                        vo = vt[:, d, :].rearrange("p (h e) -> p h e", h=4)
                        wo = Vw[:, d, :].rearrange("p (h e) -> p h e", h=4)
                        O.tt('dve', vo[:, :, 0:256], vv, bc_mid(sw, [128, 4, 256]), ALU.mult, [v_r, sw_r], [vt_r])
                        O.copy('dve', vo[:, :, 256], sw, [sw_r], [vt_r])
                        O.tt('dve', wo[:, :, 0:256], vv, bc_mid(ww, [128, 4, 256]), ALU.mult, [v_r, xw_r], [Vw_r])
                        O.copy('dve', wo[:, :, 256], ww, [xw_r], [Vw_r])
                        O.dma(mVt[d, r0:r0 + 128, :], vt[:, d, :], [vt_r], rs_of(mVt_r, r0, 128))
                    sb_t, sb_r = Sfb[c % 2]
                    lb_t, lb_r = Lb[c % 2]
                    for d in range(2):
                        for h in range(4):
                            O.mm(pL[:, h, 0:257], Ktm[:, h * 128:(h + 1) * 128], Vw[:, d, h * 257:(h + 1) * 257], True, True,
                                 [Ktm_r, Vw_r], [pL_r], inc=(h == 3))
                        if d == 0:
                            O.copy('act', sb_t[:], Sf[:], [Sf_r], [sb_r])
                            O.dma(mprev[0, c, :, :], sb_t[:], [sb_r], [mprev_r[c]])
                            O.tt('dve', Stmp[:].rearrange("p (h e) -> p h e", h=4), Sf[:].rearrange("p (h e) -> p h e", h=4),
                                 bc_mid(cdall[:, c, 0:4], [128, 4, 257]), ALU.mult, [Sf_r, cd_r], [Stmp_r])
                            O.tt('dve', Sf[:].rearrange("p (h e) -> p h e", h=4), Stmp[:].rearrange("p (h e) -> p h e", h=4),
                                 pL[:, :, 0:257], ALU.add, [Stmp_r, pL_r], [Sf_r])
                        else:
                            O.copy('act', lb_t[:].rearrange("p (h e) -> p h e", h=4), pL[:, :, 0:257], [pL_r], [lb_r])
                            O.dma(mLb[c, :, :], lb_t[:], [lb_r], [mLb_r[c]])
            O.memset('dve', Sf[:], 0.0, [Sf_r])
            for c in BWD_ORDER:
                lb_t, lb_r = Lb[c % 2]
                sb_t, sb_r = Sfb[c % 2]
                O.dma(lb_t[:], mLb[c, :, :], [mLb_r[c]], [lb_r])
                O.copy('act', sb_t[:], Sf[:], [Sf_r], [sb_r])
                O.dma(mprev[1, c, :, :], sb_t[:], [sb_r], [mprev_r[NCH + c]])
                O.tt('dve', Stmp[:].rearrange("p (h e) -> p h e", h=4), Sf[:].rearrange("p (h e) -> p h e", h=4),
                     bc_mid(cdall[:, c, 4:8], [128, 4, 257]), ALU.mult, [Sf_r, cd_r], [Stmp_r])
                O.tt('dve', Sf[:], Stmp[:], lb_t[:], ALU.add, [Stmp_r, lb_r], [Sf_r])
            P.barrier()

        mngt, mng_r = C['mng']
        yT, yT_r = G['yT'], G['yT_r']
        for pair in range(2):
            with ExitStack() as es:
                dn, dn_r = b.sb(es, "dn", [128, 8], F32)
                hp, hp_r = b.sb(es, "hp", [128, 2, 256], F32)
                hq, hq_r = b.sb(es, "hq", [128, 2, 256], F32)
                hall = [b.sb(es, "hall", [128, D], F32) for _ in range(2)]
                o2 = [b.sb(es, "o2", [128, D], BF16) for _ in range(2)]
                z2 = [b.sb(es, "mz2", [128, D], BF16) for _ in range(2)]
                sg, sg_r = b.sb(es, "sg", [128, D], F32)
                st4, st4_r = b.sb(es, "st4", [128, 8], F32)
                ybf, ybf_r = b.sb(es, "mybf", [128, D], BF16)
                yTs = [b.sb(es, "myTs", [128, 8, 128], BF16) for _ in range(2)]
                pYT, pYT_r = b.ps(es, "mpYT", [128, D], BF16)

                def load_extra(c, pair=pair):
                    if pair == 0:
                        return
                    r0 = c * 128
                    h_t, h_r = hall[c % 2]
                    o_t, o_r = o2[c % 2]
                    z_t, z_r = z2[c % 2]
                    O.dma(h_t[:, 0:512], mh[r0:r0 + 128, :], rs_of(mh_r, r0, 128), [h_r])
                    O.dma(o_t[:], mvo[r0:r0 + 128, D:2 * D], rs_of(mvo_r, r0, 128), [o_r])
                    O.dma(z_t[:], mz[r0:r0 + 128, :], rs_of(mz_r, r0, 128), [z_r])

                def finish(c, Yf, Yf_r, Yb, Yb_r, pair=pair):
                    r0 = c * 128
                    h_t, h_r = hall[c % 2]
                    Yfv = Yf.rearrange("p (g w) -> p g w", g=2)
                    Ybv = Yb.rearrange("p (g w) -> p g w", g=2)
                    O.act(dn[:, 0:2], Yfv[:, :, 256], AF.Abs, [Yf_r], [dn_r])
                    O.act(dn[:, 2:4], Ybv[:, :, 256], AF.Abs, [Yb_r], [dn_r])
                    O.ts('dve', dn[:, 0:4], dn[:, 0:4], 1.0, None, ALU.max, None, [dn_r], [dn_r])
                    O.recip(dn[:, 4:8], dn[:, 0:4], [dn_r], [dn_r])
                    O.tt('dve', hp[:], Yfv[:, :, 0:256], bc_mid(dn[:, 4:6], [128, 2, 256]), ALU.mult, [Yf_r, dn_r], [hp_r])
                    O.tt('dve', hq[:], Ybv[:, :, 0:256], bc_mid(dn[:, 6:8], [128, 2, 256]), ALU.mult, [Yb_r, dn_r], [hq_r])
                    if pair == 0:
                        O.tt('pool', hp[:], hp[:], hq[:], ALU.add, [hp_r, hq_r], [hp_r])
                        O.dma(mh[r0:r0 + 128, :], hp[:].rearrange("p g e -> p (g e)"), [hp_r], rs_of(mh_r, r0, 128))
                        return
                    o_t, o_r = o2[c % 2]
                    z_t, z_r = z2[c % 2]
                    O.tt('pool', h_t[:, 512:1024], hp[:].rearrange("p g e -> p (g e)"), hq[:].rearrange("p g e -> p (g e)"),
                         ALU.add, [hp_r, hq_r], [h_r])
                    O.act(sg[:], o_t[:], AF.Sigmoid, [o_r], [sg_r])
                    O.tt('dve', h_t[:], h_t[:], sg[:], ALU.mult, [h_r, sg_r], [h_r])
                    O.tt('dve', sg[:], h_t[:], h_t[:], ALU.mult, [h_r, sg_r], [sg_r])
                    O.reduce(st4[:, 0:4], sg[:].rearrange("p (h e) -> p h e", h=4), ALU.add, [sg_r], [st4_r])
                    O.act(st4[:, 4:8], st4[:, 0:4], AF.Sqrt, [st4_r], [st4_r], scale=1.0 / 256, bias=EPS)
                    O.recip(st4[:, 4:8], st4[:, 4:8], [st4_r], [st4_r])
                    O.tt('dve', h_t[:].rearrange("p (h e) -> p h e", h=4), h_t[:].rearrange("p (h e) -> p h e", h=4),
                         bc_mid(st4[:, 4:8], [128, 4, 256]), ALU.mult, [h_r, st4_r], [h_r])
                    O.tt('dve', h_t[:], h_t[:], mngt[:], ALU.mult, [h_r, mng_r], [h_r])
                    O.act(sg[:], z_t[:], AF.Silu, [z_r, sg_r], [sg_r])
                    O.tt('dve', ybf[:], h_t[:], sg[:], ALU.mult, [h_r, sg_r], [ybf_r])
                    yt, yt_r = yTs[c % 2]
                    for k in range(8):
                        O.tr(pYT[:, k * 128:(k + 1) * 128], ybf[:, k * 128:(k + 1) * 128], identb[:], [ybf_r, identb_r], [pYT_r], inc=(k == 7))
                    O.copy('act', yt[:].rearrange("p k t -> p (k t)"), pYT[:], [pYT_r], [yt_r])
                    O.dma(yT[2, :, :, r0:r0 + 128].rearrange("k p t -> p k t"), yt[:], [yt_r], rs_of(yT_r, r0, 128))

                la_pass2(b, O, G, es, NG=2, NSUB=1, PV=257, QT=mQT, QT_r=mQT_r, KT=mKT, KT_r=mKT_r, g0=2 * pair,
                         Vt=mVt, Vt_r=mVt_r, vcol0=2 * pair * 257, sm=msm, sm_r=msm_r, NA=4, a0=2 * pair,
                         sprev=mprev, sprev_r=mprev_r, load_extra=load_extra, finish=finish)
                P.barrier()


def phase_attn(b, l, G, C):
    P = b.P
    O = Ops(P)
    S = G['scr']
    identb, identb_r = G['identb'], G['identb_r']
    onesb, onesb_r = G['onesb'], G['onesb_r']
    aqk, aqk_r = G['aqk'], G['aqk_r']
    avz, avz_r = G['avz'], G['avz_r']
    azT, azT_r = G['azT'], G['azT_r']
    yT, yT_r = G['yT'], G['yT_r']
    cos_in, sin_in = G['cos_in'], G['sin_in']
    aQT, aQT_r = S('aQT', [8, 128, NT], BF16)
    aKT, aKT_r = S('aKT', [8, 128, NT], BF16)
    gqt, gq_r = C['gq']
    gkt, gk_r = C['gk']
    lampt, lamp_r = C['lamp']
    subgt, subg_r = C['subg']
    lamit, lami_r = C['lami']

    with ExitStack() as es0:
        sc, sc_r = b.sb(es0, "asc", [128, 16], F32)
        ggt, gg_r = b.sb(es0, "ggt", [128, 2, 64], F32)
        tmp64, tmp64_r = b.sb(es0, "tmp64", [128, 64], F32)
        O.ts('dve', ggt[:, 0, :], gqt[:], 0.125, None, ALU.mult, None, [gq_r], [gg_r])
        O.copy('dve', ggt[:, 1, :], gkt[:], [gk_r], [gg_r])
        O.reduce(sc[:, 4:6], ggt[:], ALU.max, [gg_r], [sc_r], absval=True)
        O.tt('dve', sc[:, 6:7], sc[:, 4:5], sc[:, 5:6], ALU.mult, [sc_r], [sc_r])
        O.ts('dve', sc[:, 0:1], sc[:, 6:7], -64.0, None, ALU.mult, None, [sc_r], [sc_r])
        for i in range(2):
            O.tt('dve', tmp64[:], lampt[:, 2 * i, :], lampt[:, 2 * i + 1, :], ALU.mult, [lamp_r], [tmp64_r])
            O.reduce(sc[:, 7 + i:8 + i], tmp64[:], ALU.add, [tmp64_r], [sc_r])
        O.act(sc[:, 7:9], sc[:, 7:9], AF.Exp, [sc_r], [sc_r])
        O.tt('dve', sc[:, 9:10], sc[:, 7:8], sc[:, 8:9], ALU.subtract, [sc_r], [sc_r])
        O.tt('dve', sc[:, 1:2], sc[:, 9:10], lamit[:, 0:1], ALU.add, [sc_r, lami_r], [sc_r])
        O.ts('dve', sc[:, 2:3], sc[:, 1:2], -1.0, None, ALU.mult, None, [sc_r], [sc_r])
        O.ts('dve', sc[:, 10:11], lamit[:, 0:1], -1.0, 1.0, ALU.mult, ALU.add, [lami_r], [sc_r])
        O.tt('dve', sc[:, 3:4], sc[:, 10:11], subgt[:, 0:1], ALU.mult, [sc_r, subg_r], [sc_r])

        with ExitStack() as es:
            qk = [b.sb(es, "qkraw", [128, 2048], BF16) for _ in range(2)]
            cs = [b.sb(es, "cs", [128, 2, 64], F32) for _ in range(2)]
            sq, sq_r = b.sb(es, "sq", [128, 2048], F32)
            xn, xn_r = b.sb(es, "xn", [128, 2048], F32)
            t2, t2_r = b.sb(es, "t2", [128, 2048], F32)
            st, st_r = b.sb(es, "ast", [128, 64], F32)
            xr, xr_r = b.sb(es, "xr", [128, 2048], BF16)
            pQ = [b.ps(es, "pQ", [128, 1024], BF16) for _ in range(2)]
            qTt = [b.sb(es, "qTt", [128, 2, 8, 128], BF16) for _ in range(2)]
            for i in range(NCH):
                r0 = i * 128
                q_t, q_r = qk[i % 2]
                O.dma(q_t[:], aqk[r0:r0 + 128, :], rs_of(aqk_r, r0, 128), [q_r])
                O.tt('dve', sq[:], q_t[:], q_t[:], ALU.mult, [q_r], [sq_r])
                O.reduce(st[:, 0:32], sq[:].rearrange("p (g e) -> p g e", e=64), ALU.add, [sq_r], [st_r])
                O.act(st[:, 32:64], st[:, 0:32], AF.Sqrt, [st_r], [st_r], scale=1.0 / 64, bias=EPS)
                O.recip(st[:, 32:64], st[:, 32:64], [st_r], [st_r])
                O.tt('dve', xn[:].rearrange("p (g e) -> p g e", e=64), q_t[:].rearrange("p (g e) -> p g e", e=64),
                     bc_mid(st[:, 32:64], [128, 32, 64]), ALU.mult, [q_r, st_r], [xn_r])
                for hf in range(2):
                    xv = xn[:, hf * 1024:(hf + 1) * 1024].rearrange("p (g e) -> p g e", e=64)
                    O.tt('pool', xv, xv, bc_lead(ggt[:, hf, :], [128, 16, 64]), ALU.mult, [xn_r, gg_r], [xn_r])
                if i >= 2:
                    c_t, c_r = cs[i % 2]
                    lt = r0 - TC
                    O.dma(c_t[:, 0, :], cos_in[lt:lt + 128, :], [], [c_r])
                    O.dma(c_t[:, 1, :], sin_in[lt:lt + 128, :], [], [c_r])
                    xg = xn[:].rearrange("p (g r q e) -> p g r q e", g=32, r=2, q=2, e=16)
                    tg = t2[:].rearrange("p (g r q e) -> p g r q e", g=32, r=2, q=2, e=16)
                    sv = c_t[:, 1, :].rearrange("p (r q e) -> p r q e", r=2, q=2, e=16)
                    for qq in range(2):
                        O.tt('dve', tg[:, :, :, qq, :], xg[:, :, :, 1 - qq, :],
                             sv[:, :, qq, :].unsqueeze(1).to_broadcast([128, 32, 2, 16]), ALU.mult, [xn_r, c_r], [t2_r])
                    O.tt('pool', xn[:].rearrange("p (g e) -> p g e", e=64), xn[:].rearrange("p (g e) -> p g e", e=64),
                         bc_lead(c_t[:, 0, :], [128, 32, 64]), ALU.mult, [xn_r, c_r], [xn_r])
                    O.tt('dve', xr[:], xn[:], t2[:], ALU.add, [xn_r, t2_r], [xr_r])
                else:
                    O.copy('dve', xr[:], xn[:], [xn_r], [xr_r])
                qt_t, qt_r = qTt[i % 2]
                for w in range(2):
                    p_t, p_r = pQ[w]
                    for h in range(8):
                        O.tr(p_t[:, h * 128:(h + 1) * 128], xr[:, w * 1024 + h * 128:w * 1024 + (h + 1) * 128], identb[:],
                             [xr_r, identb_r], [p_r], inc=(h == 7))
                    O.copy('act', qt_t[:, w, :, :].rearrange("p h t -> p (h t)"), p_t[:], [p_r], [qt_r])
                O.dma(aQT[:, :, r0:r0 + 128].rearrange("h p t -> p h t"), qt_t[:, 0, :, :], [qt_r], rs_of(aQT_r, r0, 128))
                O.dma(aKT[:, :, r0:r0 + 128].rearrange("h p t -> p h t"), qt_t[:, 1, :, :], [qt_r], rs_of(aKT_r, r0, 128))
            P.barrier()

        with ExitStack() as es:
            KTs = [b.sb(es, "KTs", [128, NT], BF16) for _ in range(2)]
            Vh = [b.sb(es, "Vh", [128, NCH, 128], BF16) for _ in range(2)]
            qb = [b.sb(es, "qb", [128, 512], BF16) for _ in range(2)]
            zb = [b.sb(es, "zb", [128, 512], BF16) for _ in range(2)]
            Pt = [b.sb(es, "Pt", [128, 2, 512], BF16) for _ in range(3)]
            rc, rc_r = b.sb(es, "rc", [128, 2, 512], F32)
            accs, accs_r = b.sb(es, "accs", [128, 2, 512], F32)
            o0, o0_r = b.sb(es, "o0", [128, 512], F32)
            o1, o1_r = b.sb(es, "o1", [128, 512], F32)
            osq, osq_r = b.sb(es, "osq", [128, 512], BF16)
            rst, rst_r = b.sb(es, "rst", [128, 512], F32)
            szt, sz_r = b.sb(es, "aszt", [128, 512], F32)
            yo = [b.sb(es, "yo", [128, 512], BF16) for _ in range(2)]
            pSc = [b.ps(es, "pSc", [128, 2, 512], F32) for _ in range(2)]
            pO, pO_r = b.ps(es, "pO", [128, 2, 512], F32)
            pSm, pSm_r = b.ps(es, "pSm", [128, 2, 512], F32)
            pSmA_r = Res('pSmA')
            it = 0
            nb = 0
            for h in range(8):
                k_t, k_r = KTs[h % 2]
                v_t, v_r = Vh[h % 2]
                O.dma(k_t[:], aKT[h, :, :], aKT_r, [k_r])
                for q4 in range(0, NCH, 11):
                    O.dma(v_t[:, q4:q4 + 11, :], avz[q4 * 128:(q4 + 11) * 128, h * 128:(h + 1) * 128].rearrange("(c p) e -> p c e", p=128),
                          rs_of(avz_r, q4 * 128, 11 * 128), [v_r])
                for (t0, bw) in TBLKS:
                    nk = 2 if t0 < TC else NCH
                    q_t, q_r = qb[nb % 2]
                    z_t, z_r = zb[nb % 2]
                    y_t, y_r = yo[nb % 2]
                    nb += 1
                    O.dma(q_t[:, 0:bw], aQT[h, :, t0:t0 + bw], rs_of(aQT_r, t0, bw), [q_r])
                    O.dma(z_t[:, 0:bw], azT[h, :, t0:t0 + bw], rs_of(azT_r, t0, bw), [z_r])
                    def emit_qk(kc, slot):
                        ps_t, ps_r = pSc[slot % 2]
                        for cm in range(2):
                            O.mm(ps_t[:, cm, 0:bw], k_t[cm * 64:(cm + 1) * 64, kc * 128:(kc + 1) * 128],
                                 q_t[cm * 64:(cm + 1) * 64, 0:bw], True, True, [k_r, q_r], [ps_r], inc=(cm == 1))
                    emit_qk(0, it)
                    for kc in range(nk):
                        ps_t, ps_r = pSc[it % 2]
                        p_t, p_r = Pt[it % 3]
                        if kc + 1 < nk:
                            emit_qk(kc + 1, it + 1)
                        O.act(p_t[:, :, 0:bw], ps_t[:, :, 0:bw], AF.Exp, [ps_r, sc_r], [p_r], bias=sc[:, 0:1])
                        for cm in range(2):
                            O.mm(pO[:, cm, 0:bw], v_t[:, kc, :], p_t[:, cm, 0:bw], kc == 0, kc == nk - 1, [v_r, p_r], [pO_r], inc=(cm == 1))
                        O.mm(pSm[:, 0, 0:bw], onesb[:, :], p_t[:, 0, 0:bw], kc == 0, kc == nk - 1, [onesb_r, p_r], [pSmA_r])
                        if kc == 0:
                            O.copy('dve', pSm[:, 1, 0:bw], p_t[:, 1, 0:bw], [p_r], [pSm_r])
                        else:
                            O.tt('dve', pSm[:, 1, 0:bw], pSm[:, 1, 0:bw], p_t[:, 1, 0:bw], ALU.add, [pSm_r, p_r], [pSm_r])
                        it += 1
                    O.copy('dve', accs[:, 1, 0:bw], pSm[:, 1, 0:bw], [pSm_r], [accs_r])
                    O.mm(pSm[:, 1, 0:bw], G['onesf'][:, :], accs[:, 1, 0:bw], True, True, [G['onesf_r'], accs_r], [pSm_r])
                    O.P.op('dve', lambda e, bw=bw: e.reciprocal(out=rc[:, :, 0:bw], in_=pSm[:, :, 0:bw]), [pSm_r, pSmA_r], [rc_r])
                    O.tt('dve', o0[:, 0:bw], pO[:, 0, 0:bw], rc[:, 0, 0:bw], ALU.mult, [pO_r, rc_r], [o0_r])
                    O.tt('dve', o1[:, 0:bw], pO[:, 1, 0:bw], rc[:, 1, 0:bw], ALU.mult, [pO_r, rc_r], [o1_r])
                    O.stt(o0[:, 0:bw], o1[:, 0:bw], sc[:, 2:3], o0[:, 0:bw], ALU.mult, ALU.add, [o1_r, o0_r, sc_r], [o0_r])
                    O.act(osq[:, 0:bw], o0[:, 0:bw], AF.Square, [o0_r], [osq_r])
                    ps_t, ps_r = pSc[it % 2]
                    it += 1
                    O.mm(ps_t[:, 0, 0:bw], onesb[:, :], osq[:, 0:bw], True, True, [onesb_r, osq_r], [ps_r])
                    O.act(rst[:, 0:bw], ps_t[:, 0, 0:bw], AF.Sqrt, [ps_r], [rst_r], scale=1.0 / 128, bias=EPS)
                    O.recip(rst[:, 0:bw], rst[:, 0:bw], [rst_r], [rst_r])
                    O.act(szt[:, 0:bw], z_t[:, 0:bw], AF.Silu, [z_r], [sz_r])
                    O.stt(o0[:, 0:bw], o0[:, 0:bw], sc[:, 3:4], rst[:, 0:bw], ALU.mult, ALU.mult, [o0_r, sc_r, rst_r], [o0_r])
                    O.tt('dve', y_t[:, 0:bw], o0[:, 0:bw], szt[:, 0:bw], ALU.mult, [o0_r, sz_r], [y_r])
                    O.dma(yT[1, h, :, t0:t0 + bw], y_t[:, 0:bw], [y_r], rs_of(yT_r, t0, bw))
            P.barrier()


def phase_merge(b, l, G, C):
    P = b.P
    O = Ops(P)
    yT, yT_r = G['yT'], G['yT_r']
    gtT, gtT_r = G['gtT'], G['gtT_r']
    w_br, w_out = G['w_br'], G['w_out']
    xsrc, xsrc_r, xdst, xdst_r, dst_off = G['xsrc'], G['xsrc_r'], G['xdst'], G['xdst_r'], G['dst_off']
    gbc, gbc_r = G['gbc'], G['gbc_r']
    with ExitStack() as es:
        stg = [b.sb(es, "wstg", [128, D], F32) for _ in range(2)]
        Wb, Wb_r = b.sb(es, "Wb", [128, 3, 8, D], BF16)
        Wo, Wo_r = b.sb(es, "Wo", [128, 8, D], BF16)
        yb = [b.sb(es, "yb", [128, 3, 8, 256], BF16) for _ in range(2)]
        gb = [b.sb(es, "gb", [128, 8, 256], BF16) for _ in range(2)]
        sg, sg_r = b.sb(es, "msg", [128, 8, 256], F32)
        macc, macc_r = b.sb(es, "macc", [128, 8, 256], F32)
        mt = [b.sb(es, "mtmp", [128, 512], F32) for _ in range(2)]
        mxT, mxT_r = b.sb(es, "mxT", [128, 8, 256], BF16)
        xin_t = [b.sb(es, "xin_t", [128, D], F32) for _ in range(2)]
        xo_t = [b.sb(es, "xo_t", [128, D], F32) for _ in range(2)]
        pm_ = [b.ps(es, "pmg", [128, 512], F32) for _ in range(4)]
        n = 0
        for br in range(4):
            for kc in range(8):
                st, st_r = stg[n % 2]
                n += 1
                src = w_br[l, br, kc * 128:(kc + 1) * 128, :] if br < 3 else w_out[l, kc * 128:(kc + 1) * 128, :]
                O.dma(st[:], src, [], [st_r])
                dst = Wb[:, br, kc, :] if br < 3 else Wo[:, kc, :]
                O.copy('pool', dst, st[:], [st_r], [Wb_r if br < 3 else Wo_r])
        npm = 0
        nm = 0
        for bi, (t0, bw) in enumerate([(256 * i, 256) for i in range(NT // 256)]):
            y_t, y_r = yb[bi % 2]
            for br in range(3):
                O.dma(y_t[:, br, :, 0:bw], yT[br, :, :, t0:t0 + bw].rearrange("k p t -> p k t"), rs_of(yT_r, t0, bw), [y_r])
            for br in range(3):
                g_t, g_r = gb[(bi * 3 + br) % 2]
                O.dma(g_t[:, :, 0:bw], gtT[br * 8:(br + 1) * 8, :, t0:t0 + bw].rearrange("k p t -> p k t"), rs_of(gtT_r, t0, bw), [g_r])
                O.act(sg[:, :, 0:bw], g_t[:, :, 0:bw], AF.Sigmoid, [g_r], [sg_r])
                for fc in range(8):
                    p_t, p_r = pm_[npm % 4]
                    npm += 1
                    for kc in range(8):
                        O.mm(p_t[:, 0:bw], Wb[:, br, kc, fc * 128:(fc + 1) * 128], y_t[:, br, kc, 0:bw], kc == 0, kc == 7,
                             [Wb_r, y_r], [p_r], inc=(kc == 7))
                    if br == 0:
                        O.tt('dve', macc[:, fc, 0:bw], p_t[:, 0:bw], sg[:, fc, 0:bw], ALU.mult, [p_r, sg_r], [macc_r])
                    else:
                        m_t, m_r = mt[nm % 2]
                        nm += 1
                        O.tt('dve', m_t[:, 0:bw], p_t[:, 0:bw], sg[:, fc, 0:bw], ALU.mult, [p_r, sg_r], [m_r])
                        if br == 1:
                            O.tt('pool', macc[:, fc, 0:bw], macc[:, fc, 0:bw], m_t[:, 0:bw], ALU.add, [macc_r, m_r], [macc_r])
                        else:
                            O.tt('pool', mxT[:, fc, 0:bw], macc[:, fc, 0:bw], m_t[:, 0:bw], ALU.add, [macc_r, m_r], [mxT_r])
            j = 1 if t0 < TC else 0
            for ti in range(bw // 128):
                r0 = t0 + ti * 128
                if r0 < dst_off:
                    continue
                xi, xi_r = xin_t[ti % 2]
                xo, xo_r = xo_t[ti % 2]
                O.dma(xi[:], xsrc[r0:r0 + 128, :], rs_of(xsrc_r, r0, 128), [xi_r])
                for hf in range(2):
                    p_t, p_r = pm_[npm % 4]
                    npm += 1
                    for kc in range(8):
                        O.mm(p_t[:, :], mxT[:, kc, ti * 128:(ti + 1) * 128], Wo[:, kc, hf * 512:(hf + 1) * 512], kc == 0, kc == 7,
                             [mxT_r, Wo_r], [p_r], inc=(kc == 7))
                    O.tt('dve', xo[:, hf * 512:(hf + 1) * 512], p_t[:, :], gbc[:, l, j, hf * 512:(hf + 1) * 512], ALU.mult,
                         [p_r, gbc_r], [xo_r])
                O.tt('pool', xo[:], xo[:], xi[:], ALU.add, [xo_r, xi_r], [xo_r])
                O.dma(xdst[r0 - dst_off:r0 - dst_off + 128, :], xo[:], [xo_r], rs_of(xdst_r, r0, 128))
        P.barrier()
```

```python
import math
import numpy as np
import ml_dtypes
from contextlib import ExitStack
import concourse.bass as bass
import concourse.mybir as mybir
from concourse.bass_utils import run_bass_kernel_spmd

F32 = mybir.dt.float32
BF16 = mybir.dt.bfloat16
AF = mybir.ActivationFunctionType
ALU = mybir.AluOpType
AX = mybir.AxisListType

ENGS = ['pe', 'act', 'dve', 'pool', 'sp']
NDMA = {'sp': 12, 'act': 6, 'pool': 6}
SAME_ENG_SYNC = True
INTERNAL_OK = 'ALL'

DEPTH = 4
D = 1024
TC = 256
TL = 8192
NT = TC + TL
NCH = NT // 128
D_IN = 13872
EPS = 1e-6
FWD_ORDER = list(range(NCH))
BWD_ORDER = [1, 0] + list(range(NCH - 1, 1, -1))


class Res:
    __slots__ = ('name', 'w', 'r', 'pr', 'excl')

    def __init__(self, name='', excl=False):
        self.name = name
        self.excl = excl
        self.w = {}
        self.r = {}
        self.pr = {}


class Prog:
    def __init__(self, nc):
        self.nc = nc
        self.ops = {e: [] for e in ENGS}
        self.cnt = {e: 0 for e in ENGS}
        self.known = {e: {} for e in ENGS}
        self.dcnt = {}
        self.dnext = {q: 0 for q in NDMA}
        self.rr = 0
        self.floor = {}

    def barrier(self):
        for e in ENGS:
            if self.cnt[e] > 0:
                self.floor[e] = self.cnt[e]
        for sk, v in self.dcnt.items():
            self.floor[sk] = v

    def _deps(self, eng, reads, writes, extra=(), part=True):
        toks = {}

        def add(sk, v):
            if toks.get(sk, 0) < v:
                toks[sk] = v
        for r in reads:
            for sk, v in r.w.items():
                add(sk, v)
        for w in writes:
            if w.r or not part or w.excl:
                for sk, v in w.w.items():
                    add(sk, v)
            for sk, v in w.r.items():
                add(sk, v)
            for sk, v in w.pr.items():
                add(sk, v)
        for sk, v in extra:
            add(sk, v)
        for sk, v in self.floor.items():
            add(sk, v)
        waits = []
        kn = self.known[eng]
        for sk, v in toks.items():
            if sk == eng and (eng == 'pe' or not SAME_ENG_SYNC):
                continue
            if kn.get(sk, 0) < v:
                kn[sk] = v
                waits.append((sk, v))
        return waits

    def _mark(self, tok, reads, writes, part=True):
        sk, v = tok
        for w in writes:
            if w.r or not part or w.excl:
                pr = dict(w.r)
                for k2, v2 in w.w.items():
                    if pr.get(k2, 0) < v2:
                        pr[k2] = v2
                w.pr = pr
                w.w = {sk: v}
                w.r = {}
            elif w.w.get(sk, 0) < v:
                w.w[sk] = v
        for r in reads:
            if r in writes:
                continue
            if r.r.get(sk, 0) < v:
                r.r[sk] = v

    def op(self, eng, fn, reads=(), writes=(), inc=True):
        xr = [r for r in reads if r.excl and r not in writes]
        if xr:
            writes = list(writes) + xr
        waits = self._deps(eng, reads, writes)
        if inc:
            self.cnt[eng] += 1
            tok = (eng, self.cnt[eng])
        else:
            tok = (eng, self.cnt[eng] + 1)
        self.ops[eng].append((waits, fn, eng if inc else None, 1))
        self._mark(tok, reads, writes)

    def dma(self, q, out, in_, reads=(), writes=(), **kw):
        if q is None:
            q = ('sp', 'sp', 'act')[self.rr % 3]
            self.rr += 1
        j = self.dnext[q]
        self.dnext[q] = (j + 1) % NDMA[q]
        sk = '%s_d%d' % (q, j)
        prev = self.dcnt.get(sk, 0)
        waits = self._deps(q, reads, writes, extra=[(sk, prev)] if prev else [])
        self.dcnt[sk] = prev + 16
        tok = (sk, prev + 16)
        self.ops[q].append((waits, lambda e: e.dma_start(out=out, in_=in_, **kw), sk, 16))
        self._mark(tok, reads, writes)

    def finish(self):
        nc = self.nc
        sems = {}
        for e in ENGS:
            sems[e] = nc.alloc_semaphore(name='s_' + e)
        for sk in self.dcnt:
            sems[sk] = nc.alloc_semaphore(name='s_' + sk)
        fin = [(e, self.cnt[e]) for e in ENGS if e != 'sp' and self.cnt[e] > 0]
        fin += [(sk, v) for sk, v in self.dcnt.items()]
        handles = {'pe': 'tensor', 'act': 'scalar', 'dve': 'vector', 'pool': 'gpsimd', 'sp': 'sync'}
        ops = self.ops

        def replay(eng):
            def body(e):
                for waits, fn, sk, n in ops[eng]:
                    for wk, wv in waits:
                        e.wait_ge(sems[wk], wv)
                    ins = fn(e)
                    if sk is not None:
                        ins.then_inc(sems[sk], n)
                if eng == 'sp':
                    for wk, wv in fin:
                        e.wait_ge(sems[wk], wv)
            return body

        with nc.Block() as block:
            for eng in ENGS:
                getattr(block, handles[eng])(replay(eng))


class Ops:
    def __init__(self, P):
        self.P = P

    def tt(self, eng, out, in0, in1, op, R, W):
        self.P.op(eng, lambda e: e.tensor_tensor(out=out, in0=in0, in1=in1, op=op), R, W)

    def ts(self, eng, out, in0, s1, s2, op0, op1, R, W):
        if op1 is None:
            self.P.op(eng, lambda e: e.tensor_scalar(out=out, in0=in0, scalar1=s1, scalar2=None, op0=op0), R, W)
        else:
            self.P.op(eng, lambda e: e.tensor_scalar(out=out, in0=in0, scalar1=s1, scalar2=s2, op0=op0, op1=op1), R, W)

    def stt(self, out, in0, scalar, in1, op0, op1, R, W):
        self.P.op('dve', lambda e: e.scalar_tensor_tensor(out=out, in0=in0, scalar=scalar, in1=in1, op0=op0, op1=op1), R, W)

    def act(self, out, in_, func, R, W, bias=None, scale=None, accum=None):
        kw = {}
        if bias is not None:
            kw['bias'] = bias
        if scale is not None:
            kw['scale'] = scale
        if accum is not None:
            kw['accum_out'] = accum
        self.P.op('act', lambda e: e.activation(out=out, in_=in_, func=func, **kw), R, W)

    def copy(self, eng, out, in_, R, W):
        if eng == 'act':
            self.P.op('act', lambda e: e.copy(out=out, in_=in_), R, W)
        else:
            self.P.op(eng, lambda e: e.tensor_copy(out=out, in_=in_), R, W)

    def memset(self, eng, ap, val, W):
        self.P.op(eng, lambda e: e.memset(ap, val), (), W)

    def recip(self, out, in_, R, W):
        self.P.op('dve', lambda e: e.reciprocal(out=out, in_=in_), R, W)

    def reduce(self, out, in_, op, R, W, absval=None):
        self.P.op('dve', lambda e: e.tensor_reduce(out=out, in_=in_, axis=AX.X, op=op, apply_absolute_value=absval), R, W)

    def mm(self, out, lhsT, rhs, start, stop, R, W, inc=True):
        self.P.op('pe', lambda e: e.matmul(out, lhsT, rhs, start=start, stop=stop, skip_group_check=True), R, W, inc=inc)

    def tr(self, out, in_, ident, R, W, inc=True):
        self.P.op('pe', lambda e: e.transpose(out=out, in_=in_, identity=ident), R, W, inc=inc)

    def dma(self, out, in_, R, W, q=None):
        self.P.dma(q, out, in_, R, W)


def bc_mid(ap, shape):
    return ap.unsqueeze(2).to_broadcast(shape)


def bc_lead(ap, shape):
    return ap.unsqueeze(1).to_broadcast(shape)


class B:
    def __init__(self, nc):
        self.nc = nc
        self.P = Prog(nc)
        self.uid = 0

    def name(self, n):
        self.uid += 1
        return '%s_%d' % (n, self.uid)

    def sb(self, es, n, shape, dt):
        t = es.enter_context(self.nc.sbuf_tensor(self.name(n), shape, dt))
        return t, Res(n)

    def ps(self, es, n, shape, dt=F32):
        t = es.enter_context(self.nc.psum_tensor(self.name(n), shape, dt))
        return t, Res(n, excl=True)

    def dram(self, n, shape, dt, nres=1):
        t = self.nc.dram_tensor(self.name(n), shape, dt, kind="Internal").ap()
        return t, [Res(n) for _ in range(nres)]


def blk_of(t0, n):
    return list(range(t0 // 512, (t0 + n - 1) // 512 + 1))


NBLK = (NT + 511) // 512
TBLKS = [(0, 256)] + [(256 + 512 * i, 512) for i in range(16)]


def rs_of(rl, t0, n):
    return [rl[i] for i in blk_of(t0, n)]


C_XBC = 0
C_ZS = 1536
C_DT = 2560
C_Q = 2592
C_K = 3616
C_V = 4640
C_ZA = 5664
C_MQK = 6688
C_MV = 7712
C_MO = 8736
C_MZ = 9760
C_IG = 10784
C_FG = 10792
C_GT = 10800


def build_program(n_layers, dbg=None, only=None):
    dbg = dbg or ()
    scr_cache = {}
    nc = bass.Bass("TRN2", target_bir_lowering=False)
    b = B(nc)
    P = b.P
    L = n_layers

    def inp(n, shape, dt=F32):
        return nc.dram_tensor(n, shape, dt, kind="ExternalInput").ap()

    xin = inp("xin", [NT, D])
    cc = inp("cc", [128, 16])
    ident_in = inp("ident", [128, 128])
    masks_in = inp("masks", [128, 4, 128])
    cos_in = inp("cosb", [TL, 64])
    sin_in = inp("sinb", [TL, 64])
    lami_in = inp("lami", [128, L])
    w_mod = inp("w_mod", [L, D, 3 * D])
    bmod = inp("bmod", [128, L, 24])
    ng = inp("ng", [128, L, 8])
    w_in = inp("w_in", [L, D, D_IN])
    cw = inp("cw", [128, L, 12, 3])
    cb = inp("cb", [128, L, 12])
    alog = inp("alog", [128, L, 32])
    dtb = inp("dtb", [128, L, 32])
    dsk = inp("dsk", [128, L, 16])
    sng = inp("sng", [128, L, D])
    gq = inp("gq", [128, L, 64])
    gk = inp("gk", [128, L, 64])
    lamp = inp("lamp", [128, L, 4, 64])
    subg = inp("subg", [128, L, 8])
    mcw = inp("mcw", [128, L, 8, 3])
    mcb = inp("mcb", [128, L, 8])
    ib = inp("ib", [128, L, 8])
    fb = inp("fb", [128, L, 8])
    mng = inp("mng", [128, L, D])
    sel_in = inp("sel", [2, 2, 128])
    bgate = inp("bgate", [2, L, D])
    w_br = inp("w_br", [L, 3, D, D])
    w_out = inp("w_out", [L, D, D])
    xout = nc.dram_tensor("xout", [NT, D], F32, kind="ExternalOutput").ap()

    def scratch(n, shape, dt, nres=NBLK):
        if only is not None and n in only.get('inputs', ()):
            t = nc.dram_tensor("dbg_" + n, shape, dt, kind="ExternalInput").ap()
            return t, [Res(n) for _ in range(nres)]
        if n in dbg or INTERNAL_OK is None or (INTERNAL_OK != 'ALL' and n not in INTERNAL_OK):
            t = nc.dram_tensor("dbg_" + n, shape, dt, kind="ExternalOutput").ap()
            return t, [Res(n) for _ in range(nres)]
        return b.dram(n, shape, dt, nres)

    xbufs = [(xin, [Res('xin') for _ in range(NBLK)])]
    for l in range(1, L):
        xbufs.append(scratch("xres%d" % l, [NT, D], F32))
    xout_res = [Res('xout') for _ in range(NBLK)]

    hT, hT_r = scratch("hT", [8, 128, NT], BF16)
    xbcT, xbcT_r = scratch("xbcT", [12, 128, NT], BF16)
    mqkT, mqkT_r = scratch("mqkT", [8, 128, NT], BF16)
    zs, zs_r = scratch("zs", [NT, 1024], BF16)
    small, small_r = scratch("small", [NT, 48], F32)
    aqk, aqk_r = scratch("aqk", [NT, 2048], BF16)
    avz, avz_r = scratch("avz", [NT, 1024], BF16)
    azT, azT_r = scratch("azT", [8, 128, NT], BF16)
    mvo, mvo_r = scratch("mvo", [NT, 2048], BF16)
    mz, mz_r = scratch("mz", [NT, 1024], BF16)
    gtT, gtT_r = scratch("gtT", [24, 128, NT], BF16)
    yT, yT_r = scratch("yT", [3, 8, 128, NT], BF16)

    with ExitStack() as g_es:
        identf, identf_r = b.sb(g_es, "identf", [128, 128], F32)
        identb, identb_r = b.sb(g_es, "identb", [128, 128], BF16)
        masks, masks_r = b.sb(g_es, "masks", [128, 4, 128], F32)
        onesb, onesb_r = b.sb(g_es, "onesb", [128, 128], BF16)
        onesf, onesf_r = b.sb(g_es, "onesf", [128, 128], F32)
        gbc, gbc_r = b.sb(g_es, "gbc", [128, L, 2, D], F32)
        modt, modt_r = b.sb(g_es, "modt", [128, L, 24, 2], F32)
        a1t, a1t_r = b.sb(g_es, "a1t", [128, L, 8, 2], F32)
        P.dma('sp', identf[:], ident_in[:, :], writes=[identf_r])
        P.dma('sp', masks[:], masks_in[:, :, :], writes=[masks_r])
        P.op('dve', lambda e: e.tensor_copy(out=identb[:], in_=identf[:]), reads=[identf_r], writes=[identb_r])
        P.op('dve', lambda e: e.memset(onesb[:], 1.0), writes=[onesb_r])
        P.op('dve', lambda e: e.memset(onesf[:], 1.0), writes=[onesf_r])

        with ExitStack() as es:
            cct, cct_r = b.sb(es, "cct", [128, 16], F32)
            sct, sct_r = b.sb(es, "sct", [128, 16], F32)
            bmt, bmt_r = b.sb(es, "bmt", [128, L, 24], F32)
            ngt, ngt_r = b.sb(es, "ngt", [128, L, 8], F32)
            wm, wm_r = b.sb(es, "wm", [128, 8, 3 * D], F32)
            pm, pm_r = b.ps(es, "pm", [128, 512], F32)
            pgr, pgr_r = b.ps(es, "pgr", [128, 2, 512], F32)
            pgb, pgb_r = b.ps(es, "pgb", [128, 2, 512], F32)
            selt, selt_r = b.sb(es, "selt", [2, 2, 128], F32)
            bgt, bgt_r = b.sb(es, "bgt", [2, L, D], F32)
            grow, grow_r = b.sb(es, "grow", [2, D], F32)
            P.dma('sp', selt[:], sel_in[:, :, :], writes=[selt_r])
            P.dma('sp', bgt[:], bgate[:, :, :], writes=[bgt_r])
            P.dma('sp', cct[:], cc[:, :], writes=[cct_r])
            P.dma('sp', bmt[:], bmod[:, :, :], writes=[bmt_r])
            P.dma('sp', ngt[:], ng[:, :, :], writes=[ngt_r])
            P.op('act', lambda e: e.activation(out=sct[:], in_=cct[:], func=AF.Silu), reads=[cct_r], writes=[sct_r])
            if 'mod' in dbg:
                dsc = nc.dram_tensor("dbg_sct", [128, 16], F32, kind="ExternalOutput").ap()
                P.dma('sp', dsc[:, :], sct[:], reads=[sct_r], writes=[Res('x')])
            for l in range(L):
                for kc in range(8):
                    P.dma(None, wm[:, kc, :], w_mod[l, kc * 128:(kc + 1) * 128, :], writes=[wm_r])
                for fc in range(24):
                    for kc in range(8):
                        P.op('pe', lambda e, fc=fc, kc=kc: e.matmul(
                            pm[:, fc * 2:fc * 2 + 2], wm[:, kc, fc * 128:(fc + 1) * 128], sct[:, kc * 2:kc * 2 + 2],
                            start=(kc == 0), stop=(kc == 7)),
                            reads=[wm_r, sct_r], writes=[pm_r], inc=(fc == 23 and kc == 7))
                for hf in range(2):
                    for kc in range(8):
                        P.op('pe', lambda e, hf=hf, kc=kc: e.matmul(
                            pgr[0:2, hf, :], sct[:, kc * 2:kc * 2 + 2], wm[:, kc, 2 * D + hf * 512:2 * D + (hf + 1) * 512],
                            start=(kc == 0), stop=(kc == 7)), reads=[wm_r, sct_r], writes=[pgr_r], inc=(kc == 7))
                P.op('dve', lambda e, l=l: e.tensor_tensor(out=grow[:, :].rearrange("p (h f) -> p h f", h=2), in0=pgr[0:2, :, :],
                                                       in1=bgt[:, l, :].rearrange("p (h f) -> p h f", h=2), op=ALU.add),
                     reads=[pgr_r, bgt_r], writes=[grow_r])
                for j in range(2):
                    for hf in range(2):
                        P.op('pe', lambda e, j=j, hf=hf: e.matmul(pgb[:, hf, :], selt[:, j, :], grow[:, hf * 512:(hf + 1) * 512],
                                                                start=True, stop=True), reads=[selt_r, grow_r], writes=[pgb_r])
                    P.op('dve', lambda e, l=l, j=j: e.tensor_copy(out=gbc[:, l, j, :].rearrange("p (h f) -> p h f", h=2), in_=pgb[:, :, :]),
                         reads=[pgb_r], writes=[gbc_r])
                for j in range(2):
                    P.op('dve', lambda e, l=l, j=j: e.tensor_tensor(
                        out=modt[:, l, :, j], in0=pm[:, 0:48].rearrange("p (f j) -> p f j", j=2)[:, :, j],
                        in1=bmt[:, l, :], op=ALU.add), reads=[pm_r, bmt_r], writes=[modt_r])
                for j in range(2):
                    P.op('dve', lambda e, l=l, j=j: e.scalar_tensor_tensor(
                        out=a1t[:, l, :, j], in0=modt[:, l, 8:16, j], scalar=1.0, in1=ngt[:, l, :],
                        op0=ALU.add, op1=ALU.mult), reads=[modt_r, ngt_r], writes=[a1t_r])
            P.barrier()

        if 'mod' in dbg:
            dmod = nc.dram_tensor("dbg_mod", [128, L * 48], F32, kind="ExternalOutput").ap()
            da1 = nc.dram_tensor("dbg_a1", [128, L * 16], F32, kind="ExternalOutput").ap()
            P.dma('sp', dmod[:, :], modt[:].rearrange("p l f j -> p (l f j)"), reads=[modt_r], writes=[Res('x')])
            P.dma('sp', da1[:, :], a1t[:].rearrange("p l f j -> p (l f j)"), reads=[a1t_r], writes=[Res('x')])
        for l in range(L):
            xsrc, xsrc_r = xbufs[l]
            if l + 1 < L:
                xdst, xdst_r = xbufs[l + 1]
                dst_off = 0
            else:
                xdst, xdst_r = xout, xout_res
                dst_off = 0
            layer(b, l, L, locals())

        P.finish()
        global LAST_PROG
        LAST_PROG = P
    return nc


def layer(b, l, L, G):
    cache = G['scr_cache']

    def S(n, shape, dt, nres=NBLK):
        if n not in cache:
            cache[n] = G['scratch'](n, shape, dt, nres)
        return cache[n]
    G['scr'] = S
    only = G.get('only')
    with ExitStack() as es:
        C = layer_consts(b, es, l, G)
        if only is None or 'a' in only:
            phase_a1(b, l, G)
            phase_a2(b, l, G)
        if only is None or 'ssd' in only:
            phase_ssd(b, l, G, C)
        if only is None or 'ml' in only:
            phase_mlstm(b, l, G, C)
        if only is None or 'attn' in only:
            phase_attn(b, l, G, C)
        if only is None or 'merge' in only:
            phase_merge(b, l, G, C)
        b.P.barrier()


def phase_a1(b, l, G):
    P = b.P
    xsrc, xsrc_r = G['xsrc'], G['xsrc_r']
    hT, hT_r = G['hT'], G['hT_r']
    identb, identb_r = G['identb'], G['identb_r']
    modt, modt_r, a1t, a1t_r = G['modt'], G['modt_r'], G['a1t'], G['a1t_r']
    with ExitStack() as es:
        xt = [b.sb(es, "xt", [128, D], F32) for _ in range(2)]
        junk, junk_r = b.sb(es, "junk", [128, D], BF16)
        xnb = [b.sb(es, "xnb", [128, D], BF16) for _ in range(2)]
        ss = [b.sb(es, "ss", [128, 1], F32) for _ in range(2)]
        pT = [b.ps(es, "pT", [128, D], BF16) for _ in range(2)]
        hb = [b.sb(es, "hb", [128, 8, 512], BF16) for _ in range(2)]
        for bi, (t0, bw) in enumerate(TBLKS):
            hbt, hb_r = hb[bi % 2]
            j = 1 if t0 < TC else 0
            for ti in range(bw // 128):
                i = (t0 // 128 + ti)
                xtt, xt_r = xt[i % 2]
                xn, xn_r = xnb[i % 2]
                st, st_r = ss[i % 2]
                pt, pt_r = pT[i % 2]
                P.dma(None, xtt[:], xsrc[i * 128:(i + 1) * 128, :], reads=rs_of(xsrc_r, i * 128, 128), writes=[xt_r])
                P.op('act', lambda e, xtt=xtt, st=st: e.activation(out=junk[:], in_=xtt[:], func=AF.Square, accum_out=st[:]),
                     reads=[xt_r], writes=[junk_r, st_r])
                P.op('act', lambda e, st=st: e.activation(out=st[:], in_=st[:], func=AF.Sqrt, scale=1.0 / D, bias=EPS),
                     reads=[st_r], writes=[st_r])
                P.op('dve', lambda e, st=st: e.reciprocal(out=st[:], in_=st[:]), reads=[st_r], writes=[st_r])
                P.op('dve', lambda e, xn=xn, xtt=xtt, st=st: e.tensor_scalar(out=xn[:], in0=xtt[:], scalar1=st[:, 0:1], scalar2=None, op0=ALU.mult),
                     reads=[xt_r, st_r], writes=[xn_r])
                for kc in range(8):
                    P.op('pe', lambda e, kc=kc, pt=pt, xn=xn: e.transpose(out=pt[:, kc * 128:(kc + 1) * 128], in_=xn[:, kc * 128:(kc + 1) * 128], identity=identb[:]),
                         reads=[xn_r, identb_r], writes=[pt_r], inc=(kc == 7))
                for kc in range(8):
                    P.op('dve', lambda e, kc=kc, pt=pt, hbt=hbt, ti=ti, j=j: e.tensor_scalar(
                        out=hbt[:, kc, ti * 128:(ti + 1) * 128], in0=pt[:, kc * 128:(kc + 1) * 128],
                        scalar1=a1t[:, l, kc, j:j + 1], scalar2=modt[:, l, kc, j:j + 1], op0=ALU.mult, op1=ALU.add),
                        reads=[pt_r, a1t_r, modt_r], writes=[hb_r])
            P.dma(None, hT[:, :, t0:t0 + bw].rearrange("k p t -> p k t"), hbt[:, :, 0:bw],
                  reads=[hb_r], writes=rs_of(hT_r, t0, bw))
        P.barrier()


def gemm_jobs():
    jobs = []
    jobs.append(('F', C_XBC, 1536, 'xbcT'))
    jobs.append(('F', C_MQK, 1024, 'mqkT'))
    jobs.append(('T', C_ZS, 1024, 'zs', 0))
    jobs.append(('T', C_Q, 2048, 'aqk', 0))
    jobs.append(('T', C_V, 1024, 'avz', 0))
    jobs.append(('F', C_ZA, 1024, 'azT'))
    jobs.append(('T', C_MV, 2048, 'mvo', 0))
    jobs.append(('T', C_MZ, 1024, 'mz', 0))
    jobs.append(('F', C_GT, 2048, 'gtT', 0))
    jobs.append(('F', C_GT + 2048, 1024, 'gtT', 16))
    return jobs


def phase_a2(b, l, G):
    P = b.P
    w_in = G['w_in']
    hT, hT_r = G['hT'], G['hT_r']
    with ExitStack() as es:
        stg = [b.sb(es, "stg", [128, 2048], F32) for _ in range(2)]
        Wg = [b.sb(es, "Wg", [128, 8, 2048], BF16) for _ in range(2)]
        hb = [b.sb(es, "hb2", [128, 8, 512], BF16) for _ in range(2)]
        ob = [b.sb(es, "ob", [128, 2048], BF16) for _ in range(2)]
        obs = [b.sb(es, "obs", [128, 48], F32) for _ in range(2)]
        pg = [b.ps(es, "pg", [128, 512], F32) for _ in range(4)]
        ws, ws_r = b.sb(es, "ws", [128, 8, 48], BF16)
        cnt = {'stg': 0, 'w': 0, 'hb': 0, 'ob': 0, 'pg': 0, 'ev': 0}

        def load_w(c0, n):
            wt, wt_r = Wg[cnt['w'] % 2]
            cnt['w'] += 1
            for kc in range(8):
                st, st_r = stg[cnt['stg'] % 2]
                cnt['stg'] += 1
                P.dma(None, st[:, 0:n], w_in[l, kc * 128:(kc + 1) * 128, c0:c0 + n], writes=[st_r])
                P.op('pool', lambda e, wt=wt, st=st, kc=kc, n=n: e.tensor_copy(out=wt[:, kc, 0:n], in_=st[:, 0:n]),
                     reads=[st_r], writes=[wt_r])
            return wt, wt_r

        def load_h(t0, bw):
            ht, ht_r = hb[cnt['hb'] % 2]
            cnt['hb'] += 1
            P.dma(None, ht[:, :, 0:bw], hT[:, :, t0:t0 + bw].rearrange("k p t -> p k t"),
                  reads=rs_of(hT_r, t0, bw), writes=[ht_r])
            return ht, ht_r

        def evac(out_ap, in_ap, reads, writes):
            eng = 'act' if cnt['ev'] % 2 == 0 else 'dve'
            cnt['ev'] += 1
            if eng == 'act':
                P.op('act', lambda e: e.copy(out=out_ap, in_=in_ap), reads=reads, writes=writes)
            else:
                P.op('dve', lambda e: e.tensor_copy(out=out_ap, in_=in_ap), reads=reads, writes=writes)

        for job in gemm_jobs():
            mode, c0, n = job[0], job[1], job[2]
            dst, dst_r = G[job[3]], G[job[3] + '_r']
            wt, wt_r = load_w(c0, n)
            for (t0, bw) in TBLKS:
                ht, ht_r = load_h(t0, bw)
                if mode == 'F':
                    for ch in range(n // 128):
                        pt, pt_r = pg[cnt['pg'] % 4]
                        cnt['pg'] += 1
                        for kc in range(8):
                            P.op('pe', lambda e, pt=pt, wt=wt, ht=ht, kc=kc, ch=ch, bw=bw: e.matmul(
                                pt[:, 0:bw], wt[:, kc, ch * 128:(ch + 1) * 128], ht[:, kc, 0:bw],
                                start=(kc == 0), stop=(kc == 7)), reads=[wt_r, ht_r], writes=[pt_r], inc=(kc == 7))
                        if ch % 4 == 0:
                            ot, ot_r = ob[cnt['ob'] % 2]
                            cnt['ob'] += 1
                        evac(ot[:, (ch % 4) * 512:(ch % 4) * 512 + bw], pt[:, 0:bw], [pt_r], [ot_r])
                        if ch % 4 == 3 or ch == n // 128 - 1:
                            cb0 = ch - (ch % 4) + (job[4] if len(job) > 4 else 0)
                            nchk = ch % 4 + 1
                            P.dma(None, dst[cb0:cb0 + nchk, :, t0:t0 + bw].rearrange("c p t -> p c t"),
                                  ot[:, 0:nchk * 512].rearrange("p (c t) -> p c t", c=nchk)[:, :, 0:bw],
                                  reads=[ot_r], writes=rs_of(dst_r, t0, bw))
                else:
                    dc0 = job[4]
                    for ti in range(bw // 128):
                        ot, ot_r = ob[cnt['ob'] % 2]
                        cnt['ob'] += 1
                        for sg in range(n // 512):
                            pt, pt_r = pg[cnt['pg'] % 4]
                            cnt['pg'] += 1
                            for kc in range(8):
                                P.op('pe', lambda e, pt=pt, wt=wt, ht=ht, kc=kc, sg=sg, ti=ti: e.matmul(
                                    pt[:, :], ht[:, kc, ti * 128:(ti + 1) * 128], wt[:, kc, sg * 512:(sg + 1) * 512],
                                    start=(kc == 0), stop=(kc == 7)), reads=[wt_r, ht_r], writes=[pt_r], inc=(kc == 7))
                            evac(ot[:, sg * 512:(sg + 1) * 512], pt[:, :], [pt_r], [ot_r])
                        r0 = t0 + ti * 128
                        P.dma(None, dst[r0:r0 + 128, dc0:dc0 + n], ot[:, 0:n], reads=[ot_r], writes=rs_of(dst_r, r0, 128))

        small, small_r = G['small'], G['small_r']
        for kc in range(8):
            st, st_r = stg[cnt['stg'] % 2]
            cnt['stg'] += 1
            P.dma(None, st[:, 0:32], w_in[l, kc * 128:(kc + 1) * 128, C_DT:C_DT + 32], writes=[st_r])
            P.dma(None, st[:, 32:48], w_in[l, kc * 128:(kc + 1) * 128, C_IG:C_IG + 16], writes=[st_r])
            P.op('pool', lambda e, st=st, kc=kc: e.tensor_copy(out=ws[:, kc, :], in_=st[:, 0:48]), reads=[st_r], writes=[ws_r])
        for (t0, bw) in TBLKS:
            ht, ht_r = load_h(t0, bw)
            for ti in range(bw // 128):
                pt, pt_r = pg[cnt['pg'] % 4]
                cnt['pg'] += 1
                ot, ot_r = obs[cnt['ob'] % 2]
                cnt['ob'] += 1
                for kc in range(8):
                    P.op('pe', lambda e, pt=pt, ht=ht, kc=kc, ti=ti: e.matmul(
                        pt[:, 0:48], ht[:, kc, ti * 128:(ti + 1) * 128], ws[:, kc, :],
                        start=(kc == 0), stop=(kc == 7)), reads=[ws_r, ht_r], writes=[pt_r], inc=(kc == 7))
                evac(ot[:, :], pt[:, 0:48], [pt_r], [ot_r])
                r0 = t0 + ti * 128
                P.dma(None, small[r0:r0 + 128, :], ot[:, :], reads=[ot_r], writes=rs_of(small_r, r0, 128))
        P.barrier()


def rope_tables():
    t = np.arange(TL)
    row = (t // 64).astype(np.float32)
    col = (t % 64).astype(np.float32)
    half = 32
    inv_freq = (10000.0 ** (-np.arange(0, half, 2, dtype=np.float32) / half)).astype(np.float32)
    ang_r = row[:, None] * inv_freq
    ang_c = col[:, None] * inv_freq
    cos = np.concatenate([np.cos(ang_r), np.cos(ang_r), np.cos(ang_c), np.cos(ang_c)], -1).astype(np.float32)
    sin = np.concatenate([np.sin(ang_r), np.sin(ang_r), np.sin(ang_c), np.sin(ang_c)], -1).astype(np.float32)
    sgn = np.tile(np.concatenate([-np.ones(16), np.ones(16)]), 2).astype(np.float32)
    return cos, sin * sgn


def bc(a):
    a = np.asarray(a, np.float32)
    return np.ascontiguousarray(np.broadcast_to(a[None], (128,) + a.shape))


def pmaj(v, nchunk):
    v = np.asarray(v, np.float32)
    lead = v.shape[:-1]
    r = v.reshape(lead + (nchunk, 128))
    r = np.moveaxis(r, -1, 0)
    return np.ascontiguousarray(r)


def make_inputs(inputs, bidx, layers, xin=None):
    L = len(layers)
    f = lambda k: np.asarray(inputs[k], np.float32)
    d = {}
    d['xin'] = np.ascontiguousarray(np.concatenate([f('ctx')[bidx], f('x')[bidx]], 0)) if xin is None else np.ascontiguousarray(xin, np.float32)
    cc = np.stack([f('c')[bidx], f('c_ctx')], -1)
    d['cc'] = np.ascontiguousarray(cc.reshape(8, 128, 2).transpose(1, 0, 2).reshape(128, 16))
    d['ident'] = np.eye(128, dtype=np.float32)
    t = np.arange(128)
    U = (t[:, None] <= t[None, :]).astype(np.float32)
    Lw = (t[:, None] >= t[None, :]).astype(np.float32)
    Sg = (t[:, None] > t[None, :]).astype(np.float32)
    Sl = (t[:, None] < t[None, :]).astype(np.float32)
    d['masks'] = np.ascontiguousarray(np.stack([U, Lw, Sg, Sl], 1))
    cos, sin = rope_tables()
    d['cosb'] = cos
    d['sinb'] = sin
    d['lami'] = bc(np.array([0.8 - 0.6 * math.exp(-0.3 * l) for l in layers], np.float32))
    d['w_mod'] = np.ascontiguousarray(f('w_mod')[layers])
    d['bmod'] = pmaj(f('b_mod')[layers], 24)
    d['ng'] = pmaj(f('norm_g')[layers], 8)
    d['w_in'] = np.ascontiguousarray(f('w_in')[layers])
    d['cw'] = np.ascontiguousarray(pmaj(f('ssd_conv_w')[layers], 12).transpose(0, 1, 3, 2))
    d['cb'] = pmaj(f('ssd_conv_b')[layers], 12)
    d['alog'] = bc(f('ssd_a_log')[layers].reshape(L, 32))
    d['dtb'] = bc(f('ssd_dt_bias')[layers].reshape(L, 32))
    d['dsk'] = bc(f('ssd_d')[layers])
    d['sng'] = bc(f('ssd_norm_g')[layers])
    d['gq'] = bc(f('diff_qn_g')[layers])
    d['gk'] = bc(f('diff_kn_g')[layers])
    d['lamp'] = bc(f('diff_lambda')[layers])
    d['subg'] = np.ascontiguousarray(np.broadcast_to(pmaj(f('diff_subln_g')[layers], 1), (128, L, 8)))
    d['mcw'] = np.ascontiguousarray(pmaj(f('ml_conv_w')[layers], 8).transpose(0, 1, 3, 2))
    d['mcb'] = pmaj(f('ml_conv_b')[layers], 8)
    d['ib'] = bc(f('ml_i_bias')[layers].reshape(L, 8))
    d['fb'] = bc(f('ml_f_bias')[layers].reshape(L, 8))
    d['mng'] = bc(f('ml_norm_g')[layers])
    sel = np.zeros((2, 2, 128), np.float32)
    sel[0, 0] = 1.0
    sel[1, 1] = 1.0
    d['sel'] = sel
    d['bgate'] = np.ascontiguousarray(np.broadcast_to(f('b_mod')[layers][None, :, 2 * D:3 * D], (2, L, D)))
    d['w_br'] = np.ascontiguousarray(f('w_branch')[layers])
    d['w_out'] = np.ascontiguousarray(f('w_out')[layers])
    return d


_NC_CACHE = {}


FUSED = True


def kernel(**inputs):
    nb = 4
    if FUSED:
        if DEPTH not in _NC_CACHE:
            _NC_CACHE[DEPTH] = build_program(DEPTH)
        nc = _NC_CACHE[DEPTH]
        in_maps = [make_inputs(inputs, bidx, list(range(DEPTH))) for bidx in range(nb)]
        res = run_bass_kernel_spmd(nc, in_maps, core_ids=list(range(nb)))
        return np.stack([np.asarray(r["xout"], np.float32)[TC:] for r in res.results], 0)
    if 1 not in _NC_CACHE:
        _NC_CACHE[1] = build_program(1)
    nc = _NC_CACHE[1]
    xcur = [None] * nb
    for l in range(DEPTH):
        in_maps = [make_inputs(inputs, bidx, [l], xin=xcur[bidx]) for bidx in range(nb)]
        res = run_bass_kernel_spmd(nc, in_maps, core_ids=list(range(nb)))
        xcur = [np.asarray(r["xout"], np.float32) for r in res.results]
    return np.stack([x[TC:] for x in xcur], 0)


def layer_consts(b, es, l, G):
    P = b.P
    O = Ops(P)
    C = {}

    def ld(name, src, shape, **kw):
        t, r = b.sb(es, "c_" + name, shape, F32)
        P.dma(None, t[:], src, writes=[r], **kw)
        C[name] = (t, r)
    ld('cw', G['cw'][:, l, :, :], [128, 12, 3])
    ld('cb', G['cb'][:, l, :], [128, 12])
    ld('alog', G['alog'][:, l, :], [128, 32])
    ld('dtb', G['dtb'][:, l, :], [128, 32])
    ld('dsk', G['dsk'][:, l, :], [128, 16])
    ld('sng', G['sng'][:, l, :], [128, D])
    ld('gq', G['gq'][:, l, :], [128, 64])
    ld('gk', G['gk'][:, l, :], [128, 64])
    ld('lamp', G['lamp'][:, l, :, :], [128, 4, 64])
    ld('subg', G['subg'][:, l, :], [128, 8])
    ld('mcw', G['mcw'][:, l, :, :], [128, 8, 3])
    ld('mcb', G['mcb'][:, l, :], [128, 8])
    ld('ib', G['ib'][:, l, :], [128, 8])
    ld('fb', G['fb'][:, l, :], [128, 8])
    ld('mng', G['mng'][:, l, :], [128, D])
    ld('lami', G['lami_in'][:, l:l + 1], [128, 1], allow_slow_non_contiguous=True)
    at, ar = C['alog']
    O.act(at[:], at[:], AF.Exp, [ar], [ar])
    O.ts('dve', at[:], at[:], -1.0, None, ALU.mult, None, [ar], [ar])
    return C


def softplus(b, O, out, x, tmp1, tmp2, R, W):
    O.act(tmp1, x, AF.Abs, R + W, W)
    O.act(tmp1, tmp1, AF.Exp, R + W, W, scale=-1.0)
    O.act(tmp2, tmp1, AF.Ln, R + W, W, bias=1.0)
    O.stt(out, x, 0.0, tmp2, ALU.max, ALU.add, R + W, W)


def seq_bounds(t0, bw):
    return (t0 == 0 or t0 == TC), (t0 + bw == TC or t0 + bw == NT)


def conv_silu(b, O, l, xT, xT_r, t0, bw, nchunk, cwt, cw_r, cbt, cb_r, xh, xh_r, tmps, uT, uT_r):
    P = O.P
    at_start, at_end = seq_bounds(t0, bw)
    lo = t0 if at_start else t0 - 1
    hi = t0 + bw if at_end else t0 + bw + 1
    if at_start:
        O.memset('pool', xh[:, :, 0:1], 0.0, [xh_r])
    if at_end:
        O.memset('pool', xh[:, :, bw + 1:bw + 2], 0.0, [xh_r])
    O.dma(xh[:, :, lo - (t0 - 1):hi - (t0 - 1)], xT[:, :, lo:hi].rearrange("c p t -> p c t"),
          rs_of(xT_r, lo, hi - lo), [xh_r])
    for ch in range(nchunk):
        tm, tm_r = tmps[ch % 2]
        O.ts('dve', tm[:, 0:bw], xh[:, ch, 1:bw + 1], cwt[:, ch, 1:2], None, ALU.mult, None, [xh_r, cw_r], [tm_r])
        O.stt(tm[:, 0:bw], xh[:, ch, 0:bw], cwt[:, ch, 0:1], tm[:, 0:bw], ALU.mult, ALU.add, [xh_r, cw_r, tm_r], [tm_r])
        O.stt(tm[:, 0:bw], xh[:, ch, 2:bw + 2], cwt[:, ch, 2:3], tm[:, 0:bw], ALU.mult, ALU.add, [xh_r, cw_r, tm_r], [tm_r])
        O.act(uT[:, ch, 0:bw], tm[:, 0:bw], AF.Silu, [tm_r, cb_r], [uT_r], bias=cbt[:, ch:ch + 1])


def decay_scalars(b, O, c, a_t, a_r, NA, masks, masks_r, onesf, onesf_r, pc, pc_r, PT, PT_r, E1, E2, E_r,
                  rs, de, cdall, cd_r):
    n2 = 2 * NA
    O.mm(pc[:, 0:n2], masks[:, 0, :], a_t[:, 0:n2], True, True, [masks_r, a_r], [pc_r], inc=False)
    O.mm(pc[:, n2:2 * n2], onesf[:, :], a_t[:, 0:n2], True, True, [onesf_r, a_r], [pc_r])
    O.copy('dve', PT[:, 0:2 * n2], pc[:, 0:2 * n2], [pc_r], [PT_r])
    Pf, Pb = PT[:, 0:NA], PT[:, NA:n2]
    Tf, Tb = PT[:, n2:n2 + NA], PT[:, n2 + NA:2 * n2]
    O.tt('dve', E2[:, NA:n2], Pb, a_t[:, NA:n2], ALU.subtract, [PT_r, a_r], [E_r])
    O.tt('dve', E1[:, NA:n2], Tb, E2[:, NA:n2], ALU.subtract, [PT_r, E_r], [E_r])
    O.tt('dve', E2[:, 0:NA], Tf, Pf, ALU.subtract, [PT_r], [E_r])
    O.copy('dve', E1[:, 0:NA], Pf, [PT_r], [E_r])
    O.act(rs, E1[:, 0:n2], AF.Exp, [E_r], [E_r])
    O.act(de, E2[:, 0:n2], AF.Exp, [E_r], [E_r])
    O.act(cdall[:, c, 0:n2], PT[:, n2:2 * n2], AF.Exp, [PT_r], [cd_r])


def block_scalars(b, O, G, t0, bw, xa, x_r, n2, NA, Abc, A_r, pc, pc_r, T, T_r, cdall, cd_r, sm_dram, sm_dram_r):
    masks, masks_r, onesf, onesf_r = G['masks'], G['masks_r'], G['onesf'], G['onesf_r']
    nt = bw // 128
    c0 = t0 // 128
    a = xa[:, 0:nt, 0, :]
    O.copy('dve', T[:, 0, 0:nt, :], a, [x_r], [T_r])
    O.mm(pc[:, 0:nt * n2], masks[:, 0, :], T[:, 0, 0:nt, :].rearrange("p t c -> p (t c)"), True, True, [masks_r, T_r], [pc_r], inc=False)
    O.mm(pc[:, 256:256 + nt * n2], onesf[:, :], T[:, 0, 0:nt, :].rearrange("p t c -> p (t c)"), True, True, [onesf_r, T_r], [pc_r])
    O.copy('dve', xa[:, 0:nt, 1, :], pc[:, 0:nt * n2].rearrange("p (t c) -> p t c", t=nt), [pc_r], [x_r])
    O.copy('dve', xa[:, 0:nt, 2, :], pc[:, 256:256 + nt * n2].rearrange("p (t c) -> p t c", t=nt), [pc_r], [x_r])
    Pm, Tm, E1, E2 = xa[:, 0:nt, 1, :], xa[:, 0:nt, 2, :], xa[:, 0:nt, 3, :], xa[:, 0:nt, 4, :]
    O.tt('dve', E2[:, :, NA:n2], Pm[:, :, NA:n2], a[:, :, NA:n2], ALU.subtract, [x_r], [x_r])
    O.tt('dve', E1[:, :, NA:n2], Tm[:, :, NA:n2], E2[:, :, NA:n2], ALU.subtract, [x_r], [x_r])
    O.tt('dve', E2[:, :, 0:NA], Tm[:, :, 0:NA], Pm[:, :, 0:NA], ALU.subtract, [x_r], [x_r])
    O.copy('dve', E1[:, :, 0:NA], Pm[:, :, 0:NA], [x_r], [x_r])
    O.act(xa[:, 0:nt, 5, :], E1, AF.Exp, [x_r], [x_r])
    O.act(xa[:, 0:nt, 6, :], E2, AF.Exp, [x_r], [x_r])
    O.act(cdall[:, c0:c0 + nt, 0:n2], Tm, AF.Exp, [x_r], [cd_r])
    O.dma(sm_dram[t0:t0 + bw, 0:n2].rearrange("(t p) c -> p t c", p=128), a, [x_r], rs_of(sm_dram_r, t0, bw))
    O.dma(sm_dram[t0:t0 + bw, n2:2 * n2].rearrange("(t p) c -> p t c", p=128), xa[:, 0:nt, 5, :], [x_r], rs_of(sm_dram_r, t0, bw))


def phase_ssd(b, l, G, C):
    P = b.P
    O = Ops(P)
    nc = b.nc
    S = G['scr']
    masks, masks_r, onesf, onesf_r = G['masks'], G['masks_r'], G['onesf'], G['onesf_r']
    identb, identb_r = G['identb'], G['identb_r']
    xbcT, xbcT_r = G['xbcT'], G['xbcT_r']
    small, small_r = G['small'], G['small_r']
    sBT, sBT_r = S('sBT', [2, 128, NT], BF16)
    sCT, sCT_r = S('sCT', [2, 128, NT], BF16)
    sVt, sVt_r = S('sVt', [2, NT, D], BF16)
    sxs, sxs_r = S('sxs', [NT, D], BF16)
    ssm, ssm_r = S('ssm', [NT, 64], F32)
    sprev, sprev_r = S('sprev', [2, NCH, 128, D], BF16, 2 * NCH)
    sLb, sLb_r = S('sLb', [NCH, 128, D], F32, NCH)
    cwt, cw_r = C['cw']
    cbt, cb_r = C['cb']
    At, A_r = C['alog']
    dtbt, dtb_r = C['dtb']

    with ExitStack() as es0:
        cdall, cd_r = b.sb(es0, "cdall", [128, NCH, 32], F32)
        with ExitStack() as es:
            xh, xh_r = b.sb(es, "xh", [128, 12, 514], BF16)
            tmps = [b.sb(es, "ctmp", [128, 512], F32) for _ in range(2)]
            uT, uT_r = b.sb(es, "uT", [128, 12, 512], BF16)
            pX, pX_r = b.ps(es, "pX", [128, D], BF16)
            pB, pB_r = b.ps(es, "pB", [128, 1024], BF16)
            pc, pc_r = b.ps(es, "pc", [128, 512], F32)
            pL, pL_r = b.ps(es, "pL", [128, 4, 512], F32)
            xsb = [b.sb(es, "xsb", [128, D], BF16) for _ in range(2)]
            Btm, Btm_r = b.sb(es, "Btm", [128, 256], BF16)
            sml, sml_r = b.sb(es, "sml", [128, 256], F32)
            smo = [b.sb(es, "smo", [128, 64], F32) for _ in range(2)]
            PT, PT_r = b.sb(es, "PT", [128, 128], F32)
            Et, E_r = b.sb(es, "Et", [128, 128], F32)
            Vt = [b.sb(es, "Vt", [128, 2, D], BF16) for _ in range(2)]
            Vw, Vw_r = b.sb(es, "Vw", [128, 2, D], BF16)
            xab = [b.sb(es, "xab", [128, 4, 8, 32], F32) for _ in range(2)]
            xwb = [b.sb(es, "xwb", [128, 4, 32], F32) for _ in range(2)]
            Tsc, Tsc_r = b.sb(es, "Tsc", [128, 1, 4, 32], F32)
            Sf, Sf_r = b.sb(es, "Sf", [128, D], F32)
            Stmp, Stmp_r = b.sb(es, "Stmp", [128, D], F32)
            Sfb = [b.sb(es, "Sfb", [128, D], BF16) for _ in range(2)]
            Lb = [b.sb(es, "Lb", [128, D], F32) for _ in range(2)]
            O.memset('dve', Sf[:], 0.0, [Sf_r])
            for (t0, bw) in TBLKS:
                conv_silu(b, O, l, xbcT, xbcT_r, t0, bw, 12, cwt, cw_r, cbt, cb_r, xh, xh_r, tmps, uT, uT_r)
                O.dma(sBT[:, :, t0:t0 + bw].rearrange("g p t -> p g t"), uT[:, 8:10, 0:bw], [uT_r], rs_of(sBT_r, t0, bw))
                O.dma(sCT[:, :, t0:t0 + bw].rearrange("g p t -> p g t"), uT[:, 10:12, 0:bw], [uT_r], rs_of(sCT_r, t0, bw))
                nt = bw // 128
                bslot = (t0 // 512) % 2
                xa, x_r = xab[bslot]
                xw, xw_r = xwb[bslot]
                O.dma(xa[:, 0:nt, 1, :], small[t0:t0 + bw, 0:32].rearrange("(t p) c -> p t c", p=128), rs_of(small_r, t0, bw), [x_r])
                O.tt('dve', xa[:, 0:nt, 2, :], xa[:, 0:nt, 1, :], bc_lead(dtbt[:], [128, nt, 32]), ALU.add, [x_r, dtb_r], [x_r])
                softplus(b, O, xa[:, 0:nt, 7, :], xa[:, 0:nt, 2, :], xa[:, 0:nt, 3, :], xa[:, 0:nt, 4, :], [], [x_r])
                O.tt('dve', xa[:, 0:nt, 0, :], xa[:, 0:nt, 7, :], bc_lead(At[:], [128, nt, 32]), ALU.mult, [x_r, A_r], [x_r])
                block_scalars(b, O, G, t0, bw, xa, x_r, 32, 16, At, A_r, pc, pc_r, Tsc, Tsc_r, cdall, cd_r, ssm, ssm_r)
                O.tt('dve', xw[:, 0:nt, :], xa[:, 0:nt, 7, :], xa[:, 0:nt, 6, :], ALU.mult, [x_r], [xw_r])
                for ti in range(bw // 128):
                    c = t0 // 128 + ti
                    r0 = c * 128
                    tsl = slice(ti * 128, (ti + 1) * 128)
                    for ch in range(8):
                        O.tr(pX[:, ch * 128:(ch + 1) * 128], uT[:, ch, tsl], identb[:], [uT_r, identb_r], [pX_r], inc=(ch == 7))
                    for g in range(2):
                        O.tr(pB[:, g * 128:(g + 1) * 128], uT[:, 8 + g, tsl], identb[:], [uT_r, identb_r], [pB_r], inc=(g == 1))
                    xs_t, xs_r = xsb[c % 2]
                    O.copy('act', xs_t[:], pX[:], [pX_r], [xs_r])
                    O.dma(sxs[r0:r0 + 128, :], xs_t[:], [xs_r], rs_of(sxs_r, r0, 128))
                    O.copy('act', Btm[:], pB[:, 0:256], [pB_r], [Btm_r])
                    dt = xa[:, ti, 7, :]
                    dew = xw[:, ti, :]
                    vt, vt_r = Vt[c % 2]
                    pXv = pX[:].rearrange("p (h e) -> p h e", h=16)
                    for d in range(2):
                        O.tt('dve', vt[:, d, :].rearrange("p (h e) -> p h e", h=16), pXv,
                             bc_mid(dt[:, d * 16:(d + 1) * 16], [128, 16, 64]), ALU.mult, [pX_r, x_r], [vt_r])
                        O.tt('dve', Vw[:, d, :].rearrange("p (h e) -> p h e", h=16), pXv,
                             bc_mid(dew[:, d * 16:(d + 1) * 16], [128, 16, 64]), ALU.mult, [pX_r, xw_r], [Vw_r])
                        O.dma(sVt[d, r0:r0 + 128, :], vt[:, d, :], [vt_r], rs_of(sVt_r, r0, 128))
                    for d in range(2):
                        for g in range(2):
                            O.mm(pL[:, d * 2 + g, :], Btm[:, g * 128:(g + 1) * 128], Vw[:, d, g * 512:(g + 1) * 512], True, True,
                                 [Btm_r, Vw_r], [pL_r], inc=(d == 1 and g == 1))
                    sb_t, sb_r = Sfb[c % 2]
                    O.copy('act', sb_t[:], Sf[:], [Sf_r], [sb_r])
                    O.dma(sprev[0, c, :, :], sb_t[:], [sb_r], [sprev_r[c]])
                    O.tt('dve', Stmp[:].rearrange("p (h e) -> p h e", h=16), Sf[:].rearrange("p (h e) -> p h e", h=16),
                         bc_mid(cdall[:, c, 0:16], [128, 16, 64]), ALU.mult, [Sf_r, cd_r], [Stmp_r])
                    O.tt('dve', Sf[:].rearrange("p (g n) -> p g n", g=2), Stmp[:].rearrange("p (g n) -> p g n", g=2),
                         pL[:, 0:2, :], ALU.add, [Stmp_r, pL_r], [Sf_r])
                    lb_t, lb_r = Lb[c % 2]
                    O.copy('act', lb_t[:].rearrange("p (g n) -> p g n", g=2), pL[:, 2:4, :], [pL_r], [lb_r])
                    O.dma(sLb[c, :, :], lb_t[:], [lb_r], [sLb_r[c]])
            O.memset('dve', Sf[:], 0.0, [Sf_r])
            for c in BWD_ORDER:
                lb_t, lb_r = Lb[c % 2]
                sb_t, sb_r = Sfb[c % 2]
                O.dma(lb_t[:], sLb[c, :, :], [sLb_r[c]], [lb_r])
                O.copy('act', sb_t[:], Sf[:], [Sf_r], [sb_r])
                O.dma(sprev[1, c, :, :], sb_t[:], [sb_r], [sprev_r[NCH + c]])
                O.tt('dve', Stmp[:].rearrange("p (h e) -> p h e", h=16), Sf[:].rearrange("p (h e) -> p h e", h=16),
                     bc_mid(cdall[:, c, 16:32], [128, 16, 64]), ALU.mult, [Sf_r, cd_r], [Stmp_r])
                O.tt('dve', Sf[:], Stmp[:], lb_t[:], ALU.add, [Stmp_r, lb_r], [Sf_r])
            P.barrier()

        zs, zs_r = G['zs'], G['zs_r']
        dskt, dsk_r = C['dsk']
        sngt, sng_r = C['sng']
        yT, yT_r = G['yT'], G['yT_r']
        with ExitStack() as es:
            xs2 = [b.sb(es, "xs2", [128, D], BF16) for _ in range(2)]
            z2 = [b.sb(es, "z2", [128, D], BF16) for _ in range(2)]
            yacc, yacc_r = b.sb(es, "yacc", [128, D], F32)
            szt, sz_r = b.sb(es, "szt", [128, D], F32)
            ssq, ssq_r = b.sb(es, "ssq", [128, 2], F32)
            ybf, ybf_r = b.sb(es, "ybf", [128, D], BF16)
            yTs = [b.sb(es, "yTs", [128, 8, 128], BF16) for _ in range(2)]
            pYT, pYT_r = b.ps(es, "pYT", [128, D], BF16)

            def load_extra(c):
                r0 = c * 128
                x_t, x_r = xs2[c % 2]
                z_t, z_r = z2[c % 2]
                O.dma(x_t[:], sxs[r0:r0 + 128, :], rs_of(sxs_r, r0, 128), [x_r])
                O.dma(z_t[:], zs[r0:r0 + 128, :], rs_of(zs_r, r0, 128), [z_r])

            def finish(c, Yf, Yf_r, Yb, Yb_r):
                r0 = c * 128
                x_t, x_r = xs2[c % 2]
                z_t, z_r = z2[c % 2]
                O.tt('pool', yacc[:], Yf[:, 0:D], Yb[:, 0:D], ALU.add, [Yf_r, Yb_r], [yacc_r])
                O.tt('dve', szt[:].rearrange("p (h e) -> p h e", h=16), x_t[:].rearrange("p (h e) -> p h e", h=16),
                     bc_mid(dskt[:, :], [128, 16, 64]), ALU.mult, [x_r, dsk_r], [sz_r])
                O.tt('dve', yacc[:], yacc[:], szt[:], ALU.add, [yacc_r, sz_r], [yacc_r])
                O.act(szt[:], z_t[:], AF.Silu, [z_r, sz_r], [sz_r])
                O.tt('dve', yacc[:], yacc[:], szt[:], ALU.mult, [yacc_r, sz_r], [yacc_r])
                O.act(szt[:], yacc[:], AF.Square, [yacc_r, sz_r], [sz_r, ssq_r], accum=ssq[:, 0:1])
                O.act(ssq[:, 1:2], ssq[:, 0:1], AF.Sqrt, [ssq_r], [ssq_r], scale=1.0 / D, bias=EPS)
                O.recip(ssq[:, 1:2], ssq[:, 1:2], [ssq_r], [ssq_r])
                O.stt(ybf[:], yacc[:], ssq[:, 1:2], sngt[:], ALU.mult, ALU.mult, [yacc_r, ssq_r, sng_r], [ybf_r])
                yt, yt_r = yTs[c % 2]
                for k in range(8):
                    O.tr(pYT[:, k * 128:(k + 1) * 128], ybf[:, k * 128:(k + 1) * 128], identb[:], [ybf_r, identb_r], [pYT_r], inc=(k == 7))
                O.copy('act', yt[:].rearrange("p k t -> p (k t)"), pYT[:], [pYT_r], [yt_r])
                O.dma(yT[0, :, :, r0:r0 + 128].rearrange("k p t -> p k t"), yt[:], [yt_r], rs_of(yT_r, r0, 128))

            la_pass2(b, O, G, es, NG=2, NSUB=8, PV=64, QT=sCT, QT_r=sCT_r, KT=sBT, KT_r=sBT_r, g0=0,
                     Vt=sVt, Vt_r=sVt_r, vcol0=0, sm=ssm, sm_r=ssm_r, NA=16, a0=0, sprev=sprev, sprev_r=sprev_r,
                     load_extra=load_extra, finish=finish)
            P.barrier()


def la_pass2(b, O, G, es, NG, NSUB, PV, QT, QT_r, KT, KT_r, g0, Vt, Vt_r, vcol0, sm, sm_r, NA, a0, sprev, sprev_r,
             load_extra, finish):
    P = O.P
    masks, masks_r = G['masks'], G['masks_r']
    GW = NSUB * PV
    NH = NG * NSUB
    W = NG * GW
    qt = [b.sb(es, "qt", [128, NG, 128], BF16) for _ in range(2)]
    kt = [b.sb(es, "kt", [128, NG, 128], BF16) for _ in range(2)]
    vt = [b.sb(es, "vt2", [128, 2, W], BF16) for _ in range(2)]
    smt = [b.sb(es, "smt", [128, 4 * NA], F32) for _ in range(2)]
    spt = [b.sb(es, "spt", [128, 2, W], BF16) for _ in range(2)]
    Gm, Gm_r = b.sb(es, "Gm", [128, 2, NG, 128], F32)
    segL = [b.sb(es, "segL", [128, 4, 128], F32) for _ in range(2)]
    Ee = [b.sb(es, "Ee", [128, 4, 128], F32) for _ in range(2)]
    MT = [b.sb(es, "MT", [128, 4, 128], BF16) for _ in range(2)]
    ytmp, ytmp_r = b.sb(es, "ytmp", [128, NG, GW], F32)
    Yd2 = [[b.sb(es, "Yd", [128, NG, GW], F32) for _ in range(2)] for _ in range(2)]
    pending = []
    pG, pG_r = b.ps(es, "pG", [128, 512], F32)
    pS = [b.ps(es, "pS", [128, 4, 128], F32) for _ in range(2)]
    ydg, ydg_r = b.ps(es, "ydg", [128, NG, 512], F32)
    yof, yof_r = b.ps(es, "yof", [128, NG, 512], F32)
    nseg = 0
    for c in range(NCH):
        r0 = c * 128
        Yd = Yd2[c % 2]
        q_t, q_r = qt[c % 2]
        k_t, k_r = kt[c % 2]
        v_t, v_r = vt[c % 2]
        s_t, s_r = smt[c % 2]
        p_t, p_r = spt[c % 2]
        O.dma(q_t[:], QT[g0:g0 + NG, :, r0:r0 + 128].rearrange("g p t -> p g t"), rs_of(QT_r, r0, 128), [q_r])
        O.dma(k_t[:], KT[g0:g0 + NG, :, r0:r0 + 128].rearrange("g p t -> p g t"), rs_of(KT_r, r0, 128), [k_r])
        for d in range(2):
            O.dma(v_t[:, d, :], Vt[d, r0:r0 + 128, vcol0:vcol0 + W], rs_of(Vt_r, r0, 128), [v_r])
            O.dma(p_t[:, d, :], sprev[d, c, :, vcol0:vcol0 + W], [sprev_r[d * NCH + c]], [p_r])
        O.dma(s_t[:], sm[r0:r0 + 128, :], rs_of(sm_r, r0, 128), [s_r])
        load_extra(c)
        for g in range(NG):
            O.mm(pG[:, g * 128:(g + 1) * 128], k_t[:, g, :], q_t[:, g, :], True, True, [k_r, q_r], [pG_r], inc=(g == NG - 1))
        for d in range(2):
            O.tt('dve', Gm[:, d, :, :], pG[:, 0:NG * 128].rearrange("p (g l) -> p g l", g=NG),
                 bc_lead(masks[:, d, :], [128, NG, 128]), ALU.mult, [pG_r, masks_r], [Gm_r])
        rounds = [(d, h0, min(4, NH - h0)) for d in range(2) for h0 in range(0, NH, 4)]
        first_in_bank = {0: [True] * NG, 1: [True] * NG}

        def stage_a(ri, slot):
            d, h0, nh = rounds[ri]
            sl_t, sl_r = segL[slot % 2]
            ps_t, ps_r = pS[slot % 2]
            col = d * NA + a0 + h0
            O.tt('dve', sl_t[:, 0:nh, :], bc_lead(masks[:, 2 + d, :], [128, nh, 128]),
                 bc_mid(s_t[:, col:col + nh], [128, nh, 128]), ALU.mult, [masks_r, s_r], [sl_r])
            for i in range(nh):
                O.mm(ps_t[:, i, :], sl_t[:, i, :], masks[:, d, :], True, True, [sl_r, masks_r], [ps_r], inc=(i == nh - 1))

        def stage_b(ri, slot):
            d, h0, nh = rounds[ri]
            y_t, y_r = Yd[d]
            e_t, e_r = Ee[slot % 2]
            m_t, m_r = MT[slot % 2]
            ps_t, ps_r = pS[slot % 2]
            O.act(e_t[:, 0:nh, :], ps_t[:, 0:nh, :], AF.Exp, [ps_r], [e_r])
            g = h0 // NSUB
            if NSUB >= 4:
                O.tt('dve', m_t[:, 0:nh, :], e_t[:, 0:nh, :], bc_lead(Gm[:, d, g, :], [128, nh, 128]), ALU.mult, [e_r, Gm_r], [m_r])
            else:
                for i in range(nh):
                    gi = (h0 + i) // NSUB
                    O.tt('dve', m_t[:, i, :], e_t[:, i, :], Gm[:, d, gi, :], ALU.mult, [e_r, Gm_r], [m_r])
            for i in range(nh):
                h = h0 + i
                gi, sub = h // NSUB, h % NSUB
                O.mm(ydg[:, gi, sub * PV:(sub + 1) * PV], m_t[:, i, :], v_t[:, d, h * PV:(h + 1) * PV],
                     first_in_bank[d][gi], True, [m_r, v_r], [ydg_r], inc=(i == nh - 1))
                first_in_bank[d][gi] = False
            if h0 + nh >= NH:
                for g2 in range(NG):
                    O.mm(yof[:, g2, 0:GW], q_t[:, g2, :], p_t[:, d, g2 * GW:(g2 + 1) * GW], True, True, [q_r, p_r], [yof_r], inc=(g2 == NG - 1))
                rsc = s_t[:, 2 * NA + d * NA + a0:2 * NA + d * NA + a0 + NH]
                O.tt('dve', ytmp[:].rearrange("p g (s e) -> p (g s) e", s=NSUB), yof_view(yof, NG, NSUB, PV),
                     bc_mid(rsc, [128, NH, PV]), ALU.mult, [yof_r, s_r], [ytmp_r])
                O.tt('dve', y_t[:], ytmp[:], ydg[:, :, 0:GW], ALU.add, [ytmp_r, ydg_r], [y_r])

        stage_a(0, nseg)
        for ri in range(len(rounds)):
            if ri + 1 < len(rounds):
                stage_a(ri + 1, nseg + 1)
            stage_b(ri, nseg)
            nseg += 1
            if ri == 0 and pending:
                pending.pop()()
        pending.append(lambda c=c, Yd=Yd: finish(c, Yd[0][0][:].rearrange("p g w -> p (g w)"), Yd[0][1],
                                                 Yd[1][0][:].rearrange("p g w -> p (g w)"), Yd[1][1]))
    while pending:
        pending.pop()()


def yof_view(yof, NG, NSUB, PV):
    if NSUB * PV == 512:
        return yof[:].rearrange("p g (s e) -> p (g s) e", s=NSUB)
    assert NSUB == 1
    return yof[:, :, 0:PV]


MLW = 4 * 257


def phase_mlstm(b, l, G, C):
    P = b.P
    O = Ops(P)
    S = G['scr']
    masks, masks_r, onesf, onesf_r = G['masks'], G['masks_r'], G['onesf'], G['onesf_r']
    identb, identb_r = G['identb'], G['identb_r']
    mqkT, mqkT_r = G['mqkT'], G['mqkT_r']
    small, small_r = G['small'], G['small_r']
    mvo, mvo_r = G['mvo'], G['mvo_r']
    mz, mz_r = G['mz'], G['mz_r']
    mQT, mQT_r = S('mQT', [4, 128, NT], BF16)
    mKT, mKT_r = S('mKT', [4, 128, NT], BF16)
    mVt, mVt_r = S('mVt', [2, NT, MLW], BF16)
    msm, msm_r = S('msm', [NT, 16], F32)
    mprev, mprev_r = S('mprev', [2, NCH, 128, MLW], BF16, 2 * NCH)
    mLb, mLb_r = S('mLb', [NCH, 128, MLW], F32, NCH)
    mh, mh_r = S('mh', [NT, 512], F32)
    cwt, cw_r = C['mcw']
    cbt, cb_r = C['mcb']
    ibt, ib_r = C['ib']
    fbt, fb_r = C['fb']
    LNS = math.log(128.0 ** -0.5)

    with ExitStack() as es0:
        cdall, cd_r = b.sb(es0, "mcdall", [128, NCH, 8], F32)
        with ExitStack() as es:
            xh, xh_r = b.sb(es, "mxh", [128, 8, 514], BF16)
            tmps = [b.sb(es, "mctmp", [128, 512], F32) for _ in range(2)]
            uT, uT_r = b.sb(es, "muT", [128, 8, 512], BF16)
            pK, pK_r = b.ps(es, "pK", [128, 1024], BF16)
            pc, pc_r = b.ps(es, "mpc", [128, 512], F32)
            pL, pL_r = b.ps(es, "mpL", [128, 4, 512], F32)
            Ktm, Ktm_r = b.sb(es, "Ktm", [128, 512], BF16)
            vin = [b.sb(es, "vin", [128, D], BF16) for _ in range(2)]
            sml, sml_r = b.sb(es, "msml", [128, 128], F32)
            smo = [b.sb(es, "msmo", [128, 16], F32) for _ in range(2)]
            PT, PT_r = b.sb(es, "mPT", [128, 32], F32)
            Et, E_r = b.sb(es, "mEt", [128, 32], F32)
            Vt = [b.sb(es, "mVt", [128, 2, MLW], BF16) for _ in range(2)]
            Vw, Vw_r = b.sb(es, "mVw", [128, 2, MLW], BF16)
            xab = [b.sb(es, "mxab", [128, 4, 8, 8], F32) for _ in range(2)]
            xwb = [b.sb(es, "mxwb", [128, 4, 8], F32) for _ in range(2)]
            swbb = [b.sb(es, "mswb", [128, 4, 8], F32) for _ in range(2)]
            Tsc, Tsc_r = b.sb(es, "mTsc", [128, 1, 4, 8], F32)
            Sf, Sf_r = b.sb(es, "mSf", [128, MLW], F32)
            Stmp, Stmp_r = b.sb(es, "mStmp", [128, MLW], F32)
            Sfb = [b.sb(es, "mSfb", [128, MLW], BF16) for _ in range(2)]
            Lb = [b.sb(es, "mLb", [128, MLW], F32) for _ in range(2)]
            O.memset('dve', Sf[:], 0.0, [Sf_r])
            for (t0, bw) in TBLKS:
                conv_silu(b, O, l, mqkT, mqkT_r, t0, bw, 8, cwt, cw_r, cbt, cb_r, xh, xh_r, tmps, uT, uT_r)
                O.dma(mQT[:, :, t0:t0 + bw].rearrange("g p t -> p g t"), uT[:, 0:4, 0:bw], [uT_r], rs_of(mQT_r, t0, bw))
                O.dma(mKT[:, :, t0:t0 + bw].rearrange("g p t -> p g t"), uT[:, 4:8, 0:bw], [uT_r], rs_of(mKT_r, t0, bw))
                nt = bw // 128
                bslot = (t0 // 512) % 2
                xa, x_r = xab[bslot]
                xw, xw_r = xwb[bslot]
                swb, sw_r = swbb[bslot]
                O.dma(xa[:, 0:nt, 1:3, :], small[t0:t0 + bw, 32:48].rearrange("(t p) (s c) -> p t s c", p=128, s=2), rs_of(small_r, t0, bw), [x_r])
                O.tt('dve', swb[:, 0:nt, :], xa[:, 0:nt, 1, :], bc_lead(ibt[:], [128, nt, 8]), ALU.add, [x_r, ib_r], [sw_r])
                O.act(swb[:, 0:nt, :], swb[:, 0:nt, :], AF.Exp, [sw_r], [sw_r])
                O.ts('dve', swb[:, 0:nt, :], swb[:, 0:nt, :], 128.0 ** -0.5, None, ALU.mult, None, [sw_r], [sw_r])
                O.tt('dve', xa[:, 0:nt, 3, :], xa[:, 0:nt, 2, :], bc_lead(fbt[:], [128, nt, 8]), ALU.add, [x_r, fb_r], [x_r])
                O.ts('dve', xa[:, 0:nt, 3, :], xa[:, 0:nt, 3, :], -1.0, None, ALU.mult, None, [x_r], [x_r])
                softplus(b, O, xa[:, 0:nt, 7, :], xa[:, 0:nt, 3, :], xa[:, 0:nt, 4, :], xa[:, 0:nt, 5, :], [], [x_r])
                O.ts('dve', xa[:, 0:nt, 0, :], xa[:, 0:nt, 7, :], -1.0, None, ALU.mult, None, [x_r], [x_r])
                block_scalars(b, O, G, t0, bw, xa, x_r, 8, 4, None, None, pc, pc_r, Tsc, Tsc_r, cdall, cd_r, msm, msm_r)
                O.tt('dve', xw[:, 0:nt, :], swb[:, 0:nt, :], xa[:, 0:nt, 6, :], ALU.mult, [x_r, sw_r], [xw_r])
                for ti in range(bw // 128):
                    c = t0 // 128 + ti
                    r0 = c * 128
                    tsl = slice(ti * 128, (ti + 1) * 128)
                    for h in range(4):
                        O.tr(pK[:, h * 128:(h + 1) * 128], uT[:, 4 + h, tsl], identb[:], [uT_r, identb_r], [pK_r], inc=(h == 3))
                    O.copy('act', Ktm[:], pK[:, 0:512], [pK_r], [Ktm_r])
                    v_t, v_r = vin[c % 2]
                    O.dma(v_t[:], mvo[r0:r0 + 128, 0:D], rs_of(mvo_r, r0, 128), [v_r])
                    vt, vt_r = Vt[c % 2]
                    vv = v_t[:].rearrange("p (h e) -> p h e", h=4)
                    for d in range(2):
                        sw = swb[:, ti, d * 4:(d + 1) * 4]
                        ww = xw[:, ti, d * 4:(d + 1) * 4]
                        vo = vt[:, d, :].rearrange("p (h e) -> p h e", h=4)
                        wo = Vw[:, d, :].rearrange("p (h e) -> p h e", h=4)
                        O.tt('dve', vo[:, :, 0:256], vv, bc_mid(sw, [128, 4, 256]), ALU.mult, [v_r, sw_r], [vt_r])
                        O.copy('dve', vo[:, :, 256], sw, [sw_r], [vt_r])
                        O.tt('dve', wo[:, :, 0:256], vv, bc_mid(ww, [128, 4, 256]), ALU.mult, [v_r, xw_r], [Vw_r])
                        O.copy('dve', wo[:, :, 256], ww, [xw_r], [Vw_r])
                        O.dma(mVt[d, r0:r0 + 128, :], vt[:, d, :], [vt_r], rs_of(mVt_r, r0, 128))
                    sb_t, sb_r = Sfb[c % 2]
                    lb_t, lb_r = Lb[c % 2]
                    for d in range(2):
                        for h in range(4):
                            O.mm(pL[:, h, 0:257], Ktm[:, h * 128:(h + 1) * 128], Vw[:, d, h * 257:(h + 1) * 257], True, True,
                                 [Ktm_r, Vw_r], [pL_r], inc=(h == 3))
                        if d == 0:
                            O.copy('act', sb_t[:], Sf[:], [Sf_r], [sb_r])
                            O.dma(mprev[0, c, :, :], sb_t[:], [sb_r], [mprev_r[c]])
                            O.tt('dve', Stmp[:].rearrange("p (h e) -> p h e", h=4), Sf[:].rearrange("p (h e) -> p h e", h=4),
                                 bc_mid(cdall[:, c, 0:4], [128, 4, 257]), ALU.mult, [Sf_r, cd_r], [Stmp_r])
                            O.tt('dve', Sf[:].rearrange("p (h e) -> p h e", h=4), Stmp[:].rearrange("p (h e) -> p h e", h=4),
                                 pL[:, :, 0:257], ALU.add, [Stmp_r, pL_r], [Sf_r])
                        else:
                            O.copy('act', lb_t[:].rearrange("p (h e) -> p h e", h=4), pL[:, :, 0:257], [pL_r], [lb_r])
                            O.dma(mLb[c, :, :], lb_t[:], [lb_r], [mLb_r[c]])
            O.memset('dve', Sf[:], 0.0, [Sf_r])
            for c in BWD_ORDER:
                lb_t, lb_r = Lb[c % 2]
                sb_t, sb_r = Sfb[c % 2]
                O.dma(lb_t[:], mLb[c, :, :], [mLb_r[c]], [lb_r])
                O.copy('act', sb_t[:], Sf[:], [Sf_r], [sb_r])
                O.dma(mprev[1, c, :, :], sb_t[:], [sb_r], [mprev_r[NCH + c]])
                O.tt('dve', Stmp[:].rearrange("p (h e) -> p h e", h=4), Sf[:].rearrange("p (h e) -> p h e", h=4),
                     bc_mid(cdall[:, c, 4:8], [128, 4, 257]), ALU.mult, [Sf_r, cd_r], [Stmp_r])
                O.tt('dve', Sf[:], Stmp[:], lb_t[:], ALU.add, [Stmp_r, lb_r], [Sf_r])
            P.barrier()

        mngt, mng_r = C['mng']
        yT, yT_r = G['yT'], G['yT_r']
        for pair in range(2):
            with ExitStack() as es:
                dn, dn_r = b.sb(es, "dn", [128, 8], F32)
                hp, hp_r = b.sb(es, "hp", [128, 2, 256], F32)
                hq, hq_r = b.sb(es, "hq", [128, 2, 256], F32)
                hall = [b.sb(es, "hall", [128, D], F32) for _ in range(2)]
                o2 = [b.sb(es, "o2", [128, D], BF16) for _ in range(2)]
                z2 = [b.sb(es, "mz2", [128, D], BF16) for _ in range(2)]
                sg, sg_r = b.sb(es, "sg", [128, D], F32)
                st4, st4_r = b.sb(es, "st4", [128, 8], F32)
                ybf, ybf_r = b.sb(es, "mybf", [128, D], BF16)
                yTs = [b.sb(es, "myTs", [128, 8, 128], BF16) for _ in range(2)]
                pYT, pYT_r = b.ps(es, "mpYT", [128, D], BF16)

                def load_extra(c, pair=pair):
                    if pair == 0:
                        return
                    r0 = c * 128
                    h_t, h_r = hall[c % 2]
                    o_t, o_r = o2[c % 2]
                    z_t, z_r = z2[c % 2]
                    O.dma(h_t[:, 0:512], mh[r0:r0 + 128, :], rs_of(mh_r, r0, 128), [h_r])
                    O.dma(o_t[:], mvo[r0:r0 + 128, D:2 * D], rs_of(mvo_r, r0, 128), [o_r])
                    O.dma(z_t[:], mz[r0:r0 + 128, :], rs_of(mz_r, r0, 128), [z_r])

                def finish(c, Yf, Yf_r, Yb, Yb_r, pair=pair):
                    r0 = c * 128
                    h_t, h_r = hall[c % 2]
                    Yfv = Yf.rearrange("p (g w) -> p g w", g=2)
                    Ybv = Yb.rearrange("p (g w) -> p g w", g=2)
                    O.act(dn[:, 0:2], Yfv[:, :, 256], AF.Abs, [Yf_r], [dn_r])
                    O.act(dn[:, 2:4], Ybv[:, :, 256], AF.Abs, [Yb_r], [dn_r])
                    O.ts('dve', dn[:, 0:4], dn[:, 0:4], 1.0, None, ALU.max, None, [dn_r], [dn_r])
                    O.recip(dn[:, 4:8], dn[:, 0:4], [dn_r], [dn_r])
                    O.tt('dve', hp[:], Yfv[:, :, 0:256], bc_mid(dn[:, 4:6], [128, 2, 256]), ALU.mult, [Yf_r, dn_r], [hp_r])
                    O.tt('dve', hq[:], Ybv[:, :, 0:256], bc_mid(dn[:, 6:8], [128, 2, 256]), ALU.mult, [Yb_r, dn_r], [hq_r])
                    if pair == 0:
                        O.tt('pool', hp[:], hp[:], hq[:], ALU.add, [hp_r, hq_r], [hp_r])
                        O.dma(mh[r0:r0 + 128, :], hp[:].rearrange("p g e -> p (g e)"), [hp_r], rs_of(mh_r, r0, 128))
                        return
                    o_t, o_r = o2[c % 2]
                    z_t, z_r = z2[c % 2]
                    O.tt('pool', h_t[:, 512:1024], hp[:].rearrange("p g e -> p (g e)"), hq[:].rearrange("p g e -> p (g e)"),
                         ALU.add, [hp_r, hq_r], [h_r])
                    O.act(sg[:], o_t[:], AF.Sigmoid, [o_r], [sg_r])
                    O.tt('dve', h_t[:], h_t[:], sg[:], ALU.mult, [h_r, sg_r], [h_r])
                    O.tt('dve', sg[:], h_t[:], h_t[:], ALU.mult, [h_r, sg_r], [sg_r])
                    O.reduce(st4[:, 0:4], sg[:].rearrange("p (h e) -> p h e", h=4), ALU.add, [sg_r], [st4_r])
                    O.act(st4[:, 4:8], st4[:, 0:4], AF.Sqrt, [st4_r], [st4_r], scale=1.0 / 256, bias=EPS)
                    O.recip(st4[:, 4:8], st4[:, 4:8], [st4_r], [st4_r])
                    O.tt('dve', h_t[:].rearrange("p (h e) -> p h e", h=4), h_t[:].rearrange("p (h e) -> p h e", h=4),
                         bc_mid(st4[:, 4:8], [128, 4, 256]), ALU.mult, [h_r, st4_r], [h_r])
                    O.tt('dve', h_t[:], h_t[:], mngt[:], ALU.mult, [h_r, mng_r], [h_r])
                    O.act(sg[:], z_t[:], AF.Silu, [z_r, sg_r], [sg_r])
                    O.tt('dve', ybf[:], h_t[:], sg[:], ALU.mult, [h_r, sg_r], [ybf_r])
                    yt, yt_r = yTs[c % 2]
                    for k in range(8):
                        O.tr(pYT[:, k * 128:(k + 1) * 128], ybf[:, k * 128:(k + 1) * 128], identb[:], [ybf_r, identb_r], [pYT_r], inc=(k == 7))
                    O.copy('act', yt[:].rearrange("p k t -> p (k t)"), pYT[:], [pYT_r], [yt_r])
                    O.dma(yT[2, :, :, r0:r0 + 128].rearrange("k p t -> p k t"), yt[:], [yt_r], rs_of(yT_r, r0, 128))

                la_pass2(b, O, G, es, NG=2, NSUB=1, PV=257, QT=mQT, QT_r=mQT_r, KT=mKT, KT_r=mKT_r, g0=2 * pair,
                         Vt=mVt, Vt_r=mVt_r, vcol0=2 * pair * 257, sm=msm, sm_r=msm_r, NA=4, a0=2 * pair,
                         sprev=mprev, sprev_r=mprev_r, load_extra=load_extra, finish=finish)
                P.barrier()


def phase_attn(b, l, G, C):
    P = b.P
    O = Ops(P)
    S = G['scr']
    identb, identb_r = G['identb'], G['identb_r']
    onesb, onesb_r = G['onesb'], G['onesb_r']
    aqk, aqk_r = G['aqk'], G['aqk_r']
    avz, avz_r = G['avz'], G['avz_r']
    azT, azT_r = G['azT'], G['azT_r']
    yT, yT_r = G['yT'], G['yT_r']
    cos_in, sin_in = G['cos_in'], G['sin_in']
    aQT, aQT_r = S('aQT', [8, 128, NT], BF16)
    aKT, aKT_r = S('aKT', [8, 128, NT], BF16)
    gqt, gq_r = C['gq']
    gkt, gk_r = C['gk']
    lampt, lamp_r = C['lamp']
    subgt, subg_r = C['subg']
    lamit, lami_r = C['lami']

    with ExitStack() as es0:
        sc, sc_r = b.sb(es0, "asc", [128, 16], F32)
        ggt, gg_r = b.sb(es0, "ggt", [128, 2, 64], F32)
        tmp64, tmp64_r = b.sb(es0, "tmp64", [128, 64], F32)
        O.ts('dve', ggt[:, 0, :], gqt[:], 0.125, None, ALU.mult, None, [gq_r], [gg_r])
        O.copy('dve', ggt[:, 1, :], gkt[:], [gk_r], [gg_r])
        O.reduce(sc[:, 4:6], ggt[:], ALU.max, [gg_r], [sc_r], absval=True)
        O.tt('dve', sc[:, 6:7], sc[:, 4:5], sc[:, 5:6], ALU.mult, [sc_r], [sc_r])
        O.ts('dve', sc[:, 0:1], sc[:, 6:7], -64.0, None, ALU.mult, None, [sc_r], [sc_r])
        for i in range(2):
            O.tt('dve', tmp64[:], lampt[:, 2 * i, :], lampt[:, 2 * i + 1, :], ALU.mult, [lamp_r], [tmp64_r])
            O.reduce(sc[:, 7 + i:8 + i], tmp64[:], ALU.add, [tmp64_r], [sc_r])
        O.act(sc[:, 7:9], sc[:, 7:9], AF.Exp, [sc_r], [sc_r])
        O.tt('dve', sc[:, 9:10], sc[:, 7:8], sc[:, 8:9], ALU.subtract, [sc_r], [sc_r])
        O.tt('dve', sc[:, 1:2], sc[:, 9:10], lamit[:, 0:1], ALU.add, [sc_r, lami_r], [sc_r])
        O.ts('dve', sc[:, 2:3], sc[:, 1:2], -1.0, None, ALU.mult, None, [sc_r], [sc_r])
        O.ts('dve', sc[:, 10:11], lamit[:, 0:1], -1.0, 1.0, ALU.mult, ALU.add, [lami_r], [sc_r])
        O.tt('dve', sc[:, 3:4], sc[:, 10:11], subgt[:, 0:1], ALU.mult, [sc_r, subg_r], [sc_r])

        with ExitStack() as es:
            qk = [b.sb(es, "qkraw", [128, 2048], BF16) for _ in range(2)]
            cs = [b.sb(es, "cs", [128, 2, 64], F32) for _ in range(2)]
            sq, sq_r = b.sb(es, "sq", [128, 2048], F32)
            xn, xn_r = b.sb(es, "xn", [128, 2048], F32)
            t2, t2_r = b.sb(es, "t2", [128, 2048], F32)
            st, st_r = b.sb(es, "ast", [128, 64], F32)
            xr, xr_r = b.sb(es, "xr", [128, 2048], BF16)
            pQ = [b.ps(es, "pQ", [128, 1024], BF16) for _ in range(2)]
            qTt = [b.sb(es, "qTt", [128, 2, 8, 128], BF16) for _ in range(2)]
            for i in range(NCH):
                r0 = i * 128
                q_t, q_r = qk[i % 2]
                O.dma(q_t[:], aqk[r0:r0 + 128, :], rs_of(aqk_r, r0, 128), [q_r])
                O.tt('dve', sq[:], q_t[:], q_t[:], ALU.mult, [q_r], [sq_r])
                O.reduce(st[:, 0:32], sq[:].rearrange("p (g e) -> p g e", e=64), ALU.add, [sq_r], [st_r])
                O.act(st[:, 32:64], st[:, 0:32], AF.Sqrt, [st_r], [st_r], scale=1.0 / 64, bias=EPS)
                O.recip(st[:, 32:64], st[:, 32:64], [st_r], [st_r])
                O.tt('dve', xn[:].rearrange("p (g e) -> p g e", e=64), q_t[:].rearrange("p (g e) -> p g e", e=64),
                     bc_mid(st[:, 32:64], [128, 32, 64]), ALU.mult, [q_r, st_r], [xn_r])
                for hf in range(2):
                    xv = xn[:, hf * 1024:(hf + 1) * 1024].rearrange("p (g e) -> p g e", e=64)
                    O.tt('pool', xv, xv, bc_lead(ggt[:, hf, :], [128, 16, 64]), ALU.mult, [xn_r, gg_r], [xn_r])
                if i >= 2:
                    c_t, c_r = cs[i % 2]
                    lt = r0 - TC
                    O.dma(c_t[:, 0, :], cos_in[lt:lt + 128, :], [], [c_r])
                    O.dma(c_t[:, 1, :], sin_in[lt:lt + 128, :], [], [c_r])
                    xg = xn[:].rearrange("p (g r q e) -> p g r q e", g=32, r=2, q=2, e=16)
                    tg = t2[:].rearrange("p (g r q e) -> p g r q e", g=32, r=2, q=2, e=16)
                    sv = c_t[:, 1, :].rearrange("p (r q e) -> p r q e", r=2, q=2, e=16)
                    for qq in range(2):
                        O.tt('dve', tg[:, :, :, qq, :], xg[:, :, :, 1 - qq, :],
                             sv[:, :, qq, :].unsqueeze(1).to_broadcast([128, 32, 2, 16]), ALU.mult, [xn_r, c_r], [t2_r])
                    O.tt('pool', xn[:].rearrange("p (g e) -> p g e", e=64), xn[:].rearrange("p (g e) -> p g e", e=64),
                         bc_lead(c_t[:, 0, :], [128, 32, 64]), ALU.mult, [xn_r, c_r], [xn_r])
                    O.tt('dve', xr[:], xn[:], t2[:], ALU.add, [xn_r, t2_r], [xr_r])
                else:
                    O.copy('dve', xr[:], xn[:], [xn_r], [xr_r])
                qt_t, qt_r = qTt[i % 2]
                for w in range(2):
                    p_t, p_r = pQ[w]
                    for h in range(8):
                        O.tr(p_t[:, h * 128:(h + 1) * 128], xr[:, w * 1024 + h * 128:w * 1024 + (h + 1) * 128], identb[:],
                             [xr_r, identb_r], [p_r], inc=(h == 7))
                    O.copy('act', qt_t[:, w, :, :].rearrange("p h t -> p (h t)"), p_t[:], [p_r], [qt_r])
                O.dma(aQT[:, :, r0:r0 + 128].rearrange("h p t -> p h t"), qt_t[:, 0, :, :], [qt_r], rs_of(aQT_r, r0, 128))
                O.dma(aKT[:, :, r0:r0 + 128].rearrange("h p t -> p h t"), qt_t[:, 1, :, :], [qt_r], rs_of(aKT_r, r0, 128))
            P.barrier()

        with ExitStack() as es:
            KTs = [b.sb(es, "KTs", [128, NT], BF16) for _ in range(2)]
            Vh = [b.sb(es, "Vh", [128, NCH, 128], BF16) for _ in range(2)]
            qb = [b.sb(es, "qb", [128, 512], BF16) for _ in range(2)]
            zb = [b.sb(es, "zb", [128, 512], BF16) for _ in range(2)]
            Pt = [b.sb(es, "Pt", [128, 2, 512], BF16) for _ in range(6)]
            rc, rc_r = b.sb(es, "rc", [128, 2, 512], F32)
            accs, accs_r = b.sb(es, "accs", [128, 2, 512], F32)
            o0, o0_r = b.sb(es, "o0", [128, 512], F32)
            o1, o1_r = b.sb(es, "o1", [128, 512], F32)
            osq, osq_r = b.sb(es, "osq", [128, 512], BF16)
            rst, rst_r = b.sb(es, "rst", [128, 512], F32)
            szt, sz_r = b.sb(es, "aszt", [128, 512], F32)
            yo = [b.sb(es, "yo", [128, 512], BF16) for _ in range(2)]
            pSc = [b.ps(es, "pSc", [128, 2, 512], F32) for _ in range(2)]
            pO, pO_r = b.ps(es, "pO", [128, 2, 512], F32)
            pSm, pSm_r = b.ps(es, "pSm", [128, 2, 512], F32)
            pSmA_r = Res('pSmA')
            it = 0
            nb = 0
            for h in range(8):
                k_t, k_r = KTs[h % 2]
                v_t, v_r = Vh[h % 2]
                O.dma(k_t[:], aKT[h, :, :], aKT_r, [k_r])
                for q4 in range(0, NCH, 11):
                    O.dma(v_t[:, q4:q4 + 11, :], avz[q4 * 128:(q4 + 11) * 128, h * 128:(h + 1) * 128].rearrange("(c p) e -> p c e", p=128),
                          rs_of(avz_r, q4 * 128, 11 * 128), [v_r])
                for (t0, bw) in TBLKS:
                    nk = 2 if t0 < TC else NCH
                    q_t, q_r = qb[nb % 2]
                    z_t, z_r = zb[nb % 2]
                    y_t, y_r = yo[nb % 2]
                    nb += 1
                    O.dma(q_t[:, 0:bw], aQT[h, :, t0:t0 + bw], rs_of(aQT_r, t0, bw), [q_r])
                    O.dma(z_t[:, 0:bw], azT[h, :, t0:t0 + bw], rs_of(azT_r, t0, bw), [z_r])
                    def emit_qk(kc, slot):
                        ps_t, ps_r = pSc[slot % 2]
                        for cm in range(2):
                            O.mm(ps_t[:, cm, 0:bw], k_t[cm * 64:(cm + 1) * 64, kc * 128:(kc + 1) * 128],
                                 q_t[cm * 64:(cm + 1) * 64, 0:bw], True, True, [k_r, q_r], [ps_r], inc=(cm == 1))
                    emit_qk(0, it)
                    for kc in range(nk):
                        ps_t, ps_r = pSc[it % 2]
                        p_t, p_r = Pt[it % 6]
                        if kc + 1 < nk:
                            emit_qk(kc + 1, it + 1)
                        O.act(p_t[:, :, 0:bw], ps_t[:, :, 0:bw], AF.Exp, [ps_r, sc_r], [p_r], bias=sc[:, 0:1])
                        for cm in range(2):
                            O.mm(pO[:, cm, 0:bw], v_t[:, kc, :], p_t[:, cm, 0:bw], kc == 0, kc == nk - 1, [v_r, p_r], [pO_r], inc=(cm == 1))
                        O.mm(pSm[:, 0, 0:bw], onesb[:, :], p_t[:, 0, 0:bw], kc == 0, kc == nk - 1, [onesb_r, p_r], [pSmA_r])
                        if kc == 0:
                            O.copy('dve', pSm[:, 1, 0:bw], p_t[:, 1, 0:bw], [p_r], [pSm_r])
                        else:
                            O.tt('dve', pSm[:, 1, 0:bw], pSm[:, 1, 0:bw], p_t[:, 1, 0:bw], ALU.add, [pSm_r, p_r], [pSm_r])
                        it += 1
                    O.copy('dve', accs[:, 1, 0:bw], pSm[:, 1, 0:bw], [pSm_r], [accs_r])
                    O.mm(pSm[:, 1, 0:bw], G['onesf'][:, :], accs[:, 1, 0:bw], True, True, [G['onesf_r'], accs_r], [pSm_r])
                    O.act(rc[:, :, 0:bw], pSm[:, :, 0:bw], AF.Ln, [pSm_r, pSmA_r], [rc_r])
                    O.act(rc[:, :, 0:bw], rc[:, :, 0:bw], AF.Exp, [rc_r], [rc_r], scale=-1.0)
                    O.tt('dve', o0[:, 0:bw], pO[:, 0, 0:bw], rc[:, 0, 0:bw], ALU.mult, [pO_r, rc_r], [o0_r])
                    O.tt('dve', o1[:, 0:bw], pO[:, 1, 0:bw], rc[:, 1, 0:bw], ALU.mult, [pO_r, rc_r], [o1_r])
                    O.stt(o0[:, 0:bw], o1[:, 0:bw], sc[:, 2:3], o0[:, 0:bw], ALU.mult, ALU.add, [o1_r, o0_r, sc_r], [o0_r])
                    O.tt('dve', osq[:, 0:bw], o0[:, 0:bw], o0[:, 0:bw], ALU.mult, [o0_r], [osq_r])
                    ps_t, ps_r = pSc[it % 2]
                    it += 1
                    O.mm(ps_t[:, 0, 0:bw], onesb[:, :], osq[:, 0:bw], True, True, [onesb_r, osq_r], [ps_r])
                    O.act(rst[:, 0:bw], ps_t[:, 0, 0:bw], AF.Ln, [ps_r], [rst_r], scale=1.0 / 128, bias=EPS)
                    O.act(rst[:, 0:bw], rst[:, 0:bw], AF.Exp, [rst_r], [rst_r], scale=-0.5)
                    O.act(szt[:, 0:bw], z_t[:, 0:bw], AF.Exp, [z_r], [sz_r], scale=-1.0)
                    O.act(szt[:, 0:bw], szt[:, 0:bw], AF.Ln, [sz_r], [sz_r], bias=1.0)
                    O.act(szt[:, 0:bw], szt[:, 0:bw], AF.Exp, [sz_r], [sz_r], scale=-1.0)
                    O.tt('dve', szt[:, 0:bw], szt[:, 0:bw], z_t[:, 0:bw], ALU.mult, [sz_r, z_r], [sz_r])
                    O.stt(o0[:, 0:bw], o0[:, 0:bw], sc[:, 3:4], rst[:, 0:bw], ALU.mult, ALU.mult, [o0_r, sc_r, rst_r], [o0_r])
                    O.tt('dve', y_t[:, 0:bw], o0[:, 0:bw], szt[:, 0:bw], ALU.mult, [o0_r, sz_r], [y_r])
                    O.dma(yT[1, h, :, t0:t0 + bw], y_t[:, 0:bw], [y_r], rs_of(yT_r, t0, bw))
            P.barrier()


def phase_merge(b, l, G, C):
    P = b.P
    O = Ops(P)
    yT, yT_r = G['yT'], G['yT_r']
    gtT, gtT_r = G['gtT'], G['gtT_r']
    w_br, w_out = G['w_br'], G['w_out']
    xsrc, xsrc_r, xdst, xdst_r, dst_off = G['xsrc'], G['xsrc_r'], G['xdst'], G['xdst_r'], G['dst_off']
    gbc, gbc_r = G['gbc'], G['gbc_r']
    with ExitStack() as es:
        stg = [b.sb(es, "wstg", [128, D], F32) for _ in range(2)]
        Wb, Wb_r = b.sb(es, "Wb", [128, 3, 8, D], BF16)
        Wo, Wo_r = b.sb(es, "Wo", [128, 8, D], BF16)
        yb = [b.sb(es, "yb", [128, 3, 8, 256], BF16) for _ in range(2)]
        gb = [b.sb(es, "gb", [128, 8, 256], BF16) for _ in range(2)]
        sg, sg_r = b.sb(es, "msg", [128, 8, 256], F32)
        macc, macc_r = b.sb(es, "macc", [128, 8, 256], F32)
        mt = [b.sb(es, "mtmp", [128, 512], F32) for _ in range(2)]
        mxT, mxT_r = b.sb(es, "mxT", [128, 8, 256], BF16)
        xin_t = [b.sb(es, "xin_t", [128, D], F32) for _ in range(2)]
        xo_t = [b.sb(es, "xo_t", [128, D], F32) for _ in range(2)]
        pm_ = [b.ps(es, "pmg", [128, 512], F32) for _ in range(4)]
        n = 0
        for br in range(4):
            for kc in range(8):
                st, st_r = stg[n % 2]
                n += 1
                src = w_br[l, br, kc * 128:(kc + 1) * 128, :] if br < 3 else w_out[l, kc * 128:(kc + 1) * 128, :]
                O.dma(st[:], src, [], [st_r])
                dst = Wb[:, br, kc, :] if br < 3 else Wo[:, kc, :]
                O.copy('pool', dst, st[:], [st_r], [Wb_r if br < 3 else Wo_r])
        npm = 0
        nm = 0
        for bi, (t0, bw) in enumerate([(256 * i, 256) for i in range(NT // 256)]):
            y_t, y_r = yb[bi % 2]
            for br in range(3):
                O.dma(y_t[:, br, :, 0:bw], yT[br, :, :, t0:t0 + bw].rearrange("k p t -> p k t"), rs_of(yT_r, t0, bw), [y_r])
            for br in range(3):
                g_t, g_r = gb[(bi * 3 + br) % 2]
                O.dma(g_t[:, :, 0:bw], gtT[br * 8:(br + 1) * 8, :, t0:t0 + bw].rearrange("k p t -> p k t"), rs_of(gtT_r, t0, bw), [g_r])
                O.act(sg[:, :, 0:bw], g_t[:, :, 0:bw], AF.Sigmoid, [g_r], [sg_r])
                for fc in range(8):
                    p_t, p_r = pm_[npm % 4]
                    npm += 1
                    for kc in range(8):
                        O.mm(p_t[:, 0:bw], Wb[:, br, kc, fc * 128:(fc + 1) * 128], y_t[:, br, kc, 0:bw], kc == 0, kc == 7,
                             [Wb_r, y_r], [p_r], inc=(kc == 7))
                    if br == 0:
                        O.tt('dve', macc[:, fc, 0:bw], p_t[:, 0:bw], sg[:, fc, 0:bw], ALU.mult, [p_r, sg_r], [macc_r])
                    else:
                        m_t, m_r = mt[nm % 2]
                        nm += 1
                        O.tt('dve', m_t[:, 0:bw], p_t[:, 0:bw], sg[:, fc, 0:bw], ALU.mult, [p_r, sg_r], [m_r])
                        if br == 1:
                            O.tt('pool', macc[:, fc, 0:bw], macc[:, fc, 0:bw], m_t[:, 0:bw], ALU.add, [macc_r, m_r], [macc_r])
                        else:
                            O.tt('pool', mxT[:, fc, 0:bw], macc[:, fc, 0:bw], m_t[:, 0:bw], ALU.add, [macc_r, m_r], [mxT_r])
            j = 1 if t0 < TC else 0
            for ti in range(bw // 128):
                r0 = t0 + ti * 128
                if r0 < dst_off:
                    continue
                xi, xi_r = xin_t[ti % 2]
                xo, xo_r = xo_t[ti % 2]
                O.dma(xi[:], xsrc[r0:r0 + 128, :], rs_of(xsrc_r, r0, 128), [xi_r])
                for hf in range(2):
                    p_t, p_r = pm_[npm % 4]
                    npm += 1
                    for kc in range(8):
                        O.mm(p_t[:, :], mxT[:, kc, ti * 128:(ti + 1) * 128], Wo[:, kc, hf * 512:(hf + 1) * 512], kc == 0, kc == 7,
                             [mxT_r, Wo_r], [p_r], inc=(kc == 7))
                    O.tt('dve', xo[:, hf * 512:(hf + 1) * 512], p_t[:, :], gbc[:, l, j, hf * 512:(hf + 1) * 512], ALU.mult,
                         [p_r, gbc_r], [xo_r])
                O.tt('pool', xo[:], xo[:], xi[:], ALU.add, [xo_r, xi_r], [xo_r])
                O.dma(xdst[r0 - dst_off:r0 - dst_off + 128, :], xo[:], [xo_r], rs_of(xdst_r, r0, 128))
        P.barrier()
```
